# Optimizing a Trainium2 kernel written in Bass

```python
import jax, jax.numpy as jnp
from jax import lax
import numpy as np

D_MODEL = 1024
BATCH = 8
SEQ = 4096
DEPTH = 1

HEAD_DIM = 64
DIL_PAIRS = ((128, 1), (512, 4), (2048, 16))
DIL_HEADS_PER_GROUP = 4
DIL_HEADS = DIL_HEADS_PER_GROUP * len(DIL_PAIRS)
DIL_WIDTH = DIL_HEADS * HEAD_DIM
DIL_OUT_WIDTH = DIL_HEADS_PER_GROUP * HEAD_DIM
SWA_WINDOW = 128
SWA_Q_HEADS = 8
SWA_KV_HEADS = 2
SWA_Q_WIDTH = SWA_Q_HEADS * HEAD_DIM
SWA_KV_WIDTH = SWA_KV_HEADS * HEAD_DIM
ROPE_THETA = 500000.0
ROPE_DIM = HEAD_DIM // 4
D_FF = 4 * D_MODEL
BLOCK = 128
EPS = 1e-6
NEG = -1e30
IN_SIZES = (DIL_WIDTH, DIL_WIDTH, DIL_WIDTH, SWA_Q_WIDTH, SWA_KV_WIDTH, SWA_KV_WIDTH, D_MODEL, D_MODEL)
IN_WIDTH = sum(IN_SIZES)

kernel_name = "hybrid_dilated_swa_sink_gated_block"


def rmsnorm(x, g):
    xf = x.astype(jnp.float32)
    y = xf * lax.rsqrt(jnp.mean(xf * xf, axis=-1, keepdims=True) + EPS)
    return (y * g.astype(jnp.float32)).astype(x.dtype)


def rope_tables(positions):
    inv_freq = ROPE_THETA ** (-jnp.arange(0, ROPE_DIM, 2, dtype=jnp.float32) / ROPE_DIM)
    ang = positions.astype(jnp.float32)[..., None] * inv_freq
    return jnp.cos(ang)[:, :, None, :], jnp.sin(ang)[:, :, None, :]


def apply_partial_rope(x, cos, sin):
    half = ROPE_DIM // 2
    xr = x[..., :ROPE_DIM].astype(jnp.float32)
    x1, x2 = xr[..., :half], xr[..., half:]
    rot = jnp.concatenate([x1 * cos - x2 * sin, x2 * cos + x1 * sin], axis=-1).astype(x.dtype)
    return jnp.concatenate([rot, x[..., ROPE_DIM:]], axis=-1)


def banded_attention(q, k, v, max_dist, sinks=None):
    B, N, L, H, Dh = q.shape
    Hkv = k.shape[3]
    G = H // Hkv
    Lp = -(-L // BLOCK) * BLOCK
    pad = Lp - L
    if pad:
        cfg = ((0, 0), (0, 0), (0, pad), (0, 0), (0, 0))
        q, k, v = jnp.pad(q, cfg), jnp.pad(k, cfg), jnp.pad(v, cfg)
    nb = Lp // BLOCK
    qb = q.reshape(B, N, nb, BLOCK, Hkv, G, Dh).astype(jnp.float32)

    def band(t):
        tp = jnp.pad(t, ((0, 0), (0, 0), (BLOCK, 0), (0, 0), (0, 0)))
        tb = tp.reshape(B, N, nb + 1, BLOCK, Hkv, Dh)
        return jnp.concatenate([tb[:, :, :-1], tb[:, :, 1:]], axis=3)

    kb = band(k).astype(jnp.float32)
    vb = band(v).astype(jnp.float32)
    scale = 1.0 / np.sqrt(Dh).astype(np.float32)
    s = jnp.einsum('bnjqhgd,bnjkhd->bnjhgqk', qb, kb) * scale
    blk = jnp.arange(nb)[:, None, None]
    qpos = blk * BLOCK + jnp.arange(BLOCK)[None, :, None]
    kpos = (blk - 1) * BLOCK + jnp.arange(2 * BLOCK)[None, None, :]
    dist = qpos - kpos
    mask = (dist >= 0) & (dist <= max_dist) & (kpos >= 0)
    s = jnp.where(mask[:, None, None, :, :], s, jnp.float32(NEG))
    m = jnp.max(s, axis=-1, keepdims=True)
    if sinks is not None:
        sk = sinks.astype(jnp.float32).reshape(Hkv, G)[:, :, None, None]
        m = jnp.maximum(m, sk)
        p = jnp.exp(s - m)
        denom = jnp.sum(p, axis=-1, keepdims=True) + jnp.exp(sk - m)
    else:
        p = jnp.exp(s - m)
        denom = jnp.sum(p, axis=-1, keepdims=True)
    o = jnp.einsum('bnjhgqk,bnjkhd->bnjqhgd', p / denom, vb)
    o = o.reshape(B, N, Lp, H, Dh)[:, :, :L].astype(v.dtype)
    lse = (m + jnp.log(denom))[..., 0]
    lse = lse.transpose(0, 1, 2, 5, 3, 4).reshape(B, N, Lp, H)[:, :, :L]
    return o, lse


def dilated_attention(q, k, v):
    B, S, _, Dh = q.shape
    outs, lses = [], []
    for g, (w, d) in enumerate(DIL_PAIRS):
        lo, hi = g * DIL_HEADS_PER_GROUP, (g + 1) * DIL_HEADS_PER_GROUP

        def strided(t):
            return t[:, :, lo:hi].reshape(B, S // d, d, DIL_HEADS_PER_GROUP, Dh).transpose(0, 2, 1, 3, 4)

        o, lse = banded_attention(strided(q), strided(k), strided(v), w // d)
        outs.append(o.transpose(0, 2, 1, 3, 4).reshape(B, S, DIL_HEADS_PER_GROUP, Dh))
        lses.append(lse.transpose(0, 2, 1, 3).reshape(B, S, DIL_HEADS_PER_GROUP))
    alpha = jax.nn.softmax(jnp.stack(lses, axis=0), axis=0)
    o = jnp.sum(alpha[..., None] * jnp.stack(outs, axis=0).astype(jnp.float32), axis=0)
    return o.reshape(B, S, DIL_OUT_WIDTH).astype(q.dtype)


def setup_inputs(seed: int = 0) -> dict:
    key = jax.random.key(seed)
    ks = jax.random.split(key, 16)
    f32 = jnp.float32

    def w(k, shape, fan_in):
        return jax.random.normal(k, shape, f32) * (fan_in ** -0.5)

    def gain(k, shape):
        return 1.0 + 0.02 * jax.random.normal(k, shape, f32)

    x = jax.random.normal(ks[0], (BATCH, SEQ, D_MODEL), f32)
    offset = jax.random.randint(ks[1], (BATCH, 1), 0, 1024, dtype=jnp.int32)
    positions = (offset + jnp.arange(SEQ, dtype=jnp.int32)[None, :]).astype(jnp.int32)
    return {
        "x": x,
        "positions": positions,
        "ln1_g": gain(ks[2], (DEPTH, D_MODEL)),
        "w_in": w(ks[3], (DEPTH, D_MODEL, IN_WIDTH), D_MODEL),
        "q_norm_a": gain(ks[4], (DEPTH, HEAD_DIM)),
        "k_norm_a": gain(ks[5], (DEPTH, HEAD_DIM)),
        "q_norm_b": gain(ks[6], (DEPTH, HEAD_DIM)),
        "k_norm_b": gain(ks[7], (DEPTH, HEAD_DIM)),
        "sinks": 0.5 * jax.random.normal(ks[8], (DEPTH, SWA_Q_HEADS), f32),
        "w_branch_a": w(ks[9], (DEPTH, DIL_OUT_WIDTH, D_MODEL), DIL_OUT_WIDTH),
        "w_branch_b": w(ks[10], (DEPTH, SWA_Q_WIDTH, D_MODEL), SWA_Q_WIDTH),
        "w_out": w(ks[11], (DEPTH, D_MODEL, D_MODEL), D_MODEL),
        "ln2_g": gain(ks[12], (DEPTH, D_MODEL)),
        "w_up": w(ks[13], (DEPTH, D_MODEL, D_FF), D_MODEL),
        "w_down": w(ks[14], (DEPTH, D_FF, D_MODEL), D_FF),
    }


def reference(x, positions, ln1_g, w_in, q_norm_a, k_norm_a, q_norm_b, k_norm_b, sinks,
              w_branch_a, w_branch_b, w_out, ln2_g, w_up, w_down):
    B, S, _ = x.shape
    cos, sin = rope_tables(positions)
    offsets = np.cumsum(np.array(IN_SIZES))[:-1].tolist()
    for l in range(DEPTH):
        h = rmsnorm(x, ln1_g[l])
        proj = h @ w_in[l]
        qa, ka, va, qb, kb, vb, ga, gb = jnp.split(proj, offsets, axis=-1)
        qa = apply_partial_rope(rmsnorm(qa.reshape(B, S, DIL_HEADS, HEAD_DIM), q_norm_a[l]), cos, sin)
        ka = apply_partial_rope(rmsnorm(ka.reshape(B, S, DIL_HEADS, HEAD_DIM), k_norm_a[l]), cos, sin)
        va = va.reshape(B, S, DIL_HEADS, HEAD_DIM)
        oa = dilated_attention(qa, ka, va)
        qb = apply_partial_rope(rmsnorm(qb.reshape(B, S, SWA_Q_HEADS, HEAD_DIM), q_norm_b[l]), cos, sin)
        kb = apply_partial_rope(rmsnorm(kb.reshape(B, S, SWA_KV_HEADS, HEAD_DIM), k_norm_b[l]), cos, sin)
        vb = vb.reshape(B, S, SWA_KV_HEADS, HEAD_DIM)
        ob, _ = banded_attention(qb[:, None], kb[:, None], vb[:, None], SWA_WINDOW - 1, sinks[l])
        ob = ob.reshape(B, S, SWA_Q_WIDTH)
        mix = jax.nn.sigmoid(ga) * (oa @ w_branch_a[l]) + jax.nn.sigmoid(gb) * (ob @ w_branch_b[l])
        x = x + mix @ w_out[l]
        h2 = rmsnorm(x, ln2_g[l])
        x = x + jnp.square(jax.nn.relu(h2 @ w_up[l])) @ w_down[l]
    return x
```

```python
import math
from contextlib import ExitStack

import numpy as np
import concourse.bass as bass
import concourse.mybir as mybir
from concourse.bass_utils import run_bass_kernel_spmd

F32 = mybir.dt.float32
BF16 = mybir.dt.bfloat16
I32 = mybir.dt.int32
AF = mybir.ActivationFunctionType
ALU = mybir.AluOpType

S = 4096
D = 1024
DFF = 4096
NCH = 8
NT = 32
EPS = 1e-6
ARENA_ELEMS = 105984

OFF_QA, OFF_KA, OFF_VA, OFF_QB, OFF_KB, OFF_VB, OFF_GA, OFF_GB = 0, 768, 1536, 2304, 2816, 2944, 3072, 4096

C_IDENT, C_BONES, C_PERM, C_MASK = 0, 128, 256, 384
C_BF_COLS = 384 + 4 * 512
C_INVF = C_BF_COLS
C_F32_COLS = 8
CST_COLS = C_BF_COLS + C_F32_COLS

DEBUG = False


def host_consts():
    c = np.zeros((128, CST_COLS), np.float32)
    c[:, C_IDENT:C_IDENT + 128] = np.eye(128, dtype=np.float32)
    bo = np.zeros((128, 128), np.float32)
    bo[0:64, 0:64] = 1.0
    bo[64:128, 64:128] = 1.0
    c[:, C_BONES:C_BONES + 128] = bo
    pm = np.zeros((128, 128), np.float32)
    for hb in (0, 64):
        for i in range(8):
            pm[hb + i + 8, hb + i] = -1.0
            pm[hb + i, hb + i + 8] = 1.0
    c[:, C_PERM:C_PERM + 128] = pm
    k = np.arange(128)[:, None]
    q = np.arange(128)[None, :]
    diag = (k <= q).astype(np.float32)
    prev_g = (k >= q).astype(np.float32)
    prev_b = (k > q).astype(np.float32)
    zero = np.zeros((128, 128), np.float32)
    masks = [
        np.concatenate([zero, diag, prev_g, diag], axis=1),
        np.concatenate([prev_g, diag, prev_g, diag], axis=1),
        np.concatenate([zero, diag, prev_b, diag], axis=1),
        np.concatenate([prev_b, diag, prev_b, diag], axis=1),
    ]
    for i, m in enumerate(masks):
        c[:, C_MASK + 512 * i:C_MASK + 512 * (i + 1)] = m
    inv_freq = (500000.0 ** (-np.arange(0, 16, 2, dtype=np.float32) / 16.0)).astype(np.float32)
    invf = np.zeros(128, np.float32)
    for p in range(128):
        if p % 64 < 16:
            invf[p] = inv_freq[(p % 64) % 8]
    c[:, C_INVF] = invf
    c[:, C_INVF + 1] = EPS
    return c


class Sched:
    ENGS = ("pe", "act", "dve", "pool", "sp")

    def __init__(self, nc, es):
        self.nc = nc
        self.es = es
        self.q = {e: [] for e in self.ENGS}
        self.res = {}
        self.sem = {e: es.enter_context(nc.semaphore("s_" + e)) for e in ("pe", "act", "dve", "pool")}
        self.dsem = {}
        self.dcnt = {}

    def _dma_sem(self, name):
        if name not in self.dsem:
            self.dsem[name] = self.es.enter_context(self.nc.semaphore("d_" + name))
            self.dcnt[name] = 0
        return self.dsem[name]

    def _deps(self, reads, writes):
        deps = set()
        for r in reads:
            st = self.res.get(r)
            if st and st["w"] is not None:
                deps.add(st["w"])
        for w in writes:
            st = self.res.get(w)
            if st:
                if st["w"] is not None:
                    deps.add(st["w"])
                for d in st["r"]:
                    deps.add(d)
        return deps

    def _commit(self, me, reads, writes):
        for r in reads:
            st = self.res.setdefault(r, {"w": None, "r": []})
            st["r"] = [d for d in st["r"] if d[0] != me[0]] + [me]
        for w in writes:
            self.res[w] = {"w": me, "r": []}

    def op(self, eng, fn, reads=(), writes=()):
        deps = self._deps(reads, writes)
        idx = len(self.q[eng])
        self.q[eng].append({"fn": fn, "deps": deps, "kind": "op", "marked": False})
        self._commit((eng, idx), reads, writes)
        return (eng, idx)

    def dma(self, eng, semname, fn, reads=(), writes=()):
        self._dma_sem(semname)
        deps = self._deps(reads, writes)
        self.dcnt[semname] += 1
        me = ("dma:" + semname, self.dcnt[semname])
        self.q[eng].append({"fn": fn, "deps": deps, "kind": "dma", "sem": semname})
        self._commit(me, reads, writes)
        return me

    def barrier(self):
        deps = set()
        for e in ("pe", "act", "dve", "pool"):
            for i in range(len(self.q[e]) - 1, -1, -1):
                if self.q[e][i]["kind"] == "op":
                    deps.add((e, i))
                    break
        for name, cnt in self.dcnt.items():
            if cnt:
                deps.add(("dma:" + name, cnt))
        for e in self.ENGS:
            self.q[e].append({"fn": None, "deps": set(deps), "kind": "bar"})
        self.res = {}

    def final_wait(self, eng, semnames):
        deps = set(("dma:" + n, self.dcnt[n]) for n in semnames if self.dcnt.get(n))
        self.q[eng].append({"fn": None, "deps": deps, "kind": "bar"})

    def finalize(self):
        for e in self.ENGS:
            for ins in self.q[e]:
                for (dom, idx) in ins["deps"]:
                    if not dom.startswith("dma:"):
                        if dom == "pe" and e == "pe":
                            continue
                        self.q[dom][idx]["marked"] = True
        self.ordinal = {}
        for e in ("pe", "act", "dve", "pool"):
            n = 0
            for i, ins in enumerate(self.q[e]):
                if ins.get("marked"):
                    n += 1
                    self.ordinal[(e, i)] = n
        self.total_incs = n

    def replay(self, eng, eobj):
        seen = {}
        for ins in self.q[eng]:
            need = {}
            for (dom, idx) in ins["deps"]:
                if dom.startswith("dma:"):
                    val = 16 * idx
                else:
                    if dom == "pe" and eng == "pe":
                        continue
                    val = self.ordinal[(dom, idx)]
                if val > need.get(dom, 0):
                    need[dom] = val
            for dom, val in need.items():
                if seen.get(dom, 0) >= val:
                    continue
                seen[dom] = val
                sem = self.dsem[dom[4:]] if dom.startswith("dma:") else self.sem[dom]
                eobj.wait_ge(sem, val)
            if ins["fn"] is None:
                continue
            bi = ins["fn"](eobj)
            if ins["kind"] == "dma":
                bi.then_inc(self.dsem[ins["sem"]], 16)
            elif ins.get("marked"):
                bi.then_inc(self.sem[eng], 1)


class Mem:
    def __init__(self, arena):
        self.h = {BF16: arena, F32: arena.bitcast(F32), I32: arena.bitcast(I32)}
        self.pstep = {BF16: ARENA_ELEMS, F32: ARENA_ELEMS // 2, I32: ARENA_ELEMS // 2}

    def ap(self, dt, byte_off, shape, parts=128, p0=0):
        esz = 2 if dt == BF16 else 4
        assert byte_off % esz == 0
        dims = [[self.pstep[dt], parts]]
        stride = 1
        rev = []
        for n in reversed(shape):
            rev.append([stride, n])
            stride *= n
        dims += list(reversed(rev))
        assert byte_off + stride * esz <= ARENA_ELEMS * 2, (byte_off, stride, esz)
        return bass.AP(self.h[dt], p0 * self.pstep[dt] + byte_off // esz, dims)


KB = 1024


def build_program():
    nc = bass.Bass("TRN2", target_bir_lowering=False)
    dr = {}

    def din(name, shape, dt=F32):
        dr[name] = nc.dram_tensor(name, shape, dt, kind="ExternalInput")
        return dr[name].ap()

    x_d = din("x", [S, D])
    pos_d = din("pos", [1, S], I32)
    cst_d = din("cst", [128, CST_COLS])
    ln1_d = din("ln1_g", [1, D])
    ln2_d = din("ln2_g", [1, D])
    win_d = din("w_in", [D, 5120])
    qna_d = din("q_norm_a", [1, 64])
    kna_d = din("k_norm_a", [1, 64])
    qnb_d = din("q_norm_b", [1, 64])
    knb_d = din("k_norm_b", [1, 64])
    snk_d = din("sinks", [1, 8])
    wa_d = din("w_branch_a", [256, D])
    wb_d = din("w_branch_b", [512, D])
    wo_d = din("w_out", [D, D])
    wu_d = din("w_up", [D, DFF])
    wd_d = din("w_down", [DFF, D])
    out_h = nc.dram_tensor("out", [S, D], F32, kind="ExternalOutput")
    out_d = out_h.ap()
    x1_h = nc.dram_tensor("x1_scratch", [S, D], F32, kind="Internal")
    x1_d = x1_h.ap()
    dbg = {}
    if DEBUG:
        for name, shape, dt in (("dbg_hT", [128, 8 * S], BF16), ("dbg_oaT", [128, 2 * S], BF16),
                                ("dbg_obT", [128, 4 * S], BF16), ("dbg_tab", [128, 2 * S], BF16),
                                ("dbg_qk", [128, 2 * S], BF16)):
            dbg[name] = nc.dram_tensor(name, shape, dt, kind="ExternalOutput").ap()

    with ExitStack() as es:
        arena = es.enter_context(nc.sbuf_tensor("arena", [128, ARENA_ELEMS], BF16))
        mem = Mem(arena)
        banks = [es.enter_context(nc.psum_tensor("bank%d" % i, [128, 512], F32)) for i in range(8)]
        sch = Sched(nc, es)
        import os as _os
        _stop = _os.environ.get("KSTOP", "")

        class _Stop(Exception):
            pass

        def checkpoint(name):
            if _stop == name:
                raise _Stop()

        def bank_f32(i):
            return banks[i][:, :]

        def bank_bf16(i):
            return banks[i][:, :].bitcast(BF16)

        o = 0
        IDENT = mem.ap(BF16, o, [128]); o += 256
        BONES = mem.ap(BF16, o, [128]); o += 256
        PERM = mem.ap(BF16, o, [128]); o += 256
        MASKS = mem.ap(BF16, o, [4, 512]); o += 4096
        CF32 = mem.ap(F32, o, [C_F32_COLS]); o += 4 * C_F32_COLS
        GAINS = mem.ap(F32, o, [4]); o += 16
        ESINK = mem.ap(F32, o, [8]); o += 32
        o = (o + 63) // 64 * 64
        assert o <= 6 * KB
        R_H = 6 * KB
        R_O = 70 * KB
        R_T = 118 * KB
        R_W = 134 * KB
        R_END = ARENA_ELEMS * 2
        hT = mem.ap(BF16, R_H, [8, S])
        TC = mem.ap(BF16, R_T, [S])
        TS = mem.ap(BF16, R_T + 8 * KB, [S])
        oaT = mem.ap(BF16, R_O, [2, S])
        obT = mem.ap(BF16, R_O + 16 * KB, [4, S])
        INVF = CF32[:, 0:1]
        EPSC = CF32[:, 1:2]

        def emit_all():
            cbf = mem.ap(BF16, 0, [C_BF_COLS])
            sch.dma("pool", "cstb", lambda e: e.dma_start(out=cbf, in_=cst_d[:, 0:C_BF_COLS]), writes=["consts"])
            sch.dma("sp", "cst", lambda e: e.dma_start(out=CF32, in_=cst_d[:, C_BF_COLS:CST_COLS]), writes=["consts"])
            for gi, gd in enumerate((qna_d, kna_d, qnb_d, knb_d)):
                for hb in (0, 64):
                    src = bass.AP(gd.tensor, 0, [[1, 64], [1, 1]])
                    sch.dma("sp", "cst", lambda e, gi=gi, hb=hb, src=src: e.dma_start(out=GAINS[hb:hb + 64, gi:gi + 1], in_=src),
                            writes=["consts"])
            snk_b = bass.AP(snk_d.tensor, 0, [[0, 128], [1, 8]])
            sch.dma("sp", "cst", lambda e: e.dma_start(out=ESINK, in_=snk_b), writes=["consts"])
            sch.op("act", lambda e: e.activation(out=ESINK, in_=ESINK, func=AF.Exp), reads=["consts"], writes=["esink"])

            tA = mem.ap(F32, R_W, [S])
            tAi = mem.ap(I32, R_W, [S])
            tB = mem.ap(F32, R_W + 16 * KB, [S])
            tBi = mem.ap(I32, R_W + 16 * KB, [S])
            tM = mem.ap(F32, R_W + 32 * KB, [S])
            pos_b = bass.AP(pos_d.tensor, 0, [[0, 128], [1, S]])
            sch.dma("sp", "pos", lambda e: e.dma_start(out=tAi, in_=pos_b), writes=["tA"])
            sch.op("dve", lambda e: e.tensor_copy(out=tA, in_=tAi), reads=["tA"], writes=["tA"])
            sch.op("dve", lambda e: e.tensor_scalar(out=tA, in0=tA, scalar1=INVF, scalar2=None, op0=ALU.mult),
                   reads=["tA", "consts"], writes=["tA"])
            sch.op("dve", lambda e: e.tensor_scalar(out=tA, in0=tA, scalar1=float(1.0 / (2 * math.pi)), scalar2=None, op0=ALU.mult),
                   reads=["tA"], writes=["tA"])
            for which, tab in ((0, TS), (1, TC)):
                if which == 1:
                    sch.op("dve", lambda e: e.tensor_scalar(out=tA, in0=tA, scalar1=0.25, scalar2=None, op0=ALU.add),
                           reads=["tA"], writes=["tA"])
                sch.op("dve", lambda e: e.tensor_copy(out=tBi, in_=tA), reads=["tA"], writes=["tB"])
                sch.op("dve", lambda e: e.tensor_copy(out=tB, in_=tBi), reads=["tB"], writes=["tB"])
                sch.op("dve", lambda e: e.tensor_tensor(out=tB, in0=tA, in1=tB, op=ALU.subtract), reads=["tA", "tB"], writes=["tB"])
                sch.op("dve", lambda e: e.tensor_single_scalar(out=tM, in_=tB, scalar=0.5, op=ALU.is_gt), reads=["tB"], writes=["tM"])
                sch.op("dve", lambda e: e.tensor_tensor(out=tB, in0=tB, in1=tM, op=ALU.subtract), reads=["tB", "tM"], writes=["tB"])
                sch.op("dve", lambda e: e.tensor_single_scalar(out=tM, in_=tB, scalar=-0.5, op=ALU.is_lt), reads=["tB"], writes=["tM"])
                sch.op("dve", lambda e: e.tensor_tensor(out=tB, in0=tB, in1=tM, op=ALU.add), reads=["tB", "tM"], writes=["tB"])
                sch.op("act", lambda e, tab=tab: e.activation(out=tab, in_=tB, func=AF.Sin, scale=6.283185),
                       reads=["tB"], writes=["tab%d" % which])
            if DEBUG:
                sch.dma("sp", "dbg", lambda e: e.dma_start(out=dbg["dbg_tab"][:, 0:S], in_=TC), reads=["tab1"])
                sch.dma("sp", "dbg", lambda e: e.dma_start(out=dbg["dbg_tab"][:, S:2 * S], in_=TS), reads=["tab0"])

            sch.barrier()
            checkpoint("T")
            def prenorm_tile(xsrc_ap, xs, xs_name, sem_name, g_b, hb, hb_name, junk, ss_col, ss_name, tp_bank, dst_ap, dst_names,
                             load_eng="sp", junk_name="junk"):
                sch.dma(load_eng, sem_name, lambda e: e.dma_start(out=xs, in_=xsrc_ap), writes=[xs_name])
                sch.op("act", lambda e: e.activation(out=junk, in_=xs, func=AF.Square, accum_out=ss_col),
                       reads=[xs_name], writes=[junk_name, ss_name])
                sch.op("act", lambda e: e.activation(out=ss_col, in_=ss_col, func=AF.Ln, scale=1.0 / D, bias=EPSC),
                       reads=[ss_name, "consts"], writes=[ss_name])
                sch.op("act", lambda e: e.activation(out=ss_col, in_=ss_col, func=AF.Exp, scale=-0.5),
                       reads=[ss_name], writes=[ss_name])
                sch.op("dve", lambda e: e.scalar_tensor_tensor(out=hb, in0=xs, scalar=ss_col, in1=g_b, op0=ALU.mult, op1=ALU.mult),
                       reads=[xs_name, ss_name, "gb"], writes=[hb_name])
                pT = bank_bf16(tp_bank)
                for k in range(8):
                    sch.op("pe", lambda e, k=k: e.transpose(out=pT[:, k * 128:(k + 1) * 128], in_=hb[:, k * 128:(k + 1) * 128], identity=IDENT),
                           reads=[hb_name, "consts"], writes=["bank%d" % tp_bank])
                sch.op("act", lambda e: e.activation(out=dst_ap, in_=pT.rearrange("p (k t) -> p k t", k=8), func=AF.Copy),
                       reads=["bank%d" % tp_bank], writes=dst_names)

            p0 = R_W + 48 * KB
            XS = [mem.ap(F32, p0 + i * 4 * KB, [D]) for i in range(3)]
            HB = [mem.ap(BF16, p0 + 12 * KB + i * 2 * KB, [D]) for i in range(2)]
            JUNK = mem.ap(BF16, p0 + 16 * KB, [D])
            GB1 = mem.ap(F32, p0 + 18 * KB, [D])
            SSC = mem.ap(F32, p0 + 22 * KB, [NT])
            assert p0 + 22 * KB + 4 * NT <= R_END
            g1_b = bass.AP(ln1_d.tensor, 0, [[0, 128], [1, D]])
            sch.dma("sp", "gb", lambda e: e.dma_start(out=GB1, in_=g1_b), writes=["gb"])
            for t in range(NT):
                prenorm_tile(x_d[t * 128:(t + 1) * 128, :], XS[t % 3], "xs%d" % (t % 3), "xs%d" % (t % 3), GB1,
                             HB[t % 2], "hb%d" % (t % 2), JUNK, SSC[:, t:t + 1], "ss%d" % t, t % 2,
                             hT[:, :, t * 128:(t + 1) * 128], [("hT", t)])
            if DEBUG:
                sch.dma("sp", "dbg", lambda e: e.dma_start(out=dbg["dbg_hT"], in_=hT.rearrange("p k t -> p (k t)")),
                        reads=[("hT", t) for t in range(NT)])
            sch.barrier()

            checkpoint("P0")
            w0 = R_W
            WP = [mem.ap(BF16, w0 + i * 6 * KB, [8, 384]) for i in range(2)]; w0 += 12 * KB
            QT = mem.ap(BF16, w0, [S]); w0 += 8 * KB
            KT = mem.ap(BF16, w0, [S]); w0 += 8 * KB
            VG = mem.ap(BF16, w0, [NT, 2, 128]); w0 += 16 * KB
            SQ = [mem.ap(BF16, w0 + i * KB, [512]) for i in range(2)]; w0 += 2 * KB
            RV = [mem.ap(F32, w0 + i * 2 * KB, [512]) for i in range(2)]; w0 += 4 * KB
            QN = [mem.ap(BF16, w0 + i * KB, [512]) for i in range(2)]; w0 += 2 * KB
            T1 = [mem.ap(F32, w0 + i * 2 * KB, [512]) for i in range(2)]; w0 += 4 * KB
            T2 = [mem.ap(F32, w0 + i * 2 * KB, [512]) for i in range(2)]; w0 += 4 * KB
            PT = [mem.ap(BF16, w0 + i * KB, [512]) for i in range(4)]; w0 += 4 * KB
            RD = [mem.ap(F32, w0 + i * 2 * KB, [512]) for i in range(2)]; w0 += 4 * KB
            assert w0 <= R_END, w0
            ACC = mem.ap(F32, R_O + 16 * KB, [2, S])
            sch.op("pool", lambda e: e.memset(VG[:, :, 0, 64:128], 1.0), writes=["vg_ones"])
            sch.op("pool", lambda e: e.memset(VG[:, :, 1, 0:64], 1.0), writes=["vg_ones"])

            B_PJ = (0, 1)
            B_SS, B_PM, B_SA, B_SB, B_OT, B_VP = 2, 3, 4, 5, 6, 7
            cnt = {"pj": 0, "blk": 0, "pt": 0, "rd": 0, "w": 0}

            def gcol_ap(buf, d, c):
                L = S // d
                u = 512 // d
                return buf.rearrange("p (r l) -> p r l", r=d)[:, :, u * c:u * (c + 1)]

            def nat_ap(t, d):
                return t.rearrange("p (u r) -> p r u", r=d)

            def gblocks_of_chunk(d, c):
                L = S // d
                u = 512 // d
                blks = set()
                for r in range(d):
                    for col in range(r * L + u * c, r * L + u * (c + 1), min(u, 128)):
                        blks.add(col // 128)
                return sorted(blks)

            def attention_pass(name, d, qcol, kcol, vcol, vdup, gq_idx, gk_idx, mask_base, acc_mode, sink_heads, out_fchunk_ap):
                L = S // d
                wslot = cnt["w"] % 2
                cnt["w"] += 1
                W = WP[wslot]
                wname = "wp%d" % wslot
                ncols = 384 if kcol is not None else 128

                def wload(dst_lo, src_lo, n):
                    src = win_d[:, src_lo:src_lo + n].rearrange("(k p) n -> p k n", p=128)
                    sch.dma("pool", wname, lambda e: e.dma_start(out=W[:, :, dst_lo:dst_lo + n], in_=src), writes=[wname])
                wload(0, qcol, 128)
                if kcol is not None:
                    if vdup:
                        wload(128, kcol, 64)
                        wload(192, kcol, 64)
                        wload(256, OFF_VB, 128)
                    else:
                        wload(128, kcol, 128)
                        wload(256, vcol, 128)

                def proj_qk(c, which):
                    pj = B_PJ[cnt["pj"] % 2]
                    cnt["pj"] += 1
                    b = cnt["blk"] % 2
                    cnt["blk"] += 1
                    pjn = "bank%d" % pj
                    for k in range(8):
                        sch.op("pe", lambda e, k=k: e.matmul(bank_f32(pj), lhsT=W[:, k, which * 128:(which + 1) * 128],
                                                             rhs=hT[:, k, c * 512:(c + 1) * 512], start=(k == 0), stop=(k == 7)),
                               reads=[wname] + [("hT", t) for t in range(4 * c, 4 * c + 4)], writes=[pjn])
                    sch.op("act", lambda e: e.activation(out=SQ[b], in_=bank_f32(pj), func=AF.Square), reads=[pjn], writes=["sq%d" % b])
                    sch.op("pe", lambda e: e.matmul(bank_f32(B_SS), lhsT=BONES, rhs=SQ[b], start=True, stop=True),
                           reads=["sq%d" % b, "consts"], writes=["bank%d" % B_SS])
                    sch.op("act", lambda e: e.activation(out=RV[b], in_=bank_f32(B_SS), func=AF.Ln, scale=1.0 / 64, bias=EPSC),
                           reads=["bank%d" % B_SS, "consts"], writes=["rv%d" % b])
                    sch.op("act", lambda e: e.activation(out=RV[b], in_=RV[b], func=AF.Exp, scale=-0.5), reads=["rv%d" % b], writes=["rv%d" % b])
                    gi = gq_idx if which == 0 else gk_idx
                    sch.op("dve", lambda e: e.scalar_tensor_tensor(out=QN[b], in0=bank_f32(pj), scalar=GAINS[:, gi:gi + 1], in1=RV[b],
                                                                   op0=ALU.mult, op1=ALU.mult),
                           reads=[pjn, "rv%d" % b, "consts"], writes=["qn%d" % b])
                    sch.op("pe", lambda e: e.matmul(bank_f32(B_PM), lhsT=PERM, rhs=QN[b], start=True, stop=True),
                           reads=["qn%d" % b, "consts"], writes=["bank%d" % B_PM])
                    sch.op("pool", lambda e: e.tensor_tensor(out=T1[b], in0=QN[b], in1=TC[:, c * 512:(c + 1) * 512], op=ALU.mult),
                           reads=["qn%d" % b, "tab1"], writes=["t1%d" % b])
                    sch.op("dve", lambda e: e.tensor_tensor(out=T2[b], in0=bank_f32(B_PM), in1=TS[:, c * 512:(c + 1) * 512], op=ALU.mult),
                           reads=["bank%d" % B_PM, "tab0"], writes=["t2%d" % b])
                    dst = QT if which == 0 else KT
                    dname = "qt" if which == 0 else "kt"
                    sch.op("pool", lambda e: e.tensor_tensor(out=gcol_ap(dst, d, c), in0=nat_ap(T1[b], d), in1=nat_ap(T2[b], d), op=ALU.add),
                           reads=["t1%d" % b, "t2%d" % b], writes=[(dname, g) for g in gblocks_of_chunk(d, c)])

                def proj_v(gb):
                    r, j = gb // (L // 128), gb % (L // 128)
                    t0 = r + d * 128 * j
                    nv = 128
                    toks = sorted(set((t0 + d * i) // 128 for i in (0, 127)))
                    tiles = list(range(toks[0], toks[-1] + 1))
                    vp = bank_f32(B_VP)
                    for k in range(8):
                        lhsT = hT[:, k, t0:t0 + d * 127 + 1:d]
                        sch.op("pe", lambda e, k=k, lhsT=lhsT: e.matmul(vp[:, 0:nv], lhsT=lhsT, rhs=W[:, k, 256:256 + nv],
                                                                       start=(k == 0), stop=(k == 7)),
                               reads=[wname] + [("hT", t) for t in tiles], writes=["bank%d" % B_VP])
                    vg0 = VG[:, gb, 0, 0:64]
                    dst = bass.AP(vg0.tensor, vg0.offset, [list(vg0.ap[0]), [192, 2], [1, 64]])
                    if vdup:
                        kvsel = (vcol - OFF_VB) // 64
                        v0 = vp[:, kvsel * 64:(kvsel + 1) * 64]
                        src = bass.AP(v0.tensor, v0.offset, [list(v0.ap[0]), [0, 2], [1, 64]])
                    else:
                        src = vp[:, 0:128].rearrange("p (h c) -> p h c", h=2)
                    sch.op("act", lambda e: e.activation(out=dst, in_=src, func=AF.Copy), reads=["bank%d" % B_VP, "vg_ones"],
                           writes=[("vg", gb), ("vgb", gb)])

                def attn_round(n):
                    gb0 = 2 * n
                    bpr = L // 128
                    r, j0 = gb0 // bpr, gb0 % bpr
                    first = (j0 == 0)
                    mask = MASKS[:, mask_base + (0 if first else 1), :]
                    sbanks = (B_SA, B_SB)
                    pts = []
                    for half in range(2):
                        sb = bank_f32(sbanks[half])
                        rows = slice(64 * half, 64 * half + 64)
                        for qi in range(2):
                            gq = gb0 + qi
                            for kb in range(2):
                                gk = gq - 1 + kb
                                if gk < r * bpr:
                                    gk = gq
                                sch.op("pe", lambda e, sb=sb, rows=rows, qi=qi, kb=kb, gk=gk, gq=gq: e.matmul(
                                    sb[:, (2 * qi + kb) * 128:(2 * qi + kb + 1) * 128], lhsT=KT[rows, gk * 128:(gk + 1) * 128],
                                    rhs=QT[rows, gq * 128:(gq + 1) * 128], start=True, stop=True),
                                    reads=[("kt", gk), ("qt", gq)], writes=["bank%d" % sbanks[half]])
                        p = cnt["pt"] % 4
                        cnt["pt"] += 1
                        pts.append(p)
                        sch.op("act", lambda e, sb=sb, p=p: e.activation(out=PT[p], in_=sb, func=AF.Exp, scale=0.125),
                               reads=["bank%d" % sbanks[half]], writes=["pt%d" % p])
                        sch.op("dve", lambda e, p=p: e.tensor_tensor(out=PT[p], in0=PT[p], in1=mask, op=ALU.mult),
                               reads=["pt%d" % p, "consts"], writes=["pt%d" % p])
                    ot = bank_f32(B_OT)
                    nmm = 0
                    for half in range(2):
                        for qi in range(2):
                            gq = gb0 + qi
                            for kb in range(2):
                                gk = gq - 1 + kb
                                if gk < r * bpr:
                                    gk = gq
                                item = 2 * half + qi
                                sch.op("pe", lambda e, half=half, qi=qi, kb=kb, gk=gk, item=item, nmm=nmm: e.matmul(
                                    ot[:, item * 128:(item + 1) * 128], lhsT=VG[:, gk, half, :],
                                    rhs=PT[pts[half]][:, (2 * qi + kb) * 128:(2 * qi + kb + 1) * 128],
                                    start=(nmm == 0), stop=(kb == 1), skip_group_check=True),
                                    reads=[("vg", gk), ("vgb", gk), "pt%d" % pts[half]], writes=["bank%d" % B_OT])
                                nmm += 1
                    tok0 = r + d * 128 * j0
                    otv = ot.rearrange("p (h q i) -> p h q i", h=2, q=2)
                    if sink_heads is None:
                        accv = bass.AP(ACC.tensor, ACC.offset + tok0, [list(ACC.ap[0]), [S, 2], [128 * d, 2], [d, 128]])
                        if acc_mode == "copy":
                            sch.op("act", lambda e: e.activation(out=accv, in_=otv, func=AF.Copy), reads=["bank%d" % B_OT],
                                   writes=[("acc", n2) for n2 in acc_tiles(tok0, d)])
                        else:
                            sch.op("dve", lambda e: e.tensor_tensor(out=accv, in0=otv, in1=accv, op=ALU.add), reads=["bank%d" % B_OT],
                                   writes=[("acc", n2) for n2 in acc_tiles(tok0, d)])
                    else:
                        geo = []
                        for half in range(2):
                            num = slice(0, 64) if half == 0 else slice(64, 128)
                            den = slice(64, 128) if half == 0 else slice(0, 64)
                            geo.append((half, num, den, slice(256 * half, 256 * half + 256), sink_heads[half]))
                        for (half, num, den, cols, hsink) in geo:
                            sch.op("act", lambda e, half=half, num=num, den=den, cols=cols, hsink=hsink: e.activation(
                                out=RD[half][num, 0:256], in_=ot[den, cols], func=AF.Ln, bias=ESINK[num, hsink:hsink + 1]),
                                reads=["bank%d" % B_OT, "esink"], writes=["rd%d" % half, "ot_act_done"])
                        for (half, num, den, cols, hsink) in geo:
                            sch.op("act", lambda e, half=half, num=num: e.activation(out=RD[half][num, 0:256], in_=RD[half][num, 0:256],
                                                                                     func=AF.Exp, scale=-1.0),
                                   reads=["rd%d" % half], writes=["rd%d" % half])
                        for (half, num, den, cols, hsink) in geo:
                            sch.op("dve", lambda e, half=half, num=num, cols=cols: e.tensor_tensor(
                                out=out_fchunk_ap[num, tok0:tok0 + 256], in0=ot[num, cols], in1=RD[half][num, 0:256], op=ALU.mult),
                                reads=["bank%d" % B_OT, "rd%d" % half, "ot_act_done"], writes=[("ob", name, tok0)])

                def acc_tiles(tok0, d):
                    lo = tok0 // 512
                    hi = (tok0 + d * 255) // 512
                    return list(range(lo, hi + 1))

                bpr = L // 128
                ready_rounds = []

                def rounds_ready_after_chunk(c):
                    res = []
                    for n in range(16):
                        gb1 = 2 * n + 1
                        r, j = gb1 // bpr, gb1 % bpr
                        need = (r + d * (128 * j + 127)) // 512
                        if need == c:
                            res.append(n)
                    return res

                pending = []
                for c in range(NCH):
                    proj_qk(c, 0)
                    if kcol is not None:
                        proj_qk(c, 1)
                        for gb in range(NT):
                            r, j = gb // bpr, gb % bpr
                            last_tok = r + d * (128 * j + 127)
                            if last_tok // 512 == c and not ("nov" in _os.environ.get("KDBG", "") and sink_heads is not None):
                                proj_v(gb)
                    if not ("noattn" in _os.environ.get("KDBG", "") and sink_heads is not None):
                        for n in pending:
                            attn_round(n)
                    pending = rounds_ready_after_chunk(c)
                if not ("noattn" in _os.environ.get("KDBG", "") and sink_heads is not None):
                    for n in pending:
                        attn_round(n)

                if acc_mode == "final":
                    for c in range(NCH):
                        cs = slice(c * 512, (c + 1) * 512)
                        for half in range(2):
                            rb = cnt["rd"] % 2
                            cnt["rd"] += 1
                            num = slice(0, 64) if half == 0 else slice(64, 128)
                            den = slice(64, 128) if half == 0 else slice(0, 64)
                            sch.op("act", lambda e, rb=rb, den=den, num=num, cs=cs, half=half: e.activation(
                                out=RD[rb][num, :], in_=ACC[den, half, cs], func=AF.Ln), reads=[("acc", c)], writes=["rd%d" % rb])
                            sch.op("act", lambda e, rb=rb, num=num: e.activation(out=RD[rb][num, :], in_=RD[rb][num, :], func=AF.Exp, scale=-1.0),
                                   reads=["rd%d" % rb], writes=["rd%d" % rb])
                            sch.op("pool", lambda e, rb=rb, num=num, cs=cs, half=half: e.tensor_tensor(
                                out=out_fchunk_ap[num, cs], in0=ACC[num, half, cs], in1=RD[rb][num, :], op=ALU.mult),
                                reads=[("acc", c), "rd%d" % rb], writes=[("oa", name, c)])

            _kd = _os.environ.get("KDBG", "")
            if "swafirst" in _kd:
                attention_pass("b0_0", 1, OFF_QB, OFF_KB, OFF_VB, True, 2, 3, 2, "none", (0, 1), obT[:, 0, :])
                sch.barrier()
                checkpoint("B0")
            for sp in range(2):
                for (g, d, mode) in ((2, 16, "copy"), (1, 4, "add"), (0, 1, "final")):
                    attention_pass("g%d_%d" % (g, sp), d, OFF_QA + g * 256 + sp * 128, OFF_KA + g * 256 + sp * 128,
                                   OFF_VA + g * 256 + sp * 128, False, 0, 1, 0, mode, None, oaT[:, sp, :])
                sch.barrier()
                checkpoint("A%d" % sp)
            for kv in range(2):
                for f in range(2):
                    heads = (4 * kv + 2 * f, 4 * kv + 2 * f + 1)
                    attention_pass("b%d_%d" % (kv, f), 1, OFF_QB + heads[0] * 64,
                                   (OFF_KB + kv * 64) if f == 0 else None, (OFF_VB + kv * 64) if f == 0 else None,
                                   True, 2, 3, 2, "none", heads, obT[:, 2 * kv + f, :])
            if DEBUG:
                sch.dma("sp", "dbg", lambda e: e.dma_start(out=dbg["dbg_qk"][:, 0:S], in_=QT), reads=[("qt", g) for g in range(NT)])
                sch.dma("sp", "dbg", lambda e: e.dma_start(out=dbg["dbg_qk"][:, S:2 * S], in_=KT), reads=[("kt", g) for g in range(NT)])
            sch.barrier()
            if DEBUG:
                sch.dma("sp", "dbg", lambda e: e.dma_start(out=dbg["dbg_oaT"], in_=oaT.rearrange("p k t -> p (k t)")))
                sch.dma("sp", "dbg", lambda e: e.dma_start(out=dbg["dbg_obT"], in_=obT.rearrange("p k t -> p (k t)")))

            checkpoint("A")
            w0 = R_T
            WG = mem.ap(BF16, w0, [8, 2048]); w0 += 32 * KB
            WA = mem.ap(BF16, w0, [2, D]); w0 += 4 * KB
            WB = mem.ap(BF16, w0, [4, D]); w0 += 8 * KB
            WO = mem.ap(BF16, w0, [8, D]); w0 += 16 * KB
            TA = [mem.ap(BF16, w0 + i * KB, [512]) for i in range(2)]; w0 += 2 * KB
            TB = [mem.ap(BF16, w0 + i * KB, [512]) for i in range(2)]; w0 += 2 * KB
            UU = [mem.ap(F32, w0 + i * 2 * KB, [512]) for i in range(2)]; w0 += 4 * KB
            VV = [mem.ap(F32, w0 + i * 2 * KB, [512]) for i in range(2)]; w0 += 4 * KB
            MIX = [mem.ap(BF16, w0, [8, 512]) for i in range(2)]; w0 += 8 * KB
            X5 = [mem.ap(F32, w0 + i * 4 * KB, [D]) for i in range(2)]; w0 += 8 * KB
            assert w0 <= R_END
            for piece in range(4):
                src = win_d[:, OFF_GA + piece * 512:OFF_GA + (piece + 1) * 512].rearrange("(k p) n -> p k n", p=128)
                sch.dma("pool", "wg", lambda e, piece=piece, src=src: e.dma_start(out=WG[:, :, piece * 512:(piece + 1) * 512], in_=src), writes=["wg"])
            sch.dma("pool", "wab", lambda e: e.dma_start(out=WA, in_=wa_d.rearrange("(k p) n -> p k n", p=128)), writes=["wab"])
            sch.dma("pool", "wab", lambda e: e.dma_start(out=WB, in_=wb_d.rearrange("(k p) n -> p k n", p=128)), writes=["wab"])
            sch.dma("pool", "wo", lambda e: e.dma_start(out=WO, in_=wo_d.rearrange("(k p) n -> p k n", p=128)), writes=["wo"])
            B_GA, B_GB, B_YA, B_YB, B_O = (0, 1), (2, 3), 4, 5, (6, 7)
            n5 = {"g": 0, "o": 0, "x": 0}
            mix = MIX[0]
            mixn = "mix0"

            def p5_merge(c, m):
                cs = slice(c * 512, (c + 1) * 512)
                gi = n5["g"] % 2
                n5["g"] += 1
                bga, bgb = B_GA[gi], B_GB[gi]

                def gate_mm(bk, coff):
                    for k in range(8):
                        sch.op("pe", lambda e, k=k: e.matmul(bank_f32(bk), lhsT=WG[:, k, coff:coff + 128], rhs=hT[:, k, cs],
                                                             start=(k == 0), stop=(k == 7)),
                               reads=["wg"], writes=["bank%d" % bk])
                gate_mm(bga, m * 128)
                gate_mm(bgb, 1024 + m * 128)
                for k in range(2):
                    sch.op("pe", lambda e, k=k: e.matmul(bank_f32(B_YA), lhsT=WA[:, k, m * 128:(m + 1) * 128], rhs=oaT[:, k, cs],
                                                         start=(k == 0), stop=(k == 1)), reads=["wab"], writes=["bank%d" % B_YA])
                for k in range(4):
                    sch.op("pe", lambda e, k=k: e.matmul(bank_f32(B_YB), lhsT=WB[:, k, m * 128:(m + 1) * 128], rhs=obT[:, k, cs],
                                                         start=(k == 0), stop=(k == 3)), reads=["wab"], writes=["bank%d" % B_YB])
                sch.op("act", lambda e: e.activation(out=TA[gi], in_=bank_f32(bga), func=AF.Tanh, scale=0.5),
                       reads=["bank%d" % bga], writes=["ta%d" % gi])
                sch.op("act", lambda e: e.activation(out=TB[gi], in_=bank_f32(bgb), func=AF.Tanh, scale=0.5),
                       reads=["bank%d" % bgb], writes=["tb%d" % gi])
                sch.op("dve", lambda e: e.scalar_tensor_tensor(out=UU[gi], in0=TA[gi], scalar=1.0, in1=bank_f32(B_YA), op0=ALU.add, op1=ALU.mult),
                       reads=["ta%d" % gi, "bank%d" % B_YA], writes=["uu%d" % gi])
                sch.op("dve", lambda e: e.scalar_tensor_tensor(out=VV[gi], in0=TB[gi], scalar=1.0, in1=bank_f32(B_YB), op0=ALU.add, op1=ALU.mult),
                       reads=["tb%d" % gi, "bank%d" % B_YB], writes=["vv%d" % gi])
                sch.op("pool", lambda e: e.tensor_tensor(out=mix[:, m, :], in0=UU[gi], in1=VV[gi], op=ALU.add),
                       reads=["uu%d" % gi, "vv%d" % gi], writes=[(mixn, m)])

            def p5_out(c, tt):
                t = 4 * c + tt
                xi = n5["x"] % 2
                n5["x"] += 1
                xt = X5[xi]
                sch.dma("sp", "x5l%d" % xi, lambda e: e.dma_start(out=xt, in_=x_d[t * 128:(t + 1) * 128, :]), writes=["x5_%d" % xi])

                def half(hf):
                    bo = B_O[n5["o"] % 2]
                    n5["o"] += 1
                    for k in range(8):
                        sch.op("pe", lambda e, k=k: e.matmul(bank_f32(bo), lhsT=mix[:, k, tt * 128:(tt + 1) * 128],
                                                             rhs=WO[:, k, hf * 512:(hf + 1) * 512], start=(k == 0), stop=(k == 7)),
                               reads=["wo"] + [(mixn, mm) for mm in range(8)], writes=["bank%d" % bo])
                    sch.op("dve", lambda e: e.scalar_tensor_tensor(
                        out=xt[:, hf * 512:(hf + 1) * 512], in0=bank_f32(bo), scalar=0.5, in1=xt[:, hf * 512:(hf + 1) * 512],
                        op0=ALU.mult, op1=ALU.add), reads=["bank%d" % bo, "x5_%d" % xi], writes=["x5_%d" % xi])
                half(0)
                half(1)
                sch.dma("sp", "x5s%d" % xi, lambda e: e.dma_start(out=x1_d[t * 128:(t + 1) * 128, :], in_=xt),
                        reads=["x5_%d" % xi], writes=[("x1", t)])

            for c in range(NCH):
                for m in range(8):
                    p5_merge(c, m)
                for tt in range(4):
                    p5_out(c, tt)
            sch.barrier()

            checkpoint("P5")
            w0 = 6 * KB
            WU = mem.ap(BF16, w0, [8, DFF]); w0 += 64 * KB
            WD = mem.ap(BF16, w0, [32, D]); w0 += 64 * KB
            AT = mem.ap(BF16, w0, [32, 512]); w0 += 32 * KB
            H2T = mem.ap(BF16, w0, [8, 512]); w0 += 8 * KB
            X6 = [mem.ap(F32, w0 + i * 4 * KB, [D]) for i in range(5)]; w0 += 20 * KB
            GB2 = mem.ap(F32, w0, [D]); w0 += 4 * KB
            HB6 = [mem.ap(BF16, w0 + i * 2 * KB, [D]) for i in range(2)]; w0 += 4 * KB
            RR = [mem.ap(F32, w0 + i * 2 * KB, [512]) for i in range(2)]; w0 += 4 * KB
            SS6 = mem.ap(F32, w0, [NT]); w0 += 128
            assert w0 <= R_END, w0
            g2_b = bass.AP(ln2_d.tensor, 0, [[0, 128], [1, D]])
            sch.dma("sp", "gb", lambda e: e.dma_start(out=GB2, in_=g2_b), writes=["gb"])
            for piece in range(8):
                src = wu_d[:, piece * 512:(piece + 1) * 512].rearrange("(k p) n -> p k n", p=128)
                sch.dma("pool", "wu%d" % piece, lambda e, piece=piece, src=src: e.dma_start(out=WU[:, :, piece * 512:(piece + 1) * 512], in_=src),
                        writes=[("wu", piece)])
            for piece in range(8):
                src = wd_d[piece * 512:(piece + 1) * 512, :].rearrange("(k p) n -> p k n", p=128)
                sch.dma("pool", "wd%d" % piece, lambda e, piece=piece, src=src: e.dma_start(out=WD[:, piece * 4:(piece + 1) * 4, :], in_=src),
                        writes=[("wd", piece)])
            B_TP, B_U, B_D = (0, 1), (2, 3, 4), (5, 6, 7)
            n6 = {"x": 0, "u": 0, "d": 0, "r": 0, "hb": 0}

            def p6_up(f):
                bu = B_U[n6["u"] % 3]
                n6["u"] += 1
                ri = n6["r"] % 2
                n6["r"] += 1
                for k in range(8):
                    sch.op("pe", lambda e, k=k: e.matmul(bank_f32(bu), lhsT=WU[:, k, f * 128:(f + 1) * 128], rhs=H2T[:, k, :],
                                                         start=(k == 0), stop=(k == 7)),
                           reads=[("wu", f // 4)] + [("h2t", tt) for tt in range(4)], writes=["bank%d" % bu])
                sch.op("act", lambda e: e.activation(out=RR[ri], in_=bank_f32(bu), func=AF.Relu), reads=["bank%d" % bu], writes=["rr%d" % ri])
                sch.op("dve", lambda e: e.tensor_tensor(out=AT[:, f, :], in0=RR[ri], in1=RR[ri], op=ALU.mult),
                       reads=["rr%d" % ri], writes=[("at", f)])

            def p6_down(t, tt, xi):
                xt = X6[xi]

                def half(hf):
                    bd = B_D[n6["d"] % 3]
                    n6["d"] += 1
                    for f in range(32):
                        sch.op("pe", lambda e, f=f: e.matmul(bank_f32(bd), lhsT=AT[:, f, tt * 128:(tt + 1) * 128],
                                                             rhs=WD[:, f, hf * 512:(hf + 1) * 512], start=(f == 0), stop=(f == 31)),
                               reads=[("wd", f // 4), ("at", f)], writes=["bank%d" % bd])
                    sch.op("dve", lambda e: e.tensor_tensor(out=xt[:, hf * 512:(hf + 1) * 512], in0=bank_f32(bd),
                                                            in1=xt[:, hf * 512:(hf + 1) * 512], op=ALU.add),
                           reads=["bank%d" % bd, "x6_%d" % xi], writes=["x6_%d" % xi])
                half(0)
                half(1)
                sch.dma("sp", "x6s%d" % xi, lambda e: e.dma_start(out=out_d[t * 128:(t + 1) * 128, :], in_=xt),
                        reads=["x6_%d" % xi], writes=[("out", t)])

            for tb in range(NCH):
                xis = []
                for tt in range(4):
                    t = 4 * tb + tt
                    xi = n6["x"] % 5
                    n6["x"] += 1
                    xis.append(xi)
                    hbi = n6["hb"] % 2
                    n6["hb"] += 1
                    prenorm_tile(x1_d[t * 128:(t + 1) * 128, :], X6[xi], "x6_%d" % xi, "x6l%d" % xi, GB2, HB6[hbi], "hb6_%d" % hbi, HB6[hbi],
                                 SS6[:, t:t + 1], "ss6_%d" % t, B_TP[t % 2], H2T[:, :, tt * 128:(tt + 1) * 128], [("h2t", tt)],
                                 junk_name="hb6_%d" % hbi)
                for f in range(32):
                    p6_up(f)
                for tt in range(4):
                    p6_down(4 * tb + tt, tt, xis[tt])

        try:
            emit_all()
        except _Stop:
            sch.barrier()
        sch.final_wait("sp", ["x6s%d" % i for i in range(5)] + (["dbg"] if DEBUG else []))

        sch.finalize()
        block = es.enter_context(nc.Block())

        @block.sync
        def _(e):
            sch.replay("sp", e)

        @block.gpsimd
        def _(e):
            sch.replay("pool", e)

        @block.scalar
        def _(e):
            sch.replay("act", e)

        @block.vector
        def _(e):
            sch.replay("dve", e)

        @block.tensor
        def _(e):
            sch.replay("pe", e)
    return nc


_CACHE = {}


def kernel(x, positions, ln1_g, w_in, q_norm_a, k_norm_a, q_norm_b, k_norm_b, sinks,
           w_branch_a, w_branch_b, w_out, ln2_g, w_up, w_down):
    if "nc" not in _CACHE:
        _CACHE["nc"] = build_program()
    nc = _CACHE["nc"]
    cst = host_consts()
    f32 = lambda a: np.ascontiguousarray(np.asarray(a), dtype=np.float32)
    shared = {
        "cst": cst,
        "ln1_g": f32(ln1_g), "ln2_g": f32(ln2_g), "w_in": f32(w_in)[0],
        "q_norm_a": f32(q_norm_a), "k_norm_a": f32(k_norm_a), "q_norm_b": f32(q_norm_b), "k_norm_b": f32(k_norm_b),
        "sinks": f32(sinks), "w_branch_a": f32(w_branch_a)[0], "w_branch_b": f32(w_branch_b)[0],
        "w_out": f32(w_out)[0], "w_up": f32(w_up)[0], "w_down": f32(w_down)[0],
    }
    xs = f32(x)
    ps = np.ascontiguousarray(np.asarray(positions), dtype=np.int32)
    in_maps = []
    for b in range(8):
        m = dict(shared)
        m["x"] = xs[b]
        m["pos"] = ps[b:b + 1]
        in_maps.append(m)
    res = run_bass_kernel_spmd(nc, in_maps, core_ids=list(range(8)))
    _CACHE["last"] = res
    out = np.stack([np.asarray(r["out"], dtype=np.float32) for r in res.results], axis=0)
    return out
```

```python
import math
from contextlib import ExitStack

import numpy as np
import concourse.bass as bass
import concourse.mybir as mybir
from concourse.bass_utils import run_bass_kernel_spmd

F32 = mybir.dt.float32
BF16 = mybir.dt.bfloat16
I32 = mybir.dt.int32
AF = mybir.ActivationFunctionType
ALU = mybir.AluOpType

S = 4096
D = 1024
DFF = 4096
NCH = 8
NT = 32
EPS = 1e-6
ARENA_ELEMS = 105984

OFF_QA, OFF_KA, OFF_VA, OFF_QB, OFF_KB, OFF_VB, OFF_GA, OFF_GB = 0, 768, 1536, 2304, 2816, 2944, 3072, 4096

C_IDENT, C_BONES, C_PERM, C_MASK = 0, 128, 256, 384
C_BF_COLS = 384 + 4 * 512
C_INVF = C_BF_COLS
C_F32_COLS = 8
CST_COLS = C_BF_COLS + C_F32_COLS

DEBUG = False


def host_consts():
    c = np.zeros((128, CST_COLS), np.float32)
    c[:, C_IDENT:C_IDENT + 128] = np.eye(128, dtype=np.float32)
    bo = np.zeros((128, 128), np.float32)
    bo[0:64, 0:64] = 1.0
    bo[64:128, 64:128] = 1.0
    c[:, C_BONES:C_BONES + 128] = bo
    pm = np.zeros((128, 128), np.float32)
    for hb in (0, 64):
        for i in range(8):
            pm[hb + i + 8, hb + i] = -1.0
            pm[hb + i, hb + i + 8] = 1.0
    c[:, C_PERM:C_PERM + 128] = pm
    k = np.arange(128)[:, None]
    q = np.arange(128)[None, :]
    diag = (k <= q).astype(np.float32)
    prev_g = (k >= q).astype(np.float32)
    prev_b = (k > q).astype(np.float32)
    zero = np.zeros((128, 128), np.float32)
    masks = [
        np.concatenate([zero, diag, prev_g, diag], axis=1),
        np.concatenate([prev_g, diag, prev_g, diag], axis=1),
        np.concatenate([zero, diag, prev_b, diag], axis=1),
        np.concatenate([prev_b, diag, prev_b, diag], axis=1),
    ]
    for i, m in enumerate(masks):
        c[:, C_MASK + 512 * i:C_MASK + 512 * (i + 1)] = m
    inv_freq = (500000.0 ** (-np.arange(0, 16, 2, dtype=np.float32) / 16.0)).astype(np.float32)
    invf = np.zeros(128, np.float32)
    for p in range(128):
        if p % 64 < 16:
            invf[p] = inv_freq[(p % 64) % 8]
    c[:, C_INVF] = invf
    c[:, C_INVF + 1] = EPS
    return c


class Sched:
    ENGS = ("pe", "act", "dve", "pool", "sp")

    def __init__(self, nc, es):
        self.nc = nc
        self.es = es
        self.q = {e: [] for e in self.ENGS}
        self.res = {}
        self.sem = {e: es.enter_context(nc.semaphore("s_" + e)) for e in ("pe", "act", "dve", "pool")}
        self.dsem = {}
        self.dcnt = {}
        self.defer = None
        self.base_prio = 0.0

    def _dma_sem(self, name):
        if name not in self.dsem:
            self.dsem[name] = self.es.enter_context(self.nc.semaphore("d_" + name))
            self.dcnt[name] = 0
        return self.dsem[name]

    def _deps(self, reads, writes):
        deps = set()
        for r in reads:
            st = self.res.get(r)
            if st and st["w"] is not None:
                deps.add(st["w"])
        for w in writes:
            st = self.res.get(w)
            if st:
                if st["w"] is not None:
                    deps.add(st["w"])
                for d in st["r"]:
                    deps.add(d)
        return deps

    def _commit(self, me, reads, writes):
        for r in reads:
            st = self.res.setdefault(r, {"w": None, "r": []})
            st["r"] = [d for d in st["r"] if d[0] != me[0]] + [me]
        for w in writes:
            self.res[w] = {"w": me, "r": []}

    def begin_defer(self):
        self.defer = []

    def flush(self):
        lastw = {}
        expect = []
        for it in self.defer:
            expect.append({r: lastw.get(r) for r in it[6]})
            for w in it[7]:
                lastw[w] = it[1]
        items = sorted(self.defer, key=lambda x: (x[0], x[1]))
        wnow = {}
        for it in items:
            for r, v in expect[it[1]].items():
                if wnow.get(r) != v:
                    raise RuntimeError("priority order breaks producer of %r at prio %s (%s): expected op %s, saw %s"
                                       % (r, it[0], it[3], v, wnow.get(r)))
            for w in it[7]:
                wnow[w] = it[1]
        self.defer = None
        for (_, _, kind, eng, semname, fn, reads, writes) in items:
            if kind == "op":
                self.op(eng, fn, reads, writes)
            else:
                self.dma(eng, semname, fn, reads, writes)

    def op(self, eng, fn, reads=(), writes=(), prio=None):
        if getattr(self, "defer", None) is not None:
            self.defer.append((self.base_prio + (prio or 0.0), len(self.defer), "op", eng, None, fn, tuple(reads), tuple(writes)))
            return None
        deps = self._deps(reads, writes)
        idx = len(self.q[eng])
        self.q[eng].append({"fn": fn, "deps": deps, "kind": "op", "marked": False})
        self._commit((eng, idx), reads, writes)
        return (eng, idx)

    def dma(self, eng, semname, fn, reads=(), writes=(), prio=None):
        if getattr(self, "defer", None) is not None:
            self.defer.append((self.base_prio + (prio or 0.0), len(self.defer), "dma", eng, semname, fn, tuple(reads), tuple(writes)))
            return None
        self._dma_sem(semname)
        deps = self._deps(reads, writes)
        self.dcnt[semname] += 1
        me = ("dma:" + semname, self.dcnt[semname])
        self.q[eng].append({"fn": fn, "deps": deps, "kind": "dma", "sem": semname})
        self._commit(me, reads, writes)
        return me

    def barrier(self):
        deps = set()
        for e in ("pe", "act", "dve", "pool"):
            for i in range(len(self.q[e]) - 1, -1, -1):
                if self.q[e][i]["kind"] == "op":
                    deps.add((e, i))
                    break
        for name, cnt in self.dcnt.items():
            if cnt:
                deps.add(("dma:" + name, cnt))
        for e in self.ENGS:
            self.q[e].append({"fn": None, "deps": set(deps), "kind": "bar"})
        self.res = {}

    def final_wait(self, eng, semnames):
        deps = set(("dma:" + n, self.dcnt[n]) for n in semnames if self.dcnt.get(n))
        self.q[eng].append({"fn": None, "deps": deps, "kind": "bar"})

    def finalize(self):
        for e in self.ENGS:
            for ins in self.q[e]:
                for (dom, idx) in ins["deps"]:
                    if not dom.startswith("dma:"):
                        if dom == "pe" and e == "pe":
                            continue
                        self.q[dom][idx]["marked"] = True
        self.ordinal = {}
        for e in ("pe", "act", "dve", "pool"):
            n = 0
            for i, ins in enumerate(self.q[e]):
                if ins.get("marked"):
                    n += 1
                    self.ordinal[(e, i)] = n
        self.total_incs = n

    def replay(self, eng, eobj):
        seen = {}
        for ins in self.q[eng]:
            need = {}
            for (dom, idx) in ins["deps"]:
                if dom.startswith("dma:"):
                    val = 16 * idx
                else:
                    if dom == "pe" and eng == "pe":
                        continue
                    val = self.ordinal[(dom, idx)]
                if val > need.get(dom, 0):
                    need[dom] = val
            for dom, val in need.items():
                if seen.get(dom, 0) >= val:
                    continue
                seen[dom] = val
                sem = self.dsem[dom[4:]] if dom.startswith("dma:") else self.sem[dom]
                eobj.wait_ge(sem, val)
            if ins["fn"] is None:
                continue
            bi = ins["fn"](eobj)
            if ins["kind"] == "dma":
                bi.then_inc(self.dsem[ins["sem"]], 16)
            elif ins.get("marked"):
                bi.then_inc(self.sem[eng], 1)


class Mem:
    def __init__(self, arena):
        self.h = {BF16: arena, F32: arena.bitcast(F32), I32: arena.bitcast(I32)}
        self.pstep = {BF16: ARENA_ELEMS, F32: ARENA_ELEMS // 2, I32: ARENA_ELEMS // 2}

    def ap(self, dt, byte_off, shape, parts=128, p0=0):
        esz = 2 if dt == BF16 else 4
        assert byte_off % esz == 0
        dims = [[self.pstep[dt], parts]]
        stride = 1
        rev = []
        for n in reversed(shape):
            rev.append([stride, n])
            stride *= n
        dims += list(reversed(rev))
        assert byte_off + stride * esz <= ARENA_ELEMS * 2, (byte_off, stride, esz)
        return bass.AP(self.h[dt], p0 * self.pstep[dt] + byte_off // esz, dims)


KB = 1024


def build_program():
    nc = bass.Bass("TRN2", target_bir_lowering=False)
    dr = {}

    def din(name, shape, dt=F32):
        dr[name] = nc.dram_tensor(name, shape, dt, kind="ExternalInput")
        return dr[name].ap()

    x_d = din("x", [S, D])
    pos_d = din("pos", [1, S], I32)
    cst_d = din("cst", [128, CST_COLS])
    ln1_d = din("ln1_g", [1, D])
    ln2_d = din("ln2_g", [1, D])
    win_d = din("w_in", [D, 5120])
    qna_d = din("q_norm_a", [1, 64])
    kna_d = din("k_norm_a", [1, 64])
    qnb_d = din("q_norm_b", [1, 64])
    knb_d = din("k_norm_b", [1, 64])
    snk_d = din("sinks", [1, 8])
    wa_d = din("w_branch_a", [256, D])
    wb_d = din("w_branch_b", [512, D])
    wo_d = din("w_out", [D, D])
    wu_d = din("w_up", [D, DFF])
    wd_d = din("w_down", [DFF, D])
    out_h = nc.dram_tensor("out", [S, D], F32, kind="ExternalOutput")
    out_d = out_h.ap()
    x1_h = nc.dram_tensor("x1_scratch", [S, D], F32, kind="Internal")
    x1_d = x1_h.ap()
    dbg = {}
    if DEBUG:
        for name, shape, dt in (("dbg_hT", [128, 8 * S], BF16), ("dbg_oaT", [128, 2 * S], BF16),
                                ("dbg_obT", [128, 4 * S], BF16), ("dbg_tab", [128, 2 * S], BF16),
                                ("dbg_qk", [128, 2 * S], BF16)):
            dbg[name] = nc.dram_tensor(name, shape, dt, kind="ExternalOutput").ap()

    with ExitStack() as es:
        arena = es.enter_context(nc.sbuf_tensor("arena", [128, ARENA_ELEMS], BF16))
        mem = Mem(arena)
        banks = [es.enter_context(nc.psum_tensor("bank%d" % i, [128, 512], F32)) for i in range(8)]
        sch = Sched(nc, es)
        import os as _os
        _stop = _os.environ.get("KSTOP", "")

        class _Stop(Exception):
            pass

        def checkpoint(name):
            if _stop == name:
                raise _Stop()

        def bank_f32(i):
            return banks[i][:, :]

        def bank_bf16(i):
            return banks[i][:, :].bitcast(BF16)

        o = 0
        IDENT = mem.ap(BF16, o, [128]); o += 256
        BONES = mem.ap(BF16, o, [128]); o += 256
        PERM = mem.ap(BF16, o, [128]); o += 256
        MASKS = mem.ap(BF16, o, [4, 512]); o += 4096
        CF32 = mem.ap(F32, o, [C_F32_COLS]); o += 4 * C_F32_COLS
        GAINS = mem.ap(F32, o, [4]); o += 16
        ESINK = mem.ap(F32, o, [8]); o += 32
        o = (o + 63) // 64 * 64
        assert o <= 6 * KB
        R_H = 6 * KB
        R_O = 70 * KB
        R_T = 118 * KB
        R_W = 134 * KB
        R_END = ARENA_ELEMS * 2
        hT = mem.ap(BF16, R_H, [8, S])
        TC = mem.ap(BF16, R_T, [S])
        TS = mem.ap(BF16, R_T + 8 * KB, [S])
        oaT = mem.ap(BF16, R_O, [2, S])
        obT = mem.ap(BF16, R_O + 16 * KB, [4, S])
        INVF = CF32[:, 0:1]
        EPSC = CF32[:, 1:2]

        def emit_all():
            cbf = mem.ap(BF16, 0, [C_BF_COLS])
            sch.dma("pool", "cstb", lambda e: e.dma_start(out=cbf, in_=cst_d[:, 0:C_BF_COLS]), writes=["consts"])
            sch.dma("sp", "cst", lambda e: e.dma_start(out=CF32, in_=cst_d[:, C_BF_COLS:CST_COLS]), writes=["consts"])
            for gi, gd in enumerate((qna_d, kna_d, qnb_d, knb_d)):
                for hb in (0, 64):
                    src = bass.AP(gd.tensor, 0, [[1, 64], [1, 1]])
                    sch.dma("sp", "cst", lambda e, gi=gi, hb=hb, src=src: e.dma_start(out=GAINS[hb:hb + 64, gi:gi + 1], in_=src),
                            writes=["consts"])
            snk_b = bass.AP(snk_d.tensor, 0, [[0, 128], [1, 8]])
            sch.dma("sp", "cst", lambda e: e.dma_start(out=ESINK, in_=snk_b), writes=["consts"])
            sch.op("act", lambda e: e.activation(out=ESINK, in_=ESINK, func=AF.Exp), reads=["consts"], writes=["esink"])

            WP = [mem.ap(BF16, R_W + i * 6 * KB, [8, 384]) for i in range(2)]
            PASSES = []
            for sp in range(2):
                for (g, d, mode) in ((2, 16, "copy"), (1, 4, "add"), (0, 1, "final")):
                    PASSES.append(dict(name="g%d_%d" % (g, sp), d=d, qcol=OFF_QA + g * 256 + sp * 128, kcol=OFF_KA + g * 256 + sp * 128,
                                       vcol=OFF_VA + g * 256 + sp * 128, vdup=False, gq=0, gk=1, mb=0, mode=mode, sinks=None,
                                       out=oaT[:, sp, :], barrier_after=(sp == 1 and mode == "final")))
            for kv in range(2):
                for f in range(2):
                    heads = (4 * kv + 2 * f, 4 * kv + 2 * f + 1)
                    PASSES.append(dict(name="b%d_%d" % (kv, f), d=1, qcol=OFF_QB + heads[0] * 64,
                                       kcol=(OFF_KB + kv * 64) if f == 0 else None, vcol=(OFF_VB + kv * 64) if f == 0 else None,
                                       vdup=True, gq=2, gk=3, mb=2, mode="none", sinks=heads, out=obT[:, 2 * kv + f, :]))
            def emit_wload(pd, slot):
                W = WP[slot]
                wname = "wp%d" % slot

                def wload(dst_lo, src_lo, n):
                    src = win_d[:, src_lo:src_lo + n].rearrange("(k p) n -> p k n", p=128)
                    sch.dma("pool", wname, lambda e: e.dma_start(out=W[:, :, dst_lo:dst_lo + n], in_=src), writes=[wname], prio=-1.0)
                wload(0, pd["qcol"], 128)
                if pd["kcol"] is not None:
                    if pd["vdup"]:
                        wload(128, pd["kcol"], 64)
                        wload(192, pd["kcol"], 64)
                        wload(256, OFF_VB, 128)
                    else:
                        wload(128, pd["kcol"], 128)
                        wload(256, pd["vcol"], 128)

            tA = mem.ap(F32, R_W, [S])
            tAi = mem.ap(I32, R_W, [S])
            tB = mem.ap(F32, R_W + 16 * KB, [S])
            tBi = mem.ap(I32, R_W + 16 * KB, [S])
            tM = mem.ap(F32, R_W + 32 * KB, [S])
            pos_b = bass.AP(pos_d.tensor, 0, [[0, 128], [1, S]])
            sch.dma("sp", "pos", lambda e: e.dma_start(out=tAi, in_=pos_b), writes=["tA"])
            sch.op("dve", lambda e: e.tensor_copy(out=tA, in_=tAi), reads=["tA"], writes=["tA"])
            sch.op("dve", lambda e: e.tensor_scalar(out=tA, in0=tA, scalar1=INVF, scalar2=None, op0=ALU.mult),
                   reads=["tA", "consts"], writes=["tA"])
            sch.op("dve", lambda e: e.tensor_scalar(out=tA, in0=tA, scalar1=float(1.0 / (2 * math.pi)), scalar2=None, op0=ALU.mult),
                   reads=["tA"], writes=["tA"])
            for which, tab in ((0, TS), (1, TC)):
                if which == 1:
                    sch.op("dve", lambda e: e.tensor_scalar(out=tA, in0=tA, scalar1=0.25, scalar2=None, op0=ALU.add),
                           reads=["tA"], writes=["tA"])
                sch.op("dve", lambda e: e.tensor_copy(out=tBi, in_=tA), reads=["tA"], writes=["tB"])
                sch.op("dve", lambda e: e.tensor_copy(out=tB, in_=tBi), reads=["tB"], writes=["tB"])
                sch.op("dve", lambda e: e.tensor_tensor(out=tB, in0=tA, in1=tB, op=ALU.subtract), reads=["tA", "tB"], writes=["tB"])
                sch.op("dve", lambda e: e.tensor_single_scalar(out=tM, in_=tB, scalar=0.5, op=ALU.is_gt), reads=["tB"], writes=["tM"])
                sch.op("dve", lambda e: e.tensor_tensor(out=tB, in0=tB, in1=tM, op=ALU.subtract), reads=["tB", "tM"], writes=["tB"])
                sch.op("dve", lambda e: e.tensor_single_scalar(out=tM, in_=tB, scalar=-0.5, op=ALU.is_lt), reads=["tB"], writes=["tM"])
                sch.op("dve", lambda e: e.tensor_tensor(out=tB, in0=tB, in1=tM, op=ALU.add), reads=["tB", "tM"], writes=["tB"])
                sch.op("act", lambda e, tab=tab: e.activation(out=tab, in_=tB, func=AF.Sin, scale=6.283185),
                       reads=["tB"], writes=["tab%d" % which])
            if DEBUG:
                sch.dma("sp", "dbg", lambda e: e.dma_start(out=dbg["dbg_tab"][:, 0:S], in_=TC), reads=["tab1"])
                sch.dma("sp", "dbg", lambda e: e.dma_start(out=dbg["dbg_tab"][:, S:2 * S], in_=TS), reads=["tab0"])

            sch.barrier()
            checkpoint("T")
            emit_wload(PASSES[0], 0)
            def prenorm_tile(xsrc_ap, xs, xs_name, sem_name, g_b, hb, hb_name, junk, ss_col, ss_name, tp_bank, dst_ap, dst_names,
                             load_eng="sp", junk_name="junk", P=0.0):
                sch.dma(load_eng, sem_name, lambda e: e.dma_start(out=xs, in_=xsrc_ap), writes=[xs_name], prio=P - 2.0)
                sch.op("act", lambda e: e.activation(out=junk, in_=xs, func=AF.Square, accum_out=ss_col),
                       reads=[xs_name], writes=[junk_name, ss_name], prio=P)
                sch.op("act", lambda e: e.activation(out=ss_col, in_=ss_col, func=AF.Ln, scale=1.0 / D, bias=EPSC),
                       reads=[ss_name, "consts"], writes=[ss_name], prio=P + 0.02)
                sch.op("act", lambda e: e.activation(out=ss_col, in_=ss_col, func=AF.Exp, scale=-0.5),
                       reads=[ss_name], writes=[ss_name], prio=P + 0.04)
                sch.op("dve", lambda e: e.scalar_tensor_tensor(out=hb, in0=xs, scalar=ss_col, in1=g_b, op0=ALU.mult, op1=ALU.mult),
                       reads=[xs_name, ss_name, "gb"], writes=[hb_name], prio=P + 0.06)
                pT = bank_bf16(tp_bank)
                for k in range(8):
                    sch.op("pe", lambda e, k=k: e.transpose(out=pT[:, k * 128:(k + 1) * 128], in_=hb[:, k * 128:(k + 1) * 128], identity=IDENT),
                           reads=[hb_name, "consts"], writes=["bank%d" % tp_bank], prio=P + 0.5)
                sch.op("act", lambda e: e.activation(out=dst_ap, in_=pT.rearrange("p (k t) -> p k t", k=8), func=AF.Copy),
                       reads=["bank%d" % tp_bank], writes=dst_names, prio=P + 1.5)

            p0 = R_W + 48 * KB
            XS = [mem.ap(F32, p0 + i * 4 * KB, [D]) for i in range(3)]
            HB = [mem.ap(BF16, p0 + 12 * KB + i * 2 * KB, [D]) for i in range(2)]
            JUNK = mem.ap(BF16, p0 + 16 * KB, [D])
            GB1 = mem.ap(F32, p0 + 18 * KB, [D])
            SSC = mem.ap(F32, p0 + 22 * KB, [NT])
            assert p0 + 22 * KB + 4 * NT <= R_END
            g1_b = bass.AP(ln1_d.tensor, 0, [[0, 128], [1, D]])
            sch.dma("sp", "gb", lambda e: e.dma_start(out=GB1, in_=g1_b), writes=["gb"])
            sch.begin_defer()
            for t in range(NT):
                prenorm_tile(x_d[t * 128:(t + 1) * 128, :], XS[t % 3], "xs%d" % (t % 3), "xs%d" % (t % 3), GB1,
                             HB[t % 2], "hb%d" % (t % 2), JUNK, SSC[:, t:t + 1], "ss%d" % t, t % 2,
                             hT[:, :, t * 128:(t + 1) * 128], [("hT", t)], P=float(t))
            sch.flush()
            if DEBUG:
                sch.dma("sp", "dbg", lambda e: e.dma_start(out=dbg["dbg_hT"], in_=hT.rearrange("p k t -> p (k t)")),
                        reads=[("hT", t) for t in range(NT)])
            sch.barrier()

            checkpoint("P0")
            w0 = R_W + 12 * KB
            QT = mem.ap(BF16, w0, [S]); w0 += 8 * KB
            KT = mem.ap(BF16, w0, [S]); w0 += 8 * KB
            VG = mem.ap(BF16, w0, [NT, 2, 128]); w0 += 16 * KB
            SQ = [mem.ap(BF16, w0 + i * KB, [512]) for i in range(2)]; w0 += 2 * KB
            RV = [mem.ap(F32, w0 + i * 2 * KB, [512]) for i in range(2)]; w0 += 4 * KB
            QN = [mem.ap(BF16, w0 + i * KB, [512]) for i in range(2)]; w0 += 2 * KB
            T1 = [mem.ap(F32, w0 + i * 2 * KB, [512]) for i in range(2)]; w0 += 4 * KB
            T2 = [mem.ap(F32, w0 + i * 2 * KB, [512]) for i in range(2)]; w0 += 4 * KB
            PT = [mem.ap(BF16, w0 + i * KB, [512]) for i in range(4)]; w0 += 4 * KB
            RD = [mem.ap(F32, w0 + i * 2 * KB, [512]) for i in range(2)]; w0 += 4 * KB
            assert w0 <= R_END, w0
            ACC = mem.ap(F32, R_O + 16 * KB, [2, S])
            sch.op("pool", lambda e: e.memset(VG[:, :, 0, 64:128], 1.0), writes=["vg_ones"])
            sch.op("pool", lambda e: e.memset(VG[:, :, 1, 0:64], 1.0), writes=["vg_ones"])

            B_PJ = (0, 1)
            B_SS, B_PM, B_SA, B_SB, B_OT, B_VP = 2, 3, 4, 5, 6, 7
            cnt = {"pj": 0, "blk": 0, "pt": 0, "rd": 0, "w": 0}

            def gcol_ap(buf, d, c):
                L = S // d
                u = 512 // d
                return buf.rearrange("p (r l) -> p r l", r=d)[:, :, u * c:u * (c + 1)]

            def nat_ap(t, d):
                return t.rearrange("p (u r) -> p r u", r=d)

            def gblocks_of_chunk(d, c):
                L = S // d
                u = 512 // d
                blks = set()
                for r in range(d):
                    for col in range(r * L + u * c, r * L + u * (c + 1), min(u, 128)):
                        blks.add(col // 128)
                return sorted(blks)

            def attention_pass(pd, wslot, next_pd):
                name, d, kcol, vcol, vdup = pd["name"], pd["d"], pd["kcol"], pd["vcol"], pd["vdup"]
                gq_idx, gk_idx, mask_base, acc_mode = pd["gq"], pd["gk"], pd["mb"], pd["mode"]
                sink_heads, out_fchunk_ap = pd["sinks"], pd["out"]
                L = S // d
                bpr = L // 128
                W = WP[wslot]
                wname = "wp%d" % wslot
                nq = 2 if kcol is not None else 1
                sch.begin_defer()
                if next_pd is not None:
                    emit_wload(next_pd, 1 - wslot)

                def proj_qk(i, c, which):
                    P = float(i)
                    pj = B_PJ[cnt["pj"] % 2]
                    cnt["pj"] += 1
                    b = cnt["blk"] % 2
                    cnt["blk"] += 1
                    pjn = "bank%d" % pj
                    for k in range(8):
                        sch.op("pe", lambda e, k=k: e.matmul(bank_f32(pj), lhsT=W[:, k, which * 128:(which + 1) * 128],
                                                             rhs=hT[:, k, c * 512:(c + 1) * 512], start=(k == 0), stop=(k == 7)),
                               reads=[wname] + [("hT", t) for t in range(4 * c, 4 * c + 4)], writes=[pjn], prio=P)
                    sch.op("act", lambda e: e.activation(out=SQ[b], in_=bank_f32(pj), func=AF.Square), reads=[pjn], writes=["sq%d" % b],
                           prio=P + 0.02)
                    sch.op("pe", lambda e: e.matmul(bank_f32(B_SS), lhsT=BONES, rhs=SQ[b], start=True, stop=True),
                           reads=["sq%d" % b, "consts"], writes=["bank%d" % B_SS], prio=P + 1.04)
                    sch.op("act", lambda e: e.activation(out=RV[b], in_=bank_f32(B_SS), func=AF.Ln, scale=1.0 / 64, bias=EPSC),
                           reads=["bank%d" % B_SS, "consts"], writes=["rv%d" % b], prio=P + 1.06)
                    sch.op("act", lambda e: e.activation(out=RV[b], in_=RV[b], func=AF.Exp, scale=-0.5), reads=["rv%d" % b], writes=["rv%d" % b],
                           prio=P + 1.08)
                    gi = gq_idx if which == 0 else gk_idx
                    sch.op("dve", lambda e: e.scalar_tensor_tensor(out=QN[b], in0=bank_f32(pj), scalar=GAINS[:, gi:gi + 1], in1=RV[b],
                                                                   op0=ALU.mult, op1=ALU.mult),
                           reads=[pjn, "rv%d" % b, "consts"], writes=["qn%d" % b], prio=P + 1.10)
                    sch.op("pe", lambda e: e.matmul(bank_f32(B_PM), lhsT=PERM, rhs=QN[b], start=True, stop=True),
                           reads=["qn%d" % b, "consts"], writes=["bank%d" % B_PM], prio=P + 2.12)
                    sch.op("pool", lambda e: e.tensor_tensor(out=T1[b], in0=QN[b], in1=TC[:, c * 512:(c + 1) * 512], op=ALU.mult),
                           reads=["qn%d" % b, "tab1"], writes=["t1%d" % b], prio=P + 2.14)
                    sch.op("dve", lambda e: e.tensor_tensor(out=T2[b], in0=bank_f32(B_PM), in1=TS[:, c * 512:(c + 1) * 512], op=ALU.mult),
                           reads=["bank%d" % B_PM, "tab0"], writes=["t2%d" % b], prio=P + 2.16)
                    dst = QT if which == 0 else KT
                    dname = "qt" if which == 0 else "kt"
                    sch.op("pool", lambda e: e.tensor_tensor(out=gcol_ap(dst, d, c), in0=nat_ap(T1[b], d), in1=nat_ap(T2[b], d), op=ALU.add),
                           reads=["t1%d" % b, "t2%d" % b], writes=[(dname, g) for g in gblocks_of_chunk(d, c)], prio=P + 2.18)

                def proj_v(gb, P):
                    r, j = gb // bpr, gb % bpr
                    t0 = r + d * 128 * j
                    nv = 128
                    toks = sorted(set((t0 + d * i) // 128 for i in (0, 127)))
                    tiles = list(range(toks[0], toks[-1] + 1))
                    vp = bank_f32(B_VP)
                    for k in range(8):
                        lhsT = hT[:, k, t0:t0 + d * 127 + 1:d]
                        sch.op("pe", lambda e, k=k, lhsT=lhsT: e.matmul(vp[:, 0:nv], lhsT=lhsT, rhs=W[:, k, 256:256 + nv],
                                                                       start=(k == 0), stop=(k == 7)),
                               reads=[wname] + [("hT", t) for t in tiles], writes=["bank%d" % B_VP], prio=P)
                    vg0 = VG[:, gb, 0, 0:64]
                    dst = bass.AP(vg0.tensor, vg0.offset, [list(vg0.ap[0]), [192, 2], [1, 64]])
                    if vdup:
                        kvsel = (vcol - OFF_VB) // 64
                        v0 = vp[:, kvsel * 64:(kvsel + 1) * 64]
                        src = bass.AP(v0.tensor, v0.offset, [list(v0.ap[0]), [0, 2], [1, 64]])
                    else:
                        src = vp[:, 0:128].rearrange("p (h c) -> p h c", h=2)
                    sch.op("act", lambda e: e.activation(out=dst, in_=src, func=AF.Copy), reads=["bank%d" % B_VP, "vg_ones"],
                           writes=[("vg", gb), ("vgb", gb)], prio=P + 0.02)

                def acc_tiles(tok0, d):
                    lo = tok0 // 512
                    hi = (tok0 + d * 255) // 512
                    return list(range(lo, hi + 1))

                def attn_round(n, P):
                    gb0 = 2 * n
                    r, j0 = gb0 // bpr, gb0 % bpr
                    first = (j0 == 0)
                    mask = MASKS[:, mask_base + (0 if first else 1), :]
                    sbanks = (B_SA, B_SB)
                    pts = []
                    for half in range(2):
                        sb = bank_f32(sbanks[half])
                        rows = slice(64 * half, 64 * half + 64)
                        for qi in range(2):
                            gq = gb0 + qi
                            for kb in range(2):
                                gk = gq - 1 + kb
                                if gk < r * bpr:
                                    gk = gq
                                sch.op("pe", lambda e, sb=sb, rows=rows, qi=qi, kb=kb, gk=gk, gq=gq: e.matmul(
                                    sb[:, (2 * qi + kb) * 128:(2 * qi + kb + 1) * 128], lhsT=KT[rows, gk * 128:(gk + 1) * 128],
                                    rhs=QT[rows, gq * 128:(gq + 1) * 128], start=True, stop=True),
                                    reads=[("kt", gk), ("qt", gq)], writes=["bank%d" % sbanks[half]], prio=P + 0.001 * half)
                        p = cnt["pt"] % 4
                        cnt["pt"] += 1
                        pts.append(p)
                        sch.op("act", lambda e, sb=sb, p=p: e.activation(out=PT[p], in_=sb, func=AF.Exp, scale=0.125),
                               reads=["bank%d" % sbanks[half]], writes=["pt%d" % p], prio=P + 0.03 + 0.001 * half)
                        sch.op("dve", lambda e, p=p: e.tensor_tensor(out=PT[p], in0=PT[p], in1=mask, op=ALU.mult),
                               reads=["pt%d" % p, "consts"], writes=["pt%d" % p], prio=P + 0.05 + 0.001 * half)
                    ot = bank_f32(B_OT)
                    nmm = 0
                    for half in range(2):
                        for qi in range(2):
                            gq = gb0 + qi
                            for kb in range(2):
                                gk = gq - 1 + kb
                                if gk < r * bpr:
                                    gk = gq
                                item = 2 * half + qi
                                sch.op("pe", lambda e, half=half, qi=qi, kb=kb, gk=gk, item=item, nmm=nmm: e.matmul(
                                    ot[:, item * 128:(item + 1) * 128], lhsT=VG[:, gk, half, :],
                                    rhs=PT[pts[half]][:, (2 * qi + kb) * 128:(2 * qi + kb + 1) * 128],
                                    start=(nmm == 0), stop=(kb == 1), skip_group_check=True),
                                    reads=[("vg", gk), ("vgb", gk), "pt%d" % pts[half]], writes=["bank%d" % B_OT], prio=P + 1.01)
                                nmm += 1
                    tok0 = r + d * 128 * j0
                    otv = ot.rearrange("p (h q i) -> p h q i", h=2, q=2)
                    if sink_heads is None:
                        accv = bass.AP(ACC.tensor, ACC.offset + tok0, [list(ACC.ap[0]), [S, 2], [128 * d, 2], [d, 128]])
                        if acc_mode == "copy":
                            sch.op("act", lambda e: e.activation(out=accv, in_=otv, func=AF.Copy), reads=["bank%d" % B_OT],
                                   writes=[("acc", n2) for n2 in acc_tiles(tok0, d)], prio=P + 1.03)
                        else:
                            sch.op("dve", lambda e: e.tensor_tensor(out=accv, in0=otv, in1=accv, op=ALU.add), reads=["bank%d" % B_OT],
                                   writes=[("acc", n2) for n2 in acc_tiles(tok0, d)], prio=P + 1.03)
                    else:
                        geo = []
                        for half in range(2):
                            num = slice(0, 64) if half == 0 else slice(64, 128)
                            den = slice(64, 128) if half == 0 else slice(0, 64)
                            geo.append((half, num, den, slice(256 * half, 256 * half + 256), sink_heads[half]))
                        for (half, num, den, cols, hsink) in geo:
                            sch.op("act", lambda e, half=half, num=num, den=den, cols=cols, hsink=hsink: e.activation(
                                out=RD[half][num, 0:256], in_=ot[den, cols], func=AF.Ln, bias=ESINK[num, hsink:hsink + 1]),
                                reads=["bank%d" % B_OT, "esink"], writes=["rd%d" % half, "ot_act_done"], prio=P + 1.03)
                        for (half, num, den, cols, hsink) in geo:
                            sch.op("act", lambda e, half=half, num=num: e.activation(out=RD[half][num, 0:256], in_=RD[half][num, 0:256],
                                                                                     func=AF.Exp, scale=-1.0),
                                   reads=["rd%d" % half], writes=["rd%d" % half], prio=P + 1.05)
                        for (half, num, den, cols, hsink) in geo:
                            sch.op("dve", lambda e, half=half, num=num, cols=cols: e.tensor_tensor(
                                out=out_fchunk_ap[num, tok0:tok0 + 256], in0=ot[num, cols], in1=RD[half][num, 0:256], op=ALU.mult),
                                reads=["bank%d" % B_OT, "rd%d" % half, "ot_act_done"], writes=[("ob", name, tok0)], prio=P + 1.07)

                round_prio = {}
                for c in range(NCH):
                    k = 0
                    for n in range(16):
                        gb1 = 2 * n + 1
                        r, j = gb1 // bpr, gb1 % bpr
                        if (r + d * (128 * j + 127)) // 512 == c:
                            round_prio[n] = (c * nq + nq - 1) + 2.3 + k * (nq / 2.0)
                            k += 1
                assert len(round_prio) == 16
                for c in range(NCH):
                    proj_qk(c * nq, c, 0)
                    if kcol is not None:
                        proj_qk(c * nq + 1, c, 1)
                if kcol is not None:
                    for gb in range(NT):
                        proj_v(gb, round_prio[gb // 2] - 0.7 + 0.2 * (gb % 2))
                for n in sorted(range(16), key=lambda n: (round_prio[n], n)):
                    attn_round(n, round_prio[n])

                if acc_mode == "final":
                    for c in range(NCH):
                        cs = slice(c * 512, (c + 1) * 512)
                        for half in range(2):
                            rb = cnt["rd"] % 2
                            cnt["rd"] += 1
                            num = slice(0, 64) if half == 0 else slice(64, 128)
                            den = slice(64, 128) if half == 0 else slice(0, 64)
                            P = 1000.0 + 2 * c + half
                            sch.op("act", lambda e, rb=rb, den=den, num=num, cs=cs, half=half: e.activation(
                                out=RD[rb][num, :], in_=ACC[den, half, cs], func=AF.Ln), reads=[("acc", c)], writes=["rd%d" % rb], prio=P)
                            sch.op("act", lambda e, rb=rb, num=num: e.activation(out=RD[rb][num, :], in_=RD[rb][num, :], func=AF.Exp, scale=-1.0),
                                   reads=["rd%d" % rb], writes=["rd%d" % rb], prio=P + 0.1)
                            sch.op("pool", lambda e, rb=rb, num=num, cs=cs, half=half: e.tensor_tensor(
                                out=out_fchunk_ap[num, cs], in0=ACC[num, half, cs], in1=RD[rb][num, :], op=ALU.mult),
                                reads=[("acc", c), "rd%d" % rb], writes=[("oa", name, c)], prio=P + 0.2)
                sch.flush()

            for pi, pd in enumerate(PASSES):
                attention_pass(pd, pi % 2, PASSES[pi + 1] if pi + 1 < len(PASSES) else None)
                if pd.get("barrier_after"):
                    sch.barrier()
                    checkpoint("A1")
            if DEBUG:
                sch.dma("sp", "dbg", lambda e: e.dma_start(out=dbg["dbg_qk"][:, 0:S], in_=QT), reads=[("qt", g) for g in range(NT)])
                sch.dma("sp", "dbg", lambda e: e.dma_start(out=dbg["dbg_qk"][:, S:2 * S], in_=KT), reads=[("kt", g) for g in range(NT)])
            sch.barrier()
            if DEBUG:
                sch.dma("sp", "dbg", lambda e: e.dma_start(out=dbg["dbg_oaT"], in_=oaT.rearrange("p k t -> p (k t)")))
                sch.dma("sp", "dbg", lambda e: e.dma_start(out=dbg["dbg_obT"], in_=obT.rearrange("p k t -> p (k t)")))

            checkpoint("A")
            w0 = R_T
            WG = mem.ap(BF16, w0, [8, 2048]); w0 += 32 * KB
            WA = mem.ap(BF16, w0, [2, D]); w0 += 4 * KB
            WB = mem.ap(BF16, w0, [4, D]); w0 += 8 * KB
            WO = mem.ap(BF16, w0, [8, D]); w0 += 16 * KB
            TA = [mem.ap(BF16, w0 + i * KB, [512]) for i in range(2)]; w0 += 2 * KB
            TB = [mem.ap(BF16, w0 + i * KB, [512]) for i in range(2)]; w0 += 2 * KB
            UU = [mem.ap(F32, w0 + i * 2 * KB, [512]) for i in range(2)]; w0 += 4 * KB
            VV = [mem.ap(F32, w0 + i * 2 * KB, [512]) for i in range(2)]; w0 += 4 * KB
            MIX = [mem.ap(BF16, w0, [8, 512]) for i in range(2)]; w0 += 8 * KB
            X5 = [mem.ap(F32, w0 + i * 4 * KB, [D]) for i in range(2)]; w0 += 8 * KB
            assert w0 <= R_END
            for piece in range(4):
                src = win_d[:, OFF_GA + piece * 512:OFF_GA + (piece + 1) * 512].rearrange("(k p) n -> p k n", p=128)
                sch.dma("pool", "wg", lambda e, piece=piece, src=src: e.dma_start(out=WG[:, :, piece * 512:(piece + 1) * 512], in_=src), writes=["wg"])
            sch.dma("pool", "wab", lambda e: e.dma_start(out=WA, in_=wa_d.rearrange("(k p) n -> p k n", p=128)), writes=["wab"])
            sch.dma("pool", "wab", lambda e: e.dma_start(out=WB, in_=wb_d.rearrange("(k p) n -> p k n", p=128)), writes=["wab"])
            sch.dma("pool", "wo", lambda e: e.dma_start(out=WO, in_=wo_d.rearrange("(k p) n -> p k n", p=128)), writes=["wo"])
            B_GA, B_GB, B_YA, B_YB, B_O = (0, 1), (2, 3), 4, 5, (6, 7)
            n5 = {"g": 0, "o": 0, "x": 0}
            mix = MIX[0]
            mixn = "mix0"

            def p5_merge(c, m):
                cs = slice(c * 512, (c + 1) * 512)
                gi = n5["g"] % 2
                n5["g"] += 1
                bga, bgb = B_GA[gi], B_GB[gi]

                def gate_mm(bk, coff):
                    for k in range(8):
                        sch.op("pe", lambda e, k=k: e.matmul(bank_f32(bk), lhsT=WG[:, k, coff:coff + 128], rhs=hT[:, k, cs],
                                                             start=(k == 0), stop=(k == 7)),
                               reads=["wg"], writes=["bank%d" % bk])
                gate_mm(bga, m * 128)
                gate_mm(bgb, 1024 + m * 128)
                for k in range(2):
                    sch.op("pe", lambda e, k=k: e.matmul(bank_f32(B_YA), lhsT=WA[:, k, m * 128:(m + 1) * 128], rhs=oaT[:, k, cs],
                                                         start=(k == 0), stop=(k == 1)), reads=["wab"], writes=["bank%d" % B_YA])
                for k in range(4):
                    sch.op("pe", lambda e, k=k: e.matmul(bank_f32(B_YB), lhsT=WB[:, k, m * 128:(m + 1) * 128], rhs=obT[:, k, cs],
                                                         start=(k == 0), stop=(k == 3)), reads=["wab"], writes=["bank%d" % B_YB])
                sch.op("act", lambda e: e.activation(out=TA[gi], in_=bank_f32(bga), func=AF.Tanh, scale=0.5),
                       reads=["bank%d" % bga], writes=["ta%d" % gi])
                sch.op("act", lambda e: e.activation(out=TB[gi], in_=bank_f32(bgb), func=AF.Tanh, scale=0.5),
                       reads=["bank%d" % bgb], writes=["tb%d" % gi])
                sch.op("dve", lambda e: e.scalar_tensor_tensor(out=UU[gi], in0=TA[gi], scalar=1.0, in1=bank_f32(B_YA), op0=ALU.add, op1=ALU.mult),
                       reads=["ta%d" % gi, "bank%d" % B_YA], writes=["uu%d" % gi])
                sch.op("dve", lambda e: e.scalar_tensor_tensor(out=VV[gi], in0=TB[gi], scalar=1.0, in1=bank_f32(B_YB), op0=ALU.add, op1=ALU.mult),
                       reads=["tb%d" % gi, "bank%d" % B_YB], writes=["vv%d" % gi])
                sch.op("pool", lambda e: e.tensor_tensor(out=mix[:, m, :], in0=UU[gi], in1=VV[gi], op=ALU.add),
                       reads=["uu%d" % gi, "vv%d" % gi], writes=[(mixn, m)])

            def p5_out(c, tt):
                t = 4 * c + tt
                xi = n5["x"] % 2
                n5["x"] += 1
                xt = X5[xi]
                sch.dma("sp", "x5l%d" % xi, lambda e: e.dma_start(out=xt, in_=x_d[t * 128:(t + 1) * 128, :]), writes=["x5_%d" % xi])

                def half(hf):
                    bo = B_O[n5["o"] % 2]
                    n5["o"] += 1
                    for k in range(8):
                        sch.op("pe", lambda e, k=k: e.matmul(bank_f32(bo), lhsT=mix[:, k, tt * 128:(tt + 1) * 128],
                                                             rhs=WO[:, k, hf * 512:(hf + 1) * 512], start=(k == 0), stop=(k == 7)),
                               reads=["wo"] + [(mixn, mm) for mm in range(8)], writes=["bank%d" % bo])
                    sch.op("dve", lambda e: e.scalar_tensor_tensor(
                        out=xt[:, hf * 512:(hf + 1) * 512], in0=bank_f32(bo), scalar=0.5, in1=xt[:, hf * 512:(hf + 1) * 512],
                        op0=ALU.mult, op1=ALU.add), reads=["bank%d" % bo, "x5_%d" % xi], writes=["x5_%d" % xi])
                half(0)
                half(1)
                sch.dma("sp", "x5s%d" % xi, lambda e: e.dma_start(out=x1_d[t * 128:(t + 1) * 128, :], in_=xt),
                        reads=["x5_%d" % xi], writes=[("x1", t)])

            for c in range(NCH):
                for m in range(8):
                    p5_merge(c, m)
                for tt in range(4):
                    p5_out(c, tt)
            sch.barrier()

            checkpoint("P5")
            w0 = 6 * KB
            WU = mem.ap(BF16, w0, [8, DFF]); w0 += 64 * KB
            WD = mem.ap(BF16, w0, [32, D]); w0 += 64 * KB
            AT = mem.ap(BF16, w0, [32, 512]); w0 += 32 * KB
            H2T = mem.ap(BF16, w0, [8, 512]); w0 += 8 * KB
            X6 = [mem.ap(F32, w0 + i * 4 * KB, [D]) for i in range(5)]; w0 += 20 * KB
            GB2 = mem.ap(F32, w0, [D]); w0 += 4 * KB
            HB6 = [mem.ap(BF16, w0 + i * 2 * KB, [D]) for i in range(2)]; w0 += 4 * KB
            RR = [mem.ap(F32, w0 + i * 2 * KB, [512]) for i in range(2)]; w0 += 4 * KB
            SS6 = mem.ap(F32, w0, [NT]); w0 += 128
            assert w0 <= R_END, w0
            g2_b = bass.AP(ln2_d.tensor, 0, [[0, 128], [1, D]])
            sch.dma("sp", "gb", lambda e: e.dma_start(out=GB2, in_=g2_b), writes=["gb"])
            for piece in range(8):
                src = wu_d[:, piece * 512:(piece + 1) * 512].rearrange("(k p) n -> p k n", p=128)
                sch.dma("pool", "wu%d" % piece, lambda e, piece=piece, src=src: e.dma_start(out=WU[:, :, piece * 512:(piece + 1) * 512], in_=src),
                        writes=[("wu", piece)])
            for piece in range(8):
                src = wd_d[piece * 512:(piece + 1) * 512, :].rearrange("(k p) n -> p k n", p=128)
                sch.dma("pool", "wd%d" % piece, lambda e, piece=piece, src=src: e.dma_start(out=WD[:, piece * 4:(piece + 1) * 4, :], in_=src),
                        writes=[("wd", piece)])
            B_TP, B_U, B_D = (0, 1), (2, 3, 4), (5, 6, 7)
            n6 = {"x": 0, "u": 0, "d": 0, "r": 0, "hb": 0}

            def p6_up(f):
                bu = B_U[n6["u"] % 3]
                n6["u"] += 1
                ri = n6["r"] % 2
                n6["r"] += 1
                for k in range(8):
                    sch.op("pe", lambda e, k=k: e.matmul(bank_f32(bu), lhsT=WU[:, k, f * 128:(f + 1) * 128], rhs=H2T[:, k, :],
                                                         start=(k == 0), stop=(k == 7)),
                           reads=[("wu", f // 4)] + [("h2t", tt) for tt in range(4)], writes=["bank%d" % bu])
                sch.op("act", lambda e: e.activation(out=RR[ri], in_=bank_f32(bu), func=AF.Relu), reads=["bank%d" % bu], writes=["rr%d" % ri])
                sch.op("dve", lambda e: e.tensor_tensor(out=AT[:, f, :], in0=RR[ri], in1=RR[ri], op=ALU.mult),
                       reads=["rr%d" % ri], writes=[("at", f)])

            def p6_down(t, tt, xi):
                xt = X6[xi]

                def half(hf):
                    bd = B_D[n6["d"] % 3]
                    n6["d"] += 1
                    for f in range(32):
                        sch.op("pe", lambda e, f=f: e.matmul(bank_f32(bd), lhsT=AT[:, f, tt * 128:(tt + 1) * 128],
                                                             rhs=WD[:, f, hf * 512:(hf + 1) * 512], start=(f == 0), stop=(f == 31)),
                               reads=[("wd", f // 4), ("at", f)], writes=["bank%d" % bd])
                    sch.op("dve", lambda e: e.tensor_tensor(out=xt[:, hf * 512:(hf + 1) * 512], in0=bank_f32(bd),
                                                            in1=xt[:, hf * 512:(hf + 1) * 512], op=ALU.add),
                           reads=["bank%d" % bd, "x6_%d" % xi], writes=["x6_%d" % xi])
                half(0)
                half(1)
                sch.dma("sp", "x6s%d" % xi, lambda e: e.dma_start(out=out_d[t * 128:(t + 1) * 128, :], in_=xt),
                        reads=["x6_%d" % xi], writes=[("out", t)])

            for tb in range(NCH):
                xis = []
                for tt in range(4):
                    t = 4 * tb + tt
                    xi = n6["x"] % 5
                    n6["x"] += 1
                    xis.append(xi)
                    hbi = n6["hb"] % 2
                    n6["hb"] += 1
                    prenorm_tile(x1_d[t * 128:(t + 1) * 128, :], X6[xi], "x6_%d" % xi, "x6l%d" % xi, GB2, HB6[hbi], "hb6_%d" % hbi, HB6[hbi],
                                 SS6[:, t:t + 1], "ss6_%d" % t, B_TP[t % 2], H2T[:, :, tt * 128:(tt + 1) * 128], [("h2t", tt)],
                                 junk_name="hb6_%d" % hbi)
                for f in range(32):
                    p6_up(f)
                for tt in range(4):
                    p6_down(4 * tb + tt, tt, xis[tt])

        try:
            emit_all()
        except _Stop:
            sch.barrier()
        sch.final_wait("sp", ["x6s%d" % i for i in range(5)] + (["dbg"] if DEBUG else []))

        sch.finalize()
        block = es.enter_context(nc.Block())

        @block.sync
        def _(e):
            sch.replay("sp", e)

        @block.gpsimd
        def _(e):
            sch.replay("pool", e)

        @block.scalar
        def _(e):
            sch.replay("act", e)

        @block.vector
        def _(e):
            sch.replay("dve", e)

        @block.tensor
        def _(e):
            sch.replay("pe", e)
    return nc


_CACHE = {}


def kernel(x, positions, ln1_g, w_in, q_norm_a, k_norm_a, q_norm_b, k_norm_b, sinks,
           w_branch_a, w_branch_b, w_out, ln2_g, w_up, w_down):
    if "nc" not in _CACHE:
        _CACHE["nc"] = build_program()
    nc = _CACHE["nc"]
    cst = host_consts()
    f32 = lambda a: np.ascontiguousarray(np.asarray(a), dtype=np.float32)
    shared = {
        "cst": cst,
        "ln1_g": f32(ln1_g), "ln2_g": f32(ln2_g), "w_in": f32(w_in)[0],
        "q_norm_a": f32(q_norm_a), "k_norm_a": f32(k_norm_a), "q_norm_b": f32(q_norm_b), "k_norm_b": f32(k_norm_b),
        "sinks": f32(sinks), "w_branch_a": f32(w_branch_a)[0], "w_branch_b": f32(w_branch_b)[0],
        "w_out": f32(w_out)[0], "w_up": f32(w_up)[0], "w_down": f32(w_down)[0],
    }
    xs = f32(x)
    ps = np.ascontiguousarray(np.asarray(positions), dtype=np.int32)
    in_maps = []
    for b in range(8):
        m = dict(shared)
        m["x"] = xs[b]
        m["pos"] = ps[b:b + 1]
        in_maps.append(m)
    res = run_bass_kernel_spmd(nc, in_maps, core_ids=list(range(8)))
    _CACHE["last"] = res
    out = np.stack([np.asarray(r["out"], dtype=np.float32) for r in res.results], axis=0)
    return out
```

```python
import math
from contextlib import ExitStack

import numpy as np
import concourse.bass as bass
import concourse.mybir as mybir
from concourse.bass_utils import run_bass_kernel_spmd

F32 = mybir.dt.float32
BF16 = mybir.dt.bfloat16
I32 = mybir.dt.int32
AF = mybir.ActivationFunctionType
ALU = mybir.AluOpType

S = 4096
D = 1024
DFF = 4096
NCH = 8
NT = 32
EPS = 1e-6
ARENA_ELEMS = 105984

OFF_QA, OFF_KA, OFF_VA, OFF_QB, OFF_KB, OFF_VB, OFF_GA, OFF_GB = 0, 768, 1536, 2304, 2816, 2944, 3072, 4096

C_IDENT, C_BONES, C_PERM, C_MASK = 0, 128, 256, 384
C_BF_COLS = 384 + 5 * 512
C_INVF = C_BF_COLS
C_F32_COLS = 8
CST_COLS = C_BF_COLS + C_F32_COLS

DEBUG = False


def host_consts():
    c = np.zeros((128, CST_COLS), np.float32)
    c[:, C_IDENT:C_IDENT + 128] = np.eye(128, dtype=np.float32)
    bo = np.zeros((128, 128), np.float32)
    bo[0:64, 0:64] = 1.0
    bo[64:128, 64:128] = 1.0
    c[:, C_BONES:C_BONES + 128] = bo
    pm = np.zeros((128, 128), np.float32)
    for hb in (0, 64):
        for i in range(8):
            pm[hb + i + 8, hb + i] = -1.0
            pm[hb + i, hb + i + 8] = 1.0
    c[:, C_PERM:C_PERM + 128] = pm
    k = np.arange(128)[:, None]
    q = np.arange(128)[None, :]
    diag = (k <= q).astype(np.float32)
    prev_g = (k >= q).astype(np.float32)
    prev_b = (k > q).astype(np.float32)
    zero = np.zeros((128, 128), np.float32)
    masks = [
        np.concatenate([zero, diag, prev_g, diag], axis=1),
        np.concatenate([prev_g, diag, prev_g, diag], axis=1),
        np.concatenate([zero, diag, prev_b, diag], axis=1),
        np.concatenate([prev_b, diag, prev_b, diag], axis=1),
        np.concatenate([zero, diag, zero, diag], axis=1),
    ]
    for i, m in enumerate(masks):
        c[:, C_MASK + 512 * i:C_MASK + 512 * (i + 1)] = m
    inv_freq = (500000.0 ** (-np.arange(0, 16, 2, dtype=np.float32) / 16.0)).astype(np.float32)
    invf = np.zeros(128, np.float32)
    for p in range(128):
        if p % 64 < 16:
            invf[p] = inv_freq[(p % 64) % 8]
    c[:, C_INVF] = invf
    c[:, C_INVF + 1] = EPS
    return c


class Sched:
    ENGS = ("pe", "act", "dve", "pool", "sp")

    def __init__(self, nc, es):
        self.nc = nc
        self.es = es
        self.q = {e: [] for e in self.ENGS}
        self.res = {}
        self.sem = {e: es.enter_context(nc.semaphore("s_" + e)) for e in ("pe", "act", "dve", "pool")}
        self.dsem = {}
        self.dcnt = {}
        self.defer = None
        self.base_prio = 0.0

    def _dma_sem(self, name):
        if name not in self.dsem:
            self.dsem[name] = self.es.enter_context(self.nc.semaphore("d_" + name))
            self.dcnt[name] = 0
        return self.dsem[name]

    def _deps(self, reads, writes):
        deps = set()
        for r in reads:
            st = self.res.get(r)
            if st and st["w"] is not None:
                deps.add(st["w"])
        for w in writes:
            st = self.res.get(w)
            if st:
                if st["w"] is not None:
                    deps.add(st["w"])
                for d in st["r"]:
                    deps.add(d)
        return deps

    def _commit(self, me, reads, writes):
        for r in reads:
            st = self.res.setdefault(r, {"w": None, "r": []})
            st["r"] = [d for d in st["r"] if d[0] != me[0]] + [me]
        for w in writes:
            self.res[w] = {"w": me, "r": []}

    def begin_defer(self):
        self.defer = []

    def flush(self):
        lastw = {}
        expect = []
        for it in self.defer:
            expect.append({r: lastw.get(r) for r in it[6]})
            for w in it[7]:
                lastw[w] = it[1]
        items = sorted(self.defer, key=lambda x: (x[0], x[1]))
        wnow = {}
        for it in items:
            for r, v in expect[it[1]].items():
                if wnow.get(r) != v:
                    raise RuntimeError("priority order breaks producer of %r at prio %s (%s): expected op %s, saw %s"
                                       % (r, it[0], it[3], v, wnow.get(r)))
            for w in it[7]:
                wnow[w] = it[1]
        self.defer = None
        for (_, _, kind, eng, semname, fn, reads, writes) in items:
            if kind == "op":
                self.op(eng, fn, reads, writes)
            else:
                self.dma(eng, semname, fn, reads, writes)

    def op(self, eng, fn, reads=(), writes=(), prio=None):
        if getattr(self, "defer", None) is not None:
            self.defer.append((self.base_prio + (prio or 0.0), len(self.defer), "op", eng, None, fn, tuple(reads), tuple(writes)))
            return None
        deps = self._deps(reads, writes)
        idx = len(self.q[eng])
        self.q[eng].append({"fn": fn, "deps": deps, "kind": "op", "marked": False})
        self._commit((eng, idx), reads, writes)
        return (eng, idx)

    def dma(self, eng, semname, fn, reads=(), writes=(), prio=None):
        if getattr(self, "defer", None) is not None:
            self.defer.append((self.base_prio + (prio or 0.0), len(self.defer), "dma", eng, semname, fn, tuple(reads), tuple(writes)))
            return None
        self._dma_sem(semname)
        deps = self._deps(reads, writes)
        self.dcnt[semname] += 1
        me = ("dma:" + semname, self.dcnt[semname])
        self.q[eng].append({"fn": fn, "deps": deps, "kind": "dma", "sem": semname})
        self._commit(me, reads, writes)
        return me

    def barrier(self):
        deps = set()
        for e in ("pe", "act", "dve", "pool"):
            for i in range(len(self.q[e]) - 1, -1, -1):
                if self.q[e][i]["kind"] == "op":
                    deps.add((e, i))
                    break
        for name, cnt in self.dcnt.items():
            if cnt:
                deps.add(("dma:" + name, cnt))
        for e in self.ENGS:
            self.q[e].append({"fn": None, "deps": set(deps), "kind": "bar"})
        self.res = {}

    def final_wait(self, eng, semnames):
        deps = set(("dma:" + n, self.dcnt[n]) for n in semnames if self.dcnt.get(n))
        self.q[eng].append({"fn": None, "deps": deps, "kind": "bar"})

    def finalize(self):
        for e in self.ENGS:
            for ins in self.q[e]:
                for (dom, idx) in ins["deps"]:
                    if not dom.startswith("dma:"):
                        if dom == "pe" and e == "pe":
                            continue
                        self.q[dom][idx]["marked"] = True
        self.ordinal = {}
        for e in ("pe", "act", "dve", "pool"):
            n = 0
            for i, ins in enumerate(self.q[e]):
                if ins.get("marked"):
                    n += 1
                    self.ordinal[(e, i)] = n
        self.total_incs = n

    def replay(self, eng, eobj):
        seen = {}
        for ins in self.q[eng]:
            need = {}
            for (dom, idx) in ins["deps"]:
                if dom.startswith("dma:"):
                    val = 16 * idx
                else:
                    if dom == "pe" and eng == "pe":
                        continue
                    val = self.ordinal[(dom, idx)]
                if val > need.get(dom, 0):
                    need[dom] = val
            for dom, val in need.items():
                if seen.get(dom, 0) >= val:
                    continue
                seen[dom] = val
                sem = self.dsem[dom[4:]] if dom.startswith("dma:") else self.sem[dom]
                eobj.wait_ge(sem, val)
            if ins["fn"] is None:
                continue
            bi = ins["fn"](eobj)
            if ins["kind"] == "dma":
                bi.then_inc(self.dsem[ins["sem"]], 16)
            elif ins.get("marked"):
                bi.then_inc(self.sem[eng], 1)


class Mem:
    def __init__(self, arena):
        self.h = {BF16: arena, F32: arena.bitcast(F32), I32: arena.bitcast(I32)}
        self.pstep = {BF16: ARENA_ELEMS, F32: ARENA_ELEMS // 2, I32: ARENA_ELEMS // 2}

    def ap(self, dt, byte_off, shape, parts=128, p0=0):
        esz = 2 if dt == BF16 else 4
        assert byte_off % esz == 0
        dims = [[self.pstep[dt], parts]]
        stride = 1
        rev = []
        for n in reversed(shape):
            rev.append([stride, n])
            stride *= n
        dims += list(reversed(rev))
        assert byte_off + stride * esz <= ARENA_ELEMS * 2, (byte_off, stride, esz)
        return bass.AP(self.h[dt], p0 * self.pstep[dt] + byte_off // esz, dims)


KB = 1024


def build_program():
    nc = bass.Bass("TRN2", target_bir_lowering=False)
    dr = {}

    def din(name, shape, dt=F32):
        dr[name] = nc.dram_tensor(name, shape, dt, kind="ExternalInput")
        return dr[name].ap()

    x_d = din("x", [S, D])
    pos_d = din("pos", [1, S], I32)
    cst_d = din("cst", [128, CST_COLS])
    ln1_d = din("ln1_g", [1, D])
    ln2_d = din("ln2_g", [1, D])
    win_d = din("w_in", [D, 5120])
    qna_d = din("q_norm_a", [1, 64])
    kna_d = din("k_norm_a", [1, 64])
    qnb_d = din("q_norm_b", [1, 64])
    knb_d = din("k_norm_b", [1, 64])
    snk_d = din("sinks", [1, 8])
    wa_d = din("w_branch_a", [256, D])
    wb_d = din("w_branch_b", [512, D])
    wo_d = din("w_out", [D, D])
    wu_d = din("w_up", [D, DFF])
    wd_d = din("w_down", [DFF, D])
    out_h = nc.dram_tensor("out", [S, D], F32, kind="ExternalOutput")
    out_d = out_h.ap()
    x1_h = nc.dram_tensor("x1_scratch", [S, D], F32, kind="Internal")
    x1_d = x1_h.ap()
    dbg = {}
    if DEBUG:
        for name, shape, dt in (("dbg_hT", [128, 8 * S], BF16), ("dbg_oaT", [128, 2 * S], BF16),
                                ("dbg_obT", [128, 4 * S], BF16), ("dbg_tab", [128, 2 * S], BF16),
                                ("dbg_qk", [128, 2 * S], BF16)):
            dbg[name] = nc.dram_tensor(name, shape, dt, kind="ExternalOutput").ap()

    with ExitStack() as es:
        arena = es.enter_context(nc.sbuf_tensor("arena", [128, ARENA_ELEMS], BF16))
        mem = Mem(arena)
        banks = [es.enter_context(nc.psum_tensor("bank%d" % i, [128, 512], F32)) for i in range(8)]
        sch = Sched(nc, es)
        import os as _os
        _stop = _os.environ.get("KSTOP", "")

        class _Stop(Exception):
            pass

        def checkpoint(name):
            if _stop == name:
                raise _Stop()

        def bank_f32(i):
            return banks[i][:, :]

        def bank_bf16(i):
            return banks[i][:, :].bitcast(BF16)

        R_H_START = 7 * KB
        o = 0
        IDENT = mem.ap(BF16, o, [128]); o += 256
        BONES = mem.ap(BF16, o, [128]); o += 256
        PERM = mem.ap(BF16, o, [128]); o += 256
        MASKS = mem.ap(BF16, o, [5, 512]); o += 5120
        CF32 = mem.ap(F32, o, [C_F32_COLS]); o += 4 * C_F32_COLS
        GAINS = mem.ap(F32, o, [4]); o += 16
        ESINK = mem.ap(F32, o, [8]); o += 32
        o = (o + 63) // 64 * 64
        assert o <= 6 * KB + 1024
        assert o <= R_H_START
        R_H = 7 * KB
        R_O = 71 * KB
        R_T = 119 * KB
        R_W = 135 * KB
        R_END = ARENA_ELEMS * 2
        hT = mem.ap(BF16, R_H, [8, S])
        TC = mem.ap(BF16, R_T, [S])
        TS = mem.ap(BF16, R_T + 8 * KB, [S])
        oaT = mem.ap(BF16, R_O, [2, S])
        obT = mem.ap(BF16, R_O + 16 * KB, [4, S])
        INVF = CF32[:, 0:1]
        EPSC = CF32[:, 1:2]

        def emit_all():
            cbf = mem.ap(BF16, 0, [C_BF_COLS])
            sch.dma("pool", "cstb", lambda e: e.dma_start(out=cbf, in_=cst_d[:, 0:C_BF_COLS]), writes=["consts"])
            sch.dma("sp", "cst", lambda e: e.dma_start(out=CF32, in_=cst_d[:, C_BF_COLS:CST_COLS]), writes=["consts"])
            for gi, gd in enumerate((qna_d, kna_d, qnb_d, knb_d)):
                for hb in (0, 64):
                    src = bass.AP(gd.tensor, 0, [[1, 64], [1, 1]])
                    sch.dma("sp", "cst", lambda e, gi=gi, hb=hb, src=src: e.dma_start(out=GAINS[hb:hb + 64, gi:gi + 1], in_=src),
                            writes=["consts"])
            snk_b = bass.AP(snk_d.tensor, 0, [[0, 128], [1, 8]])
            sch.dma("sp", "cst", lambda e: e.dma_start(out=ESINK, in_=snk_b), writes=["consts"])
            sch.op("act", lambda e: e.activation(out=ESINK, in_=ESINK, func=AF.Exp), reads=["consts"], writes=["esink"])

            WP = [mem.ap(BF16, R_W + i * 6 * KB, [8, 384]) for i in range(2)]
            PASSES = []
            for sp in range(2):
                for (g, d, mode) in ((2, 16, "copy"), (1, 4, "add"), (0, 1, "final")):
                    PASSES.append(dict(name="g%d_%d" % (g, sp), d=d, qcol=OFF_QA + g * 256 + sp * 128, kcol=OFF_KA + g * 256 + sp * 128,
                                       vcol=OFF_VA + g * 256 + sp * 128, vdup=False, gq=0, gk=1, mb=0, mode=mode, sinks=None,
                                       out=oaT[:, sp, :], barrier_after=(sp == 1 and mode == "final")))
            for kv in range(2):
                for f in range(2):
                    heads = (4 * kv + 2 * f, 4 * kv + 2 * f + 1)
                    PASSES.append(dict(name="b%d_%d" % (kv, f), d=1, qcol=OFF_QB + heads[0] * 64,
                                       kcol=(OFF_KB + kv * 64) if f == 0 else None, vcol=(OFF_VB + kv * 64) if f == 0 else None,
                                       vdup=True, gq=2, gk=3, mb=2, mode="none", sinks=heads, out=obT[:, 2 * kv + f, :]))
            def emit_wload(pd, slot):
                W = WP[slot]
                wname = "wp%d" % slot

                def wload(dst_lo, src_lo, n):
                    src = win_d[:, src_lo:src_lo + n].rearrange("(k p) n -> p k n", p=128)
                    sch.dma("pool", wname, lambda e: e.dma_start(out=W[:, :, dst_lo:dst_lo + n], in_=src), writes=[wname], prio=-1.0)
                wload(0, pd["qcol"], 128)
                if pd["kcol"] is not None:
                    if pd["vdup"]:
                        wload(128, pd["kcol"], 64)
                        wload(192, pd["kcol"], 64)
                        wload(256, OFF_VB, 128)
                    else:
                        wload(128, pd["kcol"], 128)
                        wload(256, pd["vcol"], 128)

            tA = mem.ap(F32, R_O, [S])
            tAi = mem.ap(I32, R_O, [S])
            tB = mem.ap(F32, R_O + 16 * KB, [S])
            tBi = mem.ap(I32, R_O + 16 * KB, [S])
            tM = mem.ap(F32, R_O + 32 * KB, [S])
            sch.begin_defer()
            _tk = [0]

            def _tbump():
                sch.base_prio = 0.4 + 1.6 * _tk[0]
                _tk[0] += 1

            pos_b = bass.AP(pos_d.tensor, 0, [[0, 128], [1, S]])
            _tbump()
            sch.dma("sp", "pos", lambda e: e.dma_start(out=tAi, in_=pos_b), writes=["tA"])
            _tbump()
            sch.op("dve", lambda e: e.tensor_copy(out=tA, in_=tAi), reads=["tA"], writes=["tA"])
            _tbump()
            sch.op("dve", lambda e: e.tensor_scalar(out=tA, in0=tA, scalar1=INVF, scalar2=None, op0=ALU.mult),
                   reads=["tA", "consts"], writes=["tA"])
            _tbump()
            sch.op("dve", lambda e: e.tensor_scalar(out=tA, in0=tA, scalar1=float(1.0 / (2 * math.pi)), scalar2=None, op0=ALU.mult),
                   reads=["tA"], writes=["tA"])
            for which, tab in ((0, TS), (1, TC)):
                if which == 1:
                    _tbump()
                    sch.op("dve", lambda e: e.tensor_scalar(out=tA, in0=tA, scalar1=0.25, scalar2=None, op0=ALU.add),
                           reads=["tA"], writes=["tA"])
                _tbump()
                sch.op("dve", lambda e: e.tensor_copy(out=tBi, in_=tA), reads=["tA"], writes=["tB"])
                _tbump()
                sch.op("dve", lambda e: e.tensor_copy(out=tB, in_=tBi), reads=["tB"], writes=["tB"])
                _tbump()
                sch.op("dve", lambda e: e.tensor_tensor(out=tB, in0=tA, in1=tB, op=ALU.subtract), reads=["tA", "tB"], writes=["tB"])
                _tbump()
                sch.op("dve", lambda e: e.tensor_single_scalar(out=tM, in_=tB, scalar=0.5, op=ALU.is_gt), reads=["tB"], writes=["tM"])
                _tbump()
                sch.op("dve", lambda e: e.tensor_tensor(out=tB, in0=tB, in1=tM, op=ALU.subtract), reads=["tB", "tM"], writes=["tB"])
                _tbump()
                sch.op("dve", lambda e: e.tensor_single_scalar(out=tM, in_=tB, scalar=-0.5, op=ALU.is_lt), reads=["tB"], writes=["tM"])
                _tbump()
                sch.op("dve", lambda e: e.tensor_tensor(out=tB, in0=tB, in1=tM, op=ALU.add), reads=["tB", "tM"], writes=["tB"])
                _tbump()
                sch.op("act", lambda e, tab=tab: e.activation(out=tab, in_=tB, func=AF.Sin, scale=6.283185),
                       reads=["tB"], writes=["tab%d" % which])
            if DEBUG:
                _tbump()
                sch.dma("sp", "dbg", lambda e: e.dma_start(out=dbg["dbg_tab"][:, 0:S], in_=TC), reads=["tab1"])
                _tbump()
                sch.dma("sp", "dbg", lambda e: e.dma_start(out=dbg["dbg_tab"][:, S:2 * S], in_=TS), reads=["tab0"])

            sch.base_prio = 0.0
            checkpoint("T")
            emit_wload(PASSES[0], 0)
            def prenorm_tile(xsrc_ap, xs, xs_name, sem_name, g_b, hb, hb_name, junk, ss_col, ss_name, tp_bank, dst_ap, dst_names,
                             load_eng="sp", junk_name="junk", P=0.0, dma_off=-2.0, tr_off=0.5, cp_off=1.5):
                sch.dma(load_eng, sem_name, lambda e: e.dma_start(out=xs, in_=xsrc_ap), writes=[xs_name], prio=P + dma_off)
                sch.op("act", lambda e: e.activation(out=junk, in_=xs, func=AF.Square, accum_out=ss_col),
                       reads=[xs_name], writes=[junk_name, ss_name], prio=P)
                sch.op("act", lambda e: e.activation(out=ss_col, in_=ss_col, func=AF.Ln, scale=1.0 / D, bias=EPSC),
                       reads=[ss_name, "consts"], writes=[ss_name], prio=P + 0.02)
                sch.op("act", lambda e: e.activation(out=ss_col, in_=ss_col, func=AF.Exp, scale=-0.5),
                       reads=[ss_name], writes=[ss_name], prio=P + 0.04)
                sch.op("dve", lambda e: e.scalar_tensor_tensor(out=hb, in0=xs, scalar=ss_col, in1=g_b, op0=ALU.mult, op1=ALU.mult),
                       reads=[xs_name, ss_name, "gb"], writes=[hb_name], prio=P + 0.06)
                pT = bank_bf16(tp_bank)
                for k in range(8):
                    sch.op("pe", lambda e, k=k: e.transpose(out=pT[:, k * 128:(k + 1) * 128], in_=hb[:, k * 128:(k + 1) * 128], identity=IDENT),
                           reads=[hb_name, "consts"], writes=["bank%d" % tp_bank], prio=P + tr_off)
                sch.op("act", lambda e: e.activation(out=dst_ap, in_=pT.rearrange("p (k t) -> p k t", k=8), func=AF.Copy),
                       reads=["bank%d" % tp_bank], writes=dst_names, prio=P + cp_off)

            p0 = R_W + 48 * KB
            XS = [mem.ap(F32, p0 + i * 4 * KB, [D]) for i in range(3)]
            HB = [mem.ap(BF16, p0 + 12 * KB + i * 2 * KB, [D]) for i in range(2)]
            JUNK = mem.ap(BF16, p0 + 16 * KB, [D])
            GB1 = mem.ap(F32, p0 + 18 * KB, [D])
            SSC = mem.ap(F32, p0 + 22 * KB, [NT])
            assert p0 + 22 * KB + 4 * NT <= R_END
            g1_b = bass.AP(ln1_d.tensor, 0, [[0, 128], [1, D]])
            sch.dma("sp", "gb", lambda e: e.dma_start(out=GB1, in_=g1_b), writes=["gb"])
            for t in range(NT):
                prenorm_tile(x_d[t * 128:(t + 1) * 128, :], XS[t % 3], "xs%d" % (t % 3), "xs%d" % (t % 3), GB1,
                             HB[t % 2], "hb%d" % (t % 2), JUNK, SSC[:, t:t + 1], "ss%d" % t, t % 2,
                             hT[:, :, t * 128:(t + 1) * 128], [("hT", t)], P=float(t))
            sch.flush()
            if DEBUG:
                sch.dma("sp", "dbg", lambda e: e.dma_start(out=dbg["dbg_hT"], in_=hT.rearrange("p k t -> p (k t)")),
                        reads=[("hT", t) for t in range(NT)])
            sch.barrier()

            checkpoint("P0")
            w0 = R_W + 12 * KB
            QT = mem.ap(BF16, w0, [S]); w0 += 8 * KB
            KT = mem.ap(BF16, w0, [S]); w0 += 8 * KB
            VG = mem.ap(BF16, w0, [NT, 2, 128]); w0 += 16 * KB
            SQ = [mem.ap(BF16, w0 + i * KB, [512]) for i in range(2)]; w0 += 2 * KB
            RV = [mem.ap(F32, w0 + i * 2 * KB, [512]) for i in range(2)]; w0 += 4 * KB
            QN = [mem.ap(BF16, w0 + i * KB, [512]) for i in range(2)]; w0 += 2 * KB
            T1 = [mem.ap(F32, w0 + i * 2 * KB, [512]) for i in range(2)]; w0 += 4 * KB
            T2 = [mem.ap(F32, w0 + i * 2 * KB, [512]) for i in range(2)]; w0 += 4 * KB
            PT = [mem.ap(BF16, w0 + i * KB, [512]) for i in range(4)]; w0 += 4 * KB
            RD = [mem.ap(F32, w0 + i * 2 * KB, [512]) for i in range(2)]; w0 += 4 * KB
            PMB = [mem.ap(BF16, w0 + i * KB, [512]) for i in range(2)]; w0 += 2 * KB
            assert w0 <= R_END, w0
            ACC = mem.ap(F32, R_O + 16 * KB, [2, S])
            sch.op("pool", lambda e: e.memset(VG[:, :, 0, 64:128], 1.0), writes=["vg_ones"])
            sch.op("pool", lambda e: e.memset(VG[:, :, 1, 0:64], 1.0), writes=["vg_ones"])

            B_PJ = (0, 1)
            B_SS, B_PM, B_SA, B_SB, B_OT, B_VP = 2, 3, 4, 5, 6, 7
            cnt = {"pj": 0, "blk": 0, "pt": 0, "rd": 0, "w": 0}

            def gcol_ap(buf, d, c):
                L = S // d
                u = 512 // d
                return buf.rearrange("p (r l) -> p r l", r=d)[:, :, u * c:u * (c + 1)]

            def nat_ap(t, d):
                return t.rearrange("p (u r) -> p r u", r=d)

            def gblocks_of_chunk(d, c):
                L = S // d
                u = 512 // d
                blks = set()
                for r in range(d):
                    for col in range(r * L + u * c, r * L + u * (c + 1), min(u, 128)):
                        blks.add(col // 128)
                return sorted(blks)

            def attention_pass(pd, wslot, next_pd):
                name, d, kcol, vcol, vdup = pd["name"], pd["d"], pd["kcol"], pd["vcol"], pd["vdup"]
                gq_idx, gk_idx, mask_base, acc_mode = pd["gq"], pd["gk"], pd["mb"], pd["mode"]
                sink_heads, out_fchunk_ap = pd["sinks"], pd["out"]
                L = S // d
                bpr = L // 128
                W = WP[wslot]
                wname = "wp%d" % wslot
                nq = 2 if kcol is not None else 1
                sch.begin_defer()
                if next_pd is not None:
                    emit_wload(next_pd, 1 - wslot)

                def proj_qk(i, c, which):
                    P = float(i)
                    pj = B_PJ[cnt["pj"] % 2]
                    cnt["pj"] += 1
                    b = cnt["blk"] % 2
                    cnt["blk"] += 1
                    pjn = "bank%d" % pj
                    for k in range(8):
                        sch.op("pe", lambda e, k=k: e.matmul(bank_f32(pj), lhsT=W[:, k, which * 128:(which + 1) * 128],
                                                             rhs=hT[:, k, c * 512:(c + 1) * 512], start=(k == 0), stop=(k == 7)),
                               reads=[wname] + [("hT", t) for t in range(4 * c, 4 * c + 4)], writes=[pjn], prio=P)
                    sch.op("act", lambda e: e.activation(out=SQ[b], in_=bank_f32(pj), func=AF.Square), reads=[pjn], writes=["sq%d" % b],
                           prio=P + 0.02)
                    sch.op("pe", lambda e: e.matmul(bank_f32(B_SS), lhsT=BONES, rhs=SQ[b], start=True, stop=True),
                           reads=["sq%d" % b, "consts"], writes=["bank%d" % B_SS], prio=P + 1.04)
                    sch.op("act", lambda e: e.activation(out=RV[b], in_=bank_f32(B_SS), func=AF.Ln, scale=1.0 / 64, bias=EPSC),
                           reads=["bank%d" % B_SS, "consts"], writes=["rv%d" % b], prio=P + 1.06)
                    sch.op("act", lambda e: e.activation(out=RV[b], in_=RV[b], func=AF.Exp, scale=-0.5), reads=["rv%d" % b], writes=["rv%d" % b],
                           prio=P + 1.08)
                    gi = gq_idx if which == 0 else gk_idx
                    sch.op("dve", lambda e: e.scalar_tensor_tensor(out=QN[b], in0=bank_f32(pj), scalar=GAINS[:, gi:gi + 1], in1=RV[b],
                                                                   op0=ALU.mult, op1=ALU.mult),
                           reads=[pjn, "rv%d" % b, "consts"], writes=["qn%d" % b], prio=P + 1.10)
                    sch.op("pe", lambda e: e.matmul(bank_f32(B_PM), lhsT=PERM, rhs=QN[b], start=True, stop=True),
                           reads=["qn%d" % b, "consts"], writes=["bank%d" % B_PM], prio=P + 2.12)
                    sch.op("dve", lambda e: e.tensor_tensor(out=T1[b], in0=QN[b], in1=TC[:, c * 512:(c + 1) * 512], op=ALU.mult),
                           reads=["qn%d" % b, "tab1"], writes=["t1%d" % b], prio=P + 2.14)
                    sch.op("act", lambda e: e.activation(out=PMB[b], in_=bank_f32(B_PM), func=AF.Copy),
                           reads=["bank%d" % B_PM], writes=["pmb%d" % b], prio=P + 2.13)
                    sch.op("dve", lambda e: e.tensor_tensor(out=T2[b], in0=PMB[b], in1=TS[:, c * 512:(c + 1) * 512], op=ALU.mult),
                           reads=["pmb%d" % b, "tab0"], writes=["t2%d" % b], prio=P + 2.16)
                    dst = QT if which == 0 else KT
                    dname = "qt" if which == 0 else "kt"
                    sch.op("pool", lambda e: e.tensor_tensor(out=gcol_ap(dst, d, c), in0=nat_ap(T1[b], d), in1=nat_ap(T2[b], d), op=ALU.add),
                           reads=["t1%d" % b, "t2%d" % b], writes=[(dname, g) for g in gblocks_of_chunk(d, c)], prio=P + 2.18)

                def proj_v(gb, P):
                    r, j = gb // bpr, gb % bpr
                    t0 = r + d * 128 * j
                    nv = 128
                    toks = sorted(set((t0 + d * i) // 128 for i in (0, 127)))
                    tiles = list(range(toks[0], toks[-1] + 1))
                    vp = bank_f32(B_VP)
                    for k in range(8):
                        lhsT = hT[:, k, t0:t0 + d * 127 + 1:d]
                        sch.op("pe", lambda e, k=k, lhsT=lhsT: e.matmul(vp[:, 0:nv], lhsT=lhsT, rhs=W[:, k, 256:256 + nv],
                                                                       start=(k == 0), stop=(k == 7)),
                               reads=[wname] + [("hT", t) for t in tiles], writes=["bank%d" % B_VP], prio=P)
                    vg0 = VG[:, gb, 0, 0:64]
                    dst = bass.AP(vg0.tensor, vg0.offset, [list(vg0.ap[0]), [192, 2], [1, 64]])
                    if vdup:
                        kvsel = (vcol - OFF_VB) // 64
                        v0 = vp[:, kvsel * 64:(kvsel + 1) * 64]
                        src = bass.AP(v0.tensor, v0.offset, [list(v0.ap[0]), [0, 2], [1, 64]])
                    else:
                        src = vp[:, 0:128].rearrange("p (h c) -> p h c", h=2)
                    sch.op("act", lambda e: e.activation(out=dst, in_=src, func=AF.Copy), reads=["bank%d" % B_VP, "vg_ones"],
                           writes=[("vg", gb), ("vgb", gb)], prio=P + 0.02)

                def acc_tiles(tok0, d):
                    lo = tok0 // 512
                    hi = (tok0 + (255 if d == 1 else 1 + d * 127)) // 512
                    return list(range(lo, hi + 1))

                def round_blocks(n):
                    if d == 1:
                        return [(0, 2 * n), (0, 2 * n + 1)]
                    j, r0 = n // (d // 2), 2 * (n % (d // 2))
                    return [(r0, j), (r0 + 1, j)]

                def attn_round(n, P):
                    qblks = round_blocks(n)
                    if d == 1:
                        first = (qblks[0][1] == 0)
                        mask = MASKS[:, mask_base + (0 if first else 1), :]
                    else:
                        mask = MASKS[:, 4 if qblks[0][1] == 0 else 1, :]
                    sbanks = (B_SA, B_SB)
                    pts = []
                    for half in range(2):
                        sb = bank_f32(sbanks[half])
                        rows = slice(64 * half, 64 * half + 64)
                        for qi in range(2):
                            rq, jq = qblks[qi]
                            gq = rq * bpr + jq
                            for kb in range(2):
                                gk = gq - 1 + kb
                                if jq - 1 + kb < 0:
                                    gk = gq
                                sch.op("pe", lambda e, sb=sb, rows=rows, qi=qi, kb=kb, gk=gk, gq=gq: e.matmul(
                                    sb[:, (2 * qi + kb) * 128:(2 * qi + kb + 1) * 128], lhsT=KT[rows, gk * 128:(gk + 1) * 128],
                                    rhs=QT[rows, gq * 128:(gq + 1) * 128], start=True, stop=True),
                                    reads=[("kt", gk), ("qt", gq)], writes=["bank%d" % sbanks[half]], prio=P + 0.001 * half)
                        p = cnt["pt"] % 4
                        cnt["pt"] += 1
                        pts.append(p)
                        sch.op("act", lambda e, sb=sb, p=p: e.activation(out=PT[p], in_=sb, func=AF.Exp, scale=0.125),
                               reads=["bank%d" % sbanks[half]], writes=["pt%d" % p], prio=P + 0.03 + 0.001 * half)
                        sch.op("dve" if half == 0 else "pool", lambda e, p=p: e.tensor_tensor(out=PT[p], in0=PT[p], in1=mask, op=ALU.mult),
                               reads=["pt%d" % p, "consts"], writes=["pt%d" % p], prio=P + 0.05 + 0.001 * half)
                    ot = bank_f32(B_OT)
                    nmm = 0
                    for half in range(2):
                        for qi in range(2):
                            rq, jq = qblks[qi]
                            gq = rq * bpr + jq
                            for kb in range(2):
                                gk = gq - 1 + kb
                                if jq - 1 + kb < 0:
                                    gk = gq
                                item = 2 * half + qi
                                sch.op("pe", lambda e, half=half, qi=qi, kb=kb, gk=gk, item=item, nmm=nmm: e.matmul(
                                    ot[:, item * 128:(item + 1) * 128], lhsT=VG[:, gk, half, :],
                                    rhs=PT[pts[half]][:, (2 * qi + kb) * 128:(2 * qi + kb + 1) * 128],
                                    start=(nmm == 0), stop=(kb == 1), skip_group_check=True),
                                    reads=[("vg", gk), ("vgb", gk), "pt%d" % pts[half]], writes=["bank%d" % B_OT], prio=P + 1.01)
                                nmm += 1
                    tok0 = qblks[0][0] + d * 128 * qblks[0][1]
                    qstride = 128 if d == 1 else 1
                    otv = ot.rearrange("p (h q i) -> p h q i", h=2, q=2)
                    if sink_heads is None:
                        accv = bass.AP(ACC.tensor, ACC.offset + tok0, [list(ACC.ap[0]), [S, 2], [qstride, 2], [d, 128]])
                        if acc_mode == "copy":
                            sch.op("act", lambda e: e.activation(out=accv, in_=otv, func=AF.Copy), reads=["bank%d" % B_OT],
                                   writes=[("acc", n2) for n2 in acc_tiles(tok0, d)], prio=P + 1.03)
                        else:
                            sch.op("dve", lambda e: e.tensor_tensor(out=accv, in0=otv, in1=accv, op=ALU.add), reads=["bank%d" % B_OT],
                                   writes=[("acc", n2) for n2 in acc_tiles(tok0, d)], prio=P + 1.03)
                    else:
                        geo = []
                        for half in range(2):
                            num = slice(0, 64) if half == 0 else slice(64, 128)
                            den = slice(64, 128) if half == 0 else slice(0, 64)
                            geo.append((half, num, den, slice(256 * half, 256 * half + 256), sink_heads[half]))
                        for (half, num, den, cols, hsink) in geo:
                            sch.op("act", lambda e, half=half, num=num, den=den, cols=cols, hsink=hsink: e.activation(
                                out=RD[half][num, 0:256], in_=ot[den, cols], func=AF.Ln, bias=ESINK[num, hsink:hsink + 1]),
                                reads=["bank%d" % B_OT, "esink"], writes=["rd%d" % half, "ot_act_done"], prio=P + 1.03)
                        for (half, num, den, cols, hsink) in geo:
                            sch.op("act", lambda e, half=half, num=num: e.activation(out=RD[half][num, 0:256], in_=RD[half][num, 0:256],
                                                                                     func=AF.Exp, scale=-1.0),
                                   reads=["rd%d" % half], writes=["rd%d" % half], prio=P + 1.05)
                        for (half, num, den, cols, hsink) in geo:
                            sch.op("dve", lambda e, half=half, num=num, cols=cols: e.tensor_tensor(
                                out=out_fchunk_ap[num, tok0:tok0 + 256], in0=ot[num, cols], in1=RD[half][num, 0:256], op=ALU.mult),
                                reads=["bank%d" % B_OT, "rd%d" % half, "ot_act_done"], writes=[("ob", name, tok0)], prio=P + 1.07)

                round_prio = {}
                cluster = {}
                for n in range(16):
                    cready = max((r + d * (128 * j + 127)) // 512 for (r, j) in round_blocks(n))
                    cluster.setdefault(cready, []).append(n)
                cl = sorted(cluster)
                for ci, c in enumerate(cl):
                    span = ((cl[ci + 1] - c) if ci + 1 < len(cl) else 1) * nq
                    for k, n in enumerate(cluster[c]):
                        round_prio[n] = (c * nq + nq - 1) + 3.3 + k * max(0.5, min(1.0, float(span) / len(cluster[c])))
                assert len(round_prio) == 16
                first_round_of_block = {}
                for n in sorted(range(16), key=lambda n: (round_prio[n], n)):
                    for (r, j) in round_blocks(n):
                        for jj in (j - 1, j):
                            if jj >= 0:
                                first_round_of_block.setdefault(r * bpr + jj, n)
                for c in range(NCH):
                    proj_qk(c * nq, c, 0)
                    if kcol is not None:
                        proj_qk(c * nq + 1, c, 1)
                if kcol is not None:
                    for gb in range(NT):
                        n0 = first_round_of_block[gb]
                        slot = sorted(g2 for g2 in first_round_of_block if first_round_of_block[g2] == n0).index(gb)
                        proj_v(gb, round_prio[n0] - 0.9 + 0.2 * slot)
                for n in sorted(range(16), key=lambda n: (round_prio[n], n)):
                    attn_round(n, round_prio[n])

                if acc_mode == "final":
                    for c in range(NCH):
                        cs = slice(c * 512, (c + 1) * 512)
                        for half in range(2):
                            rb = cnt["rd"] % 2
                            cnt["rd"] += 1
                            num = slice(0, 64) if half == 0 else slice(64, 128)
                            den = slice(64, 128) if half == 0 else slice(0, 64)
                            P = 1000.0 + 2 * c + half
                            sch.op("act", lambda e, rb=rb, den=den, num=num, cs=cs, half=half: e.activation(
                                out=RD[rb][num, :], in_=ACC[den, half, cs], func=AF.Ln), reads=[("acc", c)], writes=["rd%d" % rb], prio=P)
                            sch.op("act", lambda e, rb=rb, num=num: e.activation(out=RD[rb][num, :], in_=RD[rb][num, :], func=AF.Exp, scale=-1.0),
                                   reads=["rd%d" % rb], writes=["rd%d" % rb], prio=P + 0.1)
                            sch.op("pool", lambda e, rb=rb, num=num, cs=cs, half=half: e.tensor_tensor(
                                out=out_fchunk_ap[num, cs], in0=ACC[num, half, cs], in1=RD[rb][num, :], op=ALU.mult),
                                reads=[("acc", c), "rd%d" % rb], writes=[("oa", name, c)], prio=P + 0.2)
                sch.flush()

            for pi, pd in enumerate(PASSES):
                attention_pass(pd, pi % 2, PASSES[pi + 1] if pi + 1 < len(PASSES) else None)
                if pd.get("barrier_after"):
                    sch.barrier()
                    checkpoint("A1")
            if DEBUG:
                sch.dma("sp", "dbg", lambda e: e.dma_start(out=dbg["dbg_qk"][:, 0:S], in_=QT), reads=[("qt", g) for g in range(NT)])
                sch.dma("sp", "dbg", lambda e: e.dma_start(out=dbg["dbg_qk"][:, S:2 * S], in_=KT), reads=[("kt", g) for g in range(NT)])
            sch.barrier()
            if DEBUG:
                sch.dma("sp", "dbg", lambda e: e.dma_start(out=dbg["dbg_oaT"], in_=oaT.rearrange("p k t -> p (k t)")))
                sch.dma("sp", "dbg", lambda e: e.dma_start(out=dbg["dbg_obT"], in_=obT.rearrange("p k t -> p (k t)")))

            checkpoint("A")
            w0 = R_T
            WG = mem.ap(BF16, w0, [8, 2048]); w0 += 32 * KB
            WA = mem.ap(BF16, w0, [2, D]); w0 += 4 * KB
            WB = mem.ap(BF16, w0, [4, D]); w0 += 8 * KB
            WO = mem.ap(BF16, w0, [8, D]); w0 += 16 * KB
            TA = [mem.ap(BF16, w0 + i * KB, [512]) for i in range(2)]; w0 += 2 * KB
            TB = [mem.ap(BF16, w0 + i * KB, [512]) for i in range(2)]; w0 += 2 * KB
            UU = [mem.ap(F32, w0 + i * 2 * KB, [512]) for i in range(2)]; w0 += 4 * KB
            VV = [mem.ap(F32, w0 + i * 2 * KB, [512]) for i in range(2)]; w0 += 4 * KB
            MIX = [mem.ap(BF16, w0, [8, 512]) for i in range(2)]; w0 += 8 * KB
            X5 = [mem.ap(F32, w0 + i * 4 * KB, [D]) for i in range(2)]; w0 += 8 * KB
            assert w0 <= R_END
            for piece in range(4):
                src = win_d[:, OFF_GA + piece * 512:OFF_GA + (piece + 1) * 512].rearrange("(k p) n -> p k n", p=128)
                sch.dma("pool", "wg", lambda e, piece=piece, src=src: e.dma_start(out=WG[:, :, piece * 512:(piece + 1) * 512], in_=src), writes=["wg"])
            sch.dma("pool", "wab", lambda e: e.dma_start(out=WA, in_=wa_d.rearrange("(k p) n -> p k n", p=128)), writes=["wab"])
            sch.dma("pool", "wab", lambda e: e.dma_start(out=WB, in_=wb_d.rearrange("(k p) n -> p k n", p=128)), writes=["wab"])
            sch.dma("pool", "wo", lambda e: e.dma_start(out=WO, in_=wo_d.rearrange("(k p) n -> p k n", p=128)), writes=["wo"])
            B_GA, B_GB, B_YA, B_YB, B_O = (0, 1), (2, 3), 4, 5, (6, 7)
            n5 = {"g": 0, "o": 0, "x": 0}
            mix = MIX[0]
            mixn = "mix0"

            def p5_merge(c, m):
                cs = slice(c * 512, (c + 1) * 512)
                gi = n5["g"] % 2
                n5["g"] += 1
                bga, bgb = B_GA[gi], B_GB[gi]

                def gate_mm(bk, coff):
                    for k in range(8):
                        sch.op("pe", lambda e, k=k: e.matmul(bank_f32(bk), lhsT=WG[:, k, coff:coff + 128], rhs=hT[:, k, cs],
                                                             start=(k == 0), stop=(k == 7)),
                               reads=["wg"], writes=["bank%d" % bk])
                gate_mm(bga, m * 128)
                gate_mm(bgb, 1024 + m * 128)
                for k in range(2):
                    sch.op("pe", lambda e, k=k: e.matmul(bank_f32(B_YA), lhsT=WA[:, k, m * 128:(m + 1) * 128], rhs=oaT[:, k, cs],
                                                         start=(k == 0), stop=(k == 1)), reads=["wab"], writes=["bank%d" % B_YA])
                for k in range(4):
                    sch.op("pe", lambda e, k=k: e.matmul(bank_f32(B_YB), lhsT=WB[:, k, m * 128:(m + 1) * 128], rhs=obT[:, k, cs],
                                                         start=(k == 0), stop=(k == 3)), reads=["wab"], writes=["bank%d" % B_YB])
                sch.op("act", lambda e: e.activation(out=TA[gi], in_=bank_f32(bga), func=AF.Tanh, scale=0.5),
                       reads=["bank%d" % bga], writes=["ta%d" % gi])
                sch.op("act", lambda e: e.activation(out=TB[gi], in_=bank_f32(bgb), func=AF.Tanh, scale=0.5),
                       reads=["bank%d" % bgb], writes=["tb%d" % gi])
                sch.op("dve", lambda e: e.scalar_tensor_tensor(out=UU[gi], in0=TA[gi], scalar=1.0, in1=bank_f32(B_YA), op0=ALU.add, op1=ALU.mult),
                       reads=["ta%d" % gi, "bank%d" % B_YA], writes=["uu%d" % gi])
                sch.op("dve", lambda e: e.scalar_tensor_tensor(out=VV[gi], in0=TB[gi], scalar=1.0, in1=bank_f32(B_YB), op0=ALU.add, op1=ALU.mult),
                       reads=["tb%d" % gi, "bank%d" % B_YB], writes=["vv%d" % gi])
                sch.op("pool", lambda e: e.tensor_tensor(out=mix[:, m, :], in0=UU[gi], in1=VV[gi], op=ALU.add),
                       reads=["uu%d" % gi, "vv%d" % gi], writes=[(mixn, m)])

            def p5_out(c, tt):
                t = 4 * c + tt
                xi = n5["x"] % 2
                n5["x"] += 1
                xt = X5[xi]
                sch.dma("sp", "x5l%d" % xi, lambda e: e.dma_start(out=xt, in_=x_d[t * 128:(t + 1) * 128, :]), writes=["x5_%d" % xi])

                def half(hf):
                    bo = B_O[n5["o"] % 2]
                    n5["o"] += 1
                    for k in range(8):
                        sch.op("pe", lambda e, k=k: e.matmul(bank_f32(bo), lhsT=mix[:, k, tt * 128:(tt + 1) * 128],
                                                             rhs=WO[:, k, hf * 512:(hf + 1) * 512], start=(k == 0), stop=(k == 7)),
                               reads=["wo"] + [(mixn, mm) for mm in range(8)], writes=["bank%d" % bo])
                    sch.op("dve", lambda e: e.scalar_tensor_tensor(
                        out=xt[:, hf * 512:(hf + 1) * 512], in0=bank_f32(bo), scalar=0.5, in1=xt[:, hf * 512:(hf + 1) * 512],
                        op0=ALU.mult, op1=ALU.add), reads=["bank%d" % bo, "x5_%d" % xi], writes=["x5_%d" % xi])
                half(0)
                half(1)
                sch.dma("sp", "x5s%d" % xi, lambda e: e.dma_start(out=x1_d[t * 128:(t + 1) * 128, :], in_=xt),
                        reads=["x5_%d" % xi], writes=[("x1", t)])

            for c in range(NCH):
                for m in range(8):
                    p5_merge(c, m)
                for tt in range(4):
                    p5_out(c, tt)
            sch.barrier()

            checkpoint("P5")
            w0 = 7 * KB
            WU = mem.ap(BF16, w0, [8, DFF]); w0 += 64 * KB
            WD = mem.ap(BF16, w0, [32, D]); w0 += 64 * KB
            AT = mem.ap(BF16, w0, [32, 512]); w0 += 32 * KB
            H2T = mem.ap(BF16, w0, [8, 512]); w0 += 8 * KB
            X6 = [mem.ap(F32, w0 + i * 4 * KB, [D]) for i in range(5)]; w0 += 20 * KB
            GB2 = mem.ap(F32, w0, [D]); w0 += 4 * KB
            HB6 = [mem.ap(BF16, w0 + i * 2 * KB, [D]) for i in range(2)]; w0 += 4 * KB
            RR = [mem.ap(F32, w0 + i * 2 * KB, [512]) for i in range(2)]; w0 += 4 * KB
            SS6 = mem.ap(F32, 6 * KB + 256, [NT])
            assert w0 <= R_END, w0
            g2_b = bass.AP(ln2_d.tensor, 0, [[0, 128], [1, D]])
            sch.dma("sp", "gb", lambda e: e.dma_start(out=GB2, in_=g2_b), writes=["gb"])
            for piece in range(8):
                src = wu_d[:, piece * 512:(piece + 1) * 512].rearrange("(k p) n -> p k n", p=128)
                sch.dma("pool", "wu%d" % piece, lambda e, piece=piece, src=src: e.dma_start(out=WU[:, :, piece * 512:(piece + 1) * 512], in_=src),
                        writes=[("wu", piece)])
            for piece in range(8):
                src = wd_d[piece * 512:(piece + 1) * 512, :].rearrange("(k p) n -> p k n", p=128)
                sch.dma("pool", "wd%d" % piece, lambda e, piece=piece, src=src: e.dma_start(out=WD[:, piece * 4:(piece + 1) * 4, :], in_=src),
                        writes=[("wd", piece)])
            B_TP, B_U, B_D = (0, 1), (2, 3, 4), (5, 6, 7)
            n6 = {"x": 0, "u": 0, "d": 0, "r": 0}
            NSL, FSL = X6[0:3], X6[3:5]
            sch.begin_defer()

            def p6_prenorm(tb, P0):
                for tt in range(4):
                    t = 4 * tb + tt
                    xi = n6["x"] % 3
                    n6["x"] += 1
                    hbi = t % 2
                    prenorm_tile(x1_d[t * 128:(t + 1) * 128, :], NSL[xi], "x6n_%d" % xi, "x6nl%d" % xi, GB2, HB6[hbi], "hb6_%d" % hbi, HB6[hbi],
                                 SS6[:, t:t + 1], "ss6_%d" % t, B_TP[t % 2], H2T[:, :, tt * 128:(tt + 1) * 128], [("h2t", tt)],
                                 junk_name="hb6_%d" % hbi, P=P0 + 10.0 * tt, dma_off=-6.0, tr_off=1.5, cp_off=3.5)

            def p6_up(f, P):
                bu = B_U[n6["u"] % 3]
                n6["u"] += 1
                ri = n6["r"] % 2
                n6["r"] += 1
                for k in range(8):
                    sch.op("pe", lambda e, k=k: e.matmul(bank_f32(bu), lhsT=WU[:, k, f * 128:(f + 1) * 128], rhs=H2T[:, k, :],
                                                         start=(k == 0), stop=(k == 7)),
                           reads=[("wu", f // 4)] + [("h2t", tt) for tt in range(4)], writes=["bank%d" % bu], prio=P)
                sch.op("act", lambda e: e.activation(out=RR[ri], in_=bank_f32(bu), func=AF.Relu), reads=["bank%d" % bu], writes=["rr%d" % ri],
                       prio=P + 0.3)
                sch.op("dve", lambda e: e.tensor_tensor(out=AT[:, f, :], in0=RR[ri], in1=RR[ri], op=ALU.mult),
                       reads=["rr%d" % ri], writes=[("at", f)], prio=P + 0.6)

            def p6_down(t, tt, B):
                fi = t % 2
                xt = FSL[fi]
                sch.dma("sp", "x6fl%d" % fi, lambda e: e.dma_start(out=xt, in_=x1_d[t * 128:(t + 1) * 128, :]), writes=["x6f_%d" % fi],
                        prio=B + 40 + 10 * tt - 9)

                def half(hf):
                    g = 2 * tt + hf
                    bd = B_D[n6["d"] % 3]
                    n6["d"] += 1
                    for f in range(32):
                        sch.op("pe", lambda e, f=f: e.matmul(bank_f32(bd), lhsT=AT[:, f, tt * 128:(tt + 1) * 128],
                                                             rhs=WD[:, f, hf * 512:(hf + 1) * 512], start=(f == 0), stop=(f == 31)),
                               reads=[("wd", f // 4), ("at", f)], writes=["bank%d" % bd], prio=B + 40 + 5 * g)
                    sch.op("dve", lambda e: e.tensor_tensor(out=xt[:, hf * 512:(hf + 1) * 512], in0=bank_f32(bd),
                                                            in1=xt[:, hf * 512:(hf + 1) * 512], op=ALU.add),
                           reads=["bank%d" % bd, "x6f_%d" % fi], writes=["x6f_%d" % fi], prio=B + 40 + 5 * g + 4.5)
                half(0)
                half(1)
                sch.dma("sp", "x6fs%d" % fi, lambda e: e.dma_start(out=out_d[t * 128:(t + 1) * 128, :], in_=xt),
                        reads=["x6f_%d" % fi], writes=[("out", t)], prio=B + 40 + 5 * (2 * tt + 1) + 4.6)

            p6_prenorm(0, -50.0)
            for tb in range(NCH):
                B = 100.0 * tb
                for f in range(32):
                    p6_up(f, B + f)
                if tb + 1 < NCH:
                    p6_prenorm(tb + 1, B + 41.0)
                for tt in range(4):
                    p6_down(4 * tb + tt, tt, B)
            sch.flush()

        try:
            emit_all()
        except _Stop:
            sch.barrier()
        sch.final_wait("sp", ["x6fs%d" % i for i in range(2)] + (["dbg"] if DEBUG else []))

        sch.finalize()
        block = es.enter_context(nc.Block())

        @block.sync
        def _(e):
            sch.replay("sp", e)

        @block.gpsimd
        def _(e):
            sch.replay("pool", e)

        @block.scalar
        def _(e):
            sch.replay("act", e)

        @block.vector
        def _(e):
            sch.replay("dve", e)

        @block.tensor
        def _(e):
            sch.replay("pe", e)
    return nc


_CACHE = {}


def kernel(x, positions, ln1_g, w_in, q_norm_a, k_norm_a, q_norm_b, k_norm_b, sinks,
           w_branch_a, w_branch_b, w_out, ln2_g, w_up, w_down):
    if "nc" not in _CACHE:
        _CACHE["nc"] = build_program()
    nc = _CACHE["nc"]
    cst = host_consts()
    f32 = lambda a: np.ascontiguousarray(np.asarray(a), dtype=np.float32)
    shared = {
        "cst": cst,
        "ln1_g": f32(ln1_g), "ln2_g": f32(ln2_g), "w_in": f32(w_in)[0],
        "q_norm_a": f32(q_norm_a), "k_norm_a": f32(k_norm_a), "q_norm_b": f32(q_norm_b), "k_norm_b": f32(k_norm_b),
        "sinks": f32(sinks), "w_branch_a": f32(w_branch_a)[0], "w_branch_b": f32(w_branch_b)[0],
        "w_out": f32(w_out)[0], "w_up": f32(w_up)[0], "w_down": f32(w_down)[0],
    }
    xs = f32(x)
    ps = np.ascontiguousarray(np.asarray(positions), dtype=np.int32)
    in_maps = []
    for b in range(8):
        m = dict(shared)
        m["x"] = xs[b]
        m["pos"] = ps[b:b + 1]
        in_maps.append(m)
    res = run_bass_kernel_spmd(nc, in_maps, core_ids=list(range(8)))
    _CACHE["last"] = res
    out = np.stack([np.asarray(r["out"], dtype=np.float32) for r in res.results], axis=0)
    return out
```

```python
import math
from contextlib import ExitStack

import numpy as np
import concourse.bass as bass
import concourse.mybir as mybir
from concourse.bass_utils import run_bass_kernel_spmd

F32 = mybir.dt.float32
BF16 = mybir.dt.bfloat16
I32 = mybir.dt.int32
AF = mybir.ActivationFunctionType
ALU = mybir.AluOpType

S = 4096
D = 1024
DFF = 4096
NCH = 8
NT = 32
EPS = 1e-6
ARENA_ELEMS = 105984

OFF_QA, OFF_KA, OFF_VA, OFF_QB, OFF_KB, OFF_VB, OFF_GA, OFF_GB = 0, 768, 1536, 2304, 2816, 2944, 3072, 4096

C_IDENT, C_BONES, C_PERM, C_MASK = 0, 128, 256, 384
C_BF_COLS = 384 + 5 * 512
C_INVF = C_BF_COLS
C_F32_COLS = 8
CST_COLS = C_BF_COLS + C_F32_COLS

DEBUG = False


def host_consts():
    c = np.zeros((128, CST_COLS), np.float32)
    c[:, C_IDENT:C_IDENT + 128] = np.eye(128, dtype=np.float32)
    bo = np.zeros((128, 128), np.float32)
    bo[0:64, 0:64] = 1.0
    bo[64:128, 64:128] = 1.0
    c[:, C_BONES:C_BONES + 128] = bo
    pm = np.zeros((128, 128), np.float32)
    for hb in (0, 64):
        for i in range(8):
            pm[hb + i + 8, hb + i] = -1.0
            pm[hb + i, hb + i + 8] = 1.0
    c[:, C_PERM:C_PERM + 128] = pm
    k = np.arange(128)[:, None]
    q = np.arange(128)[None, :]
    diag = (k <= q).astype(np.float32)
    prev_g = (k >= q).astype(np.float32)
    prev_b = (k > q).astype(np.float32)
    zero = np.zeros((128, 128), np.float32)
    masks = [
        np.concatenate([zero, diag, prev_g, diag], axis=1),
        np.concatenate([prev_g, diag, prev_g, diag], axis=1),
        np.concatenate([zero, diag, prev_b, diag], axis=1),
        np.concatenate([prev_b, diag, prev_b, diag], axis=1),
        np.concatenate([zero, diag, zero, diag], axis=1),
    ]
    for i, m in enumerate(masks):
        c[:, C_MASK + 512 * i:C_MASK + 512 * (i + 1)] = m
    inv_freq = (500000.0 ** (-np.arange(0, 16, 2, dtype=np.float32) / 16.0)).astype(np.float32)
    invf = np.zeros(128, np.float32)
    for p in range(128):
        if p % 64 < 16:
            invf[p] = inv_freq[(p % 64) % 8]
    c[:, C_INVF] = invf
    c[:, C_INVF + 1] = EPS
    return c


class Sched:
    ENGS = ("pe", "act", "dve", "pool", "sp")

    def __init__(self, nc, es):
        self.nc = nc
        self.es = es
        self.q = {e: [] for e in self.ENGS}
        self.res = {}
        self.sem = {e: es.enter_context(nc.semaphore("s_" + e)) for e in ("pe", "act", "dve", "pool")}
        self.dsem = {}
        self.dcnt = {}
        self.defer = None
        self.base_prio = 0.0

    def _dma_sem(self, name):
        if name not in self.dsem:
            self.dsem[name] = self.es.enter_context(self.nc.semaphore("d_" + name))
            self.dcnt[name] = 0
        return self.dsem[name]

    def _deps(self, reads, writes):
        deps = set()
        for r in reads:
            st = self.res.get(r)
            if st and st["w"] is not None:
                deps.add(st["w"])
        for w in writes:
            st = self.res.get(w)
            if st:
                if st["w"] is not None:
                    deps.add(st["w"])
                for d in st["r"]:
                    deps.add(d)
        return deps

    def _commit(self, me, reads, writes):
        for r in reads:
            st = self.res.setdefault(r, {"w": None, "r": []})
            st["r"] = [d for d in st["r"] if d[0] != me[0]] + [me]
        for w in writes:
            self.res[w] = {"w": me, "r": []}

    def begin_defer(self):
        self.defer = []

    def flush(self):
        lastw = {}
        expect = []
        for it in self.defer:
            expect.append({r: lastw.get(r) for r in it[6]})
            for w in it[7]:
                lastw[w] = it[1]
        items = sorted(self.defer, key=lambda x: (x[0], x[1]))
        wnow = {}
        for it in items:
            for r, v in expect[it[1]].items():
                if wnow.get(r) != v:
                    raise RuntimeError("priority order breaks producer of %r at prio %s (%s): expected op %s, saw %s"
                                       % (r, it[0], it[3], v, wnow.get(r)))
            for w in it[7]:
                wnow[w] = it[1]
        self.defer = None
        for (_, _, kind, eng, semname, fn, reads, writes) in items:
            if kind == "op":
                self.op(eng, fn, reads, writes)
            else:
                self.dma(eng, semname, fn, reads, writes)

    def op(self, eng, fn, reads=(), writes=(), prio=None):
        if getattr(self, "defer", None) is not None:
            self.defer.append((self.base_prio + (prio or 0.0), len(self.defer), "op", eng, None, fn, tuple(reads), tuple(writes)))
            return None
        deps = self._deps(reads, writes)
        idx = len(self.q[eng])
        self.q[eng].append({"fn": fn, "deps": deps, "kind": "op", "marked": False})
        self._commit((eng, idx), reads, writes)
        return (eng, idx)

    def dma(self, eng, semname, fn, reads=(), writes=(), prio=None):
        if getattr(self, "defer", None) is not None:
            self.defer.append((self.base_prio + (prio or 0.0), len(self.defer), "dma", eng, semname, fn, tuple(reads), tuple(writes)))
            return None
        self._dma_sem(semname)
        deps = self._deps(reads, writes)
        self.dcnt[semname] += 1
        me = ("dma:" + semname, self.dcnt[semname])
        self.q[eng].append({"fn": fn, "deps": deps, "kind": "dma", "sem": semname})
        self._commit(me, reads, writes)
        return me

    def barrier(self):
        deps = set()
        for e in ("pe", "act", "dve", "pool"):
            for i in range(len(self.q[e]) - 1, -1, -1):
                if self.q[e][i]["kind"] == "op":
                    deps.add((e, i))
                    break
        for name, cnt in self.dcnt.items():
            if cnt:
                deps.add(("dma:" + name, cnt))
        for e in self.ENGS:
            self.q[e].append({"fn": None, "deps": set(deps), "kind": "bar"})
        self.res = {}

    def final_wait(self, eng, semnames):
        deps = set(("dma:" + n, self.dcnt[n]) for n in semnames if self.dcnt.get(n))
        self.q[eng].append({"fn": None, "deps": deps, "kind": "bar"})

    def finalize(self):
        for e in self.ENGS:
            for ins in self.q[e]:
                for (dom, idx) in ins["deps"]:
                    if not dom.startswith("dma:"):
                        if dom == "pe" and e == "pe":
                            continue
                        self.q[dom][idx]["marked"] = True
        self.ordinal = {}
        for e in ("pe", "act", "dve", "pool"):
            n = 0
            for i, ins in enumerate(self.q[e]):
                if ins.get("marked"):
                    n += 1
                    self.ordinal[(e, i)] = n
        self.total_incs = n

    def replay(self, eng, eobj):
        seen = {}
        for ins in self.q[eng]:
            need = {}
            for (dom, idx) in ins["deps"]:
                if dom.startswith("dma:"):
                    val = 16 * idx
                else:
                    if dom == "pe" and eng == "pe":
                        continue
                    val = self.ordinal[(dom, idx)]
                if val > need.get(dom, 0):
                    need[dom] = val
            for dom, val in need.items():
                if seen.get(dom, 0) >= val:
                    continue
                seen[dom] = val
                sem = self.dsem[dom[4:]] if dom.startswith("dma:") else self.sem[dom]
                eobj.wait_ge(sem, val)
            if ins["fn"] is None:
                continue
            bi = ins["fn"](eobj)
            if ins["kind"] == "dma":
                bi.then_inc(self.dsem[ins["sem"]], 16)
            elif ins.get("marked"):
                bi.then_inc(self.sem[eng], 1)


class Mem:
    def __init__(self, arena):
        self.h = {BF16: arena, F32: arena.bitcast(F32), I32: arena.bitcast(I32)}
        self.pstep = {BF16: ARENA_ELEMS, F32: ARENA_ELEMS // 2, I32: ARENA_ELEMS // 2}

    def ap(self, dt, byte_off, shape, parts=128, p0=0):
        esz = 2 if dt == BF16 else 4
        assert byte_off % esz == 0
        dims = [[self.pstep[dt], parts]]
        stride = 1
        rev = []
        for n in reversed(shape):
            rev.append([stride, n])
            stride *= n
        dims += list(reversed(rev))
        assert byte_off + stride * esz <= ARENA_ELEMS * 2, (byte_off, stride, esz)
        return bass.AP(self.h[dt], p0 * self.pstep[dt] + byte_off // esz, dims)


KB = 1024


def build_program():
    nc = bass.Bass("TRN2", target_bir_lowering=False)
    dr = {}

    def din(name, shape, dt=F32):
        dr[name] = nc.dram_tensor(name, shape, dt, kind="ExternalInput")
        return dr[name].ap()

    x_d = din("x", [S, D])
    pos_d = din("pos", [1, S], I32)
    cst_d = din("cst", [128, CST_COLS])
    ln1_d = din("ln1_g", [1, D])
    ln2_d = din("ln2_g", [1, D])
    win_d = din("w_in", [D, 5120])
    qna_d = din("q_norm_a", [1, 64])
    kna_d = din("k_norm_a", [1, 64])
    qnb_d = din("q_norm_b", [1, 64])
    knb_d = din("k_norm_b", [1, 64])
    snk_d = din("sinks", [1, 8])
    wa_d = din("w_branch_a", [256, D])
    wb_d = din("w_branch_b", [512, D])
    wo_d = din("w_out", [D, D])
    wu_d = din("w_up", [D, DFF])
    wd_d = din("w_down", [DFF, D])
    out_h = nc.dram_tensor("out", [S, D], F32, kind="ExternalOutput")
    out_d = out_h.ap()
    x1_h = nc.dram_tensor("x1_scratch", [S, D], F32, kind="Internal")
    x1_d = x1_h.ap()
    dbg = {}
    if DEBUG:
        for name, shape, dt in (("dbg_hT", [128, 8 * S], BF16), ("dbg_oaT", [128, 2 * S], BF16),
                                ("dbg_obT", [128, 4 * S], BF16), ("dbg_tab", [128, 2 * S], BF16),
                                ("dbg_qk", [128, 2 * S], BF16)):
            dbg[name] = nc.dram_tensor(name, shape, dt, kind="ExternalOutput").ap()

    with ExitStack() as es:
        arena = es.enter_context(nc.sbuf_tensor("arena", [128, ARENA_ELEMS], BF16))
        mem = Mem(arena)
        banks = [es.enter_context(nc.psum_tensor("bank%d" % i, [128, 512], F32)) for i in range(8)]
        sch = Sched(nc, es)
        import os as _os
        _stop = _os.environ.get("KSTOP", "")

        class _Stop(Exception):
            pass

        def checkpoint(name):
            if _stop == name:
                raise _Stop()

        def bank_f32(i):
            return banks[i][:, :]

        def bank_bf16(i):
            return banks[i][:, :].bitcast(BF16)

        R_H_START = 7 * KB
        o = 0
        IDENT = mem.ap(BF16, o, [128]); o += 256
        BONES = mem.ap(BF16, o, [128]); o += 256
        PERM = mem.ap(BF16, o, [128]); o += 256
        MASKS = mem.ap(BF16, o, [5, 512]); o += 5120
        CF32 = mem.ap(F32, o, [C_F32_COLS]); o += 4 * C_F32_COLS
        GAINS = mem.ap(F32, o, [4]); o += 16
        ESINK = mem.ap(F32, o, [8]); o += 32
        o = (o + 63) // 64 * 64
        assert o <= 6 * KB + 1024
        assert o <= R_H_START
        R_H = 7 * KB
        R_O = 71 * KB
        R_T = 119 * KB
        R_W = 135 * KB
        R_END = ARENA_ELEMS * 2
        hT = mem.ap(BF16, R_H, [8, S])
        TC = mem.ap(BF16, R_T, [S])
        TS = mem.ap(BF16, R_T + 8 * KB, [S])
        oaT = mem.ap(BF16, R_O, [2, S])
        obT = mem.ap(BF16, R_O + 16 * KB, [4, S])
        INVF = CF32[:, 0:1]
        EPSC = CF32[:, 1:2]

        def emit_all():
            cbf = mem.ap(BF16, 0, [C_BF_COLS])
            sch.dma("pool", "cstb", lambda e: e.dma_start(out=cbf, in_=cst_d[:, 0:C_BF_COLS]), writes=["consts"])
            sch.dma("sp", "cst", lambda e: e.dma_start(out=CF32, in_=cst_d[:, C_BF_COLS:CST_COLS]), writes=["consts"])
            for gi, gd in enumerate((qna_d, kna_d, qnb_d, knb_d)):
                for hb in (0, 64):
                    src = bass.AP(gd.tensor, 0, [[1, 64], [1, 1]])
                    sch.dma("sp", "cst", lambda e, gi=gi, hb=hb, src=src: e.dma_start(out=GAINS[hb:hb + 64, gi:gi + 1], in_=src),
                            writes=["consts"])
            snk_b = bass.AP(snk_d.tensor, 0, [[0, 128], [1, 8]])
            sch.dma("sp", "cst", lambda e: e.dma_start(out=ESINK, in_=snk_b), writes=["consts"])
            sch.op("act", lambda e: e.activation(out=ESINK, in_=ESINK, func=AF.Exp), reads=["consts"], writes=["esink"])

            WP = [mem.ap(BF16, R_W + i * 6 * KB, [8, 384]) for i in range(2)]
            PASSES = []
            for sp in range(2):
                for (g, d, mode) in ((2, 16, "copy"), (1, 4, "add"), (0, 1, "final")):
                    PASSES.append(dict(name="g%d_%d" % (g, sp), d=d, qcol=OFF_QA + g * 256 + sp * 128, kcol=OFF_KA + g * 256 + sp * 128,
                                       vcol=OFF_VA + g * 256 + sp * 128, vdup=False, gq=0, gk=1, mb=0, mode=mode, sinks=None,
                                       out=oaT[:, sp, :], barrier_after=(sp == 1 and mode == "final")))
            for kv in range(2):
                for f in range(2):
                    heads = (4 * kv + 2 * f, 4 * kv + 2 * f + 1)
                    PASSES.append(dict(name="b%d_%d" % (kv, f), d=1, qcol=OFF_QB + heads[0] * 64,
                                       kcol=(OFF_KB + kv * 64) if f == 0 else None, vcol=(OFF_VB + kv * 64) if f == 0 else None,
                                       vdup=True, gq=2, gk=3, mb=2, mode="none", sinks=heads, out=obT[:, 2 * kv + f, :]))
            def emit_wload(pd, slot):
                W = WP[slot]
                wname = "wp%d" % slot

                def wload(dst_lo, src_lo, n):
                    src = win_d[:, src_lo:src_lo + n].rearrange("(k p) n -> p k n", p=128)
                    sch.dma("pool", wname, lambda e: e.dma_start(out=W[:, :, dst_lo:dst_lo + n], in_=src), writes=[wname], prio=-1.0)
                wload(0, pd["qcol"], 128)
                if pd["kcol"] is not None:
                    if pd["vdup"]:
                        wload(128, pd["kcol"], 64)
                        wload(192, pd["kcol"], 64)
                        wload(256, OFF_VB, 128)
                    else:
                        wload(128, pd["kcol"], 128)
                        wload(256, pd["vcol"], 128)

            tA = mem.ap(F32, R_O, [S])
            tAi = mem.ap(I32, R_O, [S])
            tB = mem.ap(F32, R_O + 16 * KB, [S])
            tBi = mem.ap(I32, R_O + 16 * KB, [S])
            tM = mem.ap(F32, R_O + 32 * KB, [S])
            sch.begin_defer()
            _tk = [0]

            def _tbump():
                sch.base_prio = 0.4 + 1.6 * _tk[0]
                _tk[0] += 1

            pos_b = bass.AP(pos_d.tensor, 0, [[0, 128], [1, S]])
            _tbump()
            sch.dma("sp", "pos", lambda e: e.dma_start(out=tAi, in_=pos_b), writes=["tA"])
            _tbump()
            sch.op("dve", lambda e: e.tensor_copy(out=tA, in_=tAi), reads=["tA"], writes=["tA"])
            _tbump()
            sch.op("dve", lambda e: e.tensor_scalar(out=tA, in0=tA, scalar1=INVF, scalar2=None, op0=ALU.mult),
                   reads=["tA", "consts"], writes=["tA"])
            _tbump()
            sch.op("dve", lambda e: e.tensor_scalar(out=tA, in0=tA, scalar1=float(1.0 / (2 * math.pi)), scalar2=None, op0=ALU.mult),
                   reads=["tA"], writes=["tA"])
            for which, tab in ((0, TS), (1, TC)):
                if which == 1:
                    _tbump()
                    sch.op("dve", lambda e: e.tensor_scalar(out=tA, in0=tA, scalar1=0.25, scalar2=None, op0=ALU.add),
                           reads=["tA"], writes=["tA"])
                _tbump()
                sch.op("dve", lambda e: e.tensor_copy(out=tBi, in_=tA), reads=["tA"], writes=["tB"])
                _tbump()
                sch.op("dve", lambda e: e.tensor_copy(out=tB, in_=tBi), reads=["tB"], writes=["tB"])
                _tbump()
                sch.op("dve", lambda e: e.tensor_tensor(out=tB, in0=tA, in1=tB, op=ALU.subtract), reads=["tA", "tB"], writes=["tB"])
                _tbump()
                sch.op("dve", lambda e: e.tensor_single_scalar(out=tM, in_=tB, scalar=0.5, op=ALU.is_gt), reads=["tB"], writes=["tM"])
                _tbump()
                sch.op("dve", lambda e: e.tensor_tensor(out=tB, in0=tB, in1=tM, op=ALU.subtract), reads=["tB", "tM"], writes=["tB"])
                _tbump()
                sch.op("dve", lambda e: e.tensor_single_scalar(out=tM, in_=tB, scalar=-0.5, op=ALU.is_lt), reads=["tB"], writes=["tM"])
                _tbump()
                sch.op("dve", lambda e: e.tensor_tensor(out=tB, in0=tB, in1=tM, op=ALU.add), reads=["tB", "tM"], writes=["tB"])
                _tbump()
                sch.op("act", lambda e, tab=tab: e.activation(out=tab, in_=tB, func=AF.Sin, scale=6.283185),
                       reads=["tB"], writes=["tab%d" % which])
            if DEBUG:
                _tbump()
                sch.dma("sp", "dbg", lambda e: e.dma_start(out=dbg["dbg_tab"][:, 0:S], in_=TC), reads=["tab1"])
                _tbump()
                sch.dma("sp", "dbg", lambda e: e.dma_start(out=dbg["dbg_tab"][:, S:2 * S], in_=TS), reads=["tab0"])

            sch.base_prio = 0.0
            checkpoint("T")
            emit_wload(PASSES[0], 0)
            def prenorm_tile(xsrc_ap, xs, xs_name, sem_name, g_b, hb, hb_name, junk, ss_col, ss_name, tp_bank, dst_ap, dst_names,
                             load_eng="sp", junk_name="junk", P=0.0, dma_off=-2.0, tr_off=0.5, cp_off=1.5):
                sch.dma(load_eng, sem_name, lambda e: e.dma_start(out=xs, in_=xsrc_ap), writes=[xs_name], prio=P + dma_off)
                sch.op("act", lambda e: e.activation(out=junk, in_=xs, func=AF.Square, accum_out=ss_col),
                       reads=[xs_name], writes=[junk_name, ss_name], prio=P)
                sch.op("act", lambda e: e.activation(out=ss_col, in_=ss_col, func=AF.Ln, scale=1.0 / D, bias=EPSC),
                       reads=[ss_name, "consts"], writes=[ss_name], prio=P + 0.02)
                sch.op("act", lambda e: e.activation(out=ss_col, in_=ss_col, func=AF.Exp, scale=-0.5),
                       reads=[ss_name], writes=[ss_name], prio=P + 0.04)
                sch.op("dve", lambda e: e.scalar_tensor_tensor(out=hb, in0=xs, scalar=ss_col, in1=g_b, op0=ALU.mult, op1=ALU.mult),
                       reads=[xs_name, ss_name, "gb"], writes=[hb_name], prio=P + 0.06)
                pT = bank_bf16(tp_bank)
                for k in range(8):
                    sch.op("pe", lambda e, k=k: e.transpose(out=pT[:, k * 128:(k + 1) * 128], in_=hb[:, k * 128:(k + 1) * 128], identity=IDENT),
                           reads=[hb_name, "consts"], writes=["bank%d" % tp_bank], prio=P + tr_off)
                sch.op("act", lambda e: e.activation(out=dst_ap, in_=pT.rearrange("p (k t) -> p k t", k=8), func=AF.Copy),
                       reads=["bank%d" % tp_bank], writes=dst_names, prio=P + cp_off)

            p0 = R_W + 48 * KB
            XS = [mem.ap(F32, p0 + i * 4 * KB, [D]) for i in range(3)]
            HB = [mem.ap(BF16, p0 + 12 * KB + i * 2 * KB, [D]) for i in range(2)]
            JUNK = mem.ap(BF16, p0 + 16 * KB, [D])
            GB1 = mem.ap(F32, p0 + 18 * KB, [D])
            SSC = mem.ap(F32, p0 + 22 * KB, [NT])
            assert p0 + 22 * KB + 4 * NT <= R_END
            g1_b = bass.AP(ln1_d.tensor, 0, [[0, 128], [1, D]])
            sch.dma("sp", "gb", lambda e: e.dma_start(out=GB1, in_=g1_b), writes=["gb"])
            for t in range(NT):
                prenorm_tile(x_d[t * 128:(t + 1) * 128, :], XS[t % 3], "xs%d" % (t % 3), "xs%d" % (t % 3), GB1,
                             HB[t % 2], "hb%d" % (t % 2), JUNK, SSC[:, t:t + 1], "ss%d" % t, t % 2,
                             hT[:, :, t * 128:(t + 1) * 128], [("hT", t)], P=float(t))
            sch.flush()
            if DEBUG:
                sch.dma("sp", "dbg", lambda e: e.dma_start(out=dbg["dbg_hT"], in_=hT.rearrange("p k t -> p (k t)")),
                        reads=[("hT", t) for t in range(NT)])
            sch.barrier()

            checkpoint("P0")
            w0 = R_W + 12 * KB
            QT = mem.ap(BF16, w0, [S]); w0 += 8 * KB
            KT = mem.ap(BF16, w0, [S]); w0 += 8 * KB
            VG = mem.ap(BF16, w0, [NT, 2, 128]); w0 += 16 * KB
            SQ = [mem.ap(BF16, w0 + i * KB, [512]) for i in range(2)]; w0 += 2 * KB
            RV = [mem.ap(F32, w0 + i * 2 * KB, [512]) for i in range(2)]; w0 += 4 * KB
            QN = [mem.ap(BF16, w0 + i * KB, [512]) for i in range(2)]; w0 += 2 * KB
            T1 = [mem.ap(F32, w0 + i * 2 * KB, [512]) for i in range(2)]; w0 += 4 * KB
            T2 = [mem.ap(F32, w0 + i * 2 * KB, [512]) for i in range(2)]; w0 += 4 * KB
            PT = [mem.ap(BF16, w0 + i * KB, [512]) for i in range(4)]; w0 += 4 * KB
            RD = [mem.ap(F32, w0 + i * 2 * KB, [512]) for i in range(2)]; w0 += 4 * KB
            PMB = [mem.ap(BF16, w0 + i * KB, [512]) for i in range(2)]; w0 += 2 * KB
            assert w0 <= R_END, w0
            ACC = mem.ap(F32, R_O + 16 * KB, [2, S])
            sch.op("pool", lambda e: e.memset(VG[:, :, 0, 64:128], 1.0), writes=["vg_ones"])
            sch.op("pool", lambda e: e.memset(VG[:, :, 1, 0:64], 1.0), writes=["vg_ones"])

            B_PJ = (0, 1)
            B_SS, B_PM, B_SA, B_SB, B_OT, B_VP = 2, 3, 4, 5, 6, 7
            cnt = {"pj": 0, "blk": 0, "pt": 0, "rd": 0, "w": 0}

            def gcol_ap(buf, d, c):
                L = S // d
                u = 512 // d
                return buf.rearrange("p (r l) -> p r l", r=d)[:, :, u * c:u * (c + 1)]

            def nat_ap(t, d):
                return t.rearrange("p (u r) -> p r u", r=d)

            def gblocks_of_chunk(d, c):
                L = S // d
                u = 512 // d
                blks = set()
                for r in range(d):
                    for col in range(r * L + u * c, r * L + u * (c + 1), min(u, 128)):
                        blks.add(col // 128)
                return sorted(blks)

            def attention_pass(pd, wslot, next_pd):
                name, d, kcol, vcol, vdup = pd["name"], pd["d"], pd["kcol"], pd["vcol"], pd["vdup"]
                gq_idx, gk_idx, mask_base, acc_mode = pd["gq"], pd["gk"], pd["mb"], pd["mode"]
                sink_heads, out_fchunk_ap = pd["sinks"], pd["out"]
                L = S // d
                bpr = L // 128
                W = WP[wslot]
                wname = "wp%d" % wslot
                nq = 2 if kcol is not None else 1
                sch.begin_defer()
                if next_pd is not None:
                    emit_wload(next_pd, 1 - wslot)

                def proj_qk(i, c, which):
                    P = float(i)
                    pj = B_PJ[cnt["pj"] % 2]
                    cnt["pj"] += 1
                    b = cnt["blk"] % 2
                    cnt["blk"] += 1
                    pjn = "bank%d" % pj
                    for k in range(8):
                        sch.op("pe", lambda e, k=k: e.matmul(bank_f32(pj), lhsT=W[:, k, which * 128:(which + 1) * 128],
                                                             rhs=hT[:, k, c * 512:(c + 1) * 512], start=(k == 0), stop=(k == 7)),
                               reads=[wname] + [("hT", t) for t in range(4 * c, 4 * c + 4)], writes=[pjn], prio=P)
                    sch.op("act", lambda e: e.activation(out=SQ[b], in_=bank_f32(pj), func=AF.Square), reads=[pjn], writes=["sq%d" % b],
                           prio=P + 0.02)
                    sch.op("pe", lambda e: e.matmul(bank_f32(B_SS), lhsT=BONES, rhs=SQ[b], start=True, stop=True),
                           reads=["sq%d" % b, "consts"], writes=["bank%d" % B_SS], prio=P + 1.04)
                    sch.op("act", lambda e: e.activation(out=RV[b], in_=bank_f32(B_SS), func=AF.Ln, scale=1.0 / 64, bias=EPSC),
                           reads=["bank%d" % B_SS, "consts"], writes=["rv%d" % b], prio=P + 1.06)
                    sch.op("act", lambda e: e.activation(out=RV[b], in_=RV[b], func=AF.Exp, scale=-0.5), reads=["rv%d" % b], writes=["rv%d" % b],
                           prio=P + 1.08)
                    gi = gq_idx if which == 0 else gk_idx
                    sch.op("dve", lambda e: e.scalar_tensor_tensor(out=QN[b], in0=bank_f32(pj), scalar=GAINS[:, gi:gi + 1], in1=RV[b],
                                                                   op0=ALU.mult, op1=ALU.mult),
                           reads=[pjn, "rv%d" % b, "consts"], writes=["qn%d" % b], prio=P + 1.10)
                    sch.op("pe", lambda e: e.matmul(bank_f32(B_PM), lhsT=PERM, rhs=QN[b], start=True, stop=True),
                           reads=["qn%d" % b, "consts"], writes=["bank%d" % B_PM], prio=P + 2.12)
                    sch.op("dve", lambda e: e.tensor_tensor(out=T1[b], in0=QN[b], in1=TC[:, c * 512:(c + 1) * 512], op=ALU.mult),
                           reads=["qn%d" % b, "tab1"], writes=["t1%d" % b], prio=P + 2.14)
                    sch.op("act", lambda e: e.activation(out=PMB[b], in_=bank_f32(B_PM), func=AF.Copy),
                           reads=["bank%d" % B_PM], writes=["pmb%d" % b], prio=P + 2.13)
                    sch.op("dve", lambda e: e.tensor_tensor(out=T2[b], in0=PMB[b], in1=TS[:, c * 512:(c + 1) * 512], op=ALU.mult),
                           reads=["pmb%d" % b, "tab0"], writes=["t2%d" % b], prio=P + 2.16)
                    dst = QT if which == 0 else KT
                    dname = "qt" if which == 0 else "kt"
                    sch.op("pool", lambda e: e.tensor_tensor(out=gcol_ap(dst, d, c), in0=nat_ap(T1[b], d), in1=nat_ap(T2[b], d), op=ALU.add),
                           reads=["t1%d" % b, "t2%d" % b], writes=[(dname, g) for g in gblocks_of_chunk(d, c)], prio=P + 2.18)

                def proj_v_batch(gbs, P):
                    vp = bank_f32(B_VP)
                    for si, gb in enumerate(gbs):
                        r, j = gb // bpr, gb % bpr
                        t0 = r + d * 128 * j
                        toks = sorted(set((t0 + d * i) // 128 for i in (0, 127)))
                        tiles = list(range(toks[0], toks[-1] + 1))
                        for k in range(8):
                            lhsT = hT[:, k, t0:t0 + d * 127 + 1:d]
                            sch.op("pe", lambda e, k=k, lhsT=lhsT, si=si: e.matmul(vp[:, si * 128:(si + 1) * 128], lhsT=lhsT, rhs=W[:, k, 256:384],
                                                                                 start=(k == 0), stop=(k == 7), skip_group_check=True),
                                   reads=[wname] + [("hT", t) for t in tiles], writes=["bank%d" % B_VP], prio=P)
                    bstride = (gbs[1] - gbs[0]) * 256
                    vg0 = VG[:, gbs[0], 0, 0:64]
                    dst = bass.AP(vg0.tensor, vg0.offset, [list(vg0.ap[0]), [bstride, 4], [192, 2], [1, 64]])
                    if vdup:
                        kvsel = (vcol - OFF_VB) // 64
                        v0 = vp[:, kvsel * 64:(kvsel + 1) * 64]
                        src = bass.AP(v0.tensor, v0.offset, [list(v0.ap[0]), [128, 4], [0, 2], [1, 64]])
                    else:
                        v0 = vp[:, 0:64]
                        src = bass.AP(v0.tensor, v0.offset, [list(v0.ap[0]), [128, 4], [64, 2], [1, 64]])
                    sch.op("act", lambda e: e.activation(out=dst, in_=src, func=AF.Copy), reads=["bank%d" % B_VP, "vg_ones"],
                           writes=[("vg", gb) for gb in gbs] + [("vgb", gb) for gb in gbs], prio=P + 0.02)

                def acc_tiles(tok0, d):
                    lo = tok0 // 512
                    hi = (tok0 + (255 if d == 1 else 1 + d * 127)) // 512
                    return list(range(lo, hi + 1))

                def round_blocks(n):
                    if d == 1:
                        return [(0, 2 * n), (0, 2 * n + 1)]
                    j, r0 = n // (d // 2), 2 * (n % (d // 2))
                    return [(r0, j), (r0 + 1, j)]

                def attn_round(n, P):
                    qblks = round_blocks(n)
                    if d == 1:
                        first = (qblks[0][1] == 0)
                        mask = MASKS[:, mask_base + (0 if first else 1), :]
                    else:
                        mask = MASKS[:, 4 if qblks[0][1] == 0 else 1, :]
                    sbanks = (B_SA, B_SB)
                    pts = []
                    for half in range(2):
                        sb = bank_f32(sbanks[half])
                        rows = slice(64 * half, 64 * half + 64)
                        for qi in range(2):
                            rq, jq = qblks[qi]
                            gq = rq * bpr + jq
                            for kb in range(2):
                                gk = gq - 1 + kb
                                if jq - 1 + kb < 0:
                                    gk = gq
                                sch.op("pe", lambda e, sb=sb, rows=rows, qi=qi, kb=kb, gk=gk, gq=gq: e.matmul(
                                    sb[:, (2 * qi + kb) * 128:(2 * qi + kb + 1) * 128], lhsT=KT[rows, gk * 128:(gk + 1) * 128],
                                    rhs=QT[rows, gq * 128:(gq + 1) * 128], start=True, stop=True),
                                    reads=[("kt", gk), ("qt", gq)], writes=["bank%d" % sbanks[half]], prio=P + 0.001 * half)
                        p = cnt["pt"] % 4
                        cnt["pt"] += 1
                        pts.append(p)
                        sch.op("act", lambda e, sb=sb, p=p: e.activation(out=PT[p], in_=sb, func=AF.Exp, scale=0.125),
                               reads=["bank%d" % sbanks[half]], writes=["pt%d" % p], prio=P + 0.03 + 0.001 * half)
                        sch.op("dve" if half == 0 else "pool", lambda e, p=p: e.tensor_tensor(out=PT[p], in0=PT[p], in1=mask, op=ALU.mult),
                               reads=["pt%d" % p, "consts"], writes=["pt%d" % p], prio=P + 0.05 + 0.001 * half)
                    ot = bank_f32(B_OT)
                    nmm = 0
                    for half in range(2):
                        for qi in range(2):
                            rq, jq = qblks[qi]
                            gq = rq * bpr + jq
                            for kb in range(2):
                                gk = gq - 1 + kb
                                if jq - 1 + kb < 0:
                                    gk = gq
                                item = 2 * half + qi
                                sch.op("pe", lambda e, half=half, qi=qi, kb=kb, gk=gk, item=item, nmm=nmm: e.matmul(
                                    ot[:, item * 128:(item + 1) * 128], lhsT=VG[:, gk, half, :],
                                    rhs=PT[pts[half]][:, (2 * qi + kb) * 128:(2 * qi + kb + 1) * 128],
                                    start=(nmm == 0), stop=(kb == 1), skip_group_check=True),
                                    reads=[("vg", gk), ("vgb", gk), "pt%d" % pts[half]], writes=["bank%d" % B_OT], prio=P + 1.01)
                                nmm += 1
                    tok0 = qblks[0][0] + d * 128 * qblks[0][1]
                    qstride = 128 if d == 1 else 1
                    otv = ot.rearrange("p (h q i) -> p h q i", h=2, q=2)
                    if sink_heads is None:
                        accv = bass.AP(ACC.tensor, ACC.offset + tok0, [list(ACC.ap[0]), [S, 2], [qstride, 2], [d, 128]])
                        if acc_mode == "copy":
                            sch.op("act", lambda e: e.activation(out=accv, in_=otv, func=AF.Copy), reads=["bank%d" % B_OT],
                                   writes=[("acc", n2) for n2 in acc_tiles(tok0, d)], prio=P + 1.03)
                        else:
                            sch.op("dve", lambda e: e.tensor_tensor(out=accv, in0=otv, in1=accv, op=ALU.add), reads=["bank%d" % B_OT],
                                   writes=[("acc", n2) for n2 in acc_tiles(tok0, d)], prio=P + 1.03)
                    else:
                        geo = []
                        for half in range(2):
                            num = slice(0, 64) if half == 0 else slice(64, 128)
                            den = slice(64, 128) if half == 0 else slice(0, 64)
                            geo.append((half, num, den, slice(256 * half, 256 * half + 256), sink_heads[half]))
                        for (half, num, den, cols, hsink) in geo:
                            sch.op("act", lambda e, half=half, num=num, den=den, cols=cols, hsink=hsink: e.activation(
                                out=RD[half][num, 0:256], in_=ot[den, cols], func=AF.Ln, bias=ESINK[num, hsink:hsink + 1]),
                                reads=["bank%d" % B_OT, "esink"], writes=["rd%d" % half, "ot_act_done"], prio=P + 1.03)
                        for (half, num, den, cols, hsink) in geo:
                            sch.op("act", lambda e, half=half, num=num: e.activation(out=RD[half][num, 0:256], in_=RD[half][num, 0:256],
                                                                                     func=AF.Exp, scale=-1.0),
                                   reads=["rd%d" % half], writes=["rd%d" % half], prio=P + 1.05)
                        for (half, num, den, cols, hsink) in geo:
                            sch.op("dve", lambda e, half=half, num=num, cols=cols: e.tensor_tensor(
                                out=out_fchunk_ap[num, tok0:tok0 + 256], in0=ot[num, cols], in1=RD[half][num, 0:256], op=ALU.mult),
                                reads=["bank%d" % B_OT, "rd%d" % half, "ot_act_done"], writes=[("ob", name, tok0)], prio=P + 1.07)

                round_prio = {}
                cluster = {}
                for n in range(16):
                    cready = max((r + d * (128 * j + 127)) // 512 for (r, j) in round_blocks(n))
                    cluster.setdefault(cready, []).append(n)
                cl = sorted(cluster)
                for ci, c in enumerate(cl):
                    span = ((cl[ci + 1] - c) if ci + 1 < len(cl) else 1) * nq
                    for k, n in enumerate(cluster[c]):
                        round_prio[n] = (c * nq + nq - 1) + 3.3 + k * max(0.5, min(1.0, float(span) / len(cluster[c])))
                assert len(round_prio) == 16
                first_round_of_block = {}
                for n in sorted(range(16), key=lambda n: (round_prio[n], n)):
                    for (r, j) in round_blocks(n):
                        for jj in (j - 1, j):
                            if jj >= 0:
                                first_round_of_block.setdefault(r * bpr + jj, n)
                for c in range(NCH):
                    proj_qk(c * nq, c, 0)
                    if kcol is not None:
                        proj_qk(c * nq + 1, c, 1)
                if kcol is not None:
                    if d == 1:
                        batches = [[4 * m + s_ for s_ in range(4)] for m in range(8)]
                    else:
                        batches = [[(4 * m + s_) * bpr + j for s_ in range(4)] for j in range(bpr) for m in range(d // 4)]
                    bprio = sorted((min(round_prio[first_round_of_block[gb]] for gb in gbs) - 0.9, gbs) for gbs in batches)
                    lastp = None
                    for (pb, gbs) in bprio:
                        if lastp is not None and pb < lastp + 0.1:
                            pb = lastp + 0.1
                        lastp = pb
                        proj_v_batch(gbs, pb)
                for n in sorted(range(16), key=lambda n: (round_prio[n], n)):
                    attn_round(n, round_prio[n])

                if acc_mode == "final":
                    for c in range(NCH):
                        cs = slice(c * 512, (c + 1) * 512)
                        for half in range(2):
                            rb = cnt["rd"] % 2
                            cnt["rd"] += 1
                            num = slice(0, 64) if half == 0 else slice(64, 128)
                            den = slice(64, 128) if half == 0 else slice(0, 64)
                            P = 1000.0 + 2 * c + half
                            sch.op("act", lambda e, rb=rb, den=den, num=num, cs=cs, half=half: e.activation(
                                out=RD[rb][num, :], in_=ACC[den, half, cs], func=AF.Ln), reads=[("acc", c)], writes=["rd%d" % rb], prio=P)
                            sch.op("act", lambda e, rb=rb, num=num: e.activation(out=RD[rb][num, :], in_=RD[rb][num, :], func=AF.Exp, scale=-1.0),
                                   reads=["rd%d" % rb], writes=["rd%d" % rb], prio=P + 0.1)
                            sch.op("pool", lambda e, rb=rb, num=num, cs=cs, half=half: e.tensor_tensor(
                                out=out_fchunk_ap[num, cs], in0=ACC[num, half, cs], in1=RD[rb][num, :], op=ALU.mult),
                                reads=[("acc", c), "rd%d" % rb], writes=[("oa", name, c)], prio=P + 0.2)
                sch.flush()

            for pi, pd in enumerate(PASSES):
                attention_pass(pd, pi % 2, PASSES[pi + 1] if pi + 1 < len(PASSES) else None)
                if pd.get("barrier_after"):
                    sch.barrier()
                    checkpoint("A1")
            if DEBUG:
                sch.dma("sp", "dbg", lambda e: e.dma_start(out=dbg["dbg_qk"][:, 0:S], in_=QT), reads=[("qt", g) for g in range(NT)])
                sch.dma("sp", "dbg", lambda e: e.dma_start(out=dbg["dbg_qk"][:, S:2 * S], in_=KT), reads=[("kt", g) for g in range(NT)])
            sch.barrier()
            if DEBUG:
                sch.dma("sp", "dbg", lambda e: e.dma_start(out=dbg["dbg_oaT"], in_=oaT.rearrange("p k t -> p (k t)")))
                sch.dma("sp", "dbg", lambda e: e.dma_start(out=dbg["dbg_obT"], in_=obT.rearrange("p k t -> p (k t)")))

            checkpoint("A")
            w0 = R_T
            WG = mem.ap(BF16, w0, [8, 2048]); w0 += 32 * KB
            WA = mem.ap(BF16, w0, [2, D]); w0 += 4 * KB
            WB = mem.ap(BF16, w0, [4, D]); w0 += 8 * KB
            WO = mem.ap(BF16, w0, [8, D]); w0 += 16 * KB
            TA = [mem.ap(BF16, w0 + i * KB, [512]) for i in range(2)]; w0 += 2 * KB
            TB = [mem.ap(BF16, w0 + i * KB, [512]) for i in range(2)]; w0 += 2 * KB
            UU = [mem.ap(F32, w0 + i * 2 * KB, [512]) for i in range(2)]; w0 += 4 * KB
            VV = [mem.ap(F32, w0 + i * 2 * KB, [512]) for i in range(2)]; w0 += 4 * KB
            MIX = [mem.ap(BF16, w0, [8, 512]) for i in range(2)]; w0 += 8 * KB
            X5 = [mem.ap(F32, w0 + i * 4 * KB, [D]) for i in range(2)]; w0 += 8 * KB
            assert w0 <= R_END
            for piece in range(4):
                src = win_d[:, OFF_GA + piece * 512:OFF_GA + (piece + 1) * 512].rearrange("(k p) n -> p k n", p=128)
                sch.dma("pool", "wg", lambda e, piece=piece, src=src: e.dma_start(out=WG[:, :, piece * 512:(piece + 1) * 512], in_=src), writes=["wg"])
            sch.dma("pool", "wab", lambda e: e.dma_start(out=WA, in_=wa_d.rearrange("(k p) n -> p k n", p=128)), writes=["wab"])
            sch.dma("pool", "wab", lambda e: e.dma_start(out=WB, in_=wb_d.rearrange("(k p) n -> p k n", p=128)), writes=["wab"])
            sch.dma("pool", "wo", lambda e: e.dma_start(out=WO, in_=wo_d.rearrange("(k p) n -> p k n", p=128)), writes=["wo"])
            B_GA, B_GB, B_YA, B_YB, B_O = (0, 1), (2, 3), 4, 5, (6, 7)
            WU_early = mem.ap(BF16, R_H, [8, DFF])
            n5 = {"g": 0, "o": 0, "x": 0}
            mix = MIX[0]
            mixn = "mix0"

            def p5_merge(c, m):
                cs = slice(c * 512, (c + 1) * 512)
                gi = n5["g"] % 2
                n5["g"] += 1
                bga, bgb = B_GA[gi], B_GB[gi]

                def gate_mm(bk, coff):
                    for k in range(8):
                        sch.op("pe", lambda e, k=k: e.matmul(bank_f32(bk), lhsT=WG[:, k, coff:coff + 128], rhs=hT[:, k, cs],
                                                             start=(k == 0), stop=(k == 7)),
                               reads=["wg", ("hT5", c)], writes=["bank%d" % bk])
                gate_mm(bga, m * 128)
                gate_mm(bgb, 1024 + m * 128)
                for k in range(2):
                    sch.op("pe", lambda e, k=k: e.matmul(bank_f32(B_YA), lhsT=WA[:, k, m * 128:(m + 1) * 128], rhs=oaT[:, k, cs],
                                                         start=(k == 0), stop=(k == 1)), reads=["wab"], writes=["bank%d" % B_YA])
                for k in range(4):
                    sch.op("pe", lambda e, k=k: e.matmul(bank_f32(B_YB), lhsT=WB[:, k, m * 128:(m + 1) * 128], rhs=obT[:, k, cs],
                                                         start=(k == 0), stop=(k == 3)), reads=["wab"], writes=["bank%d" % B_YB])
                sch.op("act", lambda e: e.activation(out=TA[gi], in_=bank_f32(bga), func=AF.Tanh, scale=0.5),
                       reads=["bank%d" % bga], writes=["ta%d" % gi])
                sch.op("act", lambda e: e.activation(out=TB[gi], in_=bank_f32(bgb), func=AF.Tanh, scale=0.5),
                       reads=["bank%d" % bgb], writes=["tb%d" % gi])
                sch.op("dve", lambda e: e.scalar_tensor_tensor(out=UU[gi], in0=TA[gi], scalar=1.0, in1=bank_f32(B_YA), op0=ALU.add, op1=ALU.mult),
                       reads=["ta%d" % gi, "bank%d" % B_YA], writes=["uu%d" % gi])
                sch.op("dve", lambda e: e.scalar_tensor_tensor(out=VV[gi], in0=TB[gi], scalar=1.0, in1=bank_f32(B_YB), op0=ALU.add, op1=ALU.mult),
                       reads=["tb%d" % gi, "bank%d" % B_YB], writes=["vv%d" % gi])
                sch.op("pool", lambda e: e.tensor_tensor(out=mix[:, m, :], in0=UU[gi], in1=VV[gi], op=ALU.add),
                       reads=["uu%d" % gi, "vv%d" % gi], writes=[(mixn, m)])

            def p5_out(c, tt):
                t = 4 * c + tt
                xi = n5["x"] % 2
                n5["x"] += 1
                xt = X5[xi]
                sch.dma("sp", "x5l%d" % xi, lambda e: e.dma_start(out=xt, in_=x_d[t * 128:(t + 1) * 128, :]), writes=["x5_%d" % xi])

                def half(hf):
                    bo = B_O[n5["o"] % 2]
                    n5["o"] += 1
                    for k in range(8):
                        sch.op("pe", lambda e, k=k: e.matmul(bank_f32(bo), lhsT=mix[:, k, tt * 128:(tt + 1) * 128],
                                                             rhs=WO[:, k, hf * 512:(hf + 1) * 512], start=(k == 0), stop=(k == 7)),
                               reads=["wo"] + [(mixn, mm) for mm in range(8)], writes=["bank%d" % bo])
                    sch.op("dve", lambda e: e.scalar_tensor_tensor(
                        out=xt[:, hf * 512:(hf + 1) * 512], in0=bank_f32(bo), scalar=0.5, in1=xt[:, hf * 512:(hf + 1) * 512],
                        op0=ALU.mult, op1=ALU.add), reads=["bank%d" % bo, "x5_%d" % xi], writes=["x5_%d" % xi])
                half(0)
                half(1)
                sch.dma("sp", "x5s%d" % xi, lambda e: e.dma_start(out=x1_d[t * 128:(t + 1) * 128, :], in_=xt),
                        reads=["x5_%d" % xi], writes=[("x1", t)])

            for c in range(NCH):
                for m in range(8):
                    p5_merge(c, m)
                wsrc = wu_d[:, c * 512:(c + 1) * 512].rearrange("(k p) n -> p k n", p=128)
                sch.dma("pool", "wu%d" % c, lambda e, c=c, wsrc=wsrc: e.dma_start(out=WU_early[:, :, c * 512:(c + 1) * 512], in_=wsrc),
                        writes=[("wu", c), ("hT5", c)])
                for tt in range(4):
                    p5_out(c, tt)
            sch.barrier()

            checkpoint("P5")
            w0 = 7 * KB
            WU = mem.ap(BF16, w0, [8, DFF]); w0 += 64 * KB
            WD = mem.ap(BF16, w0, [32, D]); w0 += 64 * KB
            AT = mem.ap(BF16, w0, [32, 512]); w0 += 32 * KB
            H2T = mem.ap(BF16, w0, [8, 512]); w0 += 8 * KB
            X6 = [mem.ap(F32, w0 + i * 4 * KB, [D]) for i in range(5)]; w0 += 20 * KB
            GB2 = mem.ap(F32, w0, [D]); w0 += 4 * KB
            HB6 = [mem.ap(BF16, w0 + i * 2 * KB, [D]) for i in range(2)]; w0 += 4 * KB
            RR = [mem.ap(F32, w0 + i * 2 * KB, [512]) for i in range(2)]; w0 += 4 * KB
            SS6 = mem.ap(F32, 6 * KB + 256, [NT])
            assert w0 <= R_END, w0
            g2_b = bass.AP(ln2_d.tensor, 0, [[0, 128], [1, D]])
            sch.dma("sp", "gb", lambda e: e.dma_start(out=GB2, in_=g2_b), writes=["gb"])
            for piece in range(8):
                src = wd_d[piece * 512:(piece + 1) * 512, :].rearrange("(k p) n -> p k n", p=128)
                sch.dma("pool", "wd%d" % piece, lambda e, piece=piece, src=src: e.dma_start(out=WD[:, piece * 4:(piece + 1) * 4, :], in_=src),
                        writes=[("wd", piece)])
            B_TP, B_U, B_D = (0, 1), (2, 3, 4), (5, 6, 7)
            n6 = {"x": 0, "u": 0, "d": 0, "r": 0}
            NSL, FSL = X6[0:3], X6[3:5]
            sch.begin_defer()

            def p6_prenorm(tb, P0):
                for tt in range(4):
                    t = 4 * tb + tt
                    xi = n6["x"] % 3
                    n6["x"] += 1
                    hbi = t % 2
                    prenorm_tile(x1_d[t * 128:(t + 1) * 128, :], NSL[xi], "x6n_%d" % xi, "x6nl%d" % xi, GB2, HB6[hbi], "hb6_%d" % hbi, HB6[hbi],
                                 SS6[:, t:t + 1], "ss6_%d" % t, B_TP[t % 2], H2T[:, :, tt * 128:(tt + 1) * 128], [("h2t", tt)],
                                 junk_name="hb6_%d" % hbi, P=P0 + 10.0 * tt, dma_off=-6.0, tr_off=1.5, cp_off=3.5)

            def p6_up(f, P):
                bu = B_U[n6["u"] % 3]
                n6["u"] += 1
                ri = n6["r"] % 2
                n6["r"] += 1
                for k in range(8):
                    sch.op("pe", lambda e, k=k: e.matmul(bank_f32(bu), lhsT=WU[:, k, f * 128:(f + 1) * 128], rhs=H2T[:, k, :],
                                                         start=(k == 0), stop=(k == 7)),
                           reads=[("wu", f // 4)] + [("h2t", tt) for tt in range(4)], writes=["bank%d" % bu], prio=P)
                sch.op("act", lambda e: e.activation(out=RR[ri], in_=bank_f32(bu), func=AF.Relu), reads=["bank%d" % bu], writes=["rr%d" % ri],
                       prio=P + 0.3)
                sch.op("dve", lambda e: e.tensor_tensor(out=AT[:, f, :], in0=RR[ri], in1=RR[ri], op=ALU.mult),
                       reads=["rr%d" % ri], writes=[("at", f)], prio=P + 0.6)

            def p6_down(t, tt, B):
                fi = t % 2
                xt = FSL[fi]
                sch.dma("sp", "x6fl%d" % fi, lambda e: e.dma_start(out=xt, in_=x1_d[t * 128:(t + 1) * 128, :]), writes=["x6f_%d" % fi],
                        prio=B + 40 + 10 * tt - 9)

                def half(hf):
                    g = 2 * tt + hf
                    bd = B_D[n6["d"] % 3]
                    n6["d"] += 1
                    for f in range(32):
                        sch.op("pe", lambda e, f=f: e.matmul(bank_f32(bd), lhsT=AT[:, f, tt * 128:(tt + 1) * 128],
                                                             rhs=WD[:, f, hf * 512:(hf + 1) * 512], start=(f == 0), stop=(f == 31)),
                               reads=[("wd", f // 4), ("at", f)], writes=["bank%d" % bd], prio=B + 40 + 5 * g)
                    sch.op("dve", lambda e: e.tensor_tensor(out=xt[:, hf * 512:(hf + 1) * 512], in0=bank_f32(bd),
                                                            in1=xt[:, hf * 512:(hf + 1) * 512], op=ALU.add),
                           reads=["bank%d" % bd, "x6f_%d" % fi], writes=["x6f_%d" % fi], prio=B + 40 + 5 * g + 4.5)
                half(0)
                half(1)
                sch.dma("sp", "x6fs%d" % fi, lambda e: e.dma_start(out=out_d[t * 128:(t + 1) * 128, :], in_=xt),
                        reads=["x6f_%d" % fi], writes=[("out", t)], prio=B + 40 + 5 * (2 * tt + 1) + 4.6)

            p6_prenorm(0, -50.0)
            for tb in range(NCH):
                B = 100.0 * tb
                for f in range(32):
                    p6_up(f, B + f)
                if tb + 1 < NCH:
                    p6_prenorm(tb + 1, B + 41.0)
                for tt in range(4):
                    p6_down(4 * tb + tt, tt, B)
            sch.flush()

        try:
            emit_all()
        except _Stop:
            sch.barrier()
        sch.final_wait("sp", ["x6fs%d" % i for i in range(2)] + (["dbg"] if DEBUG else []))

        sch.finalize()
        block = es.enter_context(nc.Block())

        @block.sync
        def _(e):
            sch.replay("sp", e)

        @block.gpsimd
        def _(e):
            sch.replay("pool", e)

        @block.scalar
        def _(e):
            sch.replay("act", e)

        @block.vector
        def _(e):
            sch.replay("dve", e)

        @block.tensor
        def _(e):
            sch.replay("pe", e)
    return nc


_CACHE = {}


def kernel(x, positions, ln1_g, w_in, q_norm_a, k_norm_a, q_norm_b, k_norm_b, sinks,
           w_branch_a, w_branch_b, w_out, ln2_g, w_up, w_down):
    if "nc" not in _CACHE:
        _CACHE["nc"] = build_program()
    nc = _CACHE["nc"]
    cst = host_consts()
    f32 = lambda a: np.ascontiguousarray(np.asarray(a), dtype=np.float32)
    shared = {
        "cst": cst,
        "ln1_g": f32(ln1_g), "ln2_g": f32(ln2_g), "w_in": f32(w_in)[0],
        "q_norm_a": f32(q_norm_a), "k_norm_a": f32(k_norm_a), "q_norm_b": f32(q_norm_b), "k_norm_b": f32(k_norm_b),
        "sinks": f32(sinks), "w_branch_a": f32(w_branch_a)[0], "w_branch_b": f32(w_branch_b)[0],
        "w_out": f32(w_out)[0], "w_up": f32(w_up)[0], "w_down": f32(w_down)[0],
    }
    xs = f32(x)
    ps = np.ascontiguousarray(np.asarray(positions), dtype=np.int32)
    in_maps = []
    for b in range(8):
        m = dict(shared)
        m["x"] = xs[b]
        m["pos"] = ps[b:b + 1]
        in_maps.append(m)
    res = run_bass_kernel_spmd(nc, in_maps, core_ids=list(range(8)))
    _CACHE["last"] = res
    out = np.stack([np.asarray(r["out"], dtype=np.float32) for r in res.results], axis=0)
    return out
```

```python
import math
from contextlib import ExitStack

import numpy as np
import concourse.bass as bass
import concourse.mybir as mybir
from concourse.bass_utils import run_bass_kernel_spmd

F32 = mybir.dt.float32
BF16 = mybir.dt.bfloat16
I32 = mybir.dt.int32
AF = mybir.ActivationFunctionType
ALU = mybir.AluOpType

S = 4096
D = 1024
DFF = 4096
NCH = 8
NT = 32
EPS = 1e-6
ARENA_ELEMS = 105984

OFF_QA, OFF_KA, OFF_VA, OFF_QB, OFF_KB, OFF_VB, OFF_GA, OFF_GB = 0, 768, 1536, 2304, 2816, 2944, 3072, 4096

C_IDENT, C_BONES, C_PERM, C_MASK = 0, 128, 256, 384
C_BF_COLS = 384 + 5 * 512
C_INVF = C_BF_COLS
C_F32_COLS = 8
CST_COLS = C_BF_COLS + C_F32_COLS

DEBUG = False


def host_consts():
    c = np.zeros((128, CST_COLS), np.float32)
    c[:, C_IDENT:C_IDENT + 128] = np.eye(128, dtype=np.float32)
    bo = np.zeros((128, 128), np.float32)
    bo[0:64, 0:64] = 1.0
    bo[64:128, 64:128] = 1.0
    c[:, C_BONES:C_BONES + 128] = bo
    pm = np.zeros((128, 128), np.float32)
    for hb in (0, 64):
        for i in range(8):
            pm[hb + i + 8, hb + i] = -1.0
            pm[hb + i, hb + i + 8] = 1.0
    c[:, C_PERM:C_PERM + 128] = pm
    k = np.arange(128)[:, None]
    q = np.arange(128)[None, :]
    diag = (k <= q).astype(np.float32)
    prev_g = (k >= q).astype(np.float32)
    prev_b = (k > q).astype(np.float32)
    zero = np.zeros((128, 128), np.float32)
    masks = [
        np.concatenate([zero, diag, prev_g, diag], axis=1),
        np.concatenate([prev_g, diag, prev_g, diag], axis=1),
        np.concatenate([zero, diag, prev_b, diag], axis=1),
        np.concatenate([prev_b, diag, prev_b, diag], axis=1),
        np.concatenate([zero, diag, zero, diag], axis=1),
    ]
    for i, m in enumerate(masks):
        c[:, C_MASK + 512 * i:C_MASK + 512 * (i + 1)] = m
    inv_freq = (500000.0 ** (-np.arange(0, 16, 2, dtype=np.float32) / 16.0)).astype(np.float32)
    invf = np.zeros(128, np.float32)
    for p in range(128):
        if p % 64 < 16:
            invf[p] = inv_freq[(p % 64) % 8]
    c[:, C_INVF] = invf
    c[:, C_INVF + 1] = EPS
    return c


class Sched:
    ENGS = ("pe", "act", "dve", "pool", "sp")

    def __init__(self, nc, es):
        self.nc = nc
        self.es = es
        self.q = {e: [] for e in self.ENGS}
        self.res = {}
        self.sem = {e: es.enter_context(nc.semaphore("s_" + e)) for e in ("pe", "act", "dve", "pool")}
        self.dsem = {}
        self.dcnt = {}
        self.defer = None
        self.base_prio = 0.0

    def _dma_sem(self, name):
        if name not in self.dsem:
            self.dsem[name] = self.es.enter_context(self.nc.semaphore("d_" + name))
            self.dcnt[name] = 0
        return self.dsem[name]

    def _deps(self, reads, writes):
        deps = set()
        for r in reads:
            st = self.res.get(r)
            if st and st["w"] is not None:
                deps.add(st["w"])
        for w in writes:
            st = self.res.get(w)
            if st:
                if st["w"] is not None:
                    deps.add(st["w"])
                for d in st["r"]:
                    deps.add(d)
        return deps

    def _commit(self, me, reads, writes):
        for r in reads:
            st = self.res.setdefault(r, {"w": None, "r": []})
            st["r"] = [d for d in st["r"] if d[0] != me[0]] + [me]
        for w in writes:
            self.res[w] = {"w": me, "r": []}

    def begin_defer(self):
        self.defer = []

    def flush(self):
        lastw = {}
        expect = []
        for it in self.defer:
            expect.append({r: lastw.get(r) for r in it[6]})
            for w in it[7]:
                lastw[w] = it[1]
        items = sorted(self.defer, key=lambda x: (x[0], x[1]))
        wnow = {}
        for it in items:
            for r, v in expect[it[1]].items():
                if wnow.get(r) != v:
                    raise RuntimeError("priority order breaks producer of %r at prio %s (%s): expected op %s, saw %s"
                                       % (r, it[0], it[3], v, wnow.get(r)))
            for w in it[7]:
                wnow[w] = it[1]
        self.defer = None
        for (_, _, kind, eng, semname, fn, reads, writes) in items:
            if kind == "op":
                self.op(eng, fn, reads, writes)
            else:
                self.dma(eng, semname, fn, reads, writes)

    def op(self, eng, fn, reads=(), writes=(), prio=None):
        if getattr(self, "defer", None) is not None:
            self.defer.append((self.base_prio + (prio or 0.0), len(self.defer), "op", eng, None, fn, tuple(reads), tuple(writes)))
            return None
        deps = self._deps(reads, writes)
        idx = len(self.q[eng])
        self.q[eng].append({"fn": fn, "deps": deps, "kind": "op", "marked": False})
        self._commit((eng, idx), reads, writes)
        return (eng, idx)

    def dma(self, eng, semname, fn, reads=(), writes=(), prio=None):
        if getattr(self, "defer", None) is not None:
            self.defer.append((self.base_prio + (prio or 0.0), len(self.defer), "dma", eng, semname, fn, tuple(reads), tuple(writes)))
            return None
        self._dma_sem(semname)
        deps = self._deps(reads, writes)
        self.dcnt[semname] += 1
        me = ("dma:" + semname, self.dcnt[semname])
        self.q[eng].append({"fn": fn, "deps": deps, "kind": "dma", "sem": semname})
        self._commit(me, reads, writes)
        return me

    def barrier(self):
        deps = set()
        for e in ("pe", "act", "dve", "pool"):
            for i in range(len(self.q[e]) - 1, -1, -1):
                if self.q[e][i]["kind"] == "op":
                    deps.add((e, i))
                    break
        for name, cnt in self.dcnt.items():
            if cnt:
                deps.add(("dma:" + name, cnt))
        for e in self.ENGS:
            self.q[e].append({"fn": None, "deps": set(deps), "kind": "bar"})
        self.res = {}

    def final_wait(self, eng, semnames):
        deps = set(("dma:" + n, self.dcnt[n]) for n in semnames if self.dcnt.get(n))
        self.q[eng].append({"fn": None, "deps": deps, "kind": "bar"})

    def finalize(self):
        for e in self.ENGS:
            for ins in self.q[e]:
                for (dom, idx) in ins["deps"]:
                    if not dom.startswith("dma:"):
                        if dom == "pe" and e == "pe":
                            continue
                        self.q[dom][idx]["marked"] = True
        self.ordinal = {}
        for e in ("pe", "act", "dve", "pool"):
            n = 0
            for i, ins in enumerate(self.q[e]):
                if ins.get("marked"):
                    n += 1
                    self.ordinal[(e, i)] = n
        self.total_incs = n

    def replay(self, eng, eobj):
        seen = {}
        for ins in self.q[eng]:
            need = {}
            for (dom, idx) in ins["deps"]:
                if dom.startswith("dma:"):
                    val = 16 * idx
                else:
                    if dom == "pe" and eng == "pe":
                        continue
                    val = self.ordinal[(dom, idx)]
                if val > need.get(dom, 0):
                    need[dom] = val
            for dom, val in need.items():
                if seen.get(dom, 0) >= val:
                    continue
                seen[dom] = val
                sem = self.dsem[dom[4:]] if dom.startswith("dma:") else self.sem[dom]
                eobj.wait_ge(sem, val)
            if ins["fn"] is None:
                continue
            bi = ins["fn"](eobj)
            if ins["kind"] == "dma":
                bi.then_inc(self.dsem[ins["sem"]], 16)
            elif ins.get("marked"):
                bi.then_inc(self.sem[eng], 1)


class Mem:
    def __init__(self, arena):
        self.h = {BF16: arena, F32: arena.bitcast(F32), I32: arena.bitcast(I32)}
        self.pstep = {BF16: ARENA_ELEMS, F32: ARENA_ELEMS // 2, I32: ARENA_ELEMS // 2}

    def ap(self, dt, byte_off, shape, parts=128, p0=0):
        esz = 2 if dt == BF16 else 4
        assert byte_off % esz == 0
        dims = [[self.pstep[dt], parts]]
        stride = 1
        rev = []
        for n in reversed(shape):
            rev.append([stride, n])
            stride *= n
        dims += list(reversed(rev))
        assert byte_off + stride * esz <= ARENA_ELEMS * 2, (byte_off, stride, esz)
        return bass.AP(self.h[dt], p0 * self.pstep[dt] + byte_off // esz, dims)


KB = 1024


def build_program():
    nc = bass.Bass("TRN2", target_bir_lowering=False)
    dr = {}

    def din(name, shape, dt=F32):
        dr[name] = nc.dram_tensor(name, shape, dt, kind="ExternalInput")
        return dr[name].ap()

    x_d = din("x", [S, D])
    pos_d = din("pos", [1, S], I32)
    cst_d = din("cst", [128, CST_COLS])
    ln1_d = din("ln1_g", [1, D])
    ln2_d = din("ln2_g", [1, D])
    win_d = din("w_in", [D, 5120])
    qna_d = din("q_norm_a", [1, 64])
    kna_d = din("k_norm_a", [1, 64])
    qnb_d = din("q_norm_b", [1, 64])
    knb_d = din("k_norm_b", [1, 64])
    snk_d = din("sinks", [1, 8])
    wa_d = din("w_branch_a", [256, D])
    wb_d = din("w_branch_b", [512, D])
    wo_d = din("w_out", [D, D])
    wu_d = din("w_up", [D, DFF])
    wd_d = din("w_down", [DFF, D])
    out_h = nc.dram_tensor("out", [S, D], F32, kind="ExternalOutput")
    out_d = out_h.ap()
    x1_h = nc.dram_tensor("x1_scratch", [S, D], F32, kind="Internal")
    x1_d = x1_h.ap()
    dbg = {}
    if DEBUG:
        for name, shape, dt in (("dbg_hT", [128, 8 * S], BF16), ("dbg_oaT", [128, 2 * S], BF16),
                                ("dbg_obT", [128, 4 * S], BF16), ("dbg_tab", [128, 2 * S], BF16),
                                ("dbg_qk", [128, 2 * S], BF16)):
            dbg[name] = nc.dram_tensor(name, shape, dt, kind="ExternalOutput").ap()

    with ExitStack() as es:
        arena = es.enter_context(nc.sbuf_tensor("arena", [128, ARENA_ELEMS], BF16))
        mem = Mem(arena)
        banks = [es.enter_context(nc.psum_tensor("bank%d" % i, [128, 512], F32)) for i in range(8)]
        sch = Sched(nc, es)
        import os as _os
        _stop = _os.environ.get("KSTOP", "")

        class _Stop(Exception):
            pass

        def checkpoint(name):
            if _stop == name:
                raise _Stop()

        def bank_f32(i):
            return banks[i][:, :]

        def bank_bf16(i):
            return banks[i][:, :].bitcast(BF16)

        R_H_START = 7 * KB
        o = 0
        IDENT = mem.ap(BF16, o, [128]); o += 256
        BONES = mem.ap(BF16, o, [128]); o += 256
        PERM = mem.ap(BF16, o, [128]); o += 256
        MASKS = mem.ap(BF16, o, [5, 512]); o += 5120
        CF32 = mem.ap(F32, o, [C_F32_COLS]); o += 4 * C_F32_COLS
        GAINS = mem.ap(F32, o, [4]); o += 16
        ESINK = mem.ap(F32, o, [8]); o += 32
        o = (o + 63) // 64 * 64
        assert o <= 6 * KB + 1024
        assert o <= R_H_START
        R_H = 7 * KB
        R_O = 71 * KB
        R_T = 119 * KB
        R_W = 135 * KB
        R_END = ARENA_ELEMS * 2
        hT = mem.ap(BF16, R_H, [8, S])
        TC = mem.ap(BF16, R_T, [S])
        TS = mem.ap(BF16, R_T + 8 * KB, [S])
        oaT = mem.ap(BF16, R_O, [2, S])
        obT = mem.ap(BF16, R_O + 16 * KB, [4, S])
        INVF = CF32[:, 0:1]
        EPSC = CF32[:, 1:2]

        def emit_all():
            cbf = mem.ap(BF16, 0, [C_BF_COLS])
            sch.dma("pool", "cstb", lambda e: e.dma_start(out=cbf, in_=cst_d[:, 0:C_BF_COLS]), writes=["consts"])
            sch.dma("sp", "cst", lambda e: e.dma_start(out=CF32, in_=cst_d[:, C_BF_COLS:CST_COLS]), writes=["consts"])
            for gi, gd in enumerate((qna_d, kna_d, qnb_d, knb_d)):
                for hb in (0, 64):
                    src = bass.AP(gd.tensor, 0, [[1, 64], [1, 1]])
                    sch.dma("sp", "cst", lambda e, gi=gi, hb=hb, src=src: e.dma_start(out=GAINS[hb:hb + 64, gi:gi + 1], in_=src),
                            writes=["consts"])
            snk_b = bass.AP(snk_d.tensor, 0, [[0, 128], [1, 8]])
            sch.dma("sp", "cst", lambda e: e.dma_start(out=ESINK, in_=snk_b), writes=["consts"])
            sch.op("act", lambda e: e.activation(out=ESINK, in_=ESINK, func=AF.Exp), reads=["consts"], writes=["esink"])

            WP = [mem.ap(BF16, R_W + i * 6 * KB, [8, 384]) for i in range(2)]
            PASSES = []
            for sp in range(2):
                for (g, d, mode) in ((2, 16, "copy"), (1, 4, "add"), (0, 1, "final")):
                    PASSES.append(dict(name="g%d_%d" % (g, sp), d=d, qcol=OFF_QA + g * 256 + sp * 128, kcol=OFF_KA + g * 256 + sp * 128,
                                       vcol=OFF_VA + g * 256 + sp * 128, vdup=False, gq=0, gk=1, mb=0, mode=mode, sinks=None,
                                       out=oaT[:, sp, :], barrier_after=(sp == 1 and mode == "final")))
            for kv in range(2):
                for f in range(2):
                    heads = (4 * kv + 2 * f, 4 * kv + 2 * f + 1)
                    PASSES.append(dict(name="b%d_%d" % (kv, f), d=1, qcol=OFF_QB + heads[0] * 64,
                                       kcol=(OFF_KB + kv * 64) if f == 0 else None, vcol=(OFF_VB + kv * 64) if f == 0 else None,
                                       vdup=True, gq=2, gk=3, mb=2, mode="none", sinks=heads, out=obT[:, 2 * kv + f, :]))
            def emit_wload(pd, slot, wprio=-1.0):
                W = WP[slot]
                wname = "wp%d" % slot

                def wload(dst_lo, src_lo, n):
                    src = win_d[:, src_lo:src_lo + n].rearrange("(k p) n -> p k n", p=128)
                    sch.dma("pool", wname, lambda e: e.dma_start(out=W[:, :, dst_lo:dst_lo + n], in_=src), writes=[wname], prio=wprio)
                wload(0, pd["qcol"], 128)
                if pd["kcol"] is not None:
                    if pd["vdup"]:
                        wload(128, pd["kcol"], 64)
                        wload(192, pd["kcol"], 64)
                        wload(256, OFF_VB, 128)
                    else:
                        wload(128, pd["kcol"], 128)
                        wload(256, pd["vcol"], 128)

            tA = mem.ap(F32, R_O, [S])
            tAi = mem.ap(I32, R_O, [S])
            tB = mem.ap(F32, R_O + 16 * KB, [S])
            tBi = mem.ap(I32, R_O + 16 * KB, [S])
            tM = mem.ap(F32, R_O + 32 * KB, [S])
            sch.begin_defer()
            _tk = [0]

            def _tbump():
                sch.base_prio = 0.4 + 1.6 * _tk[0]
                _tk[0] += 1

            pos_b = bass.AP(pos_d.tensor, 0, [[0, 128], [1, S]])
            _tbump()
            sch.dma("sp", "pos", lambda e: e.dma_start(out=tAi, in_=pos_b), writes=["tA"])
            _tbump()
            sch.op("dve", lambda e: e.tensor_copy(out=tA, in_=tAi), reads=["tA"], writes=["tA"])
            _tbump()
            sch.op("dve", lambda e: e.tensor_scalar(out=tA, in0=tA, scalar1=INVF, scalar2=None, op0=ALU.mult),
                   reads=["tA", "consts"], writes=["tA"])
            _tbump()
            sch.op("dve", lambda e: e.tensor_scalar(out=tA, in0=tA, scalar1=float(1.0 / (2 * math.pi)), scalar2=None, op0=ALU.mult),
                   reads=["tA"], writes=["tA"])
            for which, tab in ((0, TS), (1, TC)):
                if which == 1:
                    _tbump()
                    sch.op("dve", lambda e: e.tensor_scalar(out=tA, in0=tA, scalar1=0.25, scalar2=None, op0=ALU.add),
                           reads=["tA"], writes=["tA"])
                _tbump()
                sch.op("dve", lambda e: e.tensor_copy(out=tBi, in_=tA), reads=["tA"], writes=["tB"])
                _tbump()
                sch.op("dve", lambda e: e.tensor_copy(out=tB, in_=tBi), reads=["tB"], writes=["tB"])
                _tbump()
                sch.op("dve", lambda e: e.tensor_tensor(out=tB, in0=tA, in1=tB, op=ALU.subtract), reads=["tA", "tB"], writes=["tB"])
                _tbump()
                sch.op("dve", lambda e: e.tensor_single_scalar(out=tM, in_=tB, scalar=0.5, op=ALU.is_gt), reads=["tB"], writes=["tM"])
                _tbump()
                sch.op("dve", lambda e: e.tensor_tensor(out=tB, in0=tB, in1=tM, op=ALU.subtract), reads=["tB", "tM"], writes=["tB"])
                _tbump()
                sch.op("dve", lambda e: e.tensor_single_scalar(out=tM, in_=tB, scalar=-0.5, op=ALU.is_lt), reads=["tB"], writes=["tM"])
                _tbump()
                sch.op("dve", lambda e: e.tensor_tensor(out=tB, in0=tB, in1=tM, op=ALU.add), reads=["tB", "tM"], writes=["tB"])
                _tbump()
                sch.op("act", lambda e, tab=tab: e.activation(out=tab, in_=tB, func=AF.Sin, scale=6.283185),
                       reads=["tB"], writes=["tab%d" % which])
            if DEBUG:
                _tbump()
                sch.dma("sp", "dbg", lambda e: e.dma_start(out=dbg["dbg_tab"][:, 0:S], in_=TC), reads=["tab1"])
                _tbump()
                sch.dma("sp", "dbg", lambda e: e.dma_start(out=dbg["dbg_tab"][:, S:2 * S], in_=TS), reads=["tab0"])

            sch.base_prio = 0.0
            checkpoint("T")
            emit_wload(PASSES[0], 0)
            def prenorm_tile(xsrc_ap, xs, xs_name, sem_name, g_b, hb, hb_name, junk, ss_col, ss_name, tp_bank, dst_ap, dst_names,
                             load_eng="sp", junk_name="junk", P=0.0, dma_off=-2.0, tr_off=0.5, cp_off=1.5):
                sch.dma(load_eng, sem_name, lambda e: e.dma_start(out=xs, in_=xsrc_ap), writes=[xs_name], prio=P + dma_off)
                sch.op("act", lambda e: e.activation(out=junk, in_=xs, func=AF.Square, accum_out=ss_col),
                       reads=[xs_name], writes=[junk_name, ss_name], prio=P)
                sch.op("act", lambda e: e.activation(out=ss_col, in_=ss_col, func=AF.Ln, scale=1.0 / D, bias=EPSC),
                       reads=[ss_name, "consts"], writes=[ss_name], prio=P + 0.02)
                sch.op("act", lambda e: e.activation(out=ss_col, in_=ss_col, func=AF.Exp, scale=-0.5),
                       reads=[ss_name], writes=[ss_name], prio=P + 0.04)
                sch.op("dve", lambda e: e.scalar_tensor_tensor(out=hb, in0=xs, scalar=ss_col, in1=g_b, op0=ALU.mult, op1=ALU.mult),
                       reads=[xs_name, ss_name, "gb"], writes=[hb_name], prio=P + 0.06)
                pT = bank_bf16(tp_bank)
                for k in range(8):
                    sch.op("pe", lambda e, k=k: e.transpose(out=pT[:, k * 128:(k + 1) * 128], in_=hb[:, k * 128:(k + 1) * 128], identity=IDENT),
                           reads=[hb_name, "consts"], writes=["bank%d" % tp_bank], prio=P + tr_off)
                sch.op("act", lambda e: e.activation(out=dst_ap, in_=pT.rearrange("p (k t) -> p k t", k=8), func=AF.Copy),
                       reads=["bank%d" % tp_bank], writes=dst_names, prio=P + cp_off)

            p0 = R_W + 48 * KB
            XS = [mem.ap(F32, p0 + i * 4 * KB, [D]) for i in range(3)]
            HB = [mem.ap(BF16, p0 + 12 * KB + i * 2 * KB, [D]) for i in range(2)]
            JUNK = mem.ap(BF16, p0 + 16 * KB, [D])
            GB1 = mem.ap(F32, p0 + 18 * KB, [D])
            SSC = mem.ap(F32, p0 + 22 * KB, [NT])
            assert p0 + 22 * KB + 4 * NT <= R_END
            g1_b = bass.AP(ln1_d.tensor, 0, [[0, 128], [1, D]])
            sch.dma("sp", "gb", lambda e: e.dma_start(out=GB1, in_=g1_b), writes=["gb"])
            for t in range(NT):
                prenorm_tile(x_d[t * 128:(t + 1) * 128, :], XS[t % 3], "xs%d" % (t % 3), "xs%d" % (t % 3), GB1,
                             HB[t % 2], "hb%d" % (t % 2), JUNK, SSC[:, t:t + 1], "ss%d" % t, t % 2,
                             hT[:, :, t * 128:(t + 1) * 128], [("hT", t)], P=float(t))
            sch.flush()
            if DEBUG:
                sch.dma("sp", "dbg", lambda e: e.dma_start(out=dbg["dbg_hT"], in_=hT.rearrange("p k t -> p (k t)")),
                        reads=[("hT", t) for t in range(NT)])
            sch.barrier()

            checkpoint("P0")
            w0 = R_W + 12 * KB
            QT = mem.ap(BF16, w0, [S]); w0 += 8 * KB
            KT = mem.ap(BF16, w0, [S]); w0 += 8 * KB
            VG = mem.ap(BF16, w0, [NT, 2, 128]); w0 += 16 * KB
            SQ = [mem.ap(BF16, w0 + i * KB, [512]) for i in range(2)]; w0 += 2 * KB
            RV = [mem.ap(F32, w0 + i * 2 * KB, [512]) for i in range(2)]; w0 += 4 * KB
            QN = [mem.ap(BF16, w0 + i * KB, [512]) for i in range(2)]; w0 += 2 * KB
            T1 = [mem.ap(F32, w0 + i * 2 * KB, [512]) for i in range(2)]; w0 += 4 * KB
            T2 = [mem.ap(F32, w0 + i * 2 * KB, [512]) for i in range(2)]; w0 += 4 * KB
            PT = [mem.ap(BF16, w0 + i * KB, [512]) for i in range(4)]; w0 += 4 * KB
            RD = [mem.ap(F32, w0 + i * 2 * KB, [512]) for i in range(2)]; w0 += 4 * KB
            PMB = [mem.ap(BF16, w0 + i * KB, [512]) for i in range(2)]; w0 += 2 * KB
            assert w0 <= R_END, w0
            ACC = mem.ap(F32, R_O + 16 * KB, [2, S])
            sch.op("pool", lambda e: e.memset(VG[:, :, 0, 64:128], 1.0), writes=["vg_ones"])
            sch.op("pool", lambda e: e.memset(VG[:, :, 1, 0:64], 1.0), writes=["vg_ones"])

            B_PJ = (0, 1)
            B_SS, B_PM, B_SA, B_SB, B_OT, B_VP = 2, 3, 4, 5, 6, 7
            cnt = {"pj": 0, "blk": 0, "pt": 0, "rd": 0, "w": 0}

            def gcol_ap(buf, d, c):
                L = S // d
                u = 512 // d
                return buf.rearrange("p (r l) -> p r l", r=d)[:, :, u * c:u * (c + 1)]

            def nat_ap(t, d):
                return t.rearrange("p (u r) -> p r u", r=d)

            def gblocks_of_chunk(d, c):
                L = S // d
                u = 512 // d
                blks = set()
                for r in range(d):
                    for col in range(r * L + u * c, r * L + u * (c + 1), min(u, 128)):
                        blks.add(col // 128)
                return sorted(blks)

            def attention_pass(pd, wslot, next_pd):
                name, d, kcol, vcol, vdup = pd["name"], pd["d"], pd["kcol"], pd["vcol"], pd["vdup"]
                gq_idx, gk_idx, mask_base, acc_mode = pd["gq"], pd["gk"], pd["mb"], pd["mode"]
                sink_heads, out_fchunk_ap = pd["sinks"], pd["out"]
                L = S // d
                bpr = L // 128
                W = WP[wslot]
                wname = "wp%d" % wslot
                nq = 2 if kcol is not None else 1
                if next_pd is not None:
                    emit_wload(next_pd, 1 - wslot, 2.0)

                def proj_qk(i, c, which):
                    P = float(i)
                    pj = B_PJ[cnt["pj"] % 2]
                    cnt["pj"] += 1
                    b = cnt["blk"] % 2
                    cnt["blk"] += 1
                    pjn = "bank%d" % pj
                    for k in range(8):
                        sch.op("pe", lambda e, k=k: e.matmul(bank_f32(pj), lhsT=W[:, k, which * 128:(which + 1) * 128],
                                                             rhs=hT[:, k, c * 512:(c + 1) * 512], start=(k == 0), stop=(k == 7)),
                               reads=[wname] + [("hT", t) for t in range(4 * c, 4 * c + 4)], writes=[pjn], prio=P)
                    sch.op("act", lambda e: e.activation(out=SQ[b], in_=bank_f32(pj), func=AF.Square), reads=[pjn], writes=["sq%d" % b],
                           prio=P + 0.02)
                    sch.op("pe", lambda e: e.matmul(bank_f32(B_SS), lhsT=BONES, rhs=SQ[b], start=True, stop=True),
                           reads=["sq%d" % b, "consts"], writes=["bank%d" % B_SS], prio=P + 1.04)
                    sch.op("act", lambda e: e.activation(out=RV[b], in_=bank_f32(B_SS), func=AF.Ln, scale=1.0 / 64, bias=EPSC),
                           reads=["bank%d" % B_SS, "consts"], writes=["rv%d" % b], prio=P + 1.06)
                    sch.op("act", lambda e: e.activation(out=RV[b], in_=RV[b], func=AF.Exp, scale=-0.5), reads=["rv%d" % b], writes=["rv%d" % b],
                           prio=P + 1.08)
                    gi = gq_idx if which == 0 else gk_idx
                    sch.op("dve", lambda e: e.scalar_tensor_tensor(out=QN[b], in0=bank_f32(pj), scalar=GAINS[:, gi:gi + 1], in1=RV[b],
                                                                   op0=ALU.mult, op1=ALU.mult),
                           reads=[pjn, "rv%d" % b, "consts"], writes=["qn%d" % b], prio=P + 1.10)
                    sch.op("pe", lambda e: e.matmul(bank_f32(B_PM), lhsT=PERM, rhs=QN[b], start=True, stop=True),
                           reads=["qn%d" % b, "consts"], writes=["bank%d" % B_PM], prio=P + 2.12)
                    sch.op("dve", lambda e: e.tensor_tensor(out=T1[b], in0=QN[b], in1=TC[:, c * 512:(c + 1) * 512], op=ALU.mult),
                           reads=["qn%d" % b, "tab1"], writes=["t1%d" % b], prio=P + 2.14)
                    if sink_heads is None:
                        sch.op("act", lambda e: e.activation(out=PMB[b], in_=bank_f32(B_PM), func=AF.Copy),
                               reads=["bank%d" % B_PM], writes=["pmb%d" % b], prio=P + 2.13)
                        sch.op("dve", lambda e: e.tensor_tensor(out=T2[b], in0=PMB[b], in1=TS[:, c * 512:(c + 1) * 512], op=ALU.mult),
                               reads=["pmb%d" % b, "tab0"], writes=["t2%d" % b], prio=P + 2.16)
                    else:
                        sch.op("dve", lambda e: e.tensor_tensor(out=T2[b], in0=bank_f32(B_PM), in1=TS[:, c * 512:(c + 1) * 512], op=ALU.mult),
                               reads=["bank%d" % B_PM, "tab0"], writes=["t2%d" % b], prio=P + 2.16)
                    dst = QT if which == 0 else KT
                    dname = "qt" if which == 0 else "kt"
                    sch.op("pool", lambda e: e.tensor_tensor(out=gcol_ap(dst, d, c), in0=nat_ap(T1[b], d), in1=nat_ap(T2[b], d), op=ALU.add),
                           reads=["t1%d" % b, "t2%d" % b], writes=[(dname, g) for g in gblocks_of_chunk(d, c)], prio=P + 2.18)

                def proj_v_batch(gbs, P):
                    vp = bank_f32(B_VP)
                    for si, gb in enumerate(gbs):
                        r, j = gb // bpr, gb % bpr
                        t0 = r + d * 128 * j
                        toks = sorted(set((t0 + d * i) // 128 for i in (0, 127)))
                        tiles = list(range(toks[0], toks[-1] + 1))
                        for k in range(8):
                            lhsT = hT[:, k, t0:t0 + d * 127 + 1:d]
                            sch.op("pe", lambda e, k=k, lhsT=lhsT, si=si: e.matmul(vp[:, si * 128:(si + 1) * 128], lhsT=lhsT, rhs=W[:, k, 256:384],
                                                                                 start=(k == 0), stop=(k == 7), skip_group_check=True),
                                   reads=[wname] + [("hT", t) for t in tiles], writes=["bank%d" % B_VP], prio=P)
                    bstride = (gbs[1] - gbs[0]) * 256
                    vg0 = VG[:, gbs[0], 0, 0:64]
                    dst = bass.AP(vg0.tensor, vg0.offset, [list(vg0.ap[0]), [bstride, 4], [192, 2], [1, 64]])
                    if vdup:
                        kvsel = (vcol - OFF_VB) // 64
                        v0 = vp[:, kvsel * 64:(kvsel + 1) * 64]
                        src = bass.AP(v0.tensor, v0.offset, [list(v0.ap[0]), [128, 4], [0, 2], [1, 64]])
                    else:
                        v0 = vp[:, 0:64]
                        src = bass.AP(v0.tensor, v0.offset, [list(v0.ap[0]), [128, 4], [64, 2], [1, 64]])
                    sch.op("act", lambda e: e.activation(out=dst, in_=src, func=AF.Copy), reads=["bank%d" % B_VP, "vg_ones"],
                           writes=[("vg", gb) for gb in gbs] + [("vgb", gb) for gb in gbs], prio=P + 0.02)

                def acc_tiles(tok0, d):
                    lo = tok0 // 512
                    hi = (tok0 + (255 if d == 1 else 1 + d * 127)) // 512
                    return list(range(lo, hi + 1))

                def round_blocks(n):
                    if d == 1:
                        return [(0, 2 * n), (0, 2 * n + 1)]
                    j, r0 = n // (d // 2), 2 * (n % (d // 2))
                    return [(r0, j), (r0 + 1, j)]

                def attn_round(n, P):
                    qblks = round_blocks(n)
                    if d == 1:
                        first = (qblks[0][1] == 0)
                        mask = MASKS[:, mask_base + (0 if first else 1), :]
                    else:
                        mask = MASKS[:, 4 if qblks[0][1] == 0 else 1, :]
                    sbanks = (B_SA, B_SB)
                    pts = []
                    for half in range(2):
                        sb = bank_f32(sbanks[half])
                        rows = slice(64 * half, 64 * half + 64)
                        for qi in range(2):
                            rq, jq = qblks[qi]
                            gq = rq * bpr + jq
                            for kb in range(2):
                                gk = gq - 1 + kb
                                if jq - 1 + kb < 0:
                                    gk = gq
                                sch.op("pe", lambda e, sb=sb, rows=rows, qi=qi, kb=kb, gk=gk, gq=gq: e.matmul(
                                    sb[:, (2 * qi + kb) * 128:(2 * qi + kb + 1) * 128], lhsT=KT[rows, gk * 128:(gk + 1) * 128],
                                    rhs=QT[rows, gq * 128:(gq + 1) * 128], start=True, stop=True),
                                    reads=[("kt", gk), ("qt", gq)], writes=["bank%d" % sbanks[half]], prio=P + 0.001 * half)
                        p = cnt["pt"] % 4
                        cnt["pt"] += 1
                        pts.append(p)
                        sch.op("act", lambda e, sb=sb, p=p: e.activation(out=PT[p], in_=sb, func=AF.Exp, scale=0.125),
                               reads=["bank%d" % sbanks[half]], writes=["pt%d" % p], prio=P + 0.03 + 0.001 * half)
                        sch.op("dve" if half == 0 else "pool", lambda e, p=p: e.tensor_tensor(out=PT[p], in0=PT[p], in1=mask, op=ALU.mult),
                               reads=["pt%d" % p, "consts"], writes=["pt%d" % p], prio=P + 0.05 + 0.001 * half)
                    ot = bank_f32(B_OT)
                    nmm = 0
                    for half in range(2):
                        for qi in range(2):
                            rq, jq = qblks[qi]
                            gq = rq * bpr + jq
                            for kb in range(2):
                                gk = gq - 1 + kb
                                if jq - 1 + kb < 0:
                                    gk = gq
                                item = 2 * half + qi
                                sch.op("pe", lambda e, half=half, qi=qi, kb=kb, gk=gk, item=item, nmm=nmm: e.matmul(
                                    ot[:, item * 128:(item + 1) * 128], lhsT=VG[:, gk, half, :],
                                    rhs=PT[pts[half]][:, (2 * qi + kb) * 128:(2 * qi + kb + 1) * 128],
                                    start=(nmm == 0), stop=(kb == 1), skip_group_check=True),
                                    reads=[("vg", gk), ("vgb", gk), "pt%d" % pts[half]], writes=["bank%d" % B_OT], prio=P + 1.01)
                                nmm += 1
                    tok0 = qblks[0][0] + d * 128 * qblks[0][1]
                    qstride = 128 if d == 1 else 1
                    otv = ot.rearrange("p (h q i) -> p h q i", h=2, q=2)
                    if sink_heads is None:
                        accv = bass.AP(ACC.tensor, ACC.offset + tok0, [list(ACC.ap[0]), [S, 2], [qstride, 2], [d, 128]])
                        if acc_mode == "copy":
                            sch.op("act", lambda e: e.activation(out=accv, in_=otv, func=AF.Copy), reads=["bank%d" % B_OT],
                                   writes=[("acc", n2) for n2 in acc_tiles(tok0, d)], prio=P + 1.03)
                        else:
                            sch.op("dve", lambda e: e.tensor_tensor(out=accv, in0=otv, in1=accv, op=ALU.add), reads=["bank%d" % B_OT],
                                   writes=[("acc", n2) for n2 in acc_tiles(tok0, d)], prio=P + 1.03)
                    else:
                        geo = []
                        for half in range(2):
                            num = slice(0, 64) if half == 0 else slice(64, 128)
                            den = slice(64, 128) if half == 0 else slice(0, 64)
                            geo.append((half, num, den, slice(256 * half, 256 * half + 256), sink_heads[half]))
                        for (half, num, den, cols, hsink) in geo:
                            sch.op("act", lambda e, half=half, num=num, den=den, cols=cols, hsink=hsink: e.activation(
                                out=RD[half][num, 0:256], in_=ot[den, cols], func=AF.Ln, bias=ESINK[num, hsink:hsink + 1]),
                                reads=["bank%d" % B_OT, "esink"], writes=["rd%d" % half, "ot_act_done"], prio=P + 1.03)
                        for (half, num, den, cols, hsink) in geo:
                            sch.op("act", lambda e, half=half, num=num: e.activation(out=RD[half][num, 0:256], in_=RD[half][num, 0:256],
                                                                                     func=AF.Exp, scale=-1.0),
                                   reads=["rd%d" % half], writes=["rd%d" % half], prio=P + 1.05)
                        for (half, num, den, cols, hsink) in geo:
                            sch.op("dve", lambda e, half=half, num=num, cols=cols: e.tensor_tensor(
                                out=out_fchunk_ap[num, tok0:tok0 + 256], in0=ot[num, cols], in1=RD[half][num, 0:256], op=ALU.mult),
                                reads=["bank%d" % B_OT, "rd%d" % half, "ot_act_done"], writes=[("ob", name, tok0)], prio=P + 1.07)

                round_prio = {}
                cluster = {}
                for n in range(16):
                    cready = max((r + d * (128 * j + 127)) // 512 for (r, j) in round_blocks(n))
                    cluster.setdefault(cready, []).append(n)
                cl = sorted(cluster)
                for ci, c in enumerate(cl):
                    span = ((cl[ci + 1] - c) if ci + 1 < len(cl) else 1) * nq
                    for k, n in enumerate(cluster[c]):
                        round_prio[n] = (c * nq + nq - 1) + 3.3 + k * max(0.5, min(1.0, float(span) / len(cluster[c])))
                assert len(round_prio) == 16
                first_round_of_block = {}
                for n in sorted(range(16), key=lambda n: (round_prio[n], n)):
                    for (r, j) in round_blocks(n):
                        for jj in (j - 1, j):
                            if jj >= 0:
                                first_round_of_block.setdefault(r * bpr + jj, n)
                for c in range(NCH):
                    proj_qk(c * nq, c, 0)
                    if kcol is not None:
                        proj_qk(c * nq + 1, c, 1)
                if kcol is not None:
                    if d == 1:
                        batches = [[4 * m + s_ for s_ in range(4)] for m in range(8)]
                    else:
                        batches = [[(4 * m + s_) * bpr + j for s_ in range(4)] for j in range(bpr) for m in range(d // 4)]
                    bprio = sorted((min(round_prio[first_round_of_block[gb]] for gb in gbs) - 0.9, gbs) for gbs in batches)
                    lastp = None
                    for (pb, gbs) in bprio:
                        if lastp is not None and pb < lastp + 0.1:
                            pb = lastp + 0.1
                        lastp = pb
                        proj_v_batch(gbs, pb)
                for n in sorted(range(16), key=lambda n: (round_prio[n], n)):
                    attn_round(n, round_prio[n])

                if acc_mode == "final":
                    for c in range(NCH):
                        cs = slice(c * 512, (c + 1) * 512)
                        for half in range(2):
                            rb = cnt["rd"] % 2
                            cnt["rd"] += 1
                            num = slice(0, 64) if half == 0 else slice(64, 128)
                            den = slice(64, 128) if half == 0 else slice(0, 64)
                            P = max(round_prio.values()) + 1.2 + 0.2 * (2 * c + half)
                            sch.op("act", lambda e, rb=rb, den=den, num=num, cs=cs, half=half: e.activation(
                                out=RD[rb][num, :], in_=ACC[den, half, cs], func=AF.Ln), reads=[("acc", c)], writes=["rd%d" % rb], prio=P)
                            sch.op("act", lambda e, rb=rb, num=num: e.activation(out=RD[rb][num, :], in_=RD[rb][num, :], func=AF.Exp, scale=-1.0),
                                   reads=["rd%d" % rb], writes=["rd%d" % rb], prio=P + 0.1)
                            sch.op("pool", lambda e, rb=rb, num=num, cs=cs, half=half: e.tensor_tensor(
                                out=out_fchunk_ap[num, cs], in0=ACC[num, half, cs], in1=RD[rb][num, :], op=ALU.mult),
                                reads=[("acc", c), "rd%d" % rb], writes=[("oa", name, c)], prio=P + 0.2)
                return max(round_prio.values()), 8 * nq

            sch.begin_defer()
            pbase = 0.0
            for pi, pd in enumerate(PASSES):
                sch.base_prio = pbase
                last_round, nsteps = attention_pass(pd, pi % 2, PASSES[pi + 1] if pi + 1 < len(PASSES) else None)
                pbase += max(float(nsteps), last_round - 2.0) + (3.4 if pd["mode"] == "final" else 0.0)
                if pd.get("barrier_after"):
                    sch.base_prio = 0.0
                    sch.flush()
                    sch.barrier()
                    checkpoint("A1")
                    sch.begin_defer()
            sch.base_prio = 0.0
            sch.flush()
            if DEBUG:
                sch.dma("sp", "dbg", lambda e: e.dma_start(out=dbg["dbg_qk"][:, 0:S], in_=QT), reads=[("qt", g) for g in range(NT)])
                sch.dma("sp", "dbg", lambda e: e.dma_start(out=dbg["dbg_qk"][:, S:2 * S], in_=KT), reads=[("kt", g) for g in range(NT)])
            sch.barrier()
            if DEBUG:
                sch.dma("sp", "dbg", lambda e: e.dma_start(out=dbg["dbg_oaT"], in_=oaT.rearrange("p k t -> p (k t)")))
                sch.dma("sp", "dbg", lambda e: e.dma_start(out=dbg["dbg_obT"], in_=obT.rearrange("p k t -> p (k t)")))

            checkpoint("A")
            w0 = R_T
            WG = mem.ap(BF16, w0, [8, 2048]); w0 += 32 * KB
            WA = mem.ap(BF16, w0, [2, D]); w0 += 4 * KB
            WB = mem.ap(BF16, w0, [4, D]); w0 += 8 * KB
            WO = mem.ap(BF16, w0, [8, D]); w0 += 16 * KB
            TA = [mem.ap(BF16, w0 + i * KB, [512]) for i in range(2)]; w0 += 2 * KB
            TB = [mem.ap(BF16, w0 + i * KB, [512]) for i in range(2)]; w0 += 2 * KB
            UU = [mem.ap(F32, w0 + i * 2 * KB, [512]) for i in range(2)]; w0 += 4 * KB
            VV = [mem.ap(F32, w0 + i * 2 * KB, [512]) for i in range(2)]; w0 += 4 * KB
            MIX = [mem.ap(BF16, w0, [8, 512]) for i in range(2)]; w0 += 8 * KB
            X5 = [mem.ap(F32, w0 + i * 4 * KB, [D]) for i in range(2)]; w0 += 8 * KB
            assert w0 <= R_END
            for piece in range(4):
                src = win_d[:, OFF_GA + piece * 512:OFF_GA + (piece + 1) * 512].rearrange("(k p) n -> p k n", p=128)
                sch.dma("pool", "wg", lambda e, piece=piece, src=src: e.dma_start(out=WG[:, :, piece * 512:(piece + 1) * 512], in_=src), writes=["wg"])
            sch.dma("pool", "wab", lambda e: e.dma_start(out=WA, in_=wa_d.rearrange("(k p) n -> p k n", p=128)), writes=["wab"])
            sch.dma("pool", "wab", lambda e: e.dma_start(out=WB, in_=wb_d.rearrange("(k p) n -> p k n", p=128)), writes=["wab"])
            sch.dma("pool", "wo", lambda e: e.dma_start(out=WO, in_=wo_d.rearrange("(k p) n -> p k n", p=128)), writes=["wo"])
            B_GA, B_GB, B_YA, B_YB, B_O = (0, 1), (2, 3), 4, 5, (6, 7)
            WU_early = mem.ap(BF16, R_H, [8, DFF])
            n5 = {"g": 0, "o": 0, "x": 0}
            mix = MIX[0]
            mixn = "mix0"

            def p5_merge(c, m):
                cs = slice(c * 512, (c + 1) * 512)
                gi = n5["g"] % 2
                n5["g"] += 1
                bga, bgb = B_GA[gi], B_GB[gi]

                def gate_mm(bk, coff):
                    for k in range(8):
                        sch.op("pe", lambda e, k=k: e.matmul(bank_f32(bk), lhsT=WG[:, k, coff:coff + 128], rhs=hT[:, k, cs],
                                                             start=(k == 0), stop=(k == 7)),
                               reads=["wg", ("hT5", c)], writes=["bank%d" % bk])
                gate_mm(bga, m * 128)
                gate_mm(bgb, 1024 + m * 128)
                for k in range(2):
                    sch.op("pe", lambda e, k=k: e.matmul(bank_f32(B_YA), lhsT=WA[:, k, m * 128:(m + 1) * 128], rhs=oaT[:, k, cs],
                                                         start=(k == 0), stop=(k == 1)), reads=["wab"], writes=["bank%d" % B_YA])
                for k in range(4):
                    sch.op("pe", lambda e, k=k: e.matmul(bank_f32(B_YB), lhsT=WB[:, k, m * 128:(m + 1) * 128], rhs=obT[:, k, cs],
                                                         start=(k == 0), stop=(k == 3)), reads=["wab"], writes=["bank%d" % B_YB])
                sch.op("act", lambda e: e.activation(out=TA[gi], in_=bank_f32(bga), func=AF.Tanh, scale=0.5),
                       reads=["bank%d" % bga], writes=["ta%d" % gi])
                sch.op("act", lambda e: e.activation(out=TB[gi], in_=bank_f32(bgb), func=AF.Tanh, scale=0.5),
                       reads=["bank%d" % bgb], writes=["tb%d" % gi])
                sch.op("dve", lambda e: e.scalar_tensor_tensor(out=UU[gi], in0=TA[gi], scalar=1.0, in1=bank_f32(B_YA), op0=ALU.add, op1=ALU.mult),
                       reads=["ta%d" % gi, "bank%d" % B_YA], writes=["uu%d" % gi])
                sch.op("dve", lambda e: e.scalar_tensor_tensor(out=VV[gi], in0=TB[gi], scalar=1.0, in1=bank_f32(B_YB), op0=ALU.add, op1=ALU.mult),
                       reads=["tb%d" % gi, "bank%d" % B_YB], writes=["vv%d" % gi])
                sch.op("pool", lambda e: e.tensor_tensor(out=mix[:, m, :], in0=UU[gi], in1=VV[gi], op=ALU.add),
                       reads=["uu%d" % gi, "vv%d" % gi], writes=[(mixn, m)])

            def p5_out(c, tt):
                t = 4 * c + tt
                xi = n5["x"] % 2
                n5["x"] += 1
                xt = X5[xi]
                sch.dma("sp", "x5l%d" % xi, lambda e: e.dma_start(out=xt, in_=x_d[t * 128:(t + 1) * 128, :]), writes=["x5_%d" % xi])

                def half(hf):
                    bo = B_O[n5["o"] % 2]
                    n5["o"] += 1
                    for k in range(8):
                        sch.op("pe", lambda e, k=k: e.matmul(bank_f32(bo), lhsT=mix[:, k, tt * 128:(tt + 1) * 128],
                                                             rhs=WO[:, k, hf * 512:(hf + 1) * 512], start=(k == 0), stop=(k == 7)),
                               reads=["wo"] + [(mixn, mm) for mm in range(8)], writes=["bank%d" % bo])
                    sch.op("dve", lambda e: e.scalar_tensor_tensor(
                        out=xt[:, hf * 512:(hf + 1) * 512], in0=bank_f32(bo), scalar=0.5, in1=xt[:, hf * 512:(hf + 1) * 512],
                        op0=ALU.mult, op1=ALU.add), reads=["bank%d" % bo, "x5_%d" % xi], writes=["x5_%d" % xi])
                half(0)
                half(1)
                sch.dma("sp", "x5s%d" % xi, lambda e: e.dma_start(out=x1_d[t * 128:(t + 1) * 128, :], in_=xt),
                        reads=["x5_%d" % xi], writes=[("x1", t)])

            for c in range(NCH):
                for m in range(8):
                    p5_merge(c, m)
                wsrc = wu_d[:, c * 512:(c + 1) * 512].rearrange("(k p) n -> p k n", p=128)
                sch.dma("pool", "wu%d" % c, lambda e, c=c, wsrc=wsrc: e.dma_start(out=WU_early[:, :, c * 512:(c + 1) * 512], in_=wsrc),
                        writes=[("wu", c), ("hT5", c)])
                for tt in range(4):
                    p5_out(c, tt)
            sch.barrier()

            checkpoint("P5")
            w0 = 7 * KB
            WU = mem.ap(BF16, w0, [8, DFF]); w0 += 64 * KB
            WD = mem.ap(BF16, w0, [32, D]); w0 += 64 * KB
            AT = mem.ap(BF16, w0, [32, 512]); w0 += 32 * KB
            H2T = mem.ap(BF16, w0, [8, 512]); w0 += 8 * KB
            X6 = [mem.ap(F32, w0 + i * 4 * KB, [D]) for i in range(5)]; w0 += 20 * KB
            GB2 = mem.ap(F32, w0, [D]); w0 += 4 * KB
            HB6 = [mem.ap(BF16, w0 + i * 2 * KB, [D]) for i in range(2)]; w0 += 4 * KB
            RR = [mem.ap(F32, w0 + i * 2 * KB, [512]) for i in range(2)]; w0 += 4 * KB
            SS6 = mem.ap(F32, 6 * KB + 256, [NT])
            assert w0 <= R_END, w0
            g2_b = bass.AP(ln2_d.tensor, 0, [[0, 128], [1, D]])
            sch.dma("sp", "gb", lambda e: e.dma_start(out=GB2, in_=g2_b), writes=["gb"])
            for piece in range(8):
                src = wd_d[piece * 512:(piece + 1) * 512, :].rearrange("(k p) n -> p k n", p=128)
                sch.dma("pool", "wd%d" % piece, lambda e, piece=piece, src=src: e.dma_start(out=WD[:, piece * 4:(piece + 1) * 4, :], in_=src),
                        writes=[("wd", piece)])
            B_TP, B_U, B_D = (0, 1), (2, 3, 4), (5, 6, 7)
            n6 = {"x": 0, "u": 0, "d": 0, "r": 0}
            NSL, FSL = X6[0:3], X6[3:5]
            sch.begin_defer()

            def p6_prenorm(tb, P0):
                for tt in range(4):
                    t = 4 * tb + tt
                    xi = n6["x"] % 3
                    n6["x"] += 1
                    hbi = t % 2
                    prenorm_tile(x1_d[t * 128:(t + 1) * 128, :], NSL[xi], "x6n_%d" % xi, "x6nl%d" % xi, GB2, HB6[hbi], "hb6_%d" % hbi, HB6[hbi],
                                 SS6[:, t:t + 1], "ss6_%d" % t, B_TP[t % 2], H2T[:, :, tt * 128:(tt + 1) * 128], [("h2t", tt)],
                                 junk_name="hb6_%d" % hbi, P=P0 + 10.0 * tt, dma_off=-6.0, tr_off=1.5, cp_off=3.5)

            def p6_up(f, P):
                bu = B_U[n6["u"] % 3]
                n6["u"] += 1
                ri = n6["r"] % 2
                n6["r"] += 1
                for k in range(8):
                    sch.op("pe", lambda e, k=k: e.matmul(bank_f32(bu), lhsT=WU[:, k, f * 128:(f + 1) * 128], rhs=H2T[:, k, :],
                                                         start=(k == 0), stop=(k == 7)),
                           reads=[("wu", f // 4)] + [("h2t", tt) for tt in range(4)], writes=["bank%d" % bu], prio=P)
                sch.op("act", lambda e: e.activation(out=RR[ri], in_=bank_f32(bu), func=AF.Relu), reads=["bank%d" % bu], writes=["rr%d" % ri],
                       prio=P + 0.3)
                sch.op("dve", lambda e: e.tensor_tensor(out=AT[:, f, :], in0=RR[ri], in1=RR[ri], op=ALU.mult),
                       reads=["rr%d" % ri], writes=[("at", f)], prio=P + 0.6)

            def p6_down(t, tt, B):
                fi = t % 2
                xt = FSL[fi]
                sch.dma("sp", "x6fl%d" % fi, lambda e: e.dma_start(out=xt, in_=x1_d[t * 128:(t + 1) * 128, :]), writes=["x6f_%d" % fi],
                        prio=B + 40 + 10 * tt - 9)

                def half(hf):
                    g = 2 * tt + hf
                    bd = B_D[n6["d"] % 3]
                    n6["d"] += 1
                    for f in range(32):
                        sch.op("pe", lambda e, f=f: e.matmul(bank_f32(bd), lhsT=AT[:, f, tt * 128:(tt + 1) * 128],
                                                             rhs=WD[:, f, hf * 512:(hf + 1) * 512], start=(f == 0), stop=(f == 31)),
                               reads=[("wd", f // 4), ("at", f)], writes=["bank%d" % bd], prio=B + 40 + 5 * g)
                    sch.op("dve", lambda e: e.tensor_tensor(out=xt[:, hf * 512:(hf + 1) * 512], in0=bank_f32(bd),
                                                            in1=xt[:, hf * 512:(hf + 1) * 512], op=ALU.add),
                           reads=["bank%d" % bd, "x6f_%d" % fi], writes=["x6f_%d" % fi], prio=B + 40 + 5 * g + 4.5)
                half(0)
                half(1)
                sch.dma("sp", "x6fs%d" % fi, lambda e: e.dma_start(out=out_d[t * 128:(t + 1) * 128, :], in_=xt),
                        reads=["x6f_%d" % fi], writes=[("out", t)], prio=B + 40 + 5 * (2 * tt + 1) + 4.6)

            p6_prenorm(0, -50.0)
            for tb in range(NCH):
                B = 100.0 * tb
                for f in range(32):
                    p6_up(f, B + f)
                if tb + 1 < NCH:
                    p6_prenorm(tb + 1, B + 41.0)
                for tt in range(4):
                    p6_down(4 * tb + tt, tt, B)
            sch.flush()

        try:
            emit_all()
        except _Stop:
            sch.barrier()
        sch.final_wait("sp", ["x6fs%d" % i for i in range(2)] + (["dbg"] if DEBUG else []))

        sch.finalize()
        block = es.enter_context(nc.Block())

        @block.sync
        def _(e):
            sch.replay("sp", e)

        @block.gpsimd
        def _(e):
            sch.replay("pool", e)

        @block.scalar
        def _(e):
            sch.replay("act", e)

        @block.vector
        def _(e):
            sch.replay("dve", e)

        @block.tensor
        def _(e):
            sch.replay("pe", e)
    return nc


_CACHE = {}


def kernel(x, positions, ln1_g, w_in, q_norm_a, k_norm_a, q_norm_b, k_norm_b, sinks,
           w_branch_a, w_branch_b, w_out, ln2_g, w_up, w_down):
    if "nc" not in _CACHE:
        _CACHE["nc"] = build_program()
    nc = _CACHE["nc"]
    cst = host_consts()
    f32 = lambda a: np.ascontiguousarray(np.asarray(a), dtype=np.float32)
    shared = {
        "cst": cst,
        "ln1_g": f32(ln1_g), "ln2_g": f32(ln2_g), "w_in": f32(w_in)[0],
        "q_norm_a": f32(q_norm_a), "k_norm_a": f32(k_norm_a), "q_norm_b": f32(q_norm_b), "k_norm_b": f32(k_norm_b),
        "sinks": f32(sinks), "w_branch_a": f32(w_branch_a)[0], "w_branch_b": f32(w_branch_b)[0],
        "w_out": f32(w_out)[0], "w_up": f32(w_up)[0], "w_down": f32(w_down)[0],
    }
    xs = f32(x)
    ps = np.ascontiguousarray(np.asarray(positions), dtype=np.int32)
    in_maps = []
    for b in range(8):
        m = dict(shared)
        m["x"] = xs[b]
        m["pos"] = ps[b:b + 1]
        in_maps.append(m)
    res = run_bass_kernel_spmd(nc, in_maps, core_ids=list(range(8)))
    _CACHE["last"] = res
    out = np.stack([np.asarray(r["out"], dtype=np.float32) for r in res.results], axis=0)
    return out
```

```python
import math
from contextlib import ExitStack

import numpy as np
import concourse.bass as bass
import concourse.mybir as mybir
from concourse.bass_utils import run_bass_kernel_spmd

F32 = mybir.dt.float32
BF16 = mybir.dt.bfloat16
I32 = mybir.dt.int32
AF = mybir.ActivationFunctionType
ALU = mybir.AluOpType

S = 4096
D = 1024
DFF = 4096
NCH = 8
NT = 32
EPS = 1e-6
ARENA_ELEMS = 105984

OFF_QA, OFF_KA, OFF_VA, OFF_QB, OFF_KB, OFF_VB, OFF_GA, OFF_GB = 0, 768, 1536, 2304, 2816, 2944, 3072, 4096

C_IDENT, C_BONES, C_PERM, C_MASK = 0, 128, 256, 384
C_BF_COLS = 384 + 5 * 512
C_INVF = C_BF_COLS
C_F32_COLS = 8
CST_COLS = C_BF_COLS + C_F32_COLS

DEBUG = False


def host_consts():
    c = np.zeros((128, CST_COLS), np.float32)
    c[:, C_IDENT:C_IDENT + 128] = np.eye(128, dtype=np.float32)
    bo = np.zeros((128, 128), np.float32)
    bo[0:64, 0:64] = 1.0
    bo[64:128, 64:128] = 1.0
    c[:, C_BONES:C_BONES + 128] = bo
    pm = np.zeros((128, 128), np.float32)
    for hb in (0, 64):
        for i in range(8):
            pm[hb + i + 8, hb + i] = -1.0
            pm[hb + i, hb + i + 8] = 1.0
    c[:, C_PERM:C_PERM + 128] = pm
    k = np.arange(128)[:, None]
    q = np.arange(128)[None, :]
    diag = (k <= q).astype(np.float32)
    prev_g = (k >= q).astype(np.float32)
    prev_b = (k > q).astype(np.float32)
    zero = np.zeros((128, 128), np.float32)
    masks = [
        np.concatenate([zero, diag, prev_g, diag], axis=1),
        np.concatenate([prev_g, diag, prev_g, diag], axis=1),
        np.concatenate([zero, diag, prev_b, diag], axis=1),
        np.concatenate([prev_b, diag, prev_b, diag], axis=1),
        np.concatenate([zero, diag, zero, diag], axis=1),
    ]
    for i, m in enumerate(masks):
        c[:, C_MASK + 512 * i:C_MASK + 512 * (i + 1)] = m
    inv_freq = (500000.0 ** (-np.arange(0, 16, 2, dtype=np.float32) / 16.0)).astype(np.float32)
    invf = np.zeros(128, np.float32)
    for p in range(128):
        if p % 64 < 16:
            invf[p] = inv_freq[(p % 64) % 8]
    c[:, C_INVF] = invf
    c[:, C_INVF + 1] = EPS
    return c


class Sched:
    ENGS = ("pe", "act", "dve", "pool", "sp")

    def __init__(self, nc, es):
        self.nc = nc
        self.es = es
        self.q = {e: [] for e in self.ENGS}
        self.res = {}
        self.sem = {e: es.enter_context(nc.semaphore("s_" + e)) for e in ("pe", "act", "dve", "pool")}
        self.dsem = {}
        self.dcnt = {}
        self.defer = None
        self.base_prio = 0.0

    def _dma_sem(self, name):
        if name not in self.dsem:
            self.dsem[name] = self.es.enter_context(self.nc.semaphore("d_" + name))
            self.dcnt[name] = 0
        return self.dsem[name]

    def _deps(self, reads, writes):
        deps = set()
        for r in reads:
            st = self.res.get(r)
            if st and st["w"] is not None:
                deps.add(st["w"])
        for w in writes:
            st = self.res.get(w)
            if st:
                if st["w"] is not None:
                    deps.add(st["w"])
                for d in st["r"]:
                    deps.add(d)
        return deps

    def _commit(self, me, reads, writes):
        for r in reads:
            st = self.res.setdefault(r, {"w": None, "r": []})
            st["r"] = [d for d in st["r"] if d[0] != me[0]] + [me]
        for w in writes:
            self.res[w] = {"w": me, "r": []}

    def begin_defer(self):
        self.defer = []

    def flush(self):
        lastw = {}
        expect = []
        for it in self.defer:
            expect.append({r: lastw.get(r) for r in it[6]})
            for w in it[7]:
                lastw[w] = it[1]
        items = sorted(self.defer, key=lambda x: (x[0], x[1]))
        wnow = {}
        for it in items:
            for r, v in expect[it[1]].items():
                if wnow.get(r) != v:
                    raise RuntimeError("priority order breaks producer of %r at prio %s (%s): expected op %s, saw %s"
                                       % (r, it[0], it[3], v, wnow.get(r)))
            for w in it[7]:
                wnow[w] = it[1]
        self.defer = None
        for (_, _, kind, eng, semname, fn, reads, writes) in items:
            if kind == "op":
                self.op(eng, fn, reads, writes)
            else:
                self.dma(eng, semname, fn, reads, writes)

    def op(self, eng, fn, reads=(), writes=(), prio=None):
        if getattr(self, "defer", None) is not None:
            self.defer.append((self.base_prio + (prio or 0.0), len(self.defer), "op", eng, None, fn, tuple(reads), tuple(writes)))
            return None
        deps = self._deps(reads, writes)
        idx = len(self.q[eng])
        self.q[eng].append({"fn": fn, "deps": deps, "kind": "op", "marked": False})
        self._commit((eng, idx), reads, writes)
        return (eng, idx)

    def dma(self, eng, semname, fn, reads=(), writes=(), prio=None):
        if getattr(self, "defer", None) is not None:
            self.defer.append((self.base_prio + (prio or 0.0), len(self.defer), "dma", eng, semname, fn, tuple(reads), tuple(writes)))
            return None
        self._dma_sem(semname)
        deps = self._deps(reads, writes)
        self.dcnt[semname] += 1
        me = ("dma:" + semname, self.dcnt[semname])
        self.q[eng].append({"fn": fn, "deps": deps, "kind": "dma", "sem": semname})
        self._commit(me, reads, writes)
        return me

    def barrier(self):
        deps = set()
        for e in ("pe", "act", "dve", "pool"):
            for i in range(len(self.q[e]) - 1, -1, -1):
                if self.q[e][i]["kind"] == "op":
                    deps.add((e, i))
                    break
        for name, cnt in self.dcnt.items():
            if cnt:
                deps.add(("dma:" + name, cnt))
        for e in self.ENGS:
            self.q[e].append({"fn": None, "deps": set(deps), "kind": "bar"})
        self.res = {}

    def final_wait(self, eng, semnames):
        deps = set(("dma:" + n, self.dcnt[n]) for n in semnames if self.dcnt.get(n))
        self.q[eng].append({"fn": None, "deps": deps, "kind": "bar"})

    def finalize(self):
        for e in self.ENGS:
            for ins in self.q[e]:
                for (dom, idx) in ins["deps"]:
                    if not dom.startswith("dma:"):
                        if dom == "pe" and e == "pe":
                            continue
                        self.q[dom][idx]["marked"] = True
        self.ordinal = {}
        for e in ("pe", "act", "dve", "pool"):
            n = 0
            for i, ins in enumerate(self.q[e]):
                if ins.get("marked"):
                    n += 1
                    self.ordinal[(e, i)] = n
        self.total_incs = n

    def replay(self, eng, eobj):
        seen = {}
        for ins in self.q[eng]:
            need = {}
            for (dom, idx) in ins["deps"]:
                if dom.startswith("dma:"):
                    val = 16 * idx
                else:
                    if dom == "pe" and eng == "pe":
                        continue
                    val = self.ordinal[(dom, idx)]
                if val > need.get(dom, 0):
                    need[dom] = val
            for dom, val in need.items():
                if seen.get(dom, 0) >= val:
                    continue
                seen[dom] = val
                sem = self.dsem[dom[4:]] if dom.startswith("dma:") else self.sem[dom]
                eobj.wait_ge(sem, val)
            if ins["fn"] is None:
                continue
            bi = ins["fn"](eobj)
            if ins["kind"] == "dma":
                bi.then_inc(self.dsem[ins["sem"]], 16)
            elif ins.get("marked"):
                bi.then_inc(self.sem[eng], 1)


class Mem:
    def __init__(self, arena):
        self.h = {BF16: arena, F32: arena.bitcast(F32), I32: arena.bitcast(I32)}
        self.pstep = {BF16: ARENA_ELEMS, F32: ARENA_ELEMS // 2, I32: ARENA_ELEMS // 2}

    def ap(self, dt, byte_off, shape, parts=128, p0=0):
        esz = 2 if dt == BF16 else 4
        assert byte_off % esz == 0
        dims = [[self.pstep[dt], parts]]
        stride = 1
        rev = []
        for n in reversed(shape):
            rev.append([stride, n])
            stride *= n
        dims += list(reversed(rev))
        assert byte_off + stride * esz <= ARENA_ELEMS * 2, (byte_off, stride, esz)
        return bass.AP(self.h[dt], p0 * self.pstep[dt] + byte_off // esz, dims)


KB = 1024


def build_program():
    nc = bass.Bass("TRN2", target_bir_lowering=False)
    dr = {}

    def din(name, shape, dt=F32):
        dr[name] = nc.dram_tensor(name, shape, dt, kind="ExternalInput")
        return dr[name].ap()

    x_d = din("x", [S, D])
    pos_d = din("pos", [1, S], I32)
    cst_d = din("cst", [128, CST_COLS])
    ln1_d = din("ln1_g", [1, D])
    ln2_d = din("ln2_g", [1, D])
    win_d = din("w_in", [D, 5120])
    qna_d = din("q_norm_a", [1, 64])
    kna_d = din("k_norm_a", [1, 64])
    qnb_d = din("q_norm_b", [1, 64])
    knb_d = din("k_norm_b", [1, 64])
    snk_d = din("sinks", [1, 8])
    wa_d = din("w_branch_a", [256, D])
    wb_d = din("w_branch_b", [512, D])
    wo_d = din("w_out", [D, D])
    wu_d = din("w_up", [D, DFF])
    wd_d = din("w_down", [DFF, D])
    out_h = nc.dram_tensor("out", [S, D], F32, kind="ExternalOutput")
    out_d = out_h.ap()
    x1_h = nc.dram_tensor("x1_scratch", [S, D], F32, kind="Internal")
    x1_d = x1_h.ap()
    dbg = {}
    if DEBUG:
        for name, shape, dt in (("dbg_hT", [128, 8 * S], BF16), ("dbg_oaT", [128, 2 * S], BF16),
                                ("dbg_obT", [128, 4 * S], BF16), ("dbg_tab", [128, 2 * S], BF16),
                                ("dbg_qk", [128, 2 * S], BF16)):
            dbg[name] = nc.dram_tensor(name, shape, dt, kind="ExternalOutput").ap()

    with ExitStack() as es:
        arena = es.enter_context(nc.sbuf_tensor("arena", [128, ARENA_ELEMS], BF16))
        mem = Mem(arena)
        banks = [es.enter_context(nc.psum_tensor("bank%d" % i, [128, 512], F32)) for i in range(8)]
        sch = Sched(nc, es)
        import os as _os
        _stop = _os.environ.get("KSTOP", "")

        class _Stop(Exception):
            pass

        def checkpoint(name):
            if _stop == name:
                raise _Stop()

        def bank_f32(i):
            return banks[i][:, :]

        def bank_bf16(i):
            return banks[i][:, :].bitcast(BF16)

        R_H_START = 7 * KB
        o = 0
        IDENT = mem.ap(BF16, o, [128]); o += 256
        BONES = mem.ap(BF16, o, [128]); o += 256
        PERM = mem.ap(BF16, o, [128]); o += 256
        MASKS = mem.ap(BF16, o, [5, 512]); o += 5120
        CF32 = mem.ap(F32, o, [C_F32_COLS]); o += 4 * C_F32_COLS
        GAINS = mem.ap(F32, o, [4]); o += 16
        ESINK = mem.ap(F32, o, [8]); o += 32
        o = (o + 63) // 64 * 64
        assert o <= 6 * KB + 1024
        assert o <= R_H_START
        R_H = 7 * KB
        R_O = 71 * KB
        R_T = 119 * KB
        R_W = 135 * KB
        R_END = ARENA_ELEMS * 2
        hT = mem.ap(BF16, R_H, [8, S])
        TC = mem.ap(BF16, R_T, [S])
        TS = mem.ap(BF16, R_T + 8 * KB, [S])
        oaT = mem.ap(BF16, R_O, [2, S])
        obT = mem.ap(BF16, R_O + 16 * KB, [4, S])
        INVF = CF32[:, 0:1]
        EPSC = CF32[:, 1:2]

        def emit_all():
            cbf = mem.ap(BF16, 0, [C_BF_COLS])
            sch.dma("pool", "cstb", lambda e: e.dma_start(out=cbf, in_=cst_d[:, 0:C_BF_COLS]), writes=["consts"])
            sch.dma("sp", "cst", lambda e: e.dma_start(out=CF32, in_=cst_d[:, C_BF_COLS:CST_COLS]), writes=["consts"])
            for gi, gd in enumerate((qna_d, kna_d, qnb_d, knb_d)):
                for hb in (0, 64):
                    src = bass.AP(gd.tensor, 0, [[1, 64], [1, 1]])
                    sch.dma("sp", "cst", lambda e, gi=gi, hb=hb, src=src: e.dma_start(out=GAINS[hb:hb + 64, gi:gi + 1], in_=src),
                            writes=["consts"])
            snk_b = bass.AP(snk_d.tensor, 0, [[0, 128], [1, 8]])
            sch.dma("sp", "cst", lambda e: e.dma_start(out=ESINK, in_=snk_b), writes=["consts"])
            sch.op("act", lambda e: e.activation(out=ESINK, in_=ESINK, func=AF.Exp), reads=["consts"], writes=["esink"])

            WP = [mem.ap(BF16, R_W + i * 6 * KB, [8, 384]) for i in range(2)]
            PASSES = []
            for sp in range(2):
                for (g, d, mode) in ((2, 16, "copy"), (1, 4, "add"), (0, 1, "final")):
                    PASSES.append(dict(name="g%d_%d" % (g, sp), d=d, qcol=OFF_QA + g * 256 + sp * 128, kcol=OFF_KA + g * 256 + sp * 128,
                                       vcol=OFF_VA + g * 256 + sp * 128, vdup=False, gq=0, gk=1, mb=0, mode=mode, sinks=None,
                                       out=oaT[:, sp, :], barrier_after=(sp == 1 and mode == "final")))
            for kv in range(2):
                for f in range(2):
                    heads = (4 * kv + 2 * f, 4 * kv + 2 * f + 1)
                    PASSES.append(dict(name="b%d_%d" % (kv, f), d=1, qcol=OFF_QB + heads[0] * 64,
                                       kcol=(OFF_KB + kv * 64) if f == 0 else None, vcol=(OFF_VB + kv * 64) if f == 0 else None,
                                       vdup=True, gq=2, gk=3, mb=2, mode="none", sinks=heads, out=obT[:, 2 * kv + f, :]))
            def emit_wload(pd, slot, wprio=-1.0):
                W = WP[slot]
                wname = "wp%d" % slot

                def wload(dst_lo, src_lo, n):
                    src = win_d[:, src_lo:src_lo + n].rearrange("(k p) n -> p k n", p=128)
                    sch.dma("pool", wname, lambda e: e.dma_start(out=W[:, :, dst_lo:dst_lo + n], in_=src), writes=[wname], prio=wprio)
                wload(0, pd["qcol"], 128)
                if pd["kcol"] is not None:
                    if pd["vdup"]:
                        wload(128, pd["kcol"], 64)
                        wload(192, pd["kcol"], 64)
                        wload(256, OFF_VB, 128)
                    else:
                        wload(128, pd["kcol"], 128)
                        wload(256, pd["vcol"], 128)

            tA = mem.ap(F32, R_O, [S])
            tAi = mem.ap(I32, R_O, [S])
            tB = mem.ap(F32, R_O + 16 * KB, [S])
            tBi = mem.ap(I32, R_O + 16 * KB, [S])
            tM = mem.ap(F32, R_O + 32 * KB, [S])
            sch.begin_defer()
            _tk = [0]

            def _tbump():
                sch.base_prio = 0.4 + 1.6 * _tk[0]
                _tk[0] += 1

            pos_b = bass.AP(pos_d.tensor, 0, [[0, 128], [1, S]])
            _tbump()
            sch.dma("sp", "pos", lambda e: e.dma_start(out=tAi, in_=pos_b), writes=["tA"])
            _tbump()
            sch.op("dve", lambda e: e.tensor_copy(out=tA, in_=tAi), reads=["tA"], writes=["tA"])
            _tbump()
            sch.op("dve", lambda e: e.tensor_scalar(out=tA, in0=tA, scalar1=INVF, scalar2=None, op0=ALU.mult),
                   reads=["tA", "consts"], writes=["tA"])
            _tbump()
            sch.op("dve", lambda e: e.tensor_scalar(out=tA, in0=tA, scalar1=float(1.0 / (2 * math.pi)), scalar2=None, op0=ALU.mult),
                   reads=["tA"], writes=["tA"])
            for which, tab in ((0, TS), (1, TC)):
                if which == 1:
                    _tbump()
                    sch.op("dve", lambda e: e.tensor_scalar(out=tA, in0=tA, scalar1=0.25, scalar2=None, op0=ALU.add),
                           reads=["tA"], writes=["tA"])
                _tbump()
                sch.op("dve", lambda e: e.tensor_copy(out=tBi, in_=tA), reads=["tA"], writes=["tB"])
                _tbump()
                sch.op("dve", lambda e: e.tensor_copy(out=tB, in_=tBi), reads=["tB"], writes=["tB"])
                _tbump()
                sch.op("dve", lambda e: e.tensor_tensor(out=tB, in0=tA, in1=tB, op=ALU.subtract), reads=["tA", "tB"], writes=["tB"])
                _tbump()
                sch.op("dve", lambda e: e.tensor_single_scalar(out=tM, in_=tB, scalar=0.5, op=ALU.is_gt), reads=["tB"], writes=["tM"])
                _tbump()
                sch.op("dve", lambda e: e.tensor_tensor(out=tB, in0=tB, in1=tM, op=ALU.subtract), reads=["tB", "tM"], writes=["tB"])
                _tbump()
                sch.op("dve", lambda e: e.tensor_single_scalar(out=tM, in_=tB, scalar=-0.5, op=ALU.is_lt), reads=["tB"], writes=["tM"])
                _tbump()
                sch.op("dve", lambda e: e.tensor_tensor(out=tB, in0=tB, in1=tM, op=ALU.add), reads=["tB", "tM"], writes=["tB"])
                _tbump()
                sch.op("act", lambda e, tab=tab: e.activation(out=tab, in_=tB, func=AF.Sin, scale=6.283185),
                       reads=["tB"], writes=["tab%d" % which])
            if DEBUG:
                _tbump()
                sch.dma("sp", "dbg", lambda e: e.dma_start(out=dbg["dbg_tab"][:, 0:S], in_=TC), reads=["tab1"])
                _tbump()
                sch.dma("sp", "dbg", lambda e: e.dma_start(out=dbg["dbg_tab"][:, S:2 * S], in_=TS), reads=["tab0"])

            sch.base_prio = 0.0
            checkpoint("T")
            emit_wload(PASSES[0], 0)
            def prenorm_tile(xsrc_ap, xs, xs_name, sem_name, g_b, hb, hb_name, junk, ss_col, ss_name, tp_bank, dst_ap, dst_names,
                             load_eng="sp", junk_name="junk", P=0.0, dma_off=-2.0, tr_off=0.5, cp_off=1.5):
                sch.dma(load_eng, sem_name, lambda e: e.dma_start(out=xs, in_=xsrc_ap), writes=[xs_name], prio=P + dma_off)
                sch.op("act", lambda e: e.activation(out=junk, in_=xs, func=AF.Square, accum_out=ss_col),
                       reads=[xs_name], writes=[junk_name, ss_name], prio=P)
                sch.op("act", lambda e: e.activation(out=ss_col, in_=ss_col, func=AF.Ln, scale=1.0 / D, bias=EPSC),
                       reads=[ss_name, "consts"], writes=[ss_name], prio=P + 0.02)
                sch.op("act", lambda e: e.activation(out=ss_col, in_=ss_col, func=AF.Exp, scale=-0.5),
                       reads=[ss_name], writes=[ss_name], prio=P + 0.04)
                sch.op("dve", lambda e: e.scalar_tensor_tensor(out=hb, in0=xs, scalar=ss_col, in1=g_b, op0=ALU.mult, op1=ALU.mult),
                       reads=[xs_name, ss_name, "gb"], writes=[hb_name], prio=P + 0.06)
                pT = bank_bf16(tp_bank)
                for k in range(8):
                    sch.op("pe", lambda e, k=k: e.transpose(out=pT[:, k * 128:(k + 1) * 128], in_=hb[:, k * 128:(k + 1) * 128], identity=IDENT),
                           reads=[hb_name, "consts"], writes=["bank%d" % tp_bank], prio=P + tr_off)
                sch.op("act", lambda e: e.activation(out=dst_ap, in_=pT.rearrange("p (k t) -> p k t", k=8), func=AF.Copy),
                       reads=["bank%d" % tp_bank], writes=dst_names, prio=P + cp_off)

            p0 = R_W + 48 * KB
            XS = [mem.ap(F32, p0 + i * 4 * KB, [D]) for i in range(3)]
            HB = [mem.ap(BF16, p0 + 12 * KB + i * 2 * KB, [D]) for i in range(2)]
            JUNK = mem.ap(BF16, p0 + 16 * KB, [D])
            GB1 = mem.ap(F32, p0 + 18 * KB, [D])
            SSC = mem.ap(F32, p0 + 22 * KB, [NT])
            assert p0 + 22 * KB + 4 * NT <= R_END
            g1_b = bass.AP(ln1_d.tensor, 0, [[0, 128], [1, D]])
            sch.dma("sp", "gb", lambda e: e.dma_start(out=GB1, in_=g1_b), writes=["gb"])
            for t in range(NT):
                prenorm_tile(x_d[t * 128:(t + 1) * 128, :], XS[t % 3], "xs%d" % (t % 3), "xs%d" % (t % 3), GB1,
                             HB[t % 2], "hb%d" % (t % 2), JUNK, SSC[:, t:t + 1], "ss%d" % t, t % 2,
                             hT[:, :, t * 128:(t + 1) * 128], [("hT", t)], P=float(t))
            sch.flush()
            if DEBUG:
                sch.dma("sp", "dbg", lambda e: e.dma_start(out=dbg["dbg_hT"], in_=hT.rearrange("p k t -> p (k t)")),
                        reads=[("hT", t) for t in range(NT)])
            sch.barrier()

            checkpoint("P0")
            w0 = R_W + 12 * KB
            QT = mem.ap(BF16, w0, [S]); w0 += 8 * KB
            KT = mem.ap(BF16, w0, [S]); w0 += 8 * KB
            VG = mem.ap(BF16, w0, [NT, 2, 128]); w0 += 16 * KB
            SQ = [mem.ap(BF16, w0 + i * KB, [512]) for i in range(2)]; w0 += 2 * KB
            RV = [mem.ap(F32, w0 + i * 2 * KB, [512]) for i in range(2)]; w0 += 4 * KB
            QN = [mem.ap(BF16, w0 + i * KB, [512]) for i in range(2)]; w0 += 2 * KB
            T1 = [mem.ap(F32, w0 + i * 2 * KB, [512]) for i in range(2)]; w0 += 4 * KB
            T2 = [mem.ap(F32, w0 + i * 2 * KB, [512]) for i in range(2)]; w0 += 4 * KB
            PT = [mem.ap(BF16, w0 + i * KB, [512]) for i in range(4)]; w0 += 4 * KB
            RD = [mem.ap(F32, w0 + i * 2 * KB, [512]) for i in range(2)]; w0 += 4 * KB
            PMB = [mem.ap(BF16, w0 + i * KB, [512]) for i in range(2)]; w0 += 2 * KB
            assert w0 <= R_END, w0
            ACC = mem.ap(F32, R_O + 16 * KB, [2, S])
            sch.op("pool", lambda e: e.memset(VG[:, :, 0, 64:128], 1.0), writes=["vg_ones"])
            sch.op("pool", lambda e: e.memset(VG[:, :, 1, 0:64], 1.0), writes=["vg_ones"])

            B_PJ = (0, 1)
            B_SS, B_PM, B_SA, B_SB, B_OT, B_VP = 2, 3, 4, 5, 6, 7
            cnt = {"pj": 0, "blk": 0, "pt": 0, "rd": 0, "w": 0}

            def gcol_ap(buf, d, c):
                L = S // d
                u = 512 // d
                return buf.rearrange("p (r l) -> p r l", r=d)[:, :, u * c:u * (c + 1)]

            def nat_ap(t, d):
                return t.rearrange("p (u r) -> p r u", r=d)

            def gblocks_of_chunk(d, c):
                L = S // d
                u = 512 // d
                blks = set()
                for r in range(d):
                    for col in range(r * L + u * c, r * L + u * (c + 1), min(u, 128)):
                        blks.add(col // 128)
                return sorted(blks)

            def attention_pass(pd, wslot, next_pd):
                name, d, kcol, vcol, vdup = pd["name"], pd["d"], pd["kcol"], pd["vcol"], pd["vdup"]
                gq_idx, gk_idx, mask_base, acc_mode = pd["gq"], pd["gk"], pd["mb"], pd["mode"]
                sink_heads, out_fchunk_ap = pd["sinks"], pd["out"]
                L = S // d
                bpr = L // 128
                W = WP[wslot]
                wname = "wp%d" % wslot
                nq = 2 if kcol is not None else 1
                if next_pd is not None:
                    emit_wload(next_pd, 1 - wslot, 2.0)

                def proj_qk(i, c, which):
                    P = float(i)
                    pj = B_PJ[cnt["pj"] % 2]
                    cnt["pj"] += 1
                    b = cnt["blk"] % 2
                    cnt["blk"] += 1
                    pjn = "bank%d" % pj
                    for k in range(8):
                        sch.op("pe", lambda e, k=k: e.matmul(bank_f32(pj), lhsT=W[:, k, which * 128:(which + 1) * 128],
                                                             rhs=hT[:, k, c * 512:(c + 1) * 512], start=(k == 0), stop=(k == 7)),
                               reads=[wname] + [("hT", t) for t in range(4 * c, 4 * c + 4)], writes=[pjn], prio=P)
                    sch.op("act", lambda e: e.activation(out=SQ[b], in_=bank_f32(pj), func=AF.Square), reads=[pjn], writes=["sq%d" % b],
                           prio=P + 0.02)
                    sch.op("pe", lambda e: e.matmul(bank_f32(B_SS), lhsT=BONES, rhs=SQ[b], start=True, stop=True),
                           reads=["sq%d" % b, "consts"], writes=["bank%d" % B_SS], prio=P + 1.04)
                    sch.op("act", lambda e: e.activation(out=RV[b], in_=bank_f32(B_SS), func=AF.Ln, scale=1.0 / 64, bias=EPSC),
                           reads=["bank%d" % B_SS, "consts"], writes=["rv%d" % b], prio=P + 1.06)
                    sch.op("act", lambda e: e.activation(out=RV[b], in_=RV[b], func=AF.Exp, scale=-0.5), reads=["rv%d" % b], writes=["rv%d" % b],
                           prio=P + 1.08)
                    gi = gq_idx if which == 0 else gk_idx
                    sch.op("dve", lambda e: e.scalar_tensor_tensor(out=QN[b], in0=bank_f32(pj), scalar=GAINS[:, gi:gi + 1], in1=RV[b],
                                                                   op0=ALU.mult, op1=ALU.mult),
                           reads=[pjn, "rv%d" % b, "consts"], writes=["qn%d" % b], prio=P + 1.10)
                    sch.op("pe", lambda e: e.matmul(bank_f32(B_PM), lhsT=PERM, rhs=QN[b], start=True, stop=True),
                           reads=["qn%d" % b, "consts"], writes=["bank%d" % B_PM], prio=P + 2.12)
                    sch.op("dve", lambda e: e.tensor_tensor(out=T1[b], in0=QN[b], in1=TC[:, c * 512:(c + 1) * 512], op=ALU.mult),
                           reads=["qn%d" % b, "tab1"], writes=["t1%d" % b], prio=P + 2.14)
                    if sink_heads is None:
                        sch.op("act", lambda e: e.activation(out=PMB[b], in_=bank_f32(B_PM), func=AF.Copy),
                               reads=["bank%d" % B_PM], writes=["pmb%d" % b], prio=P + 2.13)
                        sch.op("dve", lambda e: e.tensor_tensor(out=T2[b], in0=PMB[b], in1=TS[:, c * 512:(c + 1) * 512], op=ALU.mult),
                               reads=["pmb%d" % b, "tab0"], writes=["t2%d" % b], prio=P + 2.16)
                    else:
                        sch.op("dve", lambda e: e.tensor_tensor(out=T2[b], in0=bank_f32(B_PM), in1=TS[:, c * 512:(c + 1) * 512], op=ALU.mult),
                               reads=["bank%d" % B_PM, "tab0"], writes=["t2%d" % b], prio=P + 2.16)
                    dst = QT if which == 0 else KT
                    dname = "qt" if which == 0 else "kt"
                    sch.op("pool", lambda e: e.tensor_tensor(out=gcol_ap(dst, d, c), in0=nat_ap(T1[b], d), in1=nat_ap(T2[b], d), op=ALU.add),
                           reads=["t1%d" % b, "t2%d" % b], writes=[(dname, g) for g in gblocks_of_chunk(d, c)], prio=P + 2.18)

                def proj_v_batch(gbs, P):
                    vp = bank_f32(B_VP)
                    for si, gb in enumerate(gbs):
                        r, j = gb // bpr, gb % bpr
                        t0 = r + d * 128 * j
                        toks = sorted(set((t0 + d * i) // 128 for i in (0, 127)))
                        tiles = list(range(toks[0], toks[-1] + 1))
                        for k in range(8):
                            lhsT = hT[:, k, t0:t0 + d * 127 + 1:d]
                            sch.op("pe", lambda e, k=k, lhsT=lhsT, si=si: e.matmul(vp[:, si * 128:(si + 1) * 128], lhsT=lhsT, rhs=W[:, k, 256:384],
                                                                                 start=(k == 0), stop=(k == 7), skip_group_check=True),
                                   reads=[wname] + [("hT", t) for t in tiles], writes=["bank%d" % B_VP], prio=P)
                    bstride = (gbs[1] - gbs[0]) * 256
                    vg0 = VG[:, gbs[0], 0, 0:64]
                    dst = bass.AP(vg0.tensor, vg0.offset, [list(vg0.ap[0]), [bstride, 4], [192, 2], [1, 64]])
                    if vdup:
                        kvsel = (vcol - OFF_VB) // 64
                        v0 = vp[:, kvsel * 64:(kvsel + 1) * 64]
                        src = bass.AP(v0.tensor, v0.offset, [list(v0.ap[0]), [128, 4], [0, 2], [1, 64]])
                    else:
                        v0 = vp[:, 0:64]
                        src = bass.AP(v0.tensor, v0.offset, [list(v0.ap[0]), [128, 4], [64, 2], [1, 64]])
                    sch.op("act", lambda e: e.activation(out=dst, in_=src, func=AF.Copy), reads=["bank%d" % B_VP, "vg_ones"],
                           writes=[("vg", gb) for gb in gbs] + [("vgb", gb) for gb in gbs], prio=P + 0.02)

                def acc_tiles(tok0, d):
                    lo = tok0 // 512
                    hi = (tok0 + (255 if d == 1 else 1 + d * 127)) // 512
                    return list(range(lo, hi + 1))

                def round_blocks(n):
                    if d == 1:
                        return [(0, 2 * n), (0, 2 * n + 1)]
                    j, r0 = n // (d // 2), 2 * (n % (d // 2))
                    return [(r0, j), (r0 + 1, j)]

                def attn_round(n, P):
                    qblks = round_blocks(n)
                    if d == 1:
                        first = (qblks[0][1] == 0)
                        mask = MASKS[:, mask_base + (0 if first else 1), :]
                    else:
                        mask = MASKS[:, 4 if qblks[0][1] == 0 else 1, :]
                    sbanks = (B_SA, B_SB)
                    pts = []
                    for half in range(2):
                        sb = bank_f32(sbanks[half])
                        rows = slice(64 * half, 64 * half + 64)
                        for qi in range(2):
                            rq, jq = qblks[qi]
                            gq = rq * bpr + jq
                            for kb in range(2):
                                gk = gq - 1 + kb
                                if jq - 1 + kb < 0:
                                    gk = gq
                                sch.op("pe", lambda e, sb=sb, rows=rows, qi=qi, kb=kb, gk=gk, gq=gq: e.matmul(
                                    sb[:, (2 * qi + kb) * 128:(2 * qi + kb + 1) * 128], lhsT=KT[rows, gk * 128:(gk + 1) * 128],
                                    rhs=QT[rows, gq * 128:(gq + 1) * 128], start=True, stop=True),
                                    reads=[("kt", gk), ("qt", gq)], writes=["bank%d" % sbanks[half]], prio=P + 0.001 * half)
                        p = cnt["pt"] % 4
                        cnt["pt"] += 1
                        pts.append(p)
                        sch.op("act", lambda e, sb=sb, p=p: e.activation(out=PT[p], in_=sb, func=AF.Exp, scale=0.125),
                               reads=["bank%d" % sbanks[half]], writes=["pt%d" % p], prio=P + 0.03 + 0.001 * half)
                        sch.op("dve" if half == 0 else "pool", lambda e, p=p: e.tensor_tensor(out=PT[p], in0=PT[p], in1=mask, op=ALU.mult),
                               reads=["pt%d" % p, "consts"], writes=["pt%d" % p], prio=P + 0.05 + 0.001 * half)
                    ot = bank_f32(B_OT)
                    nmm = 0
                    for half in range(2):
                        for qi in range(2):
                            rq, jq = qblks[qi]
                            gq = rq * bpr + jq
                            for kb in range(2):
                                gk = gq - 1 + kb
                                if jq - 1 + kb < 0:
                                    gk = gq
                                item = 2 * half + qi
                                sch.op("pe", lambda e, half=half, qi=qi, kb=kb, gk=gk, item=item, nmm=nmm: e.matmul(
                                    ot[:, item * 128:(item + 1) * 128], lhsT=VG[:, gk, half, :],
                                    rhs=PT[pts[half]][:, (2 * qi + kb) * 128:(2 * qi + kb + 1) * 128],
                                    start=(nmm == 0), stop=(kb == 1), skip_group_check=True),
                                    reads=[("vg", gk), ("vgb", gk), "pt%d" % pts[half]], writes=["bank%d" % B_OT], prio=P + 1.01)
                                nmm += 1
                    tok0 = qblks[0][0] + d * 128 * qblks[0][1]
                    qstride = 128 if d == 1 else 1
                    otv = ot.rearrange("p (h q i) -> p h q i", h=2, q=2)
                    if sink_heads is None:
                        accv = bass.AP(ACC.tensor, ACC.offset + tok0, [list(ACC.ap[0]), [S, 2], [qstride, 2], [d, 128]])
                        if acc_mode == "copy":
                            sch.op("act", lambda e: e.activation(out=accv, in_=otv, func=AF.Copy), reads=["bank%d" % B_OT],
                                   writes=[("acc", n2) for n2 in acc_tiles(tok0, d)], prio=P + 1.03)
                        else:
                            sch.op("dve", lambda e: e.tensor_tensor(out=accv, in0=otv, in1=accv, op=ALU.add), reads=["bank%d" % B_OT],
                                   writes=[("acc", n2) for n2 in acc_tiles(tok0, d)], prio=P + 1.03)
                    else:
                        geo = []
                        for half in range(2):
                            num = slice(0, 64) if half == 0 else slice(64, 128)
                            den = slice(64, 128) if half == 0 else slice(0, 64)
                            geo.append((half, num, den, slice(256 * half, 256 * half + 256), sink_heads[half]))
                        for (half, num, den, cols, hsink) in geo:
                            sch.op("act", lambda e, half=half, num=num, den=den, cols=cols, hsink=hsink: e.activation(
                                out=RD[half][num, 0:256], in_=ot[den, cols], func=AF.Ln, bias=ESINK[num, hsink:hsink + 1]),
                                reads=["bank%d" % B_OT, "esink"], writes=["rd%d" % half, "ot_act_done"], prio=P + 1.03)
                        for (half, num, den, cols, hsink) in geo:
                            sch.op("act", lambda e, half=half, num=num: e.activation(out=RD[half][num, 0:256], in_=RD[half][num, 0:256],
                                                                                     func=AF.Exp, scale=-1.0),
                                   reads=["rd%d" % half], writes=["rd%d" % half], prio=P + 1.05)
                        for (half, num, den, cols, hsink) in geo:
                            sch.op("dve", lambda e, half=half, num=num, cols=cols: e.tensor_tensor(
                                out=out_fchunk_ap[num, tok0:tok0 + 256], in0=ot[num, cols], in1=RD[half][num, 0:256], op=ALU.mult),
                                reads=["bank%d" % B_OT, "rd%d" % half, "ot_act_done"], writes=[("ob", name, tok0)], prio=P + 1.07)

                round_prio = {}
                cluster = {}
                for n in range(16):
                    cready = max((r + d * (128 * j + 127)) // 512 for (r, j) in round_blocks(n))
                    cluster.setdefault(cready, []).append(n)
                cl = sorted(cluster)
                for ci, c in enumerate(cl):
                    span = ((cl[ci + 1] - c) if ci + 1 < len(cl) else 1) * nq
                    for k, n in enumerate(cluster[c]):
                        round_prio[n] = (c * nq + nq - 1) + 3.3 + k * max(0.5, min(1.0, float(span) / len(cluster[c])))
                assert len(round_prio) == 16
                first_round_of_block = {}
                for n in sorted(range(16), key=lambda n: (round_prio[n], n)):
                    for (r, j) in round_blocks(n):
                        for jj in (j - 1, j):
                            if jj >= 0:
                                first_round_of_block.setdefault(r * bpr + jj, n)
                for c in range(NCH):
                    proj_qk(c * nq, c, 0)
                    if kcol is not None:
                        proj_qk(c * nq + 1, c, 1)
                if kcol is not None:
                    if d == 1:
                        batches = [[4 * m + s_ for s_ in range(4)] for m in range(8)]
                    else:
                        batches = [[(4 * m + s_) * bpr + j for s_ in range(4)] for j in range(bpr) for m in range(d // 4)]
                    bprio = sorted((min(round_prio[first_round_of_block[gb]] for gb in gbs) - 0.9, gbs) for gbs in batches)
                    lastp = None
                    for (pb, gbs) in bprio:
                        if lastp is not None and pb < lastp + 0.1:
                            pb = lastp + 0.1
                        lastp = pb
                        proj_v_batch(gbs, pb)
                for n in sorted(range(16), key=lambda n: (round_prio[n], n)):
                    attn_round(n, round_prio[n])

                if acc_mode == "final":
                    for c in range(NCH):
                        cs = slice(c * 512, (c + 1) * 512)
                        for half in range(2):
                            rb = cnt["rd"] % 2
                            cnt["rd"] += 1
                            num = slice(0, 64) if half == 0 else slice(64, 128)
                            den = slice(64, 128) if half == 0 else slice(0, 64)
                            P = max(round_prio.values()) + 1.2 + 0.2 * (2 * c + half)
                            sch.op("act", lambda e, rb=rb, den=den, num=num, cs=cs, half=half: e.activation(
                                out=RD[rb][num, :], in_=ACC[den, half, cs], func=AF.Ln), reads=[("acc", c)], writes=["rd%d" % rb], prio=P)
                            sch.op("act", lambda e, rb=rb, num=num: e.activation(out=RD[rb][num, :], in_=RD[rb][num, :], func=AF.Exp, scale=-1.0),
                                   reads=["rd%d" % rb], writes=["rd%d" % rb], prio=P + 0.1)
                            sch.op("pool", lambda e, rb=rb, num=num, cs=cs, half=half: e.tensor_tensor(
                                out=out_fchunk_ap[num, cs], in0=ACC[num, half, cs], in1=RD[rb][num, :], op=ALU.mult),
                                reads=[("acc", c), "rd%d" % rb], writes=[("oa", name, c)], prio=P + 0.2)
                return max(round_prio.values()), 8 * nq

            sch.begin_defer()
            pbase = 0.0
            for pi, pd in enumerate(PASSES):
                sch.base_prio = pbase
                last_round, nsteps = attention_pass(pd, pi % 2, PASSES[pi + 1] if pi + 1 < len(PASSES) else None)
                pbase += max(float(nsteps), last_round - 2.0) + (3.4 if pd["mode"] == "final" else 0.0)
                if pd.get("barrier_after"):
                    sch.base_prio = 0.0
                    sch.flush()
                    sch.barrier()
                    checkpoint("A1")
                    sch.begin_defer()
            sch.base_prio = 0.0
            sch.flush()
            if DEBUG:
                sch.dma("sp", "dbg", lambda e: e.dma_start(out=dbg["dbg_qk"][:, 0:S], in_=QT), reads=[("qt", g) for g in range(NT)])
                sch.dma("sp", "dbg", lambda e: e.dma_start(out=dbg["dbg_qk"][:, S:2 * S], in_=KT), reads=[("kt", g) for g in range(NT)])
            sch.barrier()
            if DEBUG:
                sch.dma("sp", "dbg", lambda e: e.dma_start(out=dbg["dbg_oaT"], in_=oaT.rearrange("p k t -> p (k t)")))
                sch.dma("sp", "dbg", lambda e: e.dma_start(out=dbg["dbg_obT"], in_=obT.rearrange("p k t -> p (k t)")))

            checkpoint("A")
            w0 = R_T
            WG = mem.ap(BF16, w0, [8, 2048]); w0 += 32 * KB
            WA = mem.ap(BF16, w0, [2, D]); w0 += 4 * KB
            WB = mem.ap(BF16, w0, [4, D]); w0 += 8 * KB
            WO = mem.ap(BF16, w0, [8, D]); w0 += 16 * KB
            TA = [mem.ap(BF16, w0 + i * KB, [512]) for i in range(2)]; w0 += 2 * KB
            TB = [mem.ap(BF16, w0 + i * KB, [512]) for i in range(2)]; w0 += 2 * KB
            UU = [mem.ap(F32, w0 + i * 2 * KB, [512]) for i in range(2)]; w0 += 4 * KB
            VV = [mem.ap(F32, w0 + i * 2 * KB, [512]) for i in range(2)]; w0 += 4 * KB
            MIX = [mem.ap(BF16, w0, [8, 512]) for i in range(2)]; w0 += 8 * KB
            X5 = [mem.ap(F32, w0 + i * 4 * KB, [D]) for i in range(2)]; w0 += 8 * KB
            assert w0 <= R_END
            def wg_load(ab, q):
                lo = ab * 1024 + q * 256
                src = win_d[:, OFF_GA + lo:OFF_GA + lo + 256].rearrange("(k p) n -> p k n", p=128)
                sch.dma("pool", "wg%d_%d" % (ab, q), lambda e: e.dma_start(out=WG[:, :, lo:lo + 256], in_=src), writes=[("wg", ab, q)])
            wg_load(0, 0)
            wg_load(1, 0)
            sch.dma("pool", "wab", lambda e: e.dma_start(out=WA, in_=wa_d.rearrange("(k p) n -> p k n", p=128)), writes=["wab"])
            sch.dma("pool", "wab", lambda e: e.dma_start(out=WB, in_=wb_d.rearrange("(k p) n -> p k n", p=128)), writes=["wab"])
            wg_load(0, 1)
            wg_load(1, 1)
            sch.dma("pool", "wo", lambda e: e.dma_start(out=WO, in_=wo_d.rearrange("(k p) n -> p k n", p=128)), writes=["wo"])
            for q in (2, 3):
                wg_load(0, q)
                wg_load(1, q)
            B_GA, B_GB, B_YA, B_YB, B_O = (0, 1), (2, 3), 4, 5, (6, 7)
            WU_early = mem.ap(BF16, R_H, [8, DFF])
            n5 = {"g": 0, "o": 0, "x": 0}
            mix = MIX[0]
            mixn = "mix0"

            def p5_merge(c, m):
                P = 10.0 * c + m
                cs = slice(c * 512, (c + 1) * 512)
                gi = n5["g"] % 2
                n5["g"] += 1
                bga, bgb = B_GA[gi], B_GB[gi]

                def gate_mm(bk, ab):
                    coff = ab * 1024 + m * 128
                    for k in range(8):
                        sch.op("pe", lambda e, k=k: e.matmul(bank_f32(bk), lhsT=WG[:, k, coff:coff + 128], rhs=hT[:, k, cs],
                                                             start=(k == 0), stop=(k == 7)),
                               reads=[("wg", ab, m // 2), ("hT5", c)], writes=["bank%d" % bk], prio=P)
                gate_mm(bga, 0)
                gate_mm(bgb, 1)
                for k in range(2):
                    sch.op("pe", lambda e, k=k: e.matmul(bank_f32(B_YA), lhsT=WA[:, k, m * 128:(m + 1) * 128], rhs=oaT[:, k, cs],
                                                         start=(k == 0), stop=(k == 1)), reads=["wab"], writes=["bank%d" % B_YA], prio=P)
                for k in range(4):
                    sch.op("pe", lambda e, k=k: e.matmul(bank_f32(B_YB), lhsT=WB[:, k, m * 128:(m + 1) * 128], rhs=obT[:, k, cs],
                                                         start=(k == 0), stop=(k == 3)), reads=["wab"], writes=["bank%d" % B_YB], prio=P)
                sch.op("act", lambda e: e.activation(out=TA[gi], in_=bank_f32(bga), func=AF.Tanh, scale=0.5),
                       reads=["bank%d" % bga], writes=["ta%d" % gi], prio=P + 0.3)
                sch.op("act", lambda e: e.activation(out=TB[gi], in_=bank_f32(bgb), func=AF.Tanh, scale=0.5),
                       reads=["bank%d" % bgb], writes=["tb%d" % gi], prio=P + 0.32)
                sch.op("dve", lambda e: e.scalar_tensor_tensor(out=UU[gi], in0=TA[gi], scalar=1.0, in1=bank_f32(B_YA), op0=ALU.add, op1=ALU.mult),
                       reads=["ta%d" % gi, "bank%d" % B_YA], writes=["uu%d" % gi], prio=P + 0.5)
                sch.op("dve", lambda e: e.scalar_tensor_tensor(out=VV[gi], in0=TB[gi], scalar=1.0, in1=bank_f32(B_YB), op0=ALU.add, op1=ALU.mult),
                       reads=["tb%d" % gi, "bank%d" % B_YB], writes=["vv%d" % gi], prio=P + 0.52)
                sch.op("pool", lambda e: e.tensor_tensor(out=mix[:, m, :], in0=UU[gi], in1=VV[gi], op=ALU.add),
                       reads=["uu%d" % gi, "vv%d" % gi], writes=[(mixn, m)], prio=10.0 * c + max(m + 0.7, 1.75))

            def p5_out(c, tt):
                P = 10.0 * c + 11.2 + 0.1 * tt
                t = 4 * c + tt
                xi = n5["x"] % 2
                n5["x"] += 1
                xt = X5[xi]
                sch.dma("sp", "x5l%d" % xi, lambda e: e.dma_start(out=xt, in_=x_d[t * 128:(t + 1) * 128, :]), writes=["x5_%d" % xi],
                        prio=(P - 4.0) if tt < 2 else (P - 0.11))

                def half(hf):
                    bo = B_O[n5["o"] % 2]
                    n5["o"] += 1
                    for k in range(8):
                        sch.op("pe", lambda e, k=k: e.matmul(bank_f32(bo), lhsT=mix[:, k, tt * 128:(tt + 1) * 128],
                                                             rhs=WO[:, k, hf * 512:(hf + 1) * 512], start=(k == 0), stop=(k == 7)),
                               reads=["wo"] + [(mixn, mm) for mm in range(8)], writes=["bank%d" % bo], prio=P + 0.01 * hf)
                    sch.op("dve", lambda e: e.scalar_tensor_tensor(
                        out=xt[:, hf * 512:(hf + 1) * 512], in0=bank_f32(bo), scalar=0.5, in1=xt[:, hf * 512:(hf + 1) * 512],
                        op0=ALU.mult, op1=ALU.add), reads=["bank%d" % bo, "x5_%d" % xi], writes=["x5_%d" % xi], prio=P + 0.05 + 0.01 * hf)
                half(0)
                half(1)
                sch.dma("sp", "x5s%d" % xi, lambda e: e.dma_start(out=x1_d[t * 128:(t + 1) * 128, :], in_=xt),
                        reads=["x5_%d" % xi], writes=[("x1", t)], prio=P + 0.08)

            sch.begin_defer()
            for c in range(NCH):
                for m in range(8):
                    p5_merge(c, m)
                wsrc = wu_d[:, c * 512:(c + 1) * 512].rearrange("(k p) n -> p k n", p=128)
                sch.dma("pool", "wu%d" % c, lambda e, c=c, wsrc=wsrc: e.dma_start(out=WU_early[:, :, c * 512:(c + 1) * 512], in_=wsrc),
                        writes=[("wu", c), ("hT5", c)], prio=10.0 * c + 8.5)
                for tt in range(4):
                    p5_out(c, tt)
            sch.flush()
            sch.barrier()

            checkpoint("P5")
            w0 = 7 * KB
            WU = mem.ap(BF16, w0, [8, DFF]); w0 += 64 * KB
            WD = mem.ap(BF16, w0, [32, D]); w0 += 64 * KB
            AT = mem.ap(BF16, w0, [32, 512]); w0 += 32 * KB
            H2T = mem.ap(BF16, w0, [8, 512]); w0 += 8 * KB
            X6 = [mem.ap(F32, w0 + i * 4 * KB, [D]) for i in range(5)]; w0 += 20 * KB
            GB2 = mem.ap(F32, w0, [D]); w0 += 4 * KB
            HB6 = [mem.ap(BF16, w0 + i * 2 * KB, [D]) for i in range(2)]; w0 += 4 * KB
            RR = [mem.ap(F32, w0 + i * 2 * KB, [512]) for i in range(2)]; w0 += 4 * KB
            SS6 = mem.ap(F32, 6 * KB + 256, [NT])
            assert w0 <= R_END, w0
            g2_b = bass.AP(ln2_d.tensor, 0, [[0, 128], [1, D]])
            sch.dma("sp", "gb", lambda e: e.dma_start(out=GB2, in_=g2_b), writes=["gb"])
            for piece in range(8):
                src = wd_d[piece * 512:(piece + 1) * 512, :].rearrange("(k p) n -> p k n", p=128)
                sch.dma("pool", "wd%d" % piece, lambda e, piece=piece, src=src: e.dma_start(out=WD[:, piece * 4:(piece + 1) * 4, :], in_=src),
                        writes=[("wd", piece)])
            B_TP, B_U, B_D = (0, 1), (2, 3, 4), (5, 6, 7)
            n6 = {"x": 0, "u": 0, "d": 0, "r": 0}
            NSL, FSL = X6[0:3], X6[3:5]
            sch.begin_defer()

            def p6_prenorm(tb, P0):
                for tt in range(4):
                    t = 4 * tb + tt
                    xi = n6["x"] % 3
                    n6["x"] += 1
                    hbi = t % 2
                    prenorm_tile(x1_d[t * 128:(t + 1) * 128, :], NSL[xi], "x6n_%d" % xi, "x6nl%d" % xi, GB2, HB6[hbi], "hb6_%d" % hbi, HB6[hbi],
                                 SS6[:, t:t + 1], "ss6_%d" % t, B_TP[t % 2], H2T[:, :, tt * 128:(tt + 1) * 128], [("h2t", tt)],
                                 junk_name="hb6_%d" % hbi, P=P0 + 10.0 * tt, dma_off=-6.0, tr_off=1.5, cp_off=3.5)

            def p6_up(f, P):
                bu = B_U[n6["u"] % 3]
                n6["u"] += 1
                ri = n6["r"] % 2
                n6["r"] += 1
                for k in range(8):
                    sch.op("pe", lambda e, k=k: e.matmul(bank_f32(bu), lhsT=WU[:, k, f * 128:(f + 1) * 128], rhs=H2T[:, k, :],
                                                         start=(k == 0), stop=(k == 7)),
                           reads=[("wu", f // 4)] + [("h2t", tt) for tt in range(4)], writes=["bank%d" % bu], prio=P)
                sch.op("act", lambda e: e.activation(out=RR[ri], in_=bank_f32(bu), func=AF.Relu), reads=["bank%d" % bu], writes=["rr%d" % ri],
                       prio=P + 0.3)
                sch.op("dve", lambda e: e.tensor_tensor(out=AT[:, f, :], in0=RR[ri], in1=RR[ri], op=ALU.mult),
                       reads=["rr%d" % ri], writes=[("at", f)], prio=P + 0.6)

            def p6_down(t, tt, B):
                fi = t % 2
                xt = FSL[fi]
                sch.dma("sp", "x6fl%d" % fi, lambda e: e.dma_start(out=xt, in_=x1_d[t * 128:(t + 1) * 128, :]), writes=["x6f_%d" % fi],
                        prio=B + 40 + 10 * tt - 9)

                def half(hf):
                    g = 2 * tt + hf
                    bd = B_D[n6["d"] % 3]
                    n6["d"] += 1
                    for f in range(32):
                        sch.op("pe", lambda e, f=f: e.matmul(bank_f32(bd), lhsT=AT[:, f, tt * 128:(tt + 1) * 128],
                                                             rhs=WD[:, f, hf * 512:(hf + 1) * 512], start=(f == 0), stop=(f == 31)),
                               reads=[("wd", f // 4), ("at", f)], writes=["bank%d" % bd], prio=B + 40 + 5 * g)
                    sch.op("dve", lambda e: e.tensor_tensor(out=xt[:, hf * 512:(hf + 1) * 512], in0=bank_f32(bd),
                                                            in1=xt[:, hf * 512:(hf + 1) * 512], op=ALU.add),
                           reads=["bank%d" % bd, "x6f_%d" % fi], writes=["x6f_%d" % fi], prio=B + 40 + 5 * g + 4.5)
                half(0)
                half(1)
                sch.dma("sp", "x6fs%d" % fi, lambda e: e.dma_start(out=out_d[t * 128:(t + 1) * 128, :], in_=xt),
                        reads=["x6f_%d" % fi], writes=[("out", t)], prio=B + 40 + 5 * (2 * tt + 1) + 4.6)

            p6_prenorm(0, -50.0)
            for tb in range(NCH):
                B = 100.0 * tb
                for f in range(32):
                    p6_up(f, B + f)
                if tb + 1 < NCH:
                    p6_prenorm(tb + 1, B + 41.0)
                for tt in range(4):
                    p6_down(4 * tb + tt, tt, B)
            sch.flush()

        try:
            emit_all()
        except _Stop:
            sch.barrier()
        sch.final_wait("sp", ["x6fs%d" % i for i in range(2)] + (["dbg"] if DEBUG else []))

        sch.finalize()
        block = es.enter_context(nc.Block())

        @block.sync
        def _(e):
            sch.replay("sp", e)

        @block.gpsimd
        def _(e):
            sch.replay("pool", e)

        @block.scalar
        def _(e):
            sch.replay("act", e)

        @block.vector
        def _(e):
            sch.replay("dve", e)

        @block.tensor
        def _(e):
            sch.replay("pe", e)
    return nc


_CACHE = {}


def kernel(x, positions, ln1_g, w_in, q_norm_a, k_norm_a, q_norm_b, k_norm_b, sinks,
           w_branch_a, w_branch_b, w_out, ln2_g, w_up, w_down):
    if "nc" not in _CACHE:
        _CACHE["nc"] = build_program()
    nc = _CACHE["nc"]
    cst = host_consts()
    f32 = lambda a: np.ascontiguousarray(np.asarray(a), dtype=np.float32)
    shared = {
        "cst": cst,
        "ln1_g": f32(ln1_g), "ln2_g": f32(ln2_g), "w_in": f32(w_in)[0],
        "q_norm_a": f32(q_norm_a), "k_norm_a": f32(k_norm_a), "q_norm_b": f32(q_norm_b), "k_norm_b": f32(k_norm_b),
        "sinks": f32(sinks), "w_branch_a": f32(w_branch_a)[0], "w_branch_b": f32(w_branch_b)[0],
        "w_out": f32(w_out)[0], "w_up": f32(w_up)[0], "w_down": f32(w_down)[0],
    }
    xs = f32(x)
    ps = np.ascontiguousarray(np.asarray(positions), dtype=np.int32)
    in_maps = []
    for b in range(8):
        m = dict(shared)
        m["x"] = xs[b]
        m["pos"] = ps[b:b + 1]
        in_maps.append(m)
    res = run_bass_kernel_spmd(nc, in_maps, core_ids=list(range(8)))
    _CACHE["last"] = res
    out = np.stack([np.asarray(r["out"], dtype=np.float32) for r in res.results], axis=0)
    return out
```

```python
import math
from contextlib import ExitStack

import numpy as np
import concourse.bass as bass
import concourse.mybir as mybir
from concourse.bass_utils import run_bass_kernel_spmd

F32 = mybir.dt.float32
BF16 = mybir.dt.bfloat16
I32 = mybir.dt.int32
AF = mybir.ActivationFunctionType
ALU = mybir.AluOpType

S = 4096
D = 1024
DFF = 4096
NCH = 8
NT = 32
EPS = 1e-6
ARENA_ELEMS = 105984

OFF_QA, OFF_KA, OFF_VA, OFF_QB, OFF_KB, OFF_VB, OFF_GA, OFF_GB = 0, 768, 1536, 2304, 2816, 2944, 3072, 4096

C_IDENT, C_BONES, C_PERM, C_MASK = 0, 128, 256, 384
C_BF_COLS = 384 + 5 * 512
C_INVF = C_BF_COLS
C_F32_COLS = 8
CST_COLS = C_BF_COLS + C_F32_COLS

DEBUG = False


def host_consts():
    c = np.zeros((128, CST_COLS), np.float32)
    c[:, C_IDENT:C_IDENT + 128] = np.eye(128, dtype=np.float32)
    bo = np.zeros((128, 128), np.float32)
    bo[0:64, 0:64] = 1.0
    bo[64:128, 64:128] = 1.0
    c[:, C_BONES:C_BONES + 128] = bo
    pm = np.zeros((128, 128), np.float32)
    for hb in (0, 64):
        for i in range(8):
            pm[hb + i + 8, hb + i] = -1.0
            pm[hb + i, hb + i + 8] = 1.0
    c[:, C_PERM:C_PERM + 128] = pm
    k = np.arange(128)[:, None]
    q = np.arange(128)[None, :]
    diag = (k <= q).astype(np.float32)
    prev_g = (k >= q).astype(np.float32)
    prev_b = (k > q).astype(np.float32)
    zero = np.zeros((128, 128), np.float32)
    masks = [
        np.concatenate([zero, diag, prev_g, diag], axis=1),
        np.concatenate([prev_g, diag, prev_g, diag], axis=1),
        np.concatenate([zero, diag, prev_b, diag], axis=1),
        np.concatenate([prev_b, diag, prev_b, diag], axis=1),
        np.concatenate([zero, diag, zero, diag], axis=1),
    ]
    for i, m in enumerate(masks):
        c[:, C_MASK + 512 * i:C_MASK + 512 * (i + 1)] = m
    inv_freq = (500000.0 ** (-np.arange(0, 16, 2, dtype=np.float32) / 16.0)).astype(np.float32)
    invf = np.zeros(128, np.float32)
    for p in range(128):
        if p % 64 < 16:
            invf[p] = inv_freq[(p % 64) % 8]
    c[:, C_INVF] = invf
    c[:, C_INVF + 1] = EPS
    return c


class Sched:
    ENGS = ("pe", "act", "dve", "pool", "sp")

    def __init__(self, nc, es):
        self.nc = nc
        self.es = es
        self.q = {e: [] for e in self.ENGS}
        self.res = {}
        self.sem = {e: es.enter_context(nc.semaphore("s_" + e)) for e in ("pe", "act", "dve", "pool")}
        self.dsem = {}
        self.dcnt = {}
        self.defer = None
        self.base_prio = 0.0

    def _dma_sem(self, name):
        if name not in self.dsem:
            self.dsem[name] = self.es.enter_context(self.nc.semaphore("d_" + name))
            self.dcnt[name] = 0
        return self.dsem[name]

    def _deps(self, reads, writes):
        deps = set()
        for r in reads:
            st = self.res.get(r)
            if st and st["w"] is not None:
                deps.add(st["w"])
        for w in writes:
            st = self.res.get(w)
            if st:
                if st["w"] is not None:
                    deps.add(st["w"])
                for d in st["r"]:
                    deps.add(d)
        return deps

    def _commit(self, me, reads, writes):
        for r in reads:
            st = self.res.setdefault(r, {"w": None, "r": []})
            st["r"] = [d for d in st["r"] if d[0] != me[0]] + [me]
        for w in writes:
            self.res[w] = {"w": me, "r": []}

    def begin_defer(self):
        self.defer = []

    def flush(self):
        lastw = {}
        expect = []
        for it in self.defer:
            expect.append({r: lastw.get(r) for r in it[6]})
            for w in it[7]:
                lastw[w] = it[1]
        items = sorted(self.defer, key=lambda x: (x[0], x[1]))
        wnow = {}
        for it in items:
            for r, v in expect[it[1]].items():
                if wnow.get(r) != v:
                    raise RuntimeError("priority order breaks producer of %r at prio %s (%s): expected op %s, saw %s"
                                       % (r, it[0], it[3], v, wnow.get(r)))
            for w in it[7]:
                wnow[w] = it[1]
        self.defer = None
        for (_, _, kind, eng, semname, fn, reads, writes) in items:
            if kind == "op":
                self.op(eng, fn, reads, writes)
            else:
                self.dma(eng, semname, fn, reads, writes)

    def op(self, eng, fn, reads=(), writes=(), prio=None):
        if getattr(self, "defer", None) is not None:
            self.defer.append((self.base_prio + (prio or 0.0), len(self.defer), "op", eng, None, fn, tuple(reads), tuple(writes)))
            return None
        deps = self._deps(reads, writes)
        idx = len(self.q[eng])
        self.q[eng].append({"fn": fn, "deps": deps, "kind": "op", "marked": False})
        self._commit((eng, idx), reads, writes)
        return (eng, idx)

    def dma(self, eng, semname, fn, reads=(), writes=(), prio=None):
        if getattr(self, "defer", None) is not None:
            self.defer.append((self.base_prio + (prio or 0.0), len(self.defer), "dma", eng, semname, fn, tuple(reads), tuple(writes)))
            return None
        self._dma_sem(semname)
        deps = self._deps(reads, writes)
        self.dcnt[semname] += 1
        me = ("dma:" + semname, self.dcnt[semname])
        self.q[eng].append({"fn": fn, "deps": deps, "kind": "dma", "sem": semname})
        self._commit(me, reads, writes)
        return me

    def barrier(self):
        deps = set()
        for e in ("pe", "act", "dve", "pool"):
            for i in range(len(self.q[e]) - 1, -1, -1):
                if self.q[e][i]["kind"] == "op":
                    deps.add((e, i))
                    break
        for name, cnt in self.dcnt.items():
            if cnt:
                deps.add(("dma:" + name, cnt))
        for e in self.ENGS:
            self.q[e].append({"fn": None, "deps": set(deps), "kind": "bar"})
        self.res = {}

    def final_wait(self, eng, semnames):
        deps = set(("dma:" + n, self.dcnt[n]) for n in semnames if self.dcnt.get(n))
        self.q[eng].append({"fn": None, "deps": deps, "kind": "bar"})

    def finalize(self):
        for e in self.ENGS:
            for ins in self.q[e]:
                for (dom, idx) in ins["deps"]:
                    if not dom.startswith("dma:"):
                        if dom == "pe" and e == "pe":
                            continue
                        self.q[dom][idx]["marked"] = True
        self.ordinal = {}
        for e in ("pe", "act", "dve", "pool"):
            n = 0
            for i, ins in enumerate(self.q[e]):
                if ins.get("marked"):
                    n += 1
                    self.ordinal[(e, i)] = n
        self.total_incs = n

    def replay(self, eng, eobj):
        seen = {}
        for ins in self.q[eng]:
            need = {}
            for (dom, idx) in ins["deps"]:
                if dom.startswith("dma:"):
                    val = 16 * idx
                else:
                    if dom == "pe" and eng == "pe":
                        continue
                    val = self.ordinal[(dom, idx)]
                if val > need.get(dom, 0):
                    need[dom] = val
            for dom, val in need.items():
                if seen.get(dom, 0) >= val:
                    continue
                seen[dom] = val
                sem = self.dsem[dom[4:]] if dom.startswith("dma:") else self.sem[dom]
                eobj.wait_ge(sem, val)
            if ins["fn"] is None:
                continue
            bi = ins["fn"](eobj)
            if ins["kind"] == "dma":
                bi.then_inc(self.dsem[ins["sem"]], 16)
            elif ins.get("marked"):
                bi.then_inc(self.sem[eng], 1)


class Mem:
    def __init__(self, arena):
        self.h = {BF16: arena, F32: arena.bitcast(F32), I32: arena.bitcast(I32)}
        self.pstep = {BF16: ARENA_ELEMS, F32: ARENA_ELEMS // 2, I32: ARENA_ELEMS // 2}

    def ap(self, dt, byte_off, shape, parts=128, p0=0):
        esz = 2 if dt == BF16 else 4
        assert byte_off % esz == 0
        dims = [[self.pstep[dt], parts]]
        stride = 1
        rev = []
        for n in reversed(shape):
            rev.append([stride, n])
            stride *= n
        dims += list(reversed(rev))
        assert byte_off + stride * esz <= ARENA_ELEMS * 2, (byte_off, stride, esz)
        return bass.AP(self.h[dt], p0 * self.pstep[dt] + byte_off // esz, dims)


KB = 1024


def build_program():
    nc = bass.Bass("TRN2", target_bir_lowering=False)
    dr = {}

    def din(name, shape, dt=F32):
        dr[name] = nc.dram_tensor(name, shape, dt, kind="ExternalInput")
        return dr[name].ap()

    x_d = din("x", [S, D])
    pos_d = din("pos", [1, S], I32)
    cst_d = din("cst", [128, CST_COLS])
    ln1_d = din("ln1_g", [1, D])
    ln2_d = din("ln2_g", [1, D])
    win_d = din("w_in", [D, 5120])
    qna_d = din("q_norm_a", [1, 64])
    kna_d = din("k_norm_a", [1, 64])
    qnb_d = din("q_norm_b", [1, 64])
    knb_d = din("k_norm_b", [1, 64])
    snk_d = din("sinks", [1, 8])
    wa_d = din("w_branch_a", [256, D])
    wb_d = din("w_branch_b", [512, D])
    wo_d = din("w_out", [D, D])
    wu_d = din("w_up", [D, DFF])
    wd_d = din("w_down", [DFF, D])
    out_h = nc.dram_tensor("out", [S, D], F32, kind="ExternalOutput")
    out_d = out_h.ap()
    x1_h = nc.dram_tensor("x1_scratch", [S, D], F32, kind="Internal")
    x1_d = x1_h.ap()
    dbg = {}
    if DEBUG:
        for name, shape, dt in (("dbg_hT", [128, 8 * S], BF16), ("dbg_oaT", [128, 2 * S], BF16),
                                ("dbg_obT", [128, 4 * S], BF16), ("dbg_tab", [128, 2 * S], BF16),
                                ("dbg_qk", [128, 2 * S], BF16)):
            dbg[name] = nc.dram_tensor(name, shape, dt, kind="ExternalOutput").ap()

    with ExitStack() as es:
        arena = es.enter_context(nc.sbuf_tensor("arena", [128, ARENA_ELEMS], BF16))
        mem = Mem(arena)
        banks = [es.enter_context(nc.psum_tensor("bank%d" % i, [128, 512], F32)) for i in range(8)]
        sch = Sched(nc, es)
        import os as _os
        _stop = _os.environ.get("KSTOP", "")

        class _Stop(Exception):
            pass

        def checkpoint(name):
            if _stop == name:
                raise _Stop()

        def bank_f32(i):
            return banks[i][:, :]

        def bank_bf16(i):
            return banks[i][:, :].bitcast(BF16)

        R_H_START = 7 * KB
        o = 0
        IDENT = mem.ap(BF16, o, [128]); o += 256
        BONES = mem.ap(BF16, o, [128]); o += 256
        PERM = mem.ap(BF16, o, [128]); o += 256
        MASKS = mem.ap(BF16, o, [5, 512]); o += 5120
        CF32 = mem.ap(F32, o, [C_F32_COLS]); o += 4 * C_F32_COLS
        GAINS = mem.ap(F32, o, [4]); o += 16
        ESINK = mem.ap(F32, o, [8]); o += 32
        o = (o + 63) // 64 * 64
        assert o <= 6 * KB + 1024
        assert o <= R_H_START
        R_H = 7 * KB
        R_O = 71 * KB
        R_T = 119 * KB
        R_W = 135 * KB
        R_END = ARENA_ELEMS * 2
        hT = mem.ap(BF16, R_H, [8, S])
        TC = mem.ap(BF16, R_T, [S])
        TS = mem.ap(BF16, R_T + 8 * KB, [S])
        oaT = mem.ap(BF16, R_O, [2, S])
        obT = mem.ap(BF16, R_O + 16 * KB, [4, S])
        INVF = CF32[:, 0:1]
        EPSC = CF32[:, 1:2]

        def emit_all():
            cbf = mem.ap(BF16, 0, [C_BF_COLS])
            sch.dma("pool", "cstb", lambda e: e.dma_start(out=cbf, in_=cst_d[:, 0:C_BF_COLS]), writes=["consts"])
            sch.dma("sp", "cst", lambda e: e.dma_start(out=CF32, in_=cst_d[:, C_BF_COLS:CST_COLS]), writes=["consts"])
            for gi, gd in enumerate((qna_d, kna_d, qnb_d, knb_d)):
                for hb in (0, 64):
                    src = bass.AP(gd.tensor, 0, [[1, 64], [1, 1]])
                    sch.dma("sp", "cst", lambda e, gi=gi, hb=hb, src=src: e.dma_start(out=GAINS[hb:hb + 64, gi:gi + 1], in_=src),
                            writes=["consts"])
            snk_b = bass.AP(snk_d.tensor, 0, [[0, 128], [1, 8]])
            sch.dma("sp", "cst", lambda e: e.dma_start(out=ESINK, in_=snk_b), writes=["consts"])
            sch.op("act", lambda e: e.activation(out=ESINK, in_=ESINK, func=AF.Exp), reads=["consts"], writes=["esink"])

            WP = [mem.ap(BF16, R_W + i * 6 * KB, [8, 384]) for i in range(2)]
            PASSES = []
            for sp in range(2):
                for (g, d, mode) in ((2, 16, "copy"), (1, 4, "add"), (0, 1, "final")):
                    PASSES.append(dict(name="g%d_%d" % (g, sp), d=d, qcol=OFF_QA + g * 256 + sp * 128, kcol=OFF_KA + g * 256 + sp * 128,
                                       vcol=OFF_VA + g * 256 + sp * 128, vdup=False, gq=0, gk=1, mb=0, mode=mode, sinks=None,
                                       out=oaT[:, sp, :], barrier_after=(sp == 1 and mode == "final")))
            for kv in range(2):
                for f in range(2):
                    heads = (4 * kv + 2 * f, 4 * kv + 2 * f + 1)
                    PASSES.append(dict(name="b%d_%d" % (kv, f), d=1, qcol=OFF_QB + heads[0] * 64,
                                       kcol=(OFF_KB + kv * 64) if f == 0 else None, vcol=(OFF_VB + kv * 64) if f == 0 else None,
                                       vdup=True, gq=2, gk=3, mb=2, mode="none", sinks=heads, out=obT[:, 2 * kv + f, :]))
            def emit_wload(pd, slot, wprio=-1.0):
                W = WP[slot]
                wname = "wp%d" % slot

                def wload(dst_lo, src_lo, n):
                    src = win_d[:, src_lo:src_lo + n].rearrange("(k p) n -> p k n", p=128)
                    sch.dma("pool", wname, lambda e: e.dma_start(out=W[:, :, dst_lo:dst_lo + n], in_=src), writes=[wname], prio=wprio)
                wload(0, pd["qcol"], 128)
                if pd["kcol"] is not None:
                    if pd["vdup"]:
                        wload(128, pd["kcol"], 64)
                        wload(192, pd["kcol"], 64)
                        wload(256, OFF_VB, 128)
                    else:
                        wload(128, pd["kcol"], 128)
                        wload(256, pd["vcol"], 128)

            tA = mem.ap(F32, R_O, [S])
            tAi = mem.ap(I32, R_O, [S])
            tB = mem.ap(F32, R_O + 16 * KB, [S])
            tBi = mem.ap(I32, R_O + 16 * KB, [S])
            tM = mem.ap(F32, R_O + 32 * KB, [S])
            sch.begin_defer()
            _tk = [0]

            def _tbump():
                sch.base_prio = 0.4 + 2.4 * _tk[0]
                _tk[0] += 1

            pos_b = bass.AP(pos_d.tensor, 0, [[0, 128], [1, S]])
            _tbump()
            sch.dma("sp", "pos", lambda e: e.dma_start(out=tAi, in_=pos_b), writes=["tA"])
            _tbump()
            sch.op("dve", lambda e: e.tensor_copy(out=tA, in_=tAi), reads=["tA"], writes=["tA"])
            _tbump()
            sch.op("dve", lambda e: e.tensor_scalar(out=tA, in0=tA, scalar1=INVF, scalar2=None, op0=ALU.mult),
                   reads=["tA", "consts"], writes=["tA"])
            _tbump()
            sch.op("dve", lambda e: e.tensor_scalar(out=tA, in0=tA, scalar1=float(1.0 / (2 * math.pi)), scalar2=None, op0=ALU.mult),
                   reads=["tA"], writes=["tA"])
            for which, tab in ((0, TS), (1, TC)):
                if which == 1:
                    _tbump()
                    sch.op("dve", lambda e: e.tensor_scalar(out=tA, in0=tA, scalar1=0.25, scalar2=None, op0=ALU.add),
                           reads=["tA"], writes=["tA"])
                _tbump()
                sch.op("dve", lambda e: e.tensor_copy(out=tBi, in_=tA), reads=["tA"], writes=["tB"])
                _tbump()
                sch.op("dve", lambda e: e.tensor_copy(out=tB, in_=tBi), reads=["tB"], writes=["tB"])
                _tbump()
                sch.op("dve", lambda e: e.tensor_tensor(out=tB, in0=tA, in1=tB, op=ALU.subtract), reads=["tA", "tB"], writes=["tB"])
                _tbump()
                sch.op("act", lambda e, tab=tab: e.activation(out=tab, in_=tB, func=AF.Sin, scale=6.283185),
                       reads=["tB"], writes=["tab%d" % which])
            if DEBUG:
                _tbump()
                sch.dma("sp", "dbg", lambda e: e.dma_start(out=dbg["dbg_tab"][:, 0:S], in_=TC), reads=["tab1"])
                _tbump()
                sch.dma("sp", "dbg", lambda e: e.dma_start(out=dbg["dbg_tab"][:, S:2 * S], in_=TS), reads=["tab0"])

            sch.base_prio = 0.0
            checkpoint("T")
            emit_wload(PASSES[0], 0)
            def prenorm_tile(xsrc_ap, xs, xs_name, sem_name, g_b, hb, hb_name, junk, ss_col, ss_name, tp_bank, dst_ap, dst_names,
                             load_eng="sp", junk_name="junk", P=0.0, dma_off=-2.0, tr_off=0.5, cp_off=1.5):
                sch.dma(load_eng, sem_name, lambda e: e.dma_start(out=xs, in_=xsrc_ap), writes=[xs_name], prio=P + dma_off)
                sch.op("act", lambda e: e.activation(out=junk, in_=xs, func=AF.Square, accum_out=ss_col),
                       reads=[xs_name], writes=[junk_name, ss_name], prio=P)
                sch.op("act", lambda e: e.activation(out=ss_col, in_=ss_col, func=AF.Ln, scale=1.0 / D, bias=EPSC),
                       reads=[ss_name, "consts"], writes=[ss_name], prio=P + 0.02)
                sch.op("act", lambda e: e.activation(out=ss_col, in_=ss_col, func=AF.Exp, scale=-0.5),
                       reads=[ss_name], writes=[ss_name], prio=P + 0.04)
                sch.op("dve", lambda e: e.scalar_tensor_tensor(out=hb, in0=xs, scalar=ss_col, in1=g_b, op0=ALU.mult, op1=ALU.mult),
                       reads=[xs_name, ss_name, "gb"], writes=[hb_name], prio=P + 0.06)
                pT = bank_bf16(tp_bank)
                for k in range(8):
                    sch.op("pe", lambda e, k=k: e.transpose(out=pT[:, k * 128:(k + 1) * 128], in_=hb[:, k * 128:(k + 1) * 128], identity=IDENT),
                           reads=[hb_name, "consts"], writes=["bank%d" % tp_bank], prio=P + tr_off)
                sch.op("act", lambda e: e.activation(out=dst_ap, in_=pT.rearrange("p (k t) -> p k t", k=8), func=AF.Copy),
                       reads=["bank%d" % tp_bank], writes=dst_names, prio=P + cp_off)

            p0 = R_W + 48 * KB
            XS = [mem.ap(F32, p0 + i * 4 * KB, [D]) for i in range(3)]
            HB = [mem.ap(BF16, p0 + 12 * KB + i * 2 * KB, [D]) for i in range(2)]
            JUNK = mem.ap(BF16, p0 + 16 * KB, [D])
            GB1 = mem.ap(F32, p0 + 18 * KB, [D])
            SSC = mem.ap(F32, p0 + 22 * KB, [NT])
            assert p0 + 22 * KB + 4 * NT <= R_END
            g1_b = bass.AP(ln1_d.tensor, 0, [[0, 128], [1, D]])
            sch.dma("sp", "gb", lambda e: e.dma_start(out=GB1, in_=g1_b), writes=["gb"])
            for t in range(NT):
                prenorm_tile(x_d[t * 128:(t + 1) * 128, :], XS[t % 3], "xs%d" % (t % 3), "xs%d" % (t % 3), GB1,
                             HB[t % 2], "hb%d" % (t % 2), JUNK, SSC[:, t:t + 1], "ss%d" % t, t % 2,
                             hT[:, :, t * 128:(t + 1) * 128], [("hT", t)], P=float(t))
            sch.flush()
            if DEBUG:
                sch.dma("sp", "dbg", lambda e: e.dma_start(out=dbg["dbg_hT"], in_=hT.rearrange("p k t -> p (k t)")),
                        reads=[("hT", t) for t in range(NT)])
            sch.barrier()

            checkpoint("P0")
            w0 = R_W + 12 * KB
            QT = mem.ap(BF16, w0, [S]); w0 += 8 * KB
            KT = mem.ap(BF16, w0, [S]); w0 += 8 * KB
            VG = mem.ap(BF16, w0, [NT, 2, 128]); w0 += 16 * KB
            SQ = [mem.ap(BF16, w0 + i * KB, [512]) for i in range(2)]; w0 += 2 * KB
            RV = [mem.ap(F32, w0 + i * 2 * KB, [512]) for i in range(2)]; w0 += 4 * KB
            QN = [mem.ap(BF16, w0 + i * KB, [512]) for i in range(2)]; w0 += 2 * KB
            T1 = [mem.ap(F32, w0 + i * 2 * KB, [512]) for i in range(2)]; w0 += 4 * KB
            T2 = [mem.ap(F32, w0 + i * 2 * KB, [512]) for i in range(2)]; w0 += 4 * KB
            PT = [mem.ap(BF16, w0 + i * KB, [512]) for i in range(4)]; w0 += 4 * KB
            RD = [mem.ap(F32, w0 + i * 2 * KB, [512]) for i in range(2)]; w0 += 4 * KB
            PMB = [mem.ap(BF16, w0 + i * KB, [512]) for i in range(2)]; w0 += 2 * KB
            assert w0 <= R_END, w0
            ACC = mem.ap(F32, R_O + 16 * KB, [2, S])
            sch.op("pool", lambda e: e.memset(VG[:, :, 0, 64:128], 1.0), writes=["vg_ones"])
            sch.op("pool", lambda e: e.memset(VG[:, :, 1, 0:64], 1.0), writes=["vg_ones"])

            B_PJ = (0, 1)
            B_SS, B_PM, B_SA, B_SB, B_OT, B_VP = 2, 3, 4, 5, 6, 7
            cnt = {"pj": 0, "blk": 0, "pt": 0, "rd": 0, "w": 0}

            def gcol_ap(buf, d, c):
                L = S // d
                u = 512 // d
                return buf.rearrange("p (r l) -> p r l", r=d)[:, :, u * c:u * (c + 1)]

            def nat_ap(t, d):
                return t.rearrange("p (u r) -> p r u", r=d)

            def gblocks_of_chunk(d, c):
                L = S // d
                u = 512 // d
                blks = set()
                for r in range(d):
                    for col in range(r * L + u * c, r * L + u * (c + 1), min(u, 128)):
                        blks.add(col // 128)
                return sorted(blks)

            def attention_pass(pd, wslot, next_pd):
                name, d, kcol, vcol, vdup = pd["name"], pd["d"], pd["kcol"], pd["vcol"], pd["vdup"]
                gq_idx, gk_idx, mask_base, acc_mode = pd["gq"], pd["gk"], pd["mb"], pd["mode"]
                sink_heads, out_fchunk_ap = pd["sinks"], pd["out"]
                L = S // d
                bpr = L // 128
                W = WP[wslot]
                wname = "wp%d" % wslot
                nq = 2 if kcol is not None else 1
                if next_pd is not None:
                    emit_wload(next_pd, 1 - wslot, 2.0)

                def proj_qk(i, c, which):
                    P = float(i)
                    pj = B_PJ[cnt["pj"] % 2]
                    cnt["pj"] += 1
                    b = cnt["blk"] % 2
                    cnt["blk"] += 1
                    pjn = "bank%d" % pj
                    for k in range(8):
                        sch.op("pe", lambda e, k=k: e.matmul(bank_f32(pj), lhsT=W[:, k, which * 128:(which + 1) * 128],
                                                             rhs=hT[:, k, c * 512:(c + 1) * 512], start=(k == 0), stop=(k == 7)),
                               reads=[wname] + [("hT", t) for t in range(4 * c, 4 * c + 4)], writes=[pjn], prio=P)
                    sch.op("act", lambda e: e.activation(out=SQ[b], in_=bank_f32(pj), func=AF.Square), reads=[pjn], writes=["sq%d" % b],
                           prio=P + 0.02)
                    sch.op("pe", lambda e: e.matmul(bank_f32(B_SS), lhsT=BONES, rhs=SQ[b], start=True, stop=True),
                           reads=["sq%d" % b, "consts"], writes=["bank%d" % B_SS], prio=P + 1.04)
                    sch.op("act", lambda e: e.activation(out=RV[b], in_=bank_f32(B_SS), func=AF.Ln, scale=1.0 / 64, bias=EPSC),
                           reads=["bank%d" % B_SS, "consts"], writes=["rv%d" % b], prio=P + 1.06)
                    sch.op("act", lambda e: e.activation(out=RV[b], in_=RV[b], func=AF.Exp, scale=-0.5), reads=["rv%d" % b], writes=["rv%d" % b],
                           prio=P + 1.08)
                    gi = gq_idx if which == 0 else gk_idx
                    sch.op("dve", lambda e: e.scalar_tensor_tensor(out=QN[b], in0=bank_f32(pj), scalar=GAINS[:, gi:gi + 1], in1=RV[b],
                                                                   op0=ALU.mult, op1=ALU.mult),
                           reads=[pjn, "rv%d" % b, "consts"], writes=["qn%d" % b], prio=P + 1.10)
                    sch.op("pe", lambda e: e.matmul(bank_f32(B_PM), lhsT=PERM, rhs=QN[b], start=True, stop=True),
                           reads=["qn%d" % b, "consts"], writes=["bank%d" % B_PM], prio=P + 2.12)
                    sch.op("dve", lambda e: e.tensor_tensor(out=T1[b], in0=QN[b], in1=TC[:, c * 512:(c + 1) * 512], op=ALU.mult),
                           reads=["qn%d" % b, "tab1"], writes=["t1%d" % b], prio=P + 2.14)
                    if sink_heads is None:
                        sch.op("act", lambda e: e.activation(out=PMB[b], in_=bank_f32(B_PM), func=AF.Copy),
                               reads=["bank%d" % B_PM], writes=["pmb%d" % b], prio=P + 2.13)
                        sch.op("dve", lambda e: e.tensor_tensor(out=T2[b], in0=PMB[b], in1=TS[:, c * 512:(c + 1) * 512], op=ALU.mult),
                               reads=["pmb%d" % b, "tab0"], writes=["t2%d" % b], prio=P + 2.16)
                    else:
                        sch.op("dve", lambda e: e.tensor_tensor(out=T2[b], in0=bank_f32(B_PM), in1=TS[:, c * 512:(c + 1) * 512], op=ALU.mult),
                               reads=["bank%d" % B_PM, "tab0"], writes=["t2%d" % b], prio=P + 2.16)
                    dst = QT if which == 0 else KT
                    dname = "qt" if which == 0 else "kt"
                    sch.op("pool", lambda e: e.tensor_tensor(out=gcol_ap(dst, d, c), in0=nat_ap(T1[b], d), in1=nat_ap(T2[b], d), op=ALU.add),
                           reads=["t1%d" % b, "t2%d" % b], writes=[(dname, g) for g in gblocks_of_chunk(d, c)], prio=P + 2.18)

                def proj_v_batch(gbs, P):
                    vp = bank_f32(B_VP)
                    for si, gb in enumerate(gbs):
                        r, j = gb // bpr, gb % bpr
                        t0 = r + d * 128 * j
                        toks = sorted(set((t0 + d * i) // 128 for i in (0, 127)))
                        tiles = list(range(toks[0], toks[-1] + 1))
                        for k in range(8):
                            lhsT = hT[:, k, t0:t0 + d * 127 + 1:d]
                            sch.op("pe", lambda e, k=k, lhsT=lhsT, si=si: e.matmul(vp[:, si * 128:(si + 1) * 128], lhsT=lhsT, rhs=W[:, k, 256:384],
                                                                                 start=(k == 0), stop=(k == 7), skip_group_check=True),
                                   reads=[wname] + [("hT", t) for t in tiles], writes=["bank%d" % B_VP], prio=P)
                    bstride = (gbs[1] - gbs[0]) * 256
                    vg0 = VG[:, gbs[0], 0, 0:64]
                    dst = bass.AP(vg0.tensor, vg0.offset, [list(vg0.ap[0]), [bstride, 4], [192, 2], [1, 64]])
                    if vdup:
                        kvsel = (vcol - OFF_VB) // 64
                        v0 = vp[:, kvsel * 64:(kvsel + 1) * 64]
                        src = bass.AP(v0.tensor, v0.offset, [list(v0.ap[0]), [128, 4], [0, 2], [1, 64]])
                    else:
                        v0 = vp[:, 0:64]
                        src = bass.AP(v0.tensor, v0.offset, [list(v0.ap[0]), [128, 4], [64, 2], [1, 64]])
                    sch.op("act", lambda e: e.activation(out=dst, in_=src, func=AF.Copy), reads=["bank%d" % B_VP, "vg_ones"],
                           writes=[("vg", gb) for gb in gbs] + [("vgb", gb) for gb in gbs], prio=P + 0.02)

                def acc_tiles(tok0, d):
                    lo = tok0 // 512
                    hi = (tok0 + (255 if d == 1 else 1 + d * 127)) // 512
                    return list(range(lo, hi + 1))

                def round_blocks(n):
                    if d == 1:
                        return [(0, 2 * n), (0, 2 * n + 1)]
                    j, r0 = n // (d // 2), 2 * (n % (d // 2))
                    return [(r0, j), (r0 + 1, j)]

                def attn_round(n, P):
                    qblks = round_blocks(n)
                    if d == 1:
                        first = (qblks[0][1] == 0)
                        mask = MASKS[:, mask_base + (0 if first else 1), :]
                    else:
                        mask = MASKS[:, 4 if qblks[0][1] == 0 else 1, :]
                    sbanks = (B_SA, B_SB)
                    pts = []
                    for half in range(2):
                        sb = bank_f32(sbanks[half])
                        rows = slice(64 * half, 64 * half + 64)
                        for qi in range(2):
                            rq, jq = qblks[qi]
                            gq = rq * bpr + jq
                            for kb in range(2):
                                gk = gq - 1 + kb
                                if jq - 1 + kb < 0:
                                    gk = gq
                                sch.op("pe", lambda e, sb=sb, rows=rows, qi=qi, kb=kb, gk=gk, gq=gq: e.matmul(
                                    sb[:, (2 * qi + kb) * 128:(2 * qi + kb + 1) * 128], lhsT=KT[rows, gk * 128:(gk + 1) * 128],
                                    rhs=QT[rows, gq * 128:(gq + 1) * 128], start=True, stop=True),
                                    reads=[("kt", gk), ("qt", gq)], writes=["bank%d" % sbanks[half]], prio=P + 0.001 * half)
                        p = cnt["pt"] % 4
                        cnt["pt"] += 1
                        pts.append(p)
                        sch.op("act", lambda e, sb=sb, p=p: e.activation(out=PT[p], in_=sb, func=AF.Exp, scale=0.125),
                               reads=["bank%d" % sbanks[half]], writes=["pt%d" % p], prio=P + 0.03 + 0.001 * half)
                        sch.op("dve" if half == 0 else "pool", lambda e, p=p: e.tensor_tensor(out=PT[p], in0=PT[p], in1=mask, op=ALU.mult),
                               reads=["pt%d" % p, "consts"], writes=["pt%d" % p], prio=P + 0.05 + 0.001 * half)
                    ot = bank_f32(B_OT)
                    nmm = 0
                    for half in range(2):
                        for qi in range(2):
                            rq, jq = qblks[qi]
                            gq = rq * bpr + jq
                            for kb in range(2):
                                gk = gq - 1 + kb
                                if jq - 1 + kb < 0:
                                    gk = gq
                                item = 2 * half + qi
                                sch.op("pe", lambda e, half=half, qi=qi, kb=kb, gk=gk, item=item, nmm=nmm: e.matmul(
                                    ot[:, item * 128:(item + 1) * 128], lhsT=VG[:, gk, half, :],
                                    rhs=PT[pts[half]][:, (2 * qi + kb) * 128:(2 * qi + kb + 1) * 128],
                                    start=(nmm == 0), stop=(kb == 1), skip_group_check=True),
                                    reads=[("vg", gk), ("vgb", gk), "pt%d" % pts[half]], writes=["bank%d" % B_OT], prio=P + 1.01)
                                nmm += 1
                    tok0 = qblks[0][0] + d * 128 * qblks[0][1]
                    qstride = 128 if d == 1 else 1
                    otv = ot.rearrange("p (h q i) -> p h q i", h=2, q=2)
                    if sink_heads is None:
                        accv = bass.AP(ACC.tensor, ACC.offset + tok0, [list(ACC.ap[0]), [S, 2], [qstride, 2], [d, 128]])
                        if acc_mode == "copy":
                            sch.op("act", lambda e: e.activation(out=accv, in_=otv, func=AF.Copy), reads=["bank%d" % B_OT],
                                   writes=[("acc", n2) for n2 in acc_tiles(tok0, d)], prio=P + 1.03)
                        else:
                            sch.op("dve", lambda e: e.tensor_tensor(out=accv, in0=otv, in1=accv, op=ALU.add), reads=["bank%d" % B_OT],
                                   writes=[("acc", n2) for n2 in acc_tiles(tok0, d)], prio=P + 1.03)
                    else:
                        geo = []
                        for half in range(2):
                            num = slice(0, 64) if half == 0 else slice(64, 128)
                            den = slice(64, 128) if half == 0 else slice(0, 64)
                            geo.append((half, num, den, slice(256 * half, 256 * half + 256), sink_heads[half]))
                        for (half, num, den, cols, hsink) in geo:
                            sch.op("act", lambda e, half=half, num=num, den=den, cols=cols, hsink=hsink: e.activation(
                                out=RD[half][num, 0:256], in_=ot[den, cols], func=AF.Ln, bias=ESINK[num, hsink:hsink + 1]),
                                reads=["bank%d" % B_OT, "esink"], writes=["rd%d" % half, "ot_act_done"], prio=P + 1.03)
                        for (half, num, den, cols, hsink) in geo:
                            sch.op("act", lambda e, half=half, num=num: e.activation(out=RD[half][num, 0:256], in_=RD[half][num, 0:256],
                                                                                     func=AF.Exp, scale=-1.0),
                                   reads=["rd%d" % half], writes=["rd%d" % half], prio=P + 1.05)
                        for (half, num, den, cols, hsink) in geo:
                            sch.op("dve", lambda e, half=half, num=num, cols=cols: e.tensor_tensor(
                                out=out_fchunk_ap[num, tok0:tok0 + 256], in0=ot[num, cols], in1=RD[half][num, 0:256], op=ALU.mult),
                                reads=["bank%d" % B_OT, "rd%d" % half, "ot_act_done"], writes=[("ob", name, tok0)], prio=P + 1.07)

                round_prio = {}
                cluster = {}
                for n in range(16):
                    cready = max((r + d * (128 * j + 127)) // 512 for (r, j) in round_blocks(n))
                    cluster.setdefault(cready, []).append(n)
                cl = sorted(cluster)
                for ci, c in enumerate(cl):
                    span = ((cl[ci + 1] - c) if ci + 1 < len(cl) else 1) * nq
                    for k, n in enumerate(cluster[c]):
                        round_prio[n] = (c * nq + nq - 1) + 3.3 + k * max(0.5, min(1.0, float(span) / len(cluster[c])))
                assert len(round_prio) == 16
                first_round_of_block = {}
                for n in sorted(range(16), key=lambda n: (round_prio[n], n)):
                    for (r, j) in round_blocks(n):
                        for jj in (j - 1, j):
                            if jj >= 0:
                                first_round_of_block.setdefault(r * bpr + jj, n)
                for c in range(NCH):
                    proj_qk(c * nq, c, 0)
                    if kcol is not None:
                        proj_qk(c * nq + 1, c, 1)
                if kcol is not None:
                    if d == 1:
                        batches = [[4 * m + s_ for s_ in range(4)] for m in range(8)]
                    else:
                        batches = [[(4 * m + s_) * bpr + j for s_ in range(4)] for j in range(bpr) for m in range(d // 4)]
                    bprio = sorted((min(round_prio[first_round_of_block[gb]] for gb in gbs) - 0.9, gbs) for gbs in batches)
                    lastp = None
                    for (pb, gbs) in bprio:
                        if lastp is not None and pb < lastp + 0.1:
                            pb = lastp + 0.1
                        lastp = pb
                        proj_v_batch(gbs, pb)
                for n in sorted(range(16), key=lambda n: (round_prio[n], n)):
                    attn_round(n, round_prio[n])

                if acc_mode == "final":
                    for c in range(NCH):
                        cs = slice(c * 512, (c + 1) * 512)
                        for half in range(2):
                            rb = cnt["rd"] % 2
                            cnt["rd"] += 1
                            num = slice(0, 64) if half == 0 else slice(64, 128)
                            den = slice(64, 128) if half == 0 else slice(0, 64)
                            P = max(round_prio.values()) + 1.2 + 0.2 * (2 * c + half)
                            sch.op("act", lambda e, rb=rb, den=den, num=num, cs=cs, half=half: e.activation(
                                out=RD[rb][num, :], in_=ACC[den, half, cs], func=AF.Ln), reads=[("acc", c)], writes=["rd%d" % rb], prio=P)
                            sch.op("act", lambda e, rb=rb, num=num: e.activation(out=RD[rb][num, :], in_=RD[rb][num, :], func=AF.Exp, scale=-1.0),
                                   reads=["rd%d" % rb], writes=["rd%d" % rb], prio=P + 0.1)
                            sch.op("pool", lambda e, rb=rb, num=num, cs=cs, half=half: e.tensor_tensor(
                                out=out_fchunk_ap[num, cs], in0=ACC[num, half, cs], in1=RD[rb][num, :], op=ALU.mult),
                                reads=[("acc", c), "rd%d" % rb], writes=[("oa", name, c)], prio=P + 0.2)
                return max(round_prio.values()), 8 * nq

            sch.begin_defer()
            pbase = 0.0
            for pi, pd in enumerate(PASSES):
                sch.base_prio = pbase
                last_round, nsteps = attention_pass(pd, pi % 2, PASSES[pi + 1] if pi + 1 < len(PASSES) else None)
                pbase += max(float(nsteps), last_round - 2.0) + (3.4 if pd["mode"] == "final" else 0.0)
                if pd.get("barrier_after"):
                    sch.base_prio = 0.0
                    sch.flush()
                    sch.barrier()
                    checkpoint("A1")
                    sch.begin_defer()
            sch.base_prio = 0.0
            sch.flush()
            if DEBUG:
                sch.dma("sp", "dbg", lambda e: e.dma_start(out=dbg["dbg_qk"][:, 0:S], in_=QT), reads=[("qt", g) for g in range(NT)])
                sch.dma("sp", "dbg", lambda e: e.dma_start(out=dbg["dbg_qk"][:, S:2 * S], in_=KT), reads=[("kt", g) for g in range(NT)])
            sch.barrier()
            if DEBUG:
                sch.dma("sp", "dbg", lambda e: e.dma_start(out=dbg["dbg_oaT"], in_=oaT.rearrange("p k t -> p (k t)")))
                sch.dma("sp", "dbg", lambda e: e.dma_start(out=dbg["dbg_obT"], in_=obT.rearrange("p k t -> p (k t)")))

            checkpoint("A")
            w0 = R_T
            WG = mem.ap(BF16, w0, [8, 2048]); w0 += 32 * KB
            WA = mem.ap(BF16, w0, [2, D]); w0 += 4 * KB
            WB = mem.ap(BF16, w0, [4, D]); w0 += 8 * KB
            WO = mem.ap(BF16, w0, [8, D]); w0 += 16 * KB
            TA = [mem.ap(BF16, w0 + i * KB, [512]) for i in range(2)]; w0 += 2 * KB
            TB = [mem.ap(BF16, w0 + i * KB, [512]) for i in range(2)]; w0 += 2 * KB
            UU = [mem.ap(F32, w0 + i * 2 * KB, [512]) for i in range(2)]; w0 += 4 * KB
            VV = [mem.ap(F32, w0 + i * 2 * KB, [512]) for i in range(2)]; w0 += 4 * KB
            MIX = [mem.ap(BF16, w0, [8, 512]) for i in range(2)]; w0 += 8 * KB
            X5 = [mem.ap(F32, w0 + i * 4 * KB, [D]) for i in range(2)]; w0 += 8 * KB
            assert w0 <= R_END
            def wg_load(ab, q):
                lo = ab * 1024 + q * 256
                src = win_d[:, OFF_GA + lo:OFF_GA + lo + 256].rearrange("(k p) n -> p k n", p=128)
                sch.dma("pool", "wg%d_%d" % (ab, q), lambda e: e.dma_start(out=WG[:, :, lo:lo + 256], in_=src), writes=[("wg", ab, q)])
            wg_load(0, 0)
            wg_load(1, 0)
            sch.dma("pool", "wab", lambda e: e.dma_start(out=WA, in_=wa_d.rearrange("(k p) n -> p k n", p=128)), writes=["wab"])
            sch.dma("pool", "wab", lambda e: e.dma_start(out=WB, in_=wb_d.rearrange("(k p) n -> p k n", p=128)), writes=["wab"])
            wg_load(0, 1)
            wg_load(1, 1)
            sch.dma("pool", "wo", lambda e: e.dma_start(out=WO, in_=wo_d.rearrange("(k p) n -> p k n", p=128)), writes=["wo"])
            for q in (2, 3):
                wg_load(0, q)
                wg_load(1, q)
            B_GA, B_GB, B_YA, B_YB, B_O = (0, 1), (2, 3), 4, 5, (6, 7)
            WU_early = mem.ap(BF16, R_H, [8, DFF])
            n5 = {"g": 0, "o": 0, "x": 0}
            mix = MIX[0]
            mixn = "mix0"

            def p5_merge(c, m):
                P = 10.0 * c + m
                cs = slice(c * 512, (c + 1) * 512)
                gi = n5["g"] % 2
                n5["g"] += 1
                bga, bgb = B_GA[gi], B_GB[gi]

                def gate_mm(bk, ab):
                    coff = ab * 1024 + m * 128
                    for k in range(8):
                        sch.op("pe", lambda e, k=k: e.matmul(bank_f32(bk), lhsT=WG[:, k, coff:coff + 128], rhs=hT[:, k, cs],
                                                             start=(k == 0), stop=(k == 7)),
                               reads=[("wg", ab, m // 2), ("hT5", c)], writes=["bank%d" % bk], prio=P)
                gate_mm(bga, 0)
                gate_mm(bgb, 1)
                for k in range(2):
                    sch.op("pe", lambda e, k=k: e.matmul(bank_f32(B_YA), lhsT=WA[:, k, m * 128:(m + 1) * 128], rhs=oaT[:, k, cs],
                                                         start=(k == 0), stop=(k == 1)), reads=["wab"], writes=["bank%d" % B_YA], prio=P)
                for k in range(4):
                    sch.op("pe", lambda e, k=k: e.matmul(bank_f32(B_YB), lhsT=WB[:, k, m * 128:(m + 1) * 128], rhs=obT[:, k, cs],
                                                         start=(k == 0), stop=(k == 3)), reads=["wab"], writes=["bank%d" % B_YB], prio=P)
                sch.op("act", lambda e: e.activation(out=TA[gi], in_=bank_f32(bga), func=AF.Tanh, scale=0.5),
                       reads=["bank%d" % bga], writes=["ta%d" % gi], prio=P + 0.3)
                sch.op("act", lambda e: e.activation(out=TB[gi], in_=bank_f32(bgb), func=AF.Tanh, scale=0.5),
                       reads=["bank%d" % bgb], writes=["tb%d" % gi], prio=P + 0.32)
                sch.op("dve", lambda e: e.scalar_tensor_tensor(out=UU[gi], in0=TA[gi], scalar=1.0, in1=bank_f32(B_YA), op0=ALU.add, op1=ALU.mult),
                       reads=["ta%d" % gi, "bank%d" % B_YA], writes=["uu%d" % gi], prio=P + 0.5)
                sch.op("dve", lambda e: e.scalar_tensor_tensor(out=VV[gi], in0=TB[gi], scalar=1.0, in1=bank_f32(B_YB), op0=ALU.add, op1=ALU.mult),
                       reads=["tb%d" % gi, "bank%d" % B_YB], writes=["vv%d" % gi], prio=P + 0.52)
                sch.op("pool", lambda e: e.tensor_tensor(out=mix[:, m, :], in0=UU[gi], in1=VV[gi], op=ALU.add),
                       reads=["uu%d" % gi, "vv%d" % gi], writes=[(mixn, m)], prio=10.0 * c + max(m + 0.7, 1.75))

            def p5_out(c, tt):
                P = 10.0 * c + 11.2 + 0.1 * tt
                t = 4 * c + tt
                xi = n5["x"] % 2
                n5["x"] += 1
                xt = X5[xi]
                sch.dma("sp", "x5l%d" % xi, lambda e: e.dma_start(out=xt, in_=x_d[t * 128:(t + 1) * 128, :]), writes=["x5_%d" % xi],
                        prio=(P - 4.0) if tt < 2 else (P - 0.11))

                def half(hf):
                    bo = B_O[n5["o"] % 2]
                    n5["o"] += 1
                    for k in range(8):
                        sch.op("pe", lambda e, k=k: e.matmul(bank_f32(bo), lhsT=mix[:, k, tt * 128:(tt + 1) * 128],
                                                             rhs=WO[:, k, hf * 512:(hf + 1) * 512], start=(k == 0), stop=(k == 7)),
                               reads=["wo"] + [(mixn, mm) for mm in range(8)], writes=["bank%d" % bo], prio=P + 0.01 * hf)
                    sch.op("dve", lambda e: e.scalar_tensor_tensor(
                        out=xt[:, hf * 512:(hf + 1) * 512], in0=bank_f32(bo), scalar=0.5, in1=xt[:, hf * 512:(hf + 1) * 512],
                        op0=ALU.mult, op1=ALU.add), reads=["bank%d" % bo, "x5_%d" % xi], writes=["x5_%d" % xi], prio=P + 0.05 + 0.01 * hf)
                half(0)
                half(1)
                sch.dma("sp", "x5s%d" % xi, lambda e: e.dma_start(out=x1_d[t * 128:(t + 1) * 128, :], in_=xt),
                        reads=["x5_%d" % xi], writes=[("x1", t)], prio=P + 0.08)

            sch.begin_defer()
            for c in range(NCH):
                for m in range(8):
                    p5_merge(c, m)
                wsrc = wu_d[:, c * 512:(c + 1) * 512].rearrange("(k p) n -> p k n", p=128)
                sch.dma("pool", "wu%d" % c, lambda e, c=c, wsrc=wsrc: e.dma_start(out=WU_early[:, :, c * 512:(c + 1) * 512], in_=wsrc),
                        writes=[("wu", c), ("hT5", c)], prio=10.0 * c + 8.5)
                for tt in range(4):
                    p5_out(c, tt)
            sch.flush()
            sch.barrier()

            checkpoint("P5")
            w0 = 7 * KB
            WU = mem.ap(BF16, w0, [8, DFF]); w0 += 64 * KB
            WD = mem.ap(BF16, w0, [32, D]); w0 += 64 * KB
            AT = mem.ap(BF16, w0, [32, 512]); w0 += 32 * KB
            H2T = mem.ap(BF16, w0, [8, 512]); w0 += 8 * KB
            X6 = [mem.ap(F32, w0 + i * 4 * KB, [D]) for i in range(5)]; w0 += 20 * KB
            GB2 = mem.ap(F32, w0, [D]); w0 += 4 * KB
            HB6 = [mem.ap(BF16, w0 + i * 2 * KB, [D]) for i in range(2)]; w0 += 4 * KB
            RR = [mem.ap(F32, w0 + i * 2 * KB, [512]) for i in range(2)]; w0 += 4 * KB
            SS6 = mem.ap(F32, 6 * KB + 256, [NT])
            assert w0 <= R_END, w0
            g2_b = bass.AP(ln2_d.tensor, 0, [[0, 128], [1, D]])
            sch.dma("sp", "gb", lambda e: e.dma_start(out=GB2, in_=g2_b), writes=["gb"])
            for piece in range(8):
                src = wd_d[piece * 512:(piece + 1) * 512, :].rearrange("(k p) n -> p k n", p=128)
                sch.dma("pool", "wd%d" % piece, lambda e, piece=piece, src=src: e.dma_start(out=WD[:, piece * 4:(piece + 1) * 4, :], in_=src),
                        writes=[("wd", piece)])
            B_TP, B_U, B_D = (0, 1), (2, 3, 4), (5, 6, 7)
            n6 = {"x": 0, "u": 0, "d": 0, "r": 0}
            NSL, FSL = X6[0:3], X6[3:5]
            sch.begin_defer()

            def p6_prenorm(tb, P0):
                for tt in range(4):
                    t = 4 * tb + tt
                    xi = n6["x"] % 3
                    n6["x"] += 1
                    hbi = t % 2
                    prenorm_tile(x1_d[t * 128:(t + 1) * 128, :], NSL[xi], "x6n_%d" % xi, "x6nl%d" % xi, GB2, HB6[hbi], "hb6_%d" % hbi, HB6[hbi],
                                 SS6[:, t:t + 1], "ss6_%d" % t, B_TP[t % 2], H2T[:, :, tt * 128:(tt + 1) * 128], [("h2t", tt)],
                                 junk_name="hb6_%d" % hbi, P=P0 + 10.0 * tt, dma_off=-6.0, tr_off=1.5, cp_off=3.5)

            def p6_up(f, P):
                bu = B_U[n6["u"] % 3]
                n6["u"] += 1
                ri = n6["r"] % 2
                n6["r"] += 1
                for k in range(8):
                    sch.op("pe", lambda e, k=k: e.matmul(bank_f32(bu), lhsT=WU[:, k, f * 128:(f + 1) * 128], rhs=H2T[:, k, :],
                                                         start=(k == 0), stop=(k == 7)),
                           reads=[("wu", f // 4)] + [("h2t", tt) for tt in range(4)], writes=["bank%d" % bu], prio=P)
                sch.op("act", lambda e: e.activation(out=RR[ri], in_=bank_f32(bu), func=AF.Relu), reads=["bank%d" % bu], writes=["rr%d" % ri],
                       prio=P + 0.3)
                sch.op("dve", lambda e: e.tensor_tensor(out=AT[:, f, :], in0=RR[ri], in1=RR[ri], op=ALU.mult),
                       reads=["rr%d" % ri], writes=[("at", f)], prio=P + 0.6)

            def p6_down(t, tt, B):
                fi = t % 2
                xt = FSL[fi]
                sch.dma("sp", "x6fl%d" % fi, lambda e: e.dma_start(out=xt, in_=x1_d[t * 128:(t + 1) * 128, :]), writes=["x6f_%d" % fi],
                        prio=B + 40 + 10 * tt - 9)

                def half(hf):
                    g = 2 * tt + hf
                    bd = B_D[n6["d"] % 3]
                    n6["d"] += 1
                    for f in range(32):
                        sch.op("pe", lambda e, f=f: e.matmul(bank_f32(bd), lhsT=AT[:, f, tt * 128:(tt + 1) * 128],
                                                             rhs=WD[:, f, hf * 512:(hf + 1) * 512], start=(f == 0), stop=(f == 31)),
                               reads=[("wd", f // 4), ("at", f)], writes=["bank%d" % bd], prio=B + 40 + 5 * g)
                    sch.op("dve", lambda e: e.tensor_tensor(out=xt[:, hf * 512:(hf + 1) * 512], in0=bank_f32(bd),
                                                            in1=xt[:, hf * 512:(hf + 1) * 512], op=ALU.add),
                           reads=["bank%d" % bd, "x6f_%d" % fi], writes=["x6f_%d" % fi], prio=B + 40 + 5 * g + 4.5)
                half(0)
                half(1)
                sch.dma("sp", "x6fs%d" % fi, lambda e: e.dma_start(out=out_d[t * 128:(t + 1) * 128, :], in_=xt),
                        reads=["x6f_%d" % fi], writes=[("out", t)], prio=B + 40 + 5 * (2 * tt + 1) + 4.6)

            p6_prenorm(0, -50.0)
            for tb in range(NCH):
                B = 100.0 * tb
                for f in range(32):
                    p6_up(f, B + f)
                if tb + 1 < NCH:
                    p6_prenorm(tb + 1, B + 41.0)
                for tt in range(4):
                    p6_down(4 * tb + tt, tt, B)
            sch.flush()

        try:
            emit_all()
        except _Stop:
            sch.barrier()
        sch.final_wait("sp", ["x6fs%d" % i for i in range(2)] + (["dbg"] if DEBUG else []))

        sch.finalize()
        block = es.enter_context(nc.Block())

        @block.sync
        def _(e):
            sch.replay("sp", e)

        @block.gpsimd
        def _(e):
            sch.replay("pool", e)

        @block.scalar
        def _(e):
            sch.replay("act", e)

        @block.vector
        def _(e):
            sch.replay("dve", e)

        @block.tensor
        def _(e):
            sch.replay("pe", e)
    return nc


_CACHE = {}


def kernel(x, positions, ln1_g, w_in, q_norm_a, k_norm_a, q_norm_b, k_norm_b, sinks,
           w_branch_a, w_branch_b, w_out, ln2_g, w_up, w_down):
    if "nc" not in _CACHE:
        _CACHE["nc"] = build_program()
    nc = _CACHE["nc"]
    cst = host_consts()
    f32 = lambda a: np.ascontiguousarray(np.asarray(a), dtype=np.float32)
    shared = {
        "cst": cst,
        "ln1_g": f32(ln1_g), "ln2_g": f32(ln2_g), "w_in": f32(w_in)[0],
        "q_norm_a": f32(q_norm_a), "k_norm_a": f32(k_norm_a), "q_norm_b": f32(q_norm_b), "k_norm_b": f32(k_norm_b),
        "sinks": f32(sinks), "w_branch_a": f32(w_branch_a)[0], "w_branch_b": f32(w_branch_b)[0],
        "w_out": f32(w_out)[0], "w_up": f32(w_up)[0], "w_down": f32(w_down)[0],
    }
    xs = f32(x)
    ps = np.ascontiguousarray(np.asarray(positions), dtype=np.int32)
    in_maps = []
    for b in range(8):
        m = dict(shared)
        m["x"] = xs[b]
        m["pos"] = ps[b:b + 1]
        in_maps.append(m)
    res = run_bass_kernel_spmd(nc, in_maps, core_ids=list(range(8)))
    _CACHE["last"] = res
    out = np.stack([np.asarray(r["out"], dtype=np.float32) for r in res.results], axis=0)
    return out
```

```python
import math
from contextlib import ExitStack

import numpy as np
import concourse.bass as bass
import concourse.mybir as mybir
from concourse.bass_utils import run_bass_kernel_spmd

F32 = mybir.dt.float32
BF16 = mybir.dt.bfloat16
I32 = mybir.dt.int32
AF = mybir.ActivationFunctionType
ALU = mybir.AluOpType

S = 4096
D = 1024
DFF = 4096
NCH = 8
NT = 32
EPS = 1e-6
ARENA_ELEMS = 105984

OFF_QA, OFF_KA, OFF_VA, OFF_QB, OFF_KB, OFF_VB, OFF_GA, OFF_GB = 0, 768, 1536, 2304, 2816, 2944, 3072, 4096

C_IDENT, C_BONES, C_PERM, C_MASK = 0, 128, 256, 384
C_BF_COLS = 384 + 5 * 512
C_INVF = C_BF_COLS
C_F32_COLS = 8
CST_COLS = C_BF_COLS + C_F32_COLS

DEBUG = False


def host_consts():
    c = np.zeros((128, CST_COLS), np.float32)
    c[:, C_IDENT:C_IDENT + 128] = np.eye(128, dtype=np.float32)
    bo = np.zeros((128, 128), np.float32)
    bo[0:64, 0:64] = 1.0
    bo[64:128, 64:128] = 1.0
    c[:, C_BONES:C_BONES + 128] = bo
    pm = np.zeros((128, 128), np.float32)
    for hb in (0, 64):
        for i in range(8):
            pm[hb + i + 8, hb + i] = -1.0
            pm[hb + i, hb + i + 8] = 1.0
    c[:, C_PERM:C_PERM + 128] = pm
    k = np.arange(128)[:, None]
    q = np.arange(128)[None, :]
    diag = (k <= q).astype(np.float32)
    prev_g = (k >= q).astype(np.float32)
    prev_b = (k > q).astype(np.float32)
    zero = np.zeros((128, 128), np.float32)
    masks = [
        np.concatenate([zero, diag, prev_g, diag], axis=1),
        np.concatenate([prev_g, diag, prev_g, diag], axis=1),
        np.concatenate([zero, diag, prev_b, diag], axis=1),
        np.concatenate([prev_b, diag, prev_b, diag], axis=1),
        np.concatenate([zero, diag, zero, diag], axis=1),
    ]
    for i, m in enumerate(masks):
        c[:, C_MASK + 512 * i:C_MASK + 512 * (i + 1)] = m
    inv_freq = (500000.0 ** (-np.arange(0, 16, 2, dtype=np.float32) / 16.0)).astype(np.float32)
    invf = np.zeros(128, np.float32)
    for p in range(128):
        if p % 64 < 16:
            invf[p] = inv_freq[(p % 64) % 8]
    c[:, C_INVF] = invf
    c[:, C_INVF + 1] = EPS
    return c


class Sched:
    ENGS = ("pe", "act", "dve", "pool", "sp")

    def __init__(self, nc, es):
        self.nc = nc
        self.es = es
        self.q = {e: [] for e in self.ENGS}
        self.res = {}
        self.sem = {e: es.enter_context(nc.semaphore("s_" + e)) for e in ("pe", "act", "dve", "pool")}
        self.dsem = {}
        self.dcnt = {}
        self.defer = None
        self.base_prio = 0.0

    def _dma_sem(self, name):
        if name not in self.dsem:
            self.dsem[name] = self.es.enter_context(self.nc.semaphore("d_" + name))
            self.dcnt[name] = 0
        return self.dsem[name]

    def _deps(self, reads, writes):
        deps = set()
        for r in reads:
            st = self.res.get(r)
            if st and st["w"] is not None:
                deps.add(st["w"])
        for w in writes:
            st = self.res.get(w)
            if st:
                if st["w"] is not None:
                    deps.add(st["w"])
                for d in st["r"]:
                    deps.add(d)
        return deps

    def _commit(self, me, reads, writes):
        for r in reads:
            st = self.res.setdefault(r, {"w": None, "r": []})
            st["r"] = [d for d in st["r"] if d[0] != me[0]] + [me]
        for w in writes:
            self.res[w] = {"w": me, "r": []}

    def begin_defer(self):
        self.defer = []

    def flush(self):
        lastw = {}
        expect = []
        for it in self.defer:
            expect.append({r: lastw.get(r) for r in it[6]})
            for w in it[7]:
                lastw[w] = it[1]
        items = sorted(self.defer, key=lambda x: (x[0], x[1]))
        wnow = {}
        for it in items:
            for r, v in expect[it[1]].items():
                if wnow.get(r) != v:
                    raise RuntimeError("priority order breaks producer of %r at prio %s (%s): expected op %s, saw %s"
                                       % (r, it[0], it[3], v, wnow.get(r)))
            for w in it[7]:
                wnow[w] = it[1]
        self.defer = None
        for (_, _, kind, eng, semname, fn, reads, writes) in items:
            if kind == "op":
                self.op(eng, fn, reads, writes)
            else:
                self.dma(eng, semname, fn, reads, writes)

    def op(self, eng, fn, reads=(), writes=(), prio=None):
        if getattr(self, "defer", None) is not None:
            self.defer.append((self.base_prio + (prio or 0.0), len(self.defer), "op", eng, None, fn, tuple(reads), tuple(writes)))
            return None
        deps = self._deps(reads, writes)
        idx = len(self.q[eng])
        self.q[eng].append({"fn": fn, "deps": deps, "kind": "op", "marked": False})
        self._commit((eng, idx), reads, writes)
        return (eng, idx)

    def dma(self, eng, semname, fn, reads=(), writes=(), prio=None):
        if getattr(self, "defer", None) is not None:
            self.defer.append((self.base_prio + (prio or 0.0), len(self.defer), "dma", eng, semname, fn, tuple(reads), tuple(writes)))
            return None
        self._dma_sem(semname)
        deps = self._deps(reads, writes)
        self.dcnt[semname] += 1
        me = ("dma:" + semname, self.dcnt[semname])
        self.q[eng].append({"fn": fn, "deps": deps, "kind": "dma", "sem": semname})
        self._commit(me, reads, writes)
        return me

    def barrier(self):
        deps = set()
        for e in ("pe", "act", "dve", "pool"):
            for i in range(len(self.q[e]) - 1, -1, -1):
                if self.q[e][i]["kind"] == "op":
                    deps.add((e, i))
                    break
        for name, cnt in self.dcnt.items():
            if cnt:
                deps.add(("dma:" + name, cnt))
        for e in self.ENGS:
            self.q[e].append({"fn": None, "deps": set(deps), "kind": "bar"})
        self.res = {}

    def final_wait(self, eng, semnames):
        deps = set(("dma:" + n, self.dcnt[n]) for n in semnames if self.dcnt.get(n))
        self.q[eng].append({"fn": None, "deps": deps, "kind": "bar"})

    def finalize(self):
        for e in self.ENGS:
            for ins in self.q[e]:
                for (dom, idx) in ins["deps"]:
                    if not dom.startswith("dma:"):
                        if dom == "pe" and e == "pe":
                            continue
                        self.q[dom][idx]["marked"] = True
        self.ordinal = {}
        for e in ("pe", "act", "dve", "pool"):
            n = 0
            for i, ins in enumerate(self.q[e]):
                if ins.get("marked"):
                    n += 1
                    self.ordinal[(e, i)] = n
        self.total_incs = n

    def replay(self, eng, eobj):
        seen = {}
        for ins in self.q[eng]:
            need = {}
            for (dom, idx) in ins["deps"]:
                if dom.startswith("dma:"):
                    val = 16 * idx
                else:
                    if dom == "pe" and eng == "pe":
                        continue
                    val = self.ordinal[(dom, idx)]
                if val > need.get(dom, 0):
                    need[dom] = val
            for dom, val in need.items():
                if seen.get(dom, 0) >= val:
                    continue
                seen[dom] = val
                sem = self.dsem[dom[4:]] if dom.startswith("dma:") else self.sem[dom]
                eobj.wait_ge(sem, val)
            if ins["fn"] is None:
                continue
            bi = ins["fn"](eobj)
            if ins["kind"] == "dma":
                bi.then_inc(self.dsem[ins["sem"]], 16)
            elif ins.get("marked"):
                bi.then_inc(self.sem[eng], 1)


class Mem:
    def __init__(self, arena):
        self.h = {BF16: arena, F32: arena.bitcast(F32), I32: arena.bitcast(I32)}
        self.pstep = {BF16: ARENA_ELEMS, F32: ARENA_ELEMS // 2, I32: ARENA_ELEMS // 2}

    def ap(self, dt, byte_off, shape, parts=128, p0=0):
        esz = 2 if dt == BF16 else 4
        assert byte_off % esz == 0
        dims = [[self.pstep[dt], parts]]
        stride = 1
        rev = []
        for n in reversed(shape):
            rev.append([stride, n])
            stride *= n
        dims += list(reversed(rev))
        assert byte_off + stride * esz <= ARENA_ELEMS * 2, (byte_off, stride, esz)
        return bass.AP(self.h[dt], p0 * self.pstep[dt] + byte_off // esz, dims)


KB = 1024


def build_program():
    nc = bass.Bass("TRN2", target_bir_lowering=False)
    dr = {}

    def din(name, shape, dt=F32):
        dr[name] = nc.dram_tensor(name, shape, dt, kind="ExternalInput")
        return dr[name].ap()

    x_d = din("x", [S, D])
    pos_d = din("pos", [1, S], I32)
    cst_d = din("cst", [128, CST_COLS])
    ln1_d = din("ln1_g", [1, D])
    ln2_d = din("ln2_g", [1, D])
    win_d = din("w_in", [D, 5120])
    qna_d = din("q_norm_a", [1, 64])
    kna_d = din("k_norm_a", [1, 64])
    qnb_d = din("q_norm_b", [1, 64])
    knb_d = din("k_norm_b", [1, 64])
    snk_d = din("sinks", [1, 8])
    wa_d = din("w_branch_a", [256, D])
    wb_d = din("w_branch_b", [512, D])
    wo_d = din("w_out", [D, D])
    wu_d = din("w_up", [D, DFF])
    wd_d = din("w_down", [DFF, D])
    out_h = nc.dram_tensor("out", [S, D], F32, kind="ExternalOutput")
    out_d = out_h.ap()
    x1_h = nc.dram_tensor("x1_scratch", [S, D], F32, kind="Internal")
    x1_d = x1_h.ap()
    dbg = {}
    if DEBUG:
        for name, shape, dt in (("dbg_hT", [128, 8 * S], BF16), ("dbg_oaT", [128, 2 * S], BF16),
                                ("dbg_obT", [128, 4 * S], BF16), ("dbg_tab", [128, 2 * S], BF16),
                                ("dbg_qk", [128, 2 * S], BF16)):
            dbg[name] = nc.dram_tensor(name, shape, dt, kind="ExternalOutput").ap()

    with ExitStack() as es:
        arena = es.enter_context(nc.sbuf_tensor("arena", [128, ARENA_ELEMS], BF16))
        mem = Mem(arena)
        banks = [es.enter_context(nc.psum_tensor("bank%d" % i, [128, 512], F32)) for i in range(8)]
        sch = Sched(nc, es)
        import os as _os
        _stop = _os.environ.get("KSTOP", "")

        class _Stop(Exception):
            pass

        def checkpoint(name):
            if _stop == name:
                raise _Stop()

        def bank_f32(i):
            return banks[i][:, :]

        def bank_bf16(i):
            return banks[i][:, :].bitcast(BF16)

        R_H_START = 7 * KB
        o = 0
        IDENT = mem.ap(BF16, o, [128]); o += 256
        BONES = mem.ap(BF16, o, [128]); o += 256
        PERM = mem.ap(BF16, o, [128]); o += 256
        MASKS = mem.ap(BF16, o, [5, 512]); o += 5120
        CF32 = mem.ap(F32, o, [C_F32_COLS]); o += 4 * C_F32_COLS
        GAINS = mem.ap(F32, o, [4]); o += 16
        ESINK = mem.ap(F32, o, [8]); o += 32
        o = (o + 63) // 64 * 64
        assert o <= 6 * KB + 1024
        assert o <= R_H_START
        R_H = 7 * KB
        R_O = 71 * KB
        R_T = 119 * KB
        R_W = 135 * KB
        R_END = ARENA_ELEMS * 2
        hT = mem.ap(BF16, R_H, [8, S])
        TC = mem.ap(BF16, R_T, [S])
        TS = mem.ap(BF16, R_T + 8 * KB, [S])
        oaT = mem.ap(BF16, R_O, [2, S])
        obT = mem.ap(BF16, R_O + 16 * KB, [4, S])
        INVF = CF32[:, 0:1]
        EPSC = CF32[:, 1:2]

        def emit_all():
            cbf = mem.ap(BF16, 0, [C_BF_COLS])
            sch.dma("pool", "cstb", lambda e: e.dma_start(out=cbf, in_=cst_d[:, 0:C_BF_COLS]), writes=["consts"])
            sch.dma("sp", "cst", lambda e: e.dma_start(out=CF32, in_=cst_d[:, C_BF_COLS:CST_COLS]), writes=["consts"])
            for gi, gd in enumerate((qna_d, kna_d, qnb_d, knb_d)):
                for hb in (0, 64):
                    src = bass.AP(gd.tensor, 0, [[1, 64], [1, 1]])
                    sch.dma("sp", "cst", lambda e, gi=gi, hb=hb, src=src: e.dma_start(out=GAINS[hb:hb + 64, gi:gi + 1], in_=src),
                            writes=["consts"])
            snk_b = bass.AP(snk_d.tensor, 0, [[0, 128], [1, 8]])
            sch.dma("sp", "cst", lambda e: e.dma_start(out=ESINK, in_=snk_b), writes=["consts"])
            sch.op("act", lambda e: e.activation(out=ESINK, in_=ESINK, func=AF.Exp), reads=["consts"], writes=["esink"])

            WP = [mem.ap(BF16, R_W + i * 6 * KB, [8, 384]) for i in range(2)]
            PASSES = []
            for sp in range(2):
                for (g, d, mode) in ((2, 16, "copy"), (1, 4, "add"), (0, 1, "final")):
                    PASSES.append(dict(name="g%d_%d" % (g, sp), d=d, qcol=OFF_QA + g * 256 + sp * 128, kcol=OFF_KA + g * 256 + sp * 128,
                                       vcol=OFF_VA + g * 256 + sp * 128, vdup=False, gq=0, gk=1, mb=0, mode=mode, sinks=None,
                                       out=oaT[:, sp, :], barrier_after=(sp == 1 and mode == "final")))
            for kv in range(2):
                for f in range(2):
                    heads = (4 * kv + 2 * f, 4 * kv + 2 * f + 1)
                    PASSES.append(dict(name="b%d_%d" % (kv, f), d=1, qcol=OFF_QB + heads[0] * 64,
                                       kcol=(OFF_KB + kv * 64) if f == 0 else None, vcol=(OFF_VB + kv * 64) if f == 0 else None,
                                       vdup=True, gq=2, gk=3, mb=2, mode="none", sinks=heads, out=obT[:, 2 * kv + f, :]))
            def emit_wload(pd, slot, wprio=-1.0):
                W = WP[slot]
                wname = "wp%d" % slot

                def wload(dst_lo, src_lo, n):
                    src = win_d[:, src_lo:src_lo + n].rearrange("(k p) n -> p k n", p=128)
                    sch.dma("pool", wname, lambda e: e.dma_start(out=W[:, :, dst_lo:dst_lo + n], in_=src), writes=[wname], prio=wprio)
                wload(0, pd["qcol"], 128)
                if pd["kcol"] is not None:
                    if pd["vdup"]:
                        wload(128, pd["kcol"], 64)
                        wload(192, pd["kcol"], 64)
                        wload(256, OFF_VB, 128)
                    else:
                        wload(128, pd["kcol"], 128)
                        wload(256, pd["vcol"], 128)

            tA = mem.ap(F32, R_O, [S])
            tAi = mem.ap(I32, R_O, [S])
            tB = mem.ap(F32, R_O + 16 * KB, [S])
            tBi = mem.ap(I32, R_O + 16 * KB, [S])
            tM = mem.ap(F32, R_O + 32 * KB, [S])
            sch.begin_defer()
            _tk = [0]

            def _tbump():
                sch.base_prio = 0.4 + 2.4 * _tk[0]
                _tk[0] += 1

            pos_b = bass.AP(pos_d.tensor, 0, [[0, 128], [1, S]])
            _tbump()
            sch.dma("sp", "pos", lambda e: e.dma_start(out=tAi, in_=pos_b), writes=["tA"])
            _tbump()
            sch.op("dve", lambda e: e.tensor_copy(out=tA, in_=tAi), reads=["tA"], writes=["tA"])
            _tbump()
            sch.op("dve", lambda e: e.tensor_scalar(out=tA, in0=tA, scalar1=INVF, scalar2=None, op0=ALU.mult),
                   reads=["tA", "consts"], writes=["tA"])
            _tbump()
            sch.op("dve", lambda e: e.tensor_scalar(out=tA, in0=tA, scalar1=float(1.0 / (2 * math.pi)), scalar2=None, op0=ALU.mult),
                   reads=["tA"], writes=["tA"])
            for which, tab in ((0, TS), (1, TC)):
                if which == 1:
                    _tbump()
                    sch.op("dve", lambda e: e.tensor_scalar(out=tA, in0=tA, scalar1=0.25, scalar2=None, op0=ALU.add),
                           reads=["tA"], writes=["tA"])
                _tbump()
                sch.op("dve", lambda e: e.tensor_copy(out=tBi, in_=tA), reads=["tA"], writes=["tB"])
                _tbump()
                sch.op("dve", lambda e: e.tensor_copy(out=tB, in_=tBi), reads=["tB"], writes=["tB"])
                _tbump()
                sch.op("dve", lambda e: e.tensor_tensor(out=tB, in0=tA, in1=tB, op=ALU.subtract), reads=["tA", "tB"], writes=["tB"])
                _tbump()
                sch.op("act", lambda e, tab=tab: e.activation(out=tab, in_=tB, func=AF.Sin, scale=6.283185),
                       reads=["tB"], writes=["tab%d" % which])
            if DEBUG:
                _tbump()
                sch.dma("sp", "dbg", lambda e: e.dma_start(out=dbg["dbg_tab"][:, 0:S], in_=TC), reads=["tab1"])
                _tbump()
                sch.dma("sp", "dbg", lambda e: e.dma_start(out=dbg["dbg_tab"][:, S:2 * S], in_=TS), reads=["tab0"])

            sch.base_prio = 0.0
            checkpoint("T")
            emit_wload(PASSES[0], 0)
            def prenorm_tile(xsrc_ap, xs, xs_name, sem_name, g_b, hb, hb_name, junk, ss_col, ss_name, tp_bank, dst_ap, dst_names,
                             load_eng="sp", junk_name="junk", P=0.0, dma_off=-2.0, tr_off=0.5, cp_off=1.5, copy_eng="act"):
                sch.dma(load_eng, sem_name, lambda e: e.dma_start(out=xs, in_=xsrc_ap), writes=[xs_name], prio=P + dma_off)
                sch.op("act", lambda e: e.activation(out=junk, in_=xs, func=AF.Square, accum_out=ss_col),
                       reads=[xs_name], writes=[junk_name, ss_name], prio=P)
                sch.op("act", lambda e: e.activation(out=ss_col, in_=ss_col, func=AF.Ln, scale=1.0 / D, bias=EPSC),
                       reads=[ss_name, "consts"], writes=[ss_name], prio=P + 0.02)
                sch.op("act", lambda e: e.activation(out=ss_col, in_=ss_col, func=AF.Exp, scale=-0.5),
                       reads=[ss_name], writes=[ss_name], prio=P + 0.04)
                sch.op("dve", lambda e: e.scalar_tensor_tensor(out=hb, in0=xs, scalar=ss_col, in1=g_b, op0=ALU.mult, op1=ALU.mult),
                       reads=[xs_name, ss_name, "gb"], writes=[hb_name], prio=P + 0.06)
                pT = bank_bf16(tp_bank)
                for k in range(8):
                    sch.op("pe", lambda e, k=k: e.transpose(out=pT[:, k * 128:(k + 1) * 128], in_=hb[:, k * 128:(k + 1) * 128], identity=IDENT),
                           reads=[hb_name, "consts"], writes=["bank%d" % tp_bank], prio=P + tr_off)
                if copy_eng == "act":
                    sch.op("act", lambda e: e.activation(out=dst_ap, in_=pT.rearrange("p (k t) -> p k t", k=8), func=AF.Copy),
                           reads=["bank%d" % tp_bank], writes=dst_names, prio=P + cp_off)
                else:
                    sch.op("dve", lambda e: e.tensor_copy(out=dst_ap, in_=pT.rearrange("p (k t) -> p k t", k=8)),
                           reads=["bank%d" % tp_bank], writes=dst_names, prio=P + cp_off)

            p0 = R_W + 48 * KB
            XS = [mem.ap(F32, p0 + i * 4 * KB, [D]) for i in range(3)]
            HB = [mem.ap(BF16, p0 + 12 * KB + i * 2 * KB, [D]) for i in range(2)]
            JUNK = mem.ap(BF16, p0 + 16 * KB, [D])
            GB1 = mem.ap(F32, p0 + 18 * KB, [D])
            SSC = mem.ap(F32, p0 + 22 * KB, [NT])
            assert p0 + 22 * KB + 4 * NT <= R_END
            g1_b = bass.AP(ln1_d.tensor, 0, [[0, 128], [1, D]])
            sch.dma("sp", "gb", lambda e: e.dma_start(out=GB1, in_=g1_b), writes=["gb"])
            for t in range(NT):
                prenorm_tile(x_d[t * 128:(t + 1) * 128, :], XS[t % 3], "xs%d" % (t % 3), "xs%d" % (t % 3), GB1,
                             HB[t % 2], "hb%d" % (t % 2), JUNK, SSC[:, t:t + 1], "ss%d" % t, t % 2,
                             hT[:, :, t * 128:(t + 1) * 128], [("hT", t)], P=float(t),
                             copy_eng=("act" if t % 2 == 0 else "dve"))
            sch.flush()
            if DEBUG:
                sch.dma("sp", "dbg", lambda e: e.dma_start(out=dbg["dbg_hT"], in_=hT.rearrange("p k t -> p (k t)")),
                        reads=[("hT", t) for t in range(NT)])
            sch.barrier()

            checkpoint("P0")
            w0 = R_W + 12 * KB
            QT = mem.ap(BF16, w0, [S]); w0 += 8 * KB
            KT = mem.ap(BF16, w0, [S]); w0 += 8 * KB
            VG = mem.ap(BF16, w0, [NT, 2, 128]); w0 += 16 * KB
            SQ = [mem.ap(BF16, w0 + i * KB, [512]) for i in range(2)]; w0 += 2 * KB
            RV = [mem.ap(F32, w0 + i * 2 * KB, [512]) for i in range(2)]; w0 += 4 * KB
            QN = [mem.ap(BF16, w0 + i * KB, [512]) for i in range(2)]; w0 += 2 * KB
            T1 = [mem.ap(F32, w0 + i * 2 * KB, [512]) for i in range(2)]; w0 += 4 * KB
            T2 = [mem.ap(F32, w0 + i * 2 * KB, [512]) for i in range(2)]; w0 += 4 * KB
            PT = [mem.ap(BF16, w0 + i * KB, [512]) for i in range(4)]; w0 += 4 * KB
            RD = [mem.ap(F32, w0 + i * 2 * KB, [512]) for i in range(2)]; w0 += 4 * KB
            PMB = [mem.ap(BF16, w0 + i * KB, [512]) for i in range(2)]; w0 += 2 * KB
            assert w0 <= R_END, w0
            ACC = mem.ap(F32, R_O + 16 * KB, [2, S])
            sch.op("pool", lambda e: e.memset(VG[:, :, 0, 64:128], 1.0), writes=["vg_ones"])
            sch.op("pool", lambda e: e.memset(VG[:, :, 1, 0:64], 1.0), writes=["vg_ones"])

            B_PJ = (0, 1)
            B_SS, B_PM, B_SA, B_SB, B_OT, B_VP = 2, 3, 4, 5, 6, 7
            cnt = {"pj": 0, "blk": 0, "pt": 0, "rd": 0, "w": 0}

            def gcol_ap(buf, d, c):
                L = S // d
                u = 512 // d
                return buf.rearrange("p (r l) -> p r l", r=d)[:, :, u * c:u * (c + 1)]

            def nat_ap(t, d):
                return t.rearrange("p (u r) -> p r u", r=d)

            def gblocks_of_chunk(d, c):
                L = S // d
                u = 512 // d
                blks = set()
                for r in range(d):
                    for col in range(r * L + u * c, r * L + u * (c + 1), min(u, 128)):
                        blks.add(col // 128)
                return sorted(blks)

            def attention_pass(pd, wslot, next_pd):
                name, d, kcol, vcol, vdup = pd["name"], pd["d"], pd["kcol"], pd["vcol"], pd["vdup"]
                gq_idx, gk_idx, mask_base, acc_mode = pd["gq"], pd["gk"], pd["mb"], pd["mode"]
                sink_heads, out_fchunk_ap = pd["sinks"], pd["out"]
                L = S // d
                bpr = L // 128
                W = WP[wslot]
                wname = "wp%d" % wslot
                nq = 2 if kcol is not None else 1
                if next_pd is not None:
                    emit_wload(next_pd, 1 - wslot, 2.0)

                def proj_qk(i, c, which):
                    P = float(i)
                    pj = B_PJ[cnt["pj"] % 2]
                    cnt["pj"] += 1
                    b = cnt["blk"] % 2
                    cnt["blk"] += 1
                    pjn = "bank%d" % pj
                    for k in range(8):
                        sch.op("pe", lambda e, k=k: e.matmul(bank_f32(pj), lhsT=W[:, k, which * 128:(which + 1) * 128],
                                                             rhs=hT[:, k, c * 512:(c + 1) * 512], start=(k == 0), stop=(k == 7)),
                               reads=[wname] + [("hT", t) for t in range(4 * c, 4 * c + 4)], writes=[pjn], prio=P)
                    sch.op("act", lambda e: e.activation(out=SQ[b], in_=bank_f32(pj), func=AF.Square), reads=[pjn], writes=["sq%d" % b],
                           prio=P + 0.02)
                    sch.op("pe", lambda e: e.matmul(bank_f32(B_SS), lhsT=BONES, rhs=SQ[b], start=True, stop=True),
                           reads=["sq%d" % b, "consts"], writes=["bank%d" % B_SS], prio=P + 1.04)
                    sch.op("act", lambda e: e.activation(out=RV[b], in_=bank_f32(B_SS), func=AF.Ln, scale=1.0 / 64, bias=EPSC),
                           reads=["bank%d" % B_SS, "consts"], writes=["rv%d" % b], prio=P + 1.06)
                    sch.op("act", lambda e: e.activation(out=RV[b], in_=RV[b], func=AF.Exp, scale=-0.5), reads=["rv%d" % b], writes=["rv%d" % b],
                           prio=P + 1.08)
                    gi = gq_idx if which == 0 else gk_idx
                    sch.op("dve", lambda e: e.scalar_tensor_tensor(out=QN[b], in0=bank_f32(pj), scalar=GAINS[:, gi:gi + 1], in1=RV[b],
                                                                   op0=ALU.mult, op1=ALU.mult),
                           reads=[pjn, "rv%d" % b, "consts"], writes=["qn%d" % b], prio=P + 1.10)
                    sch.op("pe", lambda e: e.matmul(bank_f32(B_PM), lhsT=PERM, rhs=QN[b], start=True, stop=True),
                           reads=["qn%d" % b, "consts"], writes=["bank%d" % B_PM], prio=P + 2.12)
                    sch.op("dve", lambda e: e.tensor_tensor(out=T1[b], in0=QN[b], in1=TC[:, c * 512:(c + 1) * 512], op=ALU.mult),
                           reads=["qn%d" % b, "tab1"], writes=["t1%d" % b], prio=P + 2.14)
                    if sink_heads is None:
                        sch.op("act", lambda e: e.activation(out=PMB[b], in_=bank_f32(B_PM), func=AF.Copy),
                               reads=["bank%d" % B_PM], writes=["pmb%d" % b], prio=P + 2.13)
                        sch.op("dve", lambda e: e.tensor_tensor(out=T2[b], in0=PMB[b], in1=TS[:, c * 512:(c + 1) * 512], op=ALU.mult),
                               reads=["pmb%d" % b, "tab0"], writes=["t2%d" % b], prio=P + 2.16)
                    else:
                        sch.op("dve", lambda e: e.tensor_tensor(out=T2[b], in0=bank_f32(B_PM), in1=TS[:, c * 512:(c + 1) * 512], op=ALU.mult),
                               reads=["bank%d" % B_PM, "tab0"], writes=["t2%d" % b], prio=P + 2.16)
                    dst = QT if which == 0 else KT
                    dname = "qt" if which == 0 else "kt"
                    sch.op("pool", lambda e: e.tensor_tensor(out=gcol_ap(dst, d, c), in0=nat_ap(T1[b], d), in1=nat_ap(T2[b], d), op=ALU.add),
                           reads=["t1%d" % b, "t2%d" % b], writes=[(dname, g) for g in gblocks_of_chunk(d, c)], prio=P + 2.18)

                def proj_v_batch(gbs, P):
                    vp = bank_f32(B_VP)
                    for si, gb in enumerate(gbs):
                        r, j = gb // bpr, gb % bpr
                        t0 = r + d * 128 * j
                        toks = sorted(set((t0 + d * i) // 128 for i in (0, 127)))
                        tiles = list(range(toks[0], toks[-1] + 1))
                        for k in range(8):
                            lhsT = hT[:, k, t0:t0 + d * 127 + 1:d]
                            sch.op("pe", lambda e, k=k, lhsT=lhsT, si=si: e.matmul(vp[:, si * 128:(si + 1) * 128], lhsT=lhsT, rhs=W[:, k, 256:384],
                                                                                 start=(k == 0), stop=(k == 7), skip_group_check=True),
                                   reads=[wname] + [("hT", t) for t in tiles], writes=["bank%d" % B_VP], prio=P)
                    bstride = (gbs[1] - gbs[0]) * 256
                    vg0 = VG[:, gbs[0], 0, 0:64]
                    dst = bass.AP(vg0.tensor, vg0.offset, [list(vg0.ap[0]), [bstride, 4], [192, 2], [1, 64]])
                    if vdup:
                        kvsel = (vcol - OFF_VB) // 64
                        v0 = vp[:, kvsel * 64:(kvsel + 1) * 64]
                        src = bass.AP(v0.tensor, v0.offset, [list(v0.ap[0]), [128, 4], [0, 2], [1, 64]])
                    else:
                        v0 = vp[:, 0:64]
                        src = bass.AP(v0.tensor, v0.offset, [list(v0.ap[0]), [128, 4], [64, 2], [1, 64]])
                    sch.op("act", lambda e: e.activation(out=dst, in_=src, func=AF.Copy), reads=["bank%d" % B_VP, "vg_ones"],
                           writes=[("vg", gb) for gb in gbs] + [("vgb", gb) for gb in gbs], prio=P + 0.02)

                def acc_tiles(tok0, d):
                    lo = tok0 // 512
                    hi = (tok0 + (255 if d == 1 else 1 + d * 127)) // 512
                    return list(range(lo, hi + 1))

                def round_blocks(n):
                    if d == 1:
                        return [(0, 2 * n), (0, 2 * n + 1)]
                    j, r0 = n // (d // 2), 2 * (n % (d // 2))
                    return [(r0, j), (r0 + 1, j)]

                def attn_round(n, P):
                    qblks = round_blocks(n)
                    if d == 1:
                        first = (qblks[0][1] == 0)
                        mask = MASKS[:, mask_base + (0 if first else 1), :]
                    else:
                        mask = MASKS[:, 4 if qblks[0][1] == 0 else 1, :]
                    sbanks = (B_SA, B_SB)
                    pts = []
                    for half in range(2):
                        sb = bank_f32(sbanks[half])
                        rows = slice(64 * half, 64 * half + 64)
                        for qi in range(2):
                            rq, jq = qblks[qi]
                            gq = rq * bpr + jq
                            for kb in range(2):
                                gk = gq - 1 + kb
                                if jq - 1 + kb < 0:
                                    gk = gq
                                sch.op("pe", lambda e, sb=sb, rows=rows, qi=qi, kb=kb, gk=gk, gq=gq: e.matmul(
                                    sb[:, (2 * qi + kb) * 128:(2 * qi + kb + 1) * 128], lhsT=KT[rows, gk * 128:(gk + 1) * 128],
                                    rhs=QT[rows, gq * 128:(gq + 1) * 128], start=True, stop=True),
                                    reads=[("kt", gk), ("qt", gq)], writes=["bank%d" % sbanks[half]], prio=P + 0.001 * half)
                        p = cnt["pt"] % 4
                        cnt["pt"] += 1
                        pts.append(p)
                        sch.op("act", lambda e, sb=sb, p=p: e.activation(out=PT[p], in_=sb, func=AF.Exp, scale=0.125),
                               reads=["bank%d" % sbanks[half]], writes=["pt%d" % p], prio=P + 0.03 + 0.001 * half)
                        sch.op("dve" if half == 0 else "pool", lambda e, p=p: e.tensor_tensor(out=PT[p], in0=PT[p], in1=mask, op=ALU.mult),
                               reads=["pt%d" % p, "consts"], writes=["pt%d" % p], prio=P + 0.05 + 0.001 * half)
                    ot = bank_f32(B_OT)
                    nmm = 0
                    for half in range(2):
                        for qi in range(2):
                            rq, jq = qblks[qi]
                            gq = rq * bpr + jq
                            for kb in range(2):
                                gk = gq - 1 + kb
                                if jq - 1 + kb < 0:
                                    gk = gq
                                item = 2 * half + qi
                                sch.op("pe", lambda e, half=half, qi=qi, kb=kb, gk=gk, item=item, nmm=nmm: e.matmul(
                                    ot[:, item * 128:(item + 1) * 128], lhsT=VG[:, gk, half, :],
                                    rhs=PT[pts[half]][:, (2 * qi + kb) * 128:(2 * qi + kb + 1) * 128],
                                    start=(nmm == 0), stop=(kb == 1), skip_group_check=True),
                                    reads=[("vg", gk), ("vgb", gk), "pt%d" % pts[half]], writes=["bank%d" % B_OT], prio=P + 1.01)
                                nmm += 1
                    tok0 = qblks[0][0] + d * 128 * qblks[0][1]
                    qstride = 128 if d == 1 else 1
                    otv = ot.rearrange("p (h q i) -> p h q i", h=2, q=2)
                    if sink_heads is None:
                        accv = bass.AP(ACC.tensor, ACC.offset + tok0, [list(ACC.ap[0]), [S, 2], [qstride, 2], [d, 128]])
                        if acc_mode == "copy":
                            sch.op("act", lambda e: e.activation(out=accv, in_=otv, func=AF.Copy), reads=["bank%d" % B_OT],
                                   writes=[("acc", n2) for n2 in acc_tiles(tok0, d)], prio=P + 1.03)
                        else:
                            sch.op("dve", lambda e: e.tensor_tensor(out=accv, in0=otv, in1=accv, op=ALU.add), reads=["bank%d" % B_OT],
                                   writes=[("acc", n2) for n2 in acc_tiles(tok0, d)], prio=P + 1.03)
                    else:
                        geo = []
                        for half in range(2):
                            num = slice(0, 64) if half == 0 else slice(64, 128)
                            den = slice(64, 128) if half == 0 else slice(0, 64)
                            geo.append((half, num, den, slice(256 * half, 256 * half + 256), sink_heads[half]))
                        for (half, num, den, cols, hsink) in geo:
                            sch.op("act", lambda e, half=half, num=num, den=den, cols=cols, hsink=hsink: e.activation(
                                out=RD[half][num, 0:256], in_=ot[den, cols], func=AF.Ln, bias=ESINK[num, hsink:hsink + 1]),
                                reads=["bank%d" % B_OT, "esink"], writes=["rd%d" % half, "ot_act_done"], prio=P + 1.03)
                        for (half, num, den, cols, hsink) in geo:
                            sch.op("act", lambda e, half=half, num=num: e.activation(out=RD[half][num, 0:256], in_=RD[half][num, 0:256],
                                                                                     func=AF.Exp, scale=-1.0),
                                   reads=["rd%d" % half], writes=["rd%d" % half], prio=P + 1.05)
                        for (half, num, den, cols, hsink) in geo:
                            sch.op("dve", lambda e, half=half, num=num, cols=cols: e.tensor_tensor(
                                out=out_fchunk_ap[num, tok0:tok0 + 256], in0=ot[num, cols], in1=RD[half][num, 0:256], op=ALU.mult),
                                reads=["bank%d" % B_OT, "rd%d" % half, "ot_act_done"], writes=[("ob", name, tok0)], prio=P + 1.07)

                round_prio = {}
                cluster = {}
                for n in range(16):
                    cready = max((r + d * (128 * j + 127)) // 512 for (r, j) in round_blocks(n))
                    cluster.setdefault(cready, []).append(n)
                cl = sorted(cluster)
                for ci, c in enumerate(cl):
                    span = ((cl[ci + 1] - c) if ci + 1 < len(cl) else 1) * nq
                    for k, n in enumerate(cluster[c]):
                        round_prio[n] = (c * nq + nq - 1) + 3.3 + k * max(0.5, min(1.0, float(span) / len(cluster[c])))
                assert len(round_prio) == 16
                first_round_of_block = {}
                for n in sorted(range(16), key=lambda n: (round_prio[n], n)):
                    for (r, j) in round_blocks(n):
                        for jj in (j - 1, j):
                            if jj >= 0:
                                first_round_of_block.setdefault(r * bpr + jj, n)
                for c in range(NCH):
                    proj_qk(c * nq, c, 0)
                    if kcol is not None:
                        proj_qk(c * nq + 1, c, 1)
                if kcol is not None:
                    if d == 1:
                        batches = [[4 * m + s_ for s_ in range(4)] for m in range(8)]
                    else:
                        batches = [[(4 * m + s_) * bpr + j for s_ in range(4)] for j in range(bpr) for m in range(d // 4)]
                    bprio = sorted((min(round_prio[first_round_of_block[gb]] for gb in gbs) - 0.9, gbs) for gbs in batches)
                    lastp = None
                    for (pb, gbs) in bprio:
                        if lastp is not None and pb < lastp + 0.1:
                            pb = lastp + 0.1
                        lastp = pb
                        proj_v_batch(gbs, pb)
                for n in sorted(range(16), key=lambda n: (round_prio[n], n)):
                    attn_round(n, round_prio[n])

                if acc_mode == "final":
                    for c in range(NCH):
                        cs = slice(c * 512, (c + 1) * 512)
                        for half in range(2):
                            rb = cnt["rd"] % 2
                            cnt["rd"] += 1
                            num = slice(0, 64) if half == 0 else slice(64, 128)
                            den = slice(64, 128) if half == 0 else slice(0, 64)
                            P = max(round_prio.values()) + 1.2 + 0.2 * (2 * c + half)
                            sch.op("act", lambda e, rb=rb, den=den, num=num, cs=cs, half=half: e.activation(
                                out=RD[rb][num, :], in_=ACC[den, half, cs], func=AF.Ln), reads=[("acc", c)], writes=["rd%d" % rb], prio=P)
                            sch.op("act", lambda e, rb=rb, num=num: e.activation(out=RD[rb][num, :], in_=RD[rb][num, :], func=AF.Exp, scale=-1.0),
                                   reads=["rd%d" % rb], writes=["rd%d" % rb], prio=P + 0.1)
                            sch.op("pool", lambda e, rb=rb, num=num, cs=cs, half=half: e.tensor_tensor(
                                out=out_fchunk_ap[num, cs], in0=ACC[num, half, cs], in1=RD[rb][num, :], op=ALU.mult),
                                reads=[("acc", c), "rd%d" % rb], writes=[("oa", name, c)], prio=P + 0.2)
                return max(round_prio.values()), 8 * nq

            sch.begin_defer()
            pbase = 0.0
            for pi, pd in enumerate(PASSES):
                sch.base_prio = pbase
                last_round, nsteps = attention_pass(pd, pi % 2, PASSES[pi + 1] if pi + 1 < len(PASSES) else None)
                pbase += max(float(nsteps), last_round - 2.0) + (3.4 if pd["mode"] == "final" else 0.0)
                if pd.get("barrier_after"):
                    sch.base_prio = 0.0
                    sch.flush()
                    sch.barrier()
                    checkpoint("A1")
                    sch.begin_defer()
            sch.base_prio = 0.0
            sch.flush()
            if DEBUG:
                sch.dma("sp", "dbg", lambda e: e.dma_start(out=dbg["dbg_qk"][:, 0:S], in_=QT), reads=[("qt", g) for g in range(NT)])
                sch.dma("sp", "dbg", lambda e: e.dma_start(out=dbg["dbg_qk"][:, S:2 * S], in_=KT), reads=[("kt", g) for g in range(NT)])
            sch.barrier()
            if DEBUG:
                sch.dma("sp", "dbg", lambda e: e.dma_start(out=dbg["dbg_oaT"], in_=oaT.rearrange("p k t -> p (k t)")))
                sch.dma("sp", "dbg", lambda e: e.dma_start(out=dbg["dbg_obT"], in_=obT.rearrange("p k t -> p (k t)")))

            checkpoint("A")
            w0 = R_T
            WG = mem.ap(BF16, w0, [8, 2048]); w0 += 32 * KB
            WA = mem.ap(BF16, w0, [2, D]); w0 += 4 * KB
            WB = mem.ap(BF16, w0, [4, D]); w0 += 8 * KB
            WO = mem.ap(BF16, w0, [8, D]); w0 += 16 * KB
            TA = [mem.ap(BF16, w0 + i * KB, [512]) for i in range(2)]; w0 += 2 * KB
            TB = [mem.ap(BF16, w0 + i * KB, [512]) for i in range(2)]; w0 += 2 * KB
            UU = [mem.ap(F32, w0 + i * 2 * KB, [512]) for i in range(2)]; w0 += 4 * KB
            VV = [mem.ap(F32, w0 + i * 2 * KB, [512]) for i in range(2)]; w0 += 4 * KB
            MIX = [mem.ap(BF16, w0, [8, 512]) for i in range(2)]; w0 += 8 * KB
            X5 = [mem.ap(F32, w0 + i * 4 * KB, [D]) for i in range(2)]; w0 += 8 * KB
            assert w0 <= R_END
            def wg_load(ab, q):
                lo = ab * 1024 + q * 256
                src = win_d[:, OFF_GA + lo:OFF_GA + lo + 256].rearrange("(k p) n -> p k n", p=128)
                sch.dma("pool", "wg%d_%d" % (ab, q), lambda e: e.dma_start(out=WG[:, :, lo:lo + 256], in_=src), writes=[("wg", ab, q)])
            wg_load(0, 0)
            wg_load(1, 0)
            sch.dma("pool", "wab", lambda e: e.dma_start(out=WA, in_=wa_d.rearrange("(k p) n -> p k n", p=128)), writes=["wab"])
            sch.dma("pool", "wab", lambda e: e.dma_start(out=WB, in_=wb_d.rearrange("(k p) n -> p k n", p=128)), writes=["wab"])
            wg_load(0, 1)
            wg_load(1, 1)
            sch.dma("pool", "wo", lambda e: e.dma_start(out=WO, in_=wo_d.rearrange("(k p) n -> p k n", p=128)), writes=["wo"])
            for q in (2, 3):
                wg_load(0, q)
                wg_load(1, q)
            B_GA, B_GB, B_YA, B_YB, B_O = (0, 1), (2, 3), 4, 5, (6, 7)
            WU_early = mem.ap(BF16, R_H, [8, DFF])
            n5 = {"g": 0, "o": 0, "x": 0}
            mix = MIX[0]
            mixn = "mix0"

            def p5_merge(c, m):
                P = 10.0 * c + m
                cs = slice(c * 512, (c + 1) * 512)
                gi = n5["g"] % 2
                n5["g"] += 1
                bga, bgb = B_GA[gi], B_GB[gi]

                def gate_mm(bk, ab):
                    coff = ab * 1024 + m * 128
                    for k in range(8):
                        sch.op("pe", lambda e, k=k: e.matmul(bank_f32(bk), lhsT=WG[:, k, coff:coff + 128], rhs=hT[:, k, cs],
                                                             start=(k == 0), stop=(k == 7)),
                               reads=[("wg", ab, m // 2), ("hT5", c)], writes=["bank%d" % bk], prio=P)
                gate_mm(bga, 0)
                gate_mm(bgb, 1)
                for k in range(2):
                    sch.op("pe", lambda e, k=k: e.matmul(bank_f32(B_YA), lhsT=WA[:, k, m * 128:(m + 1) * 128], rhs=oaT[:, k, cs],
                                                         start=(k == 0), stop=(k == 1)), reads=["wab"], writes=["bank%d" % B_YA], prio=P)
                for k in range(4):
                    sch.op("pe", lambda e, k=k: e.matmul(bank_f32(B_YB), lhsT=WB[:, k, m * 128:(m + 1) * 128], rhs=obT[:, k, cs],
                                                         start=(k == 0), stop=(k == 3)), reads=["wab"], writes=["bank%d" % B_YB], prio=P)
                sch.op("act", lambda e: e.activation(out=TA[gi], in_=bank_f32(bga), func=AF.Tanh, scale=0.5),
                       reads=["bank%d" % bga], writes=["ta%d" % gi], prio=P + 0.3)
                sch.op("act", lambda e: e.activation(out=TB[gi], in_=bank_f32(bgb), func=AF.Tanh, scale=0.5),
                       reads=["bank%d" % bgb], writes=["tb%d" % gi], prio=P + 0.32)
                sch.op("dve", lambda e: e.scalar_tensor_tensor(out=UU[gi], in0=TA[gi], scalar=1.0, in1=bank_f32(B_YA), op0=ALU.add, op1=ALU.mult),
                       reads=["ta%d" % gi, "bank%d" % B_YA], writes=["uu%d" % gi], prio=P + 0.5)
                sch.op("dve", lambda e: e.scalar_tensor_tensor(out=VV[gi], in0=TB[gi], scalar=1.0, in1=bank_f32(B_YB), op0=ALU.add, op1=ALU.mult),
                       reads=["tb%d" % gi, "bank%d" % B_YB], writes=["vv%d" % gi], prio=P + 0.52)
                sch.op("pool", lambda e: e.tensor_tensor(out=mix[:, m, :], in0=UU[gi], in1=VV[gi], op=ALU.add),
                       reads=["uu%d" % gi, "vv%d" % gi], writes=[(mixn, m)], prio=10.0 * c + max(m + 0.7, 1.75))

            def p5_out(c, tt):
                P = 10.0 * c + 11.2 + 0.1 * tt
                t = 4 * c + tt
                xi = n5["x"] % 2
                n5["x"] += 1
                xt = X5[xi]
                sch.dma("sp", "x5l%d" % xi, lambda e: e.dma_start(out=xt, in_=x_d[t * 128:(t + 1) * 128, :]), writes=["x5_%d" % xi],
                        prio=(P - 4.0) if tt < 2 else (P - 0.11))

                def half(hf):
                    bo = B_O[n5["o"] % 2]
                    n5["o"] += 1
                    for k in range(8):
                        sch.op("pe", lambda e, k=k: e.matmul(bank_f32(bo), lhsT=mix[:, k, tt * 128:(tt + 1) * 128],
                                                             rhs=WO[:, k, hf * 512:(hf + 1) * 512], start=(k == 0), stop=(k == 7)),
                               reads=["wo"] + [(mixn, mm) for mm in range(8)], writes=["bank%d" % bo], prio=P + 0.01 * hf)
                    sch.op("dve", lambda e: e.scalar_tensor_tensor(
                        out=xt[:, hf * 512:(hf + 1) * 512], in0=bank_f32(bo), scalar=0.5, in1=xt[:, hf * 512:(hf + 1) * 512],
                        op0=ALU.mult, op1=ALU.add), reads=["bank%d" % bo, "x5_%d" % xi], writes=["x5_%d" % xi], prio=P + 0.05 + 0.01 * hf)
                half(0)
                half(1)
                sch.dma("sp", "x5s%d" % xi, lambda e: e.dma_start(out=x1_d[t * 128:(t + 1) * 128, :], in_=xt),
                        reads=["x5_%d" % xi], writes=[("x1", t)], prio=P + 0.08)

            sch.begin_defer()
            for c in range(NCH):
                for m in range(8):
                    p5_merge(c, m)
                wsrc = wu_d[:, c * 512:(c + 1) * 512].rearrange("(k p) n -> p k n", p=128)
                sch.dma("pool", "wu%d" % c, lambda e, c=c, wsrc=wsrc: e.dma_start(out=WU_early[:, :, c * 512:(c + 1) * 512], in_=wsrc),
                        writes=[("wu", c), ("hT5", c)], prio=10.0 * c + 8.5)
                for tt in range(4):
                    p5_out(c, tt)
            sch.flush()
            sch.barrier()

            checkpoint("P5")
            w0 = 7 * KB
            WU = mem.ap(BF16, w0, [8, DFF]); w0 += 64 * KB
            WD = mem.ap(BF16, w0, [32, D]); w0 += 64 * KB
            AT = mem.ap(BF16, w0, [32, 512]); w0 += 32 * KB
            H2T = mem.ap(BF16, w0, [8, 512]); w0 += 8 * KB
            X6 = [mem.ap(F32, w0 + i * 4 * KB, [D]) for i in range(5)]; w0 += 20 * KB
            GB2 = mem.ap(F32, w0, [D]); w0 += 4 * KB
            HB6 = [mem.ap(BF16, w0 + i * 2 * KB, [D]) for i in range(2)]; w0 += 4 * KB
            RR = [mem.ap(F32, w0 + i * 2 * KB, [512]) for i in range(2)]; w0 += 4 * KB
            SS6 = mem.ap(F32, 6 * KB + 256, [NT])
            assert w0 <= R_END, w0
            g2_b = bass.AP(ln2_d.tensor, 0, [[0, 128], [1, D]])
            sch.dma("sp", "gb", lambda e: e.dma_start(out=GB2, in_=g2_b), writes=["gb"])
            for piece in range(8):
                src = wd_d[piece * 512:(piece + 1) * 512, :].rearrange("(k p) n -> p k n", p=128)
                sch.dma("pool", "wd%d" % piece, lambda e, piece=piece, src=src: e.dma_start(out=WD[:, piece * 4:(piece + 1) * 4, :], in_=src),
                        writes=[("wd", piece)])
            B_TP, B_U, B_D = (0, 1), (2, 3, 4), (5, 6, 7)
            n6 = {"x": 0, "u": 0, "d": 0, "r": 0}
            NSL, FSL = X6[0:3], X6[3:5]
            sch.begin_defer()

            def p6_prenorm(tb, P0):
                for tt in range(4):
                    t = 4 * tb + tt
                    xi = n6["x"] % 3
                    n6["x"] += 1
                    hbi = t % 2
                    prenorm_tile(x1_d[t * 128:(t + 1) * 128, :], NSL[xi], "x6n_%d" % xi, "x6nl%d" % xi, GB2, HB6[hbi], "hb6_%d" % hbi, HB6[hbi],
                                 SS6[:, t:t + 1], "ss6_%d" % t, B_TP[t % 2], H2T[:, :, tt * 128:(tt + 1) * 128], [("h2t", tt)],
                                 junk_name="hb6_%d" % hbi, P=P0 + 10.0 * tt, dma_off=-6.0, tr_off=1.5, cp_off=3.5)

            def p6_up(f, P):
                bu = B_U[n6["u"] % 3]
                n6["u"] += 1
                ri = n6["r"] % 2
                n6["r"] += 1
                for k in range(8):
                    sch.op("pe", lambda e, k=k: e.matmul(bank_f32(bu), lhsT=WU[:, k, f * 128:(f + 1) * 128], rhs=H2T[:, k, :],
                                                         start=(k == 0), stop=(k == 7)),
                           reads=[("wu", f // 4)] + [("h2t", tt) for tt in range(4)], writes=["bank%d" % bu], prio=P)
                sch.op("act", lambda e: e.activation(out=RR[ri], in_=bank_f32(bu), func=AF.Relu), reads=["bank%d" % bu], writes=["rr%d" % ri],
                       prio=P + 0.3)
                sch.op("dve", lambda e: e.tensor_tensor(out=AT[:, f, :], in0=RR[ri], in1=RR[ri], op=ALU.mult),
                       reads=["rr%d" % ri], writes=[("at", f)], prio=P + 0.6)

            def p6_down(t, tt, B):
                fi = t % 2
                xt = FSL[fi]
                sch.dma("sp", "x6fl%d" % fi, lambda e: e.dma_start(out=xt, in_=x1_d[t * 128:(t + 1) * 128, :]), writes=["x6f_%d" % fi],
                        prio=B + 40 + 10 * tt - 9)

                def half(hf):
                    g = 2 * tt + hf
                    bd = B_D[n6["d"] % 3]
                    n6["d"] += 1
                    for f in range(32):
                        sch.op("pe", lambda e, f=f: e.matmul(bank_f32(bd), lhsT=AT[:, f, tt * 128:(tt + 1) * 128],
                                                             rhs=WD[:, f, hf * 512:(hf + 1) * 512], start=(f == 0), stop=(f == 31)),
                               reads=[("wd", f // 4), ("at", f)], writes=["bank%d" % bd], prio=B + 40 + 5 * g)
                    sch.op("dve", lambda e: e.tensor_tensor(out=xt[:, hf * 512:(hf + 1) * 512], in0=bank_f32(bd),
                                                            in1=xt[:, hf * 512:(hf + 1) * 512], op=ALU.add),
                           reads=["bank%d" % bd, "x6f_%d" % fi], writes=["x6f_%d" % fi], prio=B + 40 + 5 * g + 4.5)
                half(0)
                half(1)
                sch.dma("sp", "x6fs%d" % fi, lambda e: e.dma_start(out=out_d[t * 128:(t + 1) * 128, :], in_=xt),
                        reads=["x6f_%d" % fi], writes=[("out", t)], prio=B + 40 + 5 * (2 * tt + 1) + 4.6)

            p6_prenorm(0, -50.0)
            for tb in range(NCH):
                B = 100.0 * tb
                for f in range(32):
                    p6_up(f, B + f)
                if tb + 1 < NCH:
                    p6_prenorm(tb + 1, B + 41.0)
                for tt in range(4):
                    p6_down(4 * tb + tt, tt, B)
            sch.flush()

        try:
            emit_all()
        except _Stop:
            sch.barrier()
        sch.final_wait("sp", ["x6fs%d" % i for i in range(2)] + (["dbg"] if DEBUG else []))

        sch.finalize()
        block = es.enter_context(nc.Block())

        @block.sync
        def _(e):
            sch.replay("sp", e)

        @block.gpsimd
        def _(e):
            sch.replay("pool", e)

        @block.scalar
        def _(e):
            sch.replay("act", e)

        @block.vector
        def _(e):
            sch.replay("dve", e)

        @block.tensor
        def _(e):
            sch.replay("pe", e)
    return nc


_CACHE = {}


def kernel(x, positions, ln1_g, w_in, q_norm_a, k_norm_a, q_norm_b, k_norm_b, sinks,
           w_branch_a, w_branch_b, w_out, ln2_g, w_up, w_down):
    if "nc" not in _CACHE:
        _CACHE["nc"] = build_program()
    nc = _CACHE["nc"]
    cst = host_consts()
    f32 = lambda a: np.ascontiguousarray(np.asarray(a), dtype=np.float32)
    shared = {
        "cst": cst,
        "ln1_g": f32(ln1_g), "ln2_g": f32(ln2_g), "w_in": f32(w_in)[0],
        "q_norm_a": f32(q_norm_a), "k_norm_a": f32(k_norm_a), "q_norm_b": f32(q_norm_b), "k_norm_b": f32(k_norm_b),
        "sinks": f32(sinks), "w_branch_a": f32(w_branch_a)[0], "w_branch_b": f32(w_branch_b)[0],
        "w_out": f32(w_out)[0], "w_up": f32(w_up)[0], "w_down": f32(w_down)[0],
    }
    xs = f32(x)
    ps = np.ascontiguousarray(np.asarray(positions), dtype=np.int32)
    in_maps = []
    for b in range(8):
        m = dict(shared)
        m["x"] = xs[b]
        m["pos"] = ps[b:b + 1]
        in_maps.append(m)
    res = run_bass_kernel_spmd(nc, in_maps, core_ids=list(range(8)))
    _CACHE["last"] = res
    out = np.stack([np.asarray(r["out"], dtype=np.float32) for r in res.results], axis=0)
    return out
```
